# Optimizing a Trainium2 kernel written in Bass

```python
import math
import jax
import jax.numpy as jnp
from jax import lax
import numpy as np

D_MODEL = 1024
BATCH = 8
SEQ = 4096
DEPTH = 4

N_MIXERS = 4
NORM_EPS = 1e-6
NEG_INF = -1e30
F32 = jnp.float32

MOBA_HEADS = 16
MOBA_HEAD_DIM = D_MODEL // MOBA_HEADS
MOBA_BLOCK = 256
MOBA_TOPK = 3
MOBA_Q_CHUNK = 16

RWKV_HEAD_DIM = 64
RWKV_HEADS = D_MODEL // RWKV_HEAD_DIM
RWKV_DECAY_LORA = 64
RWKV_AAA_LORA = 64
RWKV_GATE_LORA = 160
RWKV_GN_EPS = 64e-5
RWKV_N_MIX = 6

SSD_D_INNER = 2 * D_MODEL
SSD_HEAD_DIM = 64
SSD_HEADS = SSD_D_INNER // SSD_HEAD_DIM
SSD_GROUPS = 4
SSD_STATE = 128
SSD_CONV = 4
SSD_CHUNK = 128
SSD_CONV_DIM = SSD_D_INNER + 2 * SSD_GROUPS * SSD_STATE
SSD_IN_DIM = SSD_D_INNER + SSD_CONV_DIM + SSD_HEADS
SSD_NORM_EPS = 1e-5

CONF_KERNEL = 31
CONF_LN_EPS = 1e-5

D_FF = 2816
FFN_CONV = 3

kernel_name = 'hybrid_moba_rwkv7_ssd_conformer'


def _rmsnorm(x, g):
    x32 = x.astype(F32)
    y = x32 * lax.rsqrt(jnp.mean(x32 * x32, axis=-1, keepdims=True) + NORM_EPS)
    return (y * g.astype(F32)).astype(x.dtype)


def _layernorm(x, w, b, eps):
    x32 = x.astype(F32)
    mu = jnp.mean(x32, axis=-1, keepdims=True)
    var = jnp.mean(jnp.square(x32 - mu), axis=-1, keepdims=True)
    return ((x32 - mu) * lax.rsqrt(var + eps) * w.astype(F32) + b.astype(F32)).astype(x.dtype)


def _causal_dwconv(x, w, b):
    k, c = w.shape
    y = lax.conv_general_dilated(x, w[:, None, :], window_strides=(1,), padding=[(k - 1, 0)],
                                 dimension_numbers=('NWC', 'WIO', 'NWC'), feature_group_count=c)
    return y + b


def _moba(x, w_qkv, w_o):
    b, s, d = x.shape
    h, dh, bs, qc = MOBA_HEADS, MOBA_HEAD_DIM, MOBA_BLOCK, MOBA_Q_CHUNK
    qkv = jnp.einsum('bsd,de->bse', x, w_qkv).reshape(b, s, 3, h, dh)
    q = jnp.transpose(qkv[:, :, 0], (0, 2, 1, 3)) * dh ** -0.5
    k = jnp.transpose(qkv[:, :, 1], (0, 2, 1, 3))
    v = jnp.transpose(qkv[:, :, 2], (0, 2, 1, 3))
    n_blk = -(-s // bs)
    pad = n_blk * bs - s
    kb = jnp.pad(k, ((0, 0), (0, 0), (0, pad), (0, 0))).reshape(b, h, n_blk, bs, dh)
    vb = jnp.pad(v, ((0, 0), (0, 0), (0, pad), (0, 0))).reshape(b, h, n_blk, bs, dh)
    k_mean = jnp.mean(kb, axis=3)
    q_blk = jnp.arange(s) // bs
    gate = jnp.einsum('bhsd,bhnd->bhsn', q, k_mean).astype(F32)
    fully_past = jnp.arange(n_blk)[None, :] < q_blk[:, None]
    gate = jnp.where(fully_past, gate, NEG_INF)
    n_sel = min(MOBA_TOPK, n_blk)
    _, sel = lax.top_k(gate, n_sel)
    sel_valid = sel < q_blk[:, None]
    bi = jnp.arange(b)[:, None, None, None]
    hi = jnp.arange(h)[None, :, None, None]

    def chunk(c):
        start = c * qc
        q_c = lax.dynamic_slice_in_dim(q, start, qc, axis=2)
        sel_c = lax.dynamic_slice_in_dim(sel, start, qc, axis=2)
        val_c = lax.dynamic_slice_in_dim(sel_valid, start, qc, axis=2)
        k_sel = kb[bi, hi, sel_c]
        v_sel = vb[bi, hi, sel_c]
        own = start // bs
        k_own = lax.dynamic_index_in_dim(kb, own, axis=2, keepdims=False)
        v_own = lax.dynamic_index_in_dim(vb, own, axis=2, keepdims=False)
        s_sel = jnp.einsum('bhqd,bhqnkd->bhqnk', q_c, k_sel).astype(F32)
        s_sel = jnp.where(val_c[..., None], s_sel, NEG_INF).reshape(b, h, qc, n_sel * bs)
        s_own = jnp.einsum('bhqd,bhkd->bhqk', q_c, k_own).astype(F32)
        q_pos = start + jnp.arange(qc)
        k_pos = own * bs + jnp.arange(bs)
        s_own = jnp.where(k_pos[None, :] <= q_pos[:, None], s_own, NEG_INF)
        p = jax.nn.softmax(jnp.concatenate([s_sel, s_own], axis=-1), axis=-1).astype(v.dtype)
        p_sel = p[..., :n_sel * bs].reshape(b, h, qc, n_sel, bs)
        p_own = p[..., n_sel * bs:]
        return (jnp.einsum('bhqnk,bhqnkd->bhqd', p_sel, v_sel)
                + jnp.einsum('bhqk,bhkd->bhqd', p_own, v_own))

    o = lax.map(chunk, jnp.arange(s // qc))
    o = jnp.transpose(o, (1, 0, 3, 2, 4)).reshape(b, s, d)
    return o @ w_o


def _rwkv7(x, mu, w_rkv, w0, w1, w2, a0, a1, a2, g1, g2, k_k, k_a, r_k, gn_w, gn_b, w_o):
    b, s, d = x.shape
    h, n = RWKV_HEADS, RWKV_HEAD_DIM
    xx = jnp.pad(x, ((0, 0), (1, 0), (0, 0)))[:, :-1] - x
    xm = x[:, :, None, :] + xx[:, :, None, :] * mu
    rkv = jnp.einsum('bsid,ide->bsie', xm[:, :, :3], w_rkv)
    r, k, v = rkv[:, :, 0], rkv[:, :, 1], rkv[:, :, 2]
    xw, xa, xg = xm[:, :, 3], xm[:, :, 4], xm[:, :, 5]
    w = -jax.nn.softplus(-(w0 + jnp.tanh(xw @ w1) @ w2).astype(F32)) - 0.5
    decay = jnp.exp(-jnp.exp(w))
    a = jax.nn.sigmoid(a0 + (xa @ a1) @ a2)
    g = jax.nn.sigmoid(xg @ g1) @ g2
    kk = (k * k_k).reshape(b, s, h, n).astype(F32)
    kk = kk / jnp.maximum(jnp.sqrt(jnp.sum(kk * kk, axis=-1, keepdims=True)), 1e-12)
    k = k * (1.0 + (a - 1.0) * k_a)
    r, k, v, a = (t.reshape(b, s, h, n) for t in (r, k, v, a))

    def to_time(t):
        return jnp.swapaxes(t.astype(F32), 0, 1)

    xs = (to_time(r), to_time(decay.reshape(b, s, h, n)), to_time(k), to_time(v),
          to_time(-kk), to_time(kk * a))

    def step(state, inp):
        r_t, w_t, k_t, v_t, a_t, b_t = inp
        sa = jnp.einsum('bhij,bhj->bhi', state, a_t)
        state = (state * w_t[:, :, None, :] + sa[..., None] * b_t[:, :, None, :]
                 + v_t[..., None] * k_t[:, :, None, :])
        return state, jnp.einsum('bhij,bhj->bhi', state, r_t)

    _, y = lax.scan(step, jnp.zeros((b, h, n, n), F32), xs)
    y = jnp.swapaxes(y, 0, 1)
    mu_y = jnp.mean(y, axis=-1, keepdims=True)
    var_y = jnp.mean(jnp.square(y - mu_y), axis=-1, keepdims=True)
    yn = ((y - mu_y) * lax.rsqrt(var_y + RWKV_GN_EPS)).reshape(b, s, d)
    yn = yn * gn_w.astype(F32) + gn_b.astype(F32)
    bonus = (jnp.sum(r * k * r_k, axis=-1, keepdims=True) * v).reshape(b, s, d).astype(F32)
    return ((yn + bonus).astype(x.dtype) * g) @ w_o


def _segsum(a):
    t = a.shape[-1]
    rep = jnp.broadcast_to(a[..., :, None], a.shape + (t,))
    rep = jnp.where(jnp.tril(jnp.ones((t, t), bool), -1), rep, 0.0)
    cs = jnp.cumsum(rep, axis=-2)
    return jnp.where(jnp.tril(jnp.ones((t, t), bool)), cs, -jnp.inf)


def _ssd_chunked(xd, adt, bm, cm):
    b, s, h, p = xd.shape
    g, n = bm.shape[2], bm.shape[3]
    e = h // g
    lc = SSD_CHUNK
    c = s // lc
    xc = xd.reshape(b, c, lc, g, e, p)
    bc = bm.reshape(b, c, lc, g, n)
    cc = cm.reshape(b, c, lc, g, n)
    ac = jnp.transpose(adt.reshape(b, c, lc, g, e), (0, 3, 4, 1, 2))
    a_cs = jnp.cumsum(ac, axis=-1)
    lmat = jnp.exp(_segsum(ac))
    cb = jnp.einsum('bclgn,bcsgn->bgcls', cc, bc)
    y_diag = jnp.einsum('bgecls,bcsgep->bclgep', cb[:, :, None] * lmat, xc)
    decay_states = jnp.exp(a_cs[..., -1:] - a_cs)
    states = jnp.einsum('bclgn,bgecl,bclgep->bcgepn', bc, decay_states, xc)
    states = jnp.concatenate([jnp.zeros_like(states[:, :1]), states], axis=1)
    chunk_a = jnp.pad(a_cs[..., -1], ((0, 0), (0, 0), (0, 0), (1, 0)))
    decay_chunk = jnp.exp(_segsum(chunk_a))
    states = jnp.einsum('bgezc,bcgepn->bzgepn', decay_chunk, states)[:, :-1]
    y_off = jnp.einsum('bclgn,bcgepn,bgecl->bclgep', cc, states, jnp.exp(a_cs))
    return (y_diag + y_off).reshape(b, s, h, p)


def _mamba2(x, w_in, conv_w, conv_b, dt_bias, a_log, d_skip, norm_w, w_out):
    b, s, _ = x.shape
    di, h, p, g, n = SSD_D_INNER, SSD_HEADS, SSD_HEAD_DIM, SSD_GROUPS, SSD_STATE
    zxbcdt = x @ w_in
    z, xbc, dt = jnp.split(zxbcdt, [di, di + SSD_CONV_DIM], axis=-1)
    xbc = jax.nn.silu(_causal_dwconv(xbc, conv_w, conv_b))
    xs, bm, cm = jnp.split(xbc, [di, di + g * n], axis=-1)
    dt = jax.nn.softplus((dt + dt_bias).astype(F32))
    a = -jnp.exp(a_log.astype(F32))
    xh = xs.reshape(b, s, h, p).astype(F32)
    y = _ssd_chunked(xh * dt[..., None], dt * a,
                     bm.reshape(b, s, g, n).astype(F32), cm.reshape(b, s, g, n).astype(F32))
    y = (y + xh * d_skip.astype(F32)[:, None]).reshape(b, s, di)
    yz = (y * jax.nn.silu(z.astype(F32))).reshape(b, s, g, di // g)
    yz = yz * lax.rsqrt(jnp.mean(yz * yz, axis=-1, keepdims=True) + SSD_NORM_EPS)
    yz = yz.reshape(b, s, di) * norm_w.astype(F32)
    return yz.astype(x.dtype) @ w_out


def _conformer_conv(x, w_pw1, b_pw1, dw_w, dw_b, ln_w, ln_b, w_pw2, b_pw2):
    u = jax.nn.glu(x @ w_pw1 + b_pw1, axis=-1)
    u = _causal_dwconv(u, dw_w, dw_b)
    u = jax.nn.silu(_layernorm(u, ln_w, ln_b, CONF_LN_EPS))
    return u @ w_pw2 + b_pw2


def _conv_ffn(x, w_up, conv_w, conv_b, w_down):
    u = _causal_dwconv(x @ w_up, conv_w, conv_b)
    gate, up = jnp.split(u, 2, axis=-1)
    return (jax.nn.silu(gate) * up) @ w_down


def _count(kind):
    return len(range(kind, DEPTH, N_MIXERS))


def setup_inputs(seed: int = 0) -> dict:
    key = jax.random.key(seed)
    ks = iter(jax.random.split(key, 64))

    def nrm(shape, scale):
        return jax.random.normal(next(ks), shape, F32) * scale

    def uni(shape, lo, hi):
        return jax.random.uniform(next(ks), shape, F32, lo, hi)

    d = D_MODEL
    n_a, n_b, n_c, n_d = (_count(m) for m in range(N_MIXERS))
    dt0 = jnp.exp(uni((n_c, SSD_HEADS), math.log(1e-3), math.log(1e-1)))
    return {
        'x': nrm((BATCH, SEQ, d), 1.0),
        'norm_mix': 1.0 + nrm((DEPTH, d), 0.02),
        'norm_ffn': 1.0 + nrm((DEPTH, d), 0.02),
        'norm_final': 1.0 + nrm((d,), 0.02),
        'ffn_w_up': nrm((DEPTH, d, 2 * D_FF), d ** -0.5),
        'ffn_conv_w': nrm((DEPTH, FFN_CONV, 2 * D_FF), FFN_CONV ** -0.5),
        'ffn_conv_b': nrm((DEPTH, 2 * D_FF), 0.01),
        'ffn_w_down': nrm((DEPTH, D_FF, d), D_FF ** -0.5),
        'moba_w_qkv': nrm((n_a, d, 3 * d), d ** -0.5),
        'moba_w_o': nrm((n_a, d, d), d ** -0.5),
        'rwkv_mu': uni((n_b, RWKV_N_MIX, d), 0.0, 1.0),
        'rwkv_w_rkv': nrm((n_b, 3, d, d), d ** -0.5),
        'rwkv_w0': uni((n_b, d), -6.0, 1.0),
        'rwkv_w1': nrm((n_b, d, RWKV_DECAY_LORA), d ** -0.5),
        'rwkv_w2': nrm((n_b, RWKV_DECAY_LORA, d), 0.1 * RWKV_DECAY_LORA ** -0.5),
        'rwkv_a0': nrm((n_b, d), 0.1),
        'rwkv_a1': nrm((n_b, d, RWKV_AAA_LORA), d ** -0.5),
        'rwkv_a2': nrm((n_b, RWKV_AAA_LORA, d), 0.1 * RWKV_AAA_LORA ** -0.5),
        'rwkv_g1': nrm((n_b, d, RWKV_GATE_LORA), d ** -0.5),
        'rwkv_g2': nrm((n_b, RWKV_GATE_LORA, d), RWKV_GATE_LORA ** -0.5),
        'rwkv_k_k': 0.85 + nrm((n_b, d), 0.02),
        'rwkv_k_a': 1.0 + nrm((n_b, d), 0.02),
        'rwkv_r_k': nrm((n_b, RWKV_HEADS, RWKV_HEAD_DIM), 0.1),
        'rwkv_gn_w': 1.0 + nrm((n_b, d), 0.02),
        'rwkv_gn_b': nrm((n_b, d), 0.01),
        'rwkv_w_o': nrm((n_b, d, d), d ** -0.5),
        'ssd_w_in': nrm((n_c, d, SSD_IN_DIM), d ** -0.5),
        'ssd_conv_w': nrm((n_c, SSD_CONV, SSD_CONV_DIM), SSD_CONV ** -0.5),
        'ssd_conv_b': nrm((n_c, SSD_CONV_DIM), 0.01),
        'ssd_dt_bias': dt0 + jnp.log(-jnp.expm1(-dt0)),
        'ssd_a_log': jnp.log(uni((n_c, SSD_HEADS), 1.0, 16.0)),
        'ssd_d': 1.0 + nrm((n_c, SSD_HEADS), 0.1),
        'ssd_norm_w': 1.0 + nrm((n_c, SSD_D_INNER), 0.02),
        'ssd_w_out': nrm((n_c, SSD_D_INNER, d), SSD_D_INNER ** -0.5),
        'conf_w_pw1': nrm((n_d, d, 2 * d), d ** -0.5),
        'conf_b_pw1': nrm((n_d, 2 * d), 0.01),
        'conf_dw_w': nrm((n_d, CONF_KERNEL, d), CONF_KERNEL ** -0.5),
        'conf_dw_b': nrm((n_d, d), 0.01),
        'conf_ln_w': 1.0 + nrm((n_d, d), 0.02),
        'conf_ln_b': nrm((n_d, d), 0.01),
        'conf_w_pw2': nrm((n_d, d, d), d ** -0.5),
        'conf_b_pw2': nrm((n_d, d), 0.01),
    }


def reference(x, norm_mix, norm_ffn, norm_final, ffn_w_up, ffn_conv_w, ffn_conv_b, ffn_w_down,
              moba_w_qkv, moba_w_o,
              rwkv_mu, rwkv_w_rkv, rwkv_w0, rwkv_w1, rwkv_w2, rwkv_a0, rwkv_a1, rwkv_a2,
              rwkv_g1, rwkv_g2, rwkv_k_k, rwkv_k_a, rwkv_r_k, rwkv_gn_w, rwkv_gn_b, rwkv_w_o,
              ssd_w_in, ssd_conv_w, ssd_conv_b, ssd_dt_bias, ssd_a_log, ssd_d, ssd_norm_w, ssd_w_out,
              conf_w_pw1, conf_b_pw1, conf_dw_w, conf_dw_b, conf_ln_w, conf_ln_b, conf_w_pw2, conf_b_pw2):
    for i in range(DEPTH):
        kind, j = i % N_MIXERS, i // N_MIXERS
        hn = _rmsnorm(x, norm_mix[i])
        if kind == 0:
            mix = _moba(hn, moba_w_qkv[j], moba_w_o[j])
        elif kind == 1:
            mix = _rwkv7(hn, rwkv_mu[j], rwkv_w_rkv[j], rwkv_w0[j], rwkv_w1[j], rwkv_w2[j],
                         rwkv_a0[j], rwkv_a1[j], rwkv_a2[j], rwkv_g1[j], rwkv_g2[j],
                         rwkv_k_k[j], rwkv_k_a[j], rwkv_r_k[j], rwkv_gn_w[j], rwkv_gn_b[j], rwkv_w_o[j])
        elif kind == 2:
            mix = _mamba2(hn, ssd_w_in[j], ssd_conv_w[j], ssd_conv_b[j], ssd_dt_bias[j],
                          ssd_a_log[j], ssd_d[j], ssd_norm_w[j], ssd_w_out[j])
        else:
            mix = _conformer_conv(hn, conf_w_pw1[j], conf_b_pw1[j], conf_dw_w[j], conf_dw_b[j],
                                  conf_ln_w[j], conf_ln_b[j], conf_w_pw2[j], conf_b_pw2[j])
        x = x + mix
        x = x + _conv_ffn(_rmsnorm(x, norm_ffn[i]), ffn_w_up[i], ffn_conv_w[i], ffn_conv_b[i], ffn_w_down[i])
    return _rmsnorm(x, norm_final)
```

```python
import numpy as np
from contextlib import ExitStack
import ml_dtypes
import concourse.bass as bass
import concourse.mybir as mybir
from concourse.bass_utils import run_bass_kernel_spmd

F32 = mybir.dt.float32
BF16 = mybir.dt.bfloat16
AF = mybir.ActivationFunctionType
ALU = mybir.AluOpType
AX = mybir.AxisListType

S = 4096
D = 1024
DFF = 2816
NT = S // 128
DC = D // 128
FC = DFF // 128
EPS = 1e-6
ENGS = ("pe", "act", "dve", "pool", "sp")
STRICT = ("act", "dve", "pool")
EPOCH = 16384


class Buf:
    __slots__ = ("w", "r", "name")

    def __init__(self, name=""):
        self.w = None
        self.r = {}
        self.name = name


class Prog:
    def __init__(self):
        self.nc = bass.Bass("TRN2", target_bir_lowering=False)
        self.es = ExitStack()
        self.streams = {e: [] for e in ENGS}
        self.cnt = {e: 0 for e in ENGS}
        self.seen = {e: {} for e in ENGS}
        self.sems = {}
        self.epochs = set()
        self.nbuf = 0

    def sb(self, name, shape, dt):
        return self.es.enter_context(self.nc.sbuf_tensor(name, list(shape), dt))

    def ps(self, name, shape, dt):
        return self.es.enter_context(self.nc.psum_tensor(name, list(shape), dt))

    def dram(self, name, shape, dt, kind="Internal"):
        return self.nc.dram_tensor(name, list(shape), dt, kind=kind).ap()

    def dsem(self, name):
        k = ("d", name)
        assert k not in self.cnt
        self.cnt[k] = 0
        return k

    def buf(self, name=""):
        return Buf(name)

    def bufs(self, n, name=""):
        return [Buf(name + str(i)) for i in range(n)]

    def _dep(self, eng, dep):
        if dep is None:
            return
        k, v = dep
        if k == eng and eng not in STRICT:
            return
        if self.seen[eng].get(k, 0) >= v:
            return
        self.seen[eng][k] = v
        if k in ENGS:
            ep = (v - 1) // EPOCH
            self.epochs.add((k, ep))
            self.streams[eng].append(("w", (k, ep), (v - 1) % EPOCH + 1))
        else:
            self.streams[eng].append(("w", k, v))

    def _deps(self, eng, reads, writes):
        for b in reads:
            self._dep(eng, b.w)
        for b in writes:
            self._dep(eng, b.w)
            for k, v in b.r.items():
                self._dep(eng, (k, v))

    def op(self, eng, fn, reads=(), writes=()):
        self._deps(eng, reads, writes)
        self.cnt[eng] += 1
        n = self.cnt[eng]
        self.epochs.add((eng, (n - 1) // EPOCH))
        self.streams[eng].append(("o", fn, (eng, (n - 1) // EPOCH), 1))
        for b in reads:
            b.r[eng] = n
        for b in writes:
            b.w = (eng, n)
            b.r = {}

    def dma(self, q, sem, out, in_, reads=(), writes=(), **kw):
        self._deps(q, reads, writes)
        self.cnt[sem] += 16
        n = self.cnt[sem]
        self.streams[q].append(("o", lambda e: e.dma_start(out=out, in_=in_, **kw), sem, 16))
        for b in reads:
            b.r[sem] = n
        for b in writes:
            b.w = (sem, n)
            b.r = {}

    def mm(self, out, lhsT, rhs, start, stop, reads=(), writes=()):
        self.op("pe", lambda e: e.matmul(out, lhsT, rhs, start=start, stop=stop), reads, writes)

    def tr(self, out, in_, ident, reads=(), writes=()):
        self.op("pe", lambda e: e.transpose(out, in_, ident), reads, writes)

    def act(self, out, in_, func, reads=(), writes=(), **kw):
        self.op("act", lambda e: e.activation(out, in_, func, **kw), reads, writes)

    def tt(self, eng, out, in0, in1, op, reads=(), writes=()):
        self.op(eng, lambda e: e.tensor_tensor(out, in0, in1, op), reads, writes)

    def ts(self, eng, out, in0, s1, s2, op0, op1=None, reads=(), writes=()):
        if op1 is None:
            self.op(eng, lambda e: e.tensor_scalar(out, in0, s1, None, op0), reads, writes)
        else:
            self.op(eng, lambda e: e.tensor_scalar(out, in0, s1, s2, op0, op1), reads, writes)

    def stt(self, eng, out, in0, scalar, in1, op0, op1, reads=(), writes=()):
        self.op(eng, lambda e: e.scalar_tensor_tensor(out, in0, scalar, in1, op0, op1), reads, writes)

    def copy(self, eng, out, in_, reads=(), writes=()):
        if eng == "act":
            self.op(eng, lambda e: e.copy(out, in_), reads, writes)
        else:
            self.op(eng, lambda e: e.tensor_copy(out, in_), reads, writes)

    def memset(self, eng, ap, val, writes=()):
        self.op(eng, lambda e: e.memset(ap, val), (), writes)

    def barrier(self):
        for e in ENGS:
            for k, v in self.cnt.items():
                if v > 0 and k != e:
                    self._dep(e, (k, v))

    def finish(self):
        nc = self.nc
        for k, v in self.cnt.items():
            if v > 0 and k != "sp":
                self._dep("sp", (k, v))
        for k in self.cnt:
            if k not in ENGS:
                self.sems[k] = self.es.enter_context(nc.semaphore("s_" + k[1]))
        for (k, ep) in sorted(self.epochs):
            self.sems[(k, ep)] = self.es.enter_context(nc.semaphore(f"e_{k}_{ep}"))
        streams, sems = self.streams, self.sems

        def replay(name, e):
            for it in streams[name]:
                if it[0] == "w":
                    e.wait_ge(sems[it[1]], it[2])
                else:
                    it[1](e).then_inc(sems[it[2]], it[3])

        with nc.Block() as block:
            @block.tensor
            def _(e):
                replay("pe", e)

            @block.scalar
            def _(e):
                replay("act", e)

            @block.vector
            def _(e):
                replay("dve", e)

            @block.gpsimd
            def _(e):
                replay("pool", e)

            @block.sync
            def _(e):
                replay("sp", e)
        self.es.close()
        return nc


class Arena:
    def __init__(self, P, name, nbytes):
        self.P = P
        self.t = P.sb(name, [128, nbytes // 2], BF16)
        self.n = nbytes // 2
        self.off = 0

    def reset(self, to=0):
        self.P.barrier()
        self.off = to

    def alloc(self, shape, dt):
        ne = 1
        for d in shape[1:]:
            ne *= d
        if dt == F32:
            ne *= 2
        ne = (ne + 15) // 16 * 16
        assert self.off + ne <= self.n, (self.off, ne, self.n)
        ap = self.t[0:shape[0], self.off:self.off + ne]
        self.off += ne
        if dt == F32:
            ap = ap.bitcast(F32)
        nfree = 1
        for d in shape[1:]:
            nfree *= d
        ap = ap[:, 0:nfree]
        if len(shape) == 3:
            ap = ap.rearrange("p (a b) -> p a b", a=shape[1])
        elif len(shape) == 4:
            ap = ap.rearrange("p (a b c) -> p a b c", a=shape[1], b=shape[2])
        return ap


class Model:
    def __init__(self, layers=(0, 1, 2, 3), do_mix=True, do_ffn=True):
        self.P = P = Prog()
        self.layers = layers
        self.do_mix = do_mix
        self.do_ffn = do_ffn
        d = lambda n, sh, dt=F32: P.dram(n, sh, dt, "ExternalInput")
        self.x_in = d("x", [S, D])
        self.out = P.dram("out", [S, D], F32, "ExternalOutput")
        self.ident_d = d("ident", [128, 128], BF16)
        self.norm_mix = d("norm_mix", [128, 4, DC])
        self.norm_ffn = d("norm_ffn", [128, 4, DC])
        self.norm_final = d("norm_final", [128, D])
        self.ffn_w_up = d("ffn_w_up", [4, D, 2 * DFF])
        self.ffn_w_down = d("ffn_w_down", [4, DFF, D])
        self.ffn_cw = d("ffn_cw", [128, 4, 2 * FC, 3])
        self.ffn_cb = d("ffn_cb", [128, 4, 2 * FC])
        self.conf_w1 = d("conf_w_pw1", [D, 2 * D])
        self.conf_w2 = d("conf_w_pw2", [D, D])
        self.conf_p = d("conf_p", [128, 16 + 8 * 31 + 8 + 8 + 8])
        self.conf_b2 = d("conf_b2", [128, D])
        self.moba_wqkv = d("moba_w_qkv", [D, 3 * D])
        self.moba_wo = d("moba_w_o", [D, D])
        self.blkind = d("blkind", [16, S], BF16)
        self.mconst = d("mconst", [128, 3, 16, 16])
        self.tri_d = d("tri", [128, 128], BF16)
        self.ssd_win = d("ssd_w_in", [D, 5152])
        self.ssd_wout = d("ssd_w_out", [2 * D, D])
        self.ssd_p = d("ssd_p", [128, 24 * 5])
        self.ssd_t = d("ssd_t", [128, 96])
        self.ssd_nw = d("ssd_nw", [128, 2 * D])
        self.nb_d = d("nbmask", [128, 4, 128], BF16)
        self.s_xB = P.dram("s_xB", [S, 2560], BF16)
        self.s_xB_b = P.bufs(8, "sxB")
        self.s_BCT = P.dram("s_BCT", [8, 128, S], BF16)
        self.s_BCT_b = P.bufs(8, "sBCT")
        self.s_z = P.dram("s_z", [S, 2 * D], BF16)
        self.s_z_b = P.bufs(NT, "sz")
        self.rw_rkv = d("rwkv_w_rkv", [3, D, D])
        self.rw_wo = d("rwkv_w_o", [D, D])
        self.rw_w1 = d("rwkv_w1", [D, 64])
        self.rw_a1 = d("rwkv_a1", [D, 64])
        self.rw_g1 = d("rwkv_g1", [D, 160])
        self.rw_w2 = d("rwkv_w2", [64, D])
        self.rw_a2 = d("rwkv_a2", [64, D])
        self.rw_g2 = d("rwkv_g2", [160, D])
        self.rw_p = d("rw_p", [128, 88])
        self.rw_gn = d("rw_gn", [128, 2, D])
        self.rw_c = d("rw_c", [128, 128 + 2 + 128 + 64], BF16)
        self.r_AR = P.dram("r_AR", [8, 128, 64, 2, 64], BF16)
        self.r_BK = P.dram("r_BK", [8, 128, 64, 2, 64], BF16)
        self.r_bh = P.dram("r_bh", [S, D], BF16)
        self.r_kh = P.dram("r_kh", [S, D], BF16)
        self.r_v = P.dram("r_v", [S, D], BF16)
        self.r_g = P.dram("r_g", [S, D], BF16)
        self.r_bv = P.dram("r_bv", [S, D], BF16)
        self.r_y = P.dram("r_y", [S, D], F32)
        self.r_pl = P.dram("r_pl", [128, 8, 64], F32)
        self.r1_b = P.bufs(8, "r1")
        self.ry_b = P.bufs(NT, "ry")
        self.ob = P.dram("ob", [S, 2 * D], BF16)
        self.ob_b = P.bufs(NT, "ob")
        self.xa = P.dram("xa", [S, D], F32)
        self.xa_b = P.bufs(NT, "xa")
        self.xb = P.dram("xb", [S, D], F32)
        self.xb_b = P.bufs(NT, "xb")
        self.xin_b = P.bufs(NT, "xin")
        self.out_b = P.bufs(NT, "out")
        self.c_sem = P.dsem("const")
        self.c_b = P.buf("consts")
        self.ident = P.sb("ident_sb", [128, 128], BF16)
        self.ident_b = self.c_b
        P.dma("sp", self.c_sem, self.ident[:], self.ident_d, writes=[self.c_b])
        self.gmix = P.sb("gmix", [128, 4, DC], F32)
        self.gffn = P.sb("gffn", [128, 4, DC], F32)
        self.g_b = self.c_b
        P.dma("sp", self.c_sem, self.gmix[:], self.norm_mix, writes=[self.c_b])
        P.dma("sp", self.c_sem, self.gffn[:], self.norm_ffn, writes=[self.c_b])
        self.cw = P.sb("ffn_cw_sb", [128, 4, 2 * FC, 3], F32)
        self.cb = P.sb("ffn_cb_sb", [128, 4, 2 * FC], F32)
        P.dma("sp", self.c_sem, self.cw[:], self.ffn_cw, writes=[self.c_b])
        P.dma("sp", self.c_sem, self.cb[:], self.ffn_cb, writes=[self.c_b])
        self.ones = P.sb("ones_bf", [128, 128], BF16)
        P.memset("pool", self.ones[:], 1.0, writes=[self.c_b])
        self.hnT = P.sb("hnT", [128, DC, S], BF16)
        self.hnT_b = P.bufs(S // 512, "hnT")
        self.xt = [P.sb(f"xt{i}", [128, D], F32) for i in range(2)]
        self.xt_b = P.bufs(2, "xt")
        self.xt_sem = [P.dsem(f"xt{i}") for i in range(2)]
        self.xs = [P.sb(f"xs{i}", [128, D], BF16) for i in range(2)]
        self.xs_b = P.bufs(2, "xs")
        self.sq = P.sb("sqjunk", [128, D], BF16)
        self.sq_b = P.buf("sq")
        self.ss = [P.sb(f"ss{i}", [128, 2], F32) for i in range(2)]
        self.ss_b = P.bufs(2, "ss")
        self.nk = 0
        self.A = Arena(P, "arena", 122 * 1024)
        self.psum = P.ps("psum", [128, 4096], F32)
        self.pb = P.bufs(8, "psb")

    def bank(self, i):
        return self.psum[:, i * 512:(i + 1) * 512]

    def wload(self, dst_sb, src_ap, sem, b, nsplit=1):
        P = self.P
        n = dst_sb.shape[1]
        step = n // nsplit
        for i in range(nsplit):
            P.dma("pool", sem, dst_sb[:, i * step:(i + 1) * step], src_ap[:, i * step:(i + 1) * step], writes=[b])

    def rms_tile(self, xt, xt_b, k):
        P = self.P
        ss, ss_b = self.ss[k % 2], self.ss_b[k % 2]
        P.act(self.sq[:], xt[:], AF.Square, reads=[xt_b], writes=[self.sq_b, ss_b], accum_out=ss[:, 0:1])
        P.ts("dve", ss[:, 1:2], ss[:, 0:1], 1.0 / D, EPS, ALU.mult, ALU.add, reads=[ss_b], writes=[ss_b])
        P.act(ss[:, 1:2], ss[:, 1:2], AF.Sqrt, reads=[ss_b], writes=[ss_b])
        P.op("dve", lambda e: e.reciprocal(ss[:, 1:2], ss[:, 1:2]), reads=[ss_b], writes=[ss_b])
        return ss[:, 1:2], ss_b

    def load_x(self, src, src_b, t):
        P = self.P
        k = self.nk
        self.nk += 1
        sl = k % 2
        xt, xt_b = self.xt[sl], self.xt_b[sl]
        P.dma("sp", self.xt_sem[sl], xt[:], src[t * 128:(t + 1) * 128, :], reads=[src_b[t]], writes=[xt_b])
        return xt, xt_b, sl, k

    def store_x(self, dst, dst_b, t, xt, xt_b, sl):
        self.P.dma("sp", self.xt_sem[sl], dst[t * 128:(t + 1) * 128, :], xt[:], reads=[xt_b], writes=[dst_b[t]])

    def norm_phase(self, src, src_b, g):
        P = self.P
        psT = [self.bank(6 + i).bitcast(BF16).rearrange("p (c t) -> p c t", c=DC) for i in range(2)]
        for t in range(NT):
            xt, xt_b, sl, k = self.load_x(src, src_b, t)
            rstd, ss_b = self.rms_tile(xt, xt_b, k)
            xs, xs_b = self.xs[sl], self.xs_b[sl]
            P.act(xs[:], xt[:], AF.Copy, reads=[xt_b, ss_b], writes=[xs_b], scale=rstd)
            pt, pt_b = psT[sl], self.pb[6 + sl]
            for c in range(DC):
                P.tr(pt[:, c, :], xs[:, c * 128:(c + 1) * 128], self.ident[:], reads=[xs_b, self.c_b], writes=[pt_b])
            hb = self.hnT_b[t // 4]
            gb = g.unsqueeze(2).to_broadcast([128, DC, 128])
            P.tt("dve", self.hnT[:, :, t * 128:(t + 1) * 128], pt, gb, ALU.mult,
                 reads=[pt_b, self.c_b], writes=[hb])

    def res_store(self, res, res_b, dst, dst_b, t, dp, dp_b, bias=None):
        P = self.P
        xt, xt_b, sl, k = self.load_x(res, res_b, t)
        P.tt("dve", xt[:], xt[:], dp, ALU.add, reads=[dp_b, xt_b], writes=[xt_b])
        if bias is not None:
            P.tt("pool", xt[:], xt[:], bias[0], ALU.add, reads=[bias[1], xt_b], writes=[xt_b])
        self.store_x(dst, dst_b, t, xt, xt_b, sl)

    def ffn_phase(self, l, res, res_b, dst, dst_b):
        P = self.P
        A = self.A
        A.reset()
        R = {}
        R["raw"] = [[self.bank(h * 2 + s) for s in range(2)] for h in range(2)]
        R["raw_b"] = [[self.pb[h * 2 + s] for s in range(2)] for h in range(2)]
        R["cps"] = [self.bank(4 + h) for h in range(2)]
        R["cps_b"] = [self.pb[4 + h] for h in range(2)]
        R["dps"] = self.psum[:, 6 * 512:8 * 512]
        R["dps_b"] = self.pb[6]
        R["wd"] = A.alloc([128, FC, D], BF16)
        R["wd_b"] = P.buf("wd")
        R["wd_sem"] = P.dsem(f"wd{l}")
        R["wu"] = [A.alloc([128, DC, 2, 128], BF16) for s in range(3)]
        R["wu_b"] = P.bufs(3, "wu")
        R["wu_sem"] = [P.dsem(f"wu{l}_{s}") for s in range(3)]
        R["dg"] = [A.alloc([128, 2, 3, 128], BF16) for s in range(2)]
        R["dg_b"] = P.bufs(2, "dg")
        R["ub"] = [[A.alloc([128, 514], BF16) for s in range(2)] for h in range(2)]
        R["ub_b"] = [P.bufs(2, f"ub{h}") for h in range(2)]
        R["halo"] = A.alloc([128, 2 * FC, 2], BF16)
        R["halo_b"] = P.buf("halo")
        R["sg"] = [A.alloc([128, 512], F32) for s in range(2)]
        R["sg_b"] = P.bufs(2, "sg")
        R["actT"] = A.alloc([128, FC, 1024], BF16)
        R["actT_b"] = P.bufs(2, "actT")
        wup = self.ffn_w_up[l].rearrange("(c p) e -> p c e", p=128)
        wdn = self.ffn_w_down[l].rearrange("(c p) e -> p c e", p=128)
        self.wload(R["wd"], wdn, R["wd_sem"], R["wd_b"], nsplit=FC)
        P.memset("pool", R["halo"], 0.0, writes=[R["halo_b"]])
        SB = 1024
        units = []
        for sbk in range(S // SB):
            for i in range(FC):
                for tb in range(SB // 512):
                    units.append((sbk, i, tb))
        nU = len(units)

        def stage_load(sbk, i):
            j = (sbk * FC + i)
            sl = j % 3
            w, w_b, w_sem = R["wu"][sl], R["wu_b"][sl], R["wu_sem"][sl]
            P.dma("pool", w_sem, w[:, :, 0, :], wup[:, :, i * 128:(i + 1) * 128], writes=[w_b])
            P.dma("pool", w_sem, w[:, :, 1, :], wup[:, :, DFF + i * 128:DFF + (i + 1) * 128], writes=[w_b])
            dg, dg_b = R["dg"][j % 2], R["dg_b"][j % 2]
            for h in range(2):
                ch = h * FC + i
                for tap in range(3):
                    P.ts("pool", dg[:, h, tap, :], self.ident[:], self.cw[:, l, ch, tap:tap + 1], None, ALU.mult,
                         reads=[self.c_b], writes=[dg_b])

        def stage_A(u):
            sbk, i, tb = units[u]
            j = sbk * FC + i
            w, w_b = R["wu"][j % 3], R["wu_b"][j % 3]
            t0 = sbk * SB + tb * 512
            for h in range(2):
                ps, ps_b = R["raw"][h][u % 2], R["raw_b"][h][u % 2]
                for c in range(DC):
                    P.mm(ps, w[:, c, h, :], self.hnT[:, c, t0:t0 + 512], c == 0, c == DC - 1,
                         reads=[w_b, self.hnT_b[t0 // 512]], writes=[ps_b])

        def stage_B(u):
            sbk, i, tb = units[u]
            j = sbk * FC + i
            dg, dg_b = R["dg"][j % 2], R["dg_b"][j % 2]
            for h in range(2):
                ch = h * FC + i
                ps, ps_b = R["raw"][h][u % 2], R["raw_b"][h][u % 2]
                ub, ub_b = R["ub"][h][u % 2], R["ub_b"][h][u % 2]
                P.copy("pool", ub[:, 0:2], R["halo"][:, ch, :], reads=[R["halo_b"]], writes=[ub_b])
                P.copy("act", ub[:, 2:514], ps, reads=[ps_b], writes=[ub_b])
                P.copy("pool", R["halo"][:, ch, :], ub[:, 512:514], reads=[ub_b], writes=[R["halo_b"]])
                cp, cp_b = R["cps"][h], R["cps_b"][h]
                for tap in range(3):
                    P.mm(cp, dg[:, h, tap, :], ub[:, tap:tap + 512], tap == 0, tap == 2,
                         reads=[dg_b, ub_b], writes=[cp_b])

        def stage_C(u):
            sbk, i, tb = units[u]
            sg, sg_b = R["sg"][u % 2], R["sg_b"][u % 2]
            P.act(sg, R["cps"][0], AF.Silu, reads=[R["cps_b"][0], self.c_b], writes=[sg_b],
                  bias=self.cb[:, l, i:i + 1])
            P.stt("dve", R["actT"][:, i, tb * 512:(tb + 1) * 512], R["cps"][1], self.cb[:, l, FC + i:FC + i + 1],
                  sg, ALU.add, ALU.mult, reads=[R["cps_b"][1], sg_b, self.c_b], writes=[R["actT_b"][tb]])

        def down(sbk):
            for tt in range(SB // 128):
                t = sbk * (SB // 128) + tt
                dp, dp_b = R["dps"], R["dps_b"]
                for hh in range(2):
                    for c in range(FC):
                        P.mm(dp[:, hh * 512:(hh + 1) * 512], R["actT"][:, c, tt * 128:(tt + 1) * 128],
                             R["wd"][:, c, hh * 512:(hh + 1) * 512], c == 0, c == FC - 1,
                             reads=[R["actT_b"][tt // 4], R["wd_b"]], writes=[dp_b])
                self.res_store(res, res_b, dst, dst_b, t, dp, dp_b)

        upb = SB // 512 * FC
        for u in range(nU + 1):
            if u < nU:
                sbk, i, tb = units[u]
                if tb == 0:
                    stage_load(sbk, i)
                stage_A(u)
            if u >= 1:
                stage_B(u - 1)
                stage_C(u - 1)
                if (u % upb) == 0:
                    down(u // upb - 1)

    def conformer(self, res, res_b, dst, dst_b):
        P = self.P
        A = self.A
        A.reset()
        NP = 16 + 8 * 31 + 24
        cp = A.alloc([128, NP], F32)
        cp_b = P.buf("confp")
        sem = P.dsem("conf")
        P.dma("sp", sem, cp, self.conf_p, writes=[cp_b])
        b1 = cp[:, 0:16]
        dww = cp[:, 16:16 + 248].rearrange("p (c j) -> p c j", c=8)
        dwb = cp[:, 264:272]
        lnw = cp[:, 272:280]
        lnb = cp[:, 280:288]
        gT = A.alloc([128, DC, 30 + S], BF16)
        gT_b = P.bufs(S // 512, "gT")
        gz_b = P.buf("gTpad")
        P.memset("pool", gT[:, :, 0:30], 0.0, writes=[gz_b])
        mark = A.off
        w1 = A.alloc([128, DC, 2 * D], BF16)
        w1_b = P.buf("w1")
        self.wload(w1, self.conf_w1.rearrange("(c p) e -> p c e", p=128), sem, w1_b, nsplit=DC)
        sg = [A.alloc([128, 512], F32) for i in range(2)]
        sg_b = P.bufs(2, "csg")
        k = 0
        for c in range(DC):
            for tb in range(8):
                psa, psa_b = self.bank(2 * (k % 2)), self.pb[2 * (k % 2)]
                psb, psb_b = self.bank(2 * (k % 2) + 1), self.pb[2 * (k % 2) + 1]
                for kc in range(DC):
                    P.mm(psa, w1[:, kc, c * 128:(c + 1) * 128], self.hnT[:, kc, tb * 512:(tb + 1) * 512],
                         kc == 0, kc == DC - 1, reads=[w1_b, self.hnT_b[tb]], writes=[psa_b])
                for kc in range(DC):
                    P.mm(psb, w1[:, kc, D + c * 128:D + (c + 1) * 128], self.hnT[:, kc, tb * 512:(tb + 1) * 512],
                         kc == 0, kc == DC - 1, reads=[w1_b, self.hnT_b[tb]], writes=[psb_b])
                P.act(sg[k % 2], psb, AF.Sigmoid, reads=[psb_b, cp_b], writes=[sg_b[k % 2]], bias=b1[:, 8 + c:9 + c])
                P.stt("dve", gT[:, c, 30 + tb * 512:30 + (tb + 1) * 512], psa, b1[:, c:c + 1], sg[k % 2],
                      ALU.add, ALU.mult, reads=[psa_b, sg_b[k % 2], cp_b], writes=[gT_b[tb]])
                k += 1
        A.reset(mark)
        w2 = A.alloc([128, DC, D], BF16)
        w2_b = P.buf("w2")
        self.wload(w2, self.conf_w2.rearrange("(c p) e -> p c e", p=128), sem, w2_b, nsplit=DC)
        b2 = A.alloc([128, D], F32)
        b2_b = P.buf("b2")
        P.dma("sp", sem, b2, self.conf_b2, writes=[b2_b])
        mark2 = A.off
        dg = [A.alloc([128, 31, 128], BF16) for i in range(2)]
        dg_b = P.bufs(2, "cdg")
        k = 0
        for c in range(DC):
            for j in range(31):
                P.ts("pool", dg[c % 2][:, j, :], self.ident[:], dww[:, c, j:j + 1], None, ALU.mult,
                     reads=[self.c_b, cp_b], writes=[dg_b[c % 2]])
            for tb in range(8):
                ps, ps_b = self.bank(k % 2), self.pb[k % 2]
                rb = [gz_b, gT_b[tb]] + ([gT_b[tb - 1]] if tb > 0 else [])
                for j in range(31):
                    P.mm(ps, dg[c % 2][:, j, :], gT[:, c, tb * 512 + j:tb * 512 + j + 512], j == 0, j == 30,
                         reads=[dg_b[c % 2]] + rb, writes=[ps_b])
                P.act(self.hnT[:, c, tb * 512:(tb + 1) * 512], ps, AF.Identity, reads=[ps_b, cp_b],
                      writes=[self.hnT_b[tb]], bias=dwb[:, c:c + 1])
                k += 1
        A.reset(mark2)
        sqv = A.alloc([128, DC, 512], BF16)
        sqv_b = P.buf("sqv")
        st = [A.alloc([128, 512], F32) for i in range(3)]
        st_b = P.buf("st")
        tmp = [A.alloc([128, 512], F32) for i in range(2)]
        tmp_b = P.bufs(2, "ctmp")
        zT = A.alloc([128, DC, 512], BF16)
        zT_b = P.buf("zT")
        vT = self.hnT
        for tb in range(8):
            blk = slice(tb * 512, (tb + 1) * 512)
            P.act(sqv, vT[:, :, blk], AF.Square, reads=[self.hnT_b[tb]], writes=[sqv_b])
            pS, pS_b = self.bank(2), self.pb[2]
            pQ, pQ_b = self.bank(3), self.pb[3]
            for c in range(DC):
                P.mm(pS, self.ones[:], vT[:, c, blk], c == 0, c == DC - 1, reads=[self.c_b, self.hnT_b[tb]], writes=[pS_b])
            for c in range(DC):
                P.mm(pQ, self.ones[:], sqv[:, c, :], c == 0, c == DC - 1, reads=[self.c_b, sqv_b], writes=[pQ_b])
            mean, var, rstd = st
            P.ts("dve", mean, pS, 1.0 / D, None, ALU.mult, reads=[pS_b], writes=[st_b])
            P.stt("dve", var, mean, -1.0, mean, ALU.mult, ALU.mult, reads=[st_b], writes=[st_b])
            P.stt("dve", var, pQ, 1.0 / D, var, ALU.mult, ALU.add, reads=[pQ_b, st_b], writes=[st_b])
            P.ts("dve", var, var, 1e-5, None, ALU.add, reads=[st_b], writes=[st_b])
            P.act(rstd, var, AF.Sqrt, reads=[st_b], writes=[st_b])
            P.op("dve", lambda e, rstd=rstd: e.reciprocal(rstd, rstd), reads=[st_b], writes=[st_b])
            for c in range(DC):
                tm, tm_b = tmp[c % 2], tmp_b[c % 2]
                P.tt("pool", tm, vT[:, c, blk], mean, ALU.subtract, reads=[self.hnT_b[tb], st_b], writes=[tm_b])
                P.tt("dve", tm, tm, rstd, ALU.mult, reads=[st_b, tm_b], writes=[tm_b])
                P.act(zT[:, c, :], tm, AF.Silu, reads=[tm_b, cp_b], writes=[zT_b],
                      scale=lnw[:, c:c + 1], bias=lnb[:, c:c + 1])
            for tq in range(4):
                t = tb * 4 + tq
                dp, dp_b = self.psum[:, 6 * 512:8 * 512], self.pb[6]
                for hh in range(2):
                    for c in range(DC):
                        P.mm(dp[:, hh * 512:(hh + 1) * 512], zT[:, c, tq * 128:(tq + 1) * 128],
                             w2[:, c, hh * 512:(hh + 1) * 512], c == 0, c == DC - 1,
                             reads=[zT_b, w2_b], writes=[dp_b])
                self.res_store(res, res_b, dst, dst_b, t, dp, dp_b, bias=(b2, b2_b))


    def tm_proj_phase(self, KC, W_dram, res, res_b, dst, dst_b, tag):
        P, A = self.P, self.A
        A.reset()
        sem = P.dsem("tmp" + tag)
        W = A.alloc([128, KC, D], BF16)
        W_b = P.buf("tmW")
        self.wload(W, W_dram.rearrange("(c p) e -> p c e", p=128), sem, W_b, nsplit=KC)
        yt = [A.alloc([128, KC * 128], BF16) for i in range(2)]
        yt_b = P.bufs(2, "tmy")
        yt_sem = [P.dsem(f"tmy{tag}{i}") for i in range(2)]
        yT = [A.alloc([128, KC, 128], BF16) for i in range(2)]
        yT_b = P.bufs(2, "tmyT")
        for t in range(NT):
            sl = t % 2
            P.dma("sp", yt_sem[sl], yt[sl], self.ob[t * 128:(t + 1) * 128, 0:KC * 128], reads=[self.ob_b[t]], writes=[yt_b[sl]])
            pt = self.psum[:, sl * 1024:(sl + 1) * 1024].bitcast(BF16)[:, 0:KC * 128].rearrange("p (c t) -> p c t", c=KC)
            pt_b = self.pb[2 * sl]
            for c in range(KC):
                P.tr(pt[:, c, :], yt[sl][:, c * 128:(c + 1) * 128], self.ident[:], reads=[yt_b[sl], self.c_b], writes=[pt_b])
            P.copy("act", yT[sl], pt, reads=[pt_b], writes=[yT_b[sl]])
            dp, dp_b = self.psum[:, (4 + 2 * sl) * 512:(6 + 2 * sl) * 512], self.pb[4 + 2 * sl]
            for hh in range(2):
                for c in range(KC):
                    P.mm(dp[:, hh * 512:(hh + 1) * 512], yT[sl][:, c, :], W[:, c, hh * 512:(hh + 1) * 512],
                         c == 0, c == KC - 1, reads=[yT_b[sl], W_b], writes=[dp_b])
            self.res_store(res, res_b, dst, dst_b, t, dp, dp_b)

    def moba(self, res, res_b, dst, dst_b):
        P, A = self.P, self.A
        A.reset()
        G = 4
        BIG = 30000.0
        sem = P.dsem("moba")
        mc = A.alloc([128, 3, 16, 16], F32)
        tri = A.alloc([128, 128], BF16)
        k_b = P.buf("mobac")
        P.dma("sp", sem, mc, self.mconst, writes=[k_b])
        P.dma("sp", sem, tri, self.tri_d, writes=[k_b])
        QA = A.alloc([128, G, S], BF16)
        KA = A.alloc([128, G, S], BF16)
        QA_b = [P.bufs(8, f"QA{h}") for h in range(G)]
        KA_b = P.bufs(G, "KA")
        ind_b = P.buf("ind")
        V = A.alloc([128, NT, G, 65], BF16)
        V_b = P.buf("V")
        one_b = P.buf("Vone")
        P.memset("pool", V[:, :, :, 64:65], 1.0, writes=[one_b])
        for hh in range(G):
            P.dma("sp", sem, KA[64:80, hh, :], self.blkind, writes=[ind_b])
        w3 = A.alloc([128, DC, 3, G * 64], BF16)
        w3_b = P.buf("w3")
        w3_sem = P.dsem("mobaw")
        km = A.alloc([128, G, 16], F32)
        kmb = A.alloc([128, G, 16], BF16)
        km_b = P.buf("km")
        g2 = A.alloc([128, G, 16], F32)
        sel = A.alloc([128, G, 16], F32)
        mx = A.alloc([128, G, 8], F32)
        gt_b = P.buf("gate")
        mbf = A.alloc([128, G, 80], BF16)
        mbf_b = P.buf("mbf")
        P.memset("pool", mbf, 0.0, writes=[mbf_b])
        mb2 = A.alloc([128, G, 128], BF16)
        mb2_b = P.buf("mb2")
        PT = [A.alloc([128, 512], BF16) for i in range(3)]
        PT_b = P.bufs(3, "PT")
        osb = [A.alloc([128, 4, G * 64], BF16) for i in range(2)]
        osb_b = P.bufs(2, "osb")
        osb_sem = [P.dsem(f"osb{i}") for i in range(2)]
        rec = A.alloc([128, 4, 1], F32)
        rec_b = P.buf("rec")
        wqkv = self.moba_wqkv.rearrange("(c p) e -> p c e", p=128)
        nps = 0
        nS = 0
        nO = 0
        nosb = 0
        for g in range(D // 64 // G):
            for i in range(3):
                P.dma("pool", w3_sem, w3[:, :, i, :], wqkv[:, :, i * D + g * G * 64:i * D + (g + 1) * G * 64], writes=[w3_b])
            for hh in range(G):
                for tb in range(8):
                    blk = slice(tb * 512, (tb + 1) * 512)
                    for i in range(2):
                        ps, ps_b = self.bank(nps % 2)[0:64, :], self.pb[nps % 2]
                        nps += 1
                        for kc in range(DC):
                            P.mm(ps, w3[:, kc, i, hh * 64:(hh + 1) * 64], self.hnT[:, kc, blk], kc == 0, kc == DC - 1,
                                 reads=[w3_b, self.hnT_b[tb]], writes=[ps_b])
                        if i == 0:
                            P.act(QA[0:64, hh, blk], ps, AF.Copy, reads=[ps_b], writes=[QA_b[hh][tb]], scale=0.125)
                        else:
                            P.copy("dve", KA[0:64, hh, blk], ps, reads=[ps_b], writes=[KA_b[hh]])
                P.op("dve", lambda e, hh=hh: e.tensor_reduce(km[0:64, hh, :], KA[0:64, hh, :].rearrange("p (n k) -> p n k", k=256),
                                                             AX.X, ALU.add), reads=[KA_b[hh]], writes=[km_b])
            P.ts("dve", kmb[0:64], km[0:64], 1.0 / 256, None, ALU.mult, reads=[km_b], writes=[km_b])
            import os
            STG = int(os.environ.get("MOBA_STAGE", "9"))
            for t in range(NT if STG >= 2 else 0):
                ps, ps_b = self.bank(nps % 2)[:, 0:G * 64], self.pb[nps % 2]
                nps += 1
                for kc in range(DC):
                    P.mm(ps, self.hnT[:, kc, t * 128:(t + 1) * 128], w3[:, kc, 2, :], kc == 0, kc == DC - 1,
                         reads=[w3_b, self.hnT_b[t // 4]], writes=[ps_b])
                P.copy("act", V[:, t, :, 0:64], ps.rearrange("p (h e) -> p h e", h=G), reads=[ps_b], writes=[V_b])
            gps = self.bank(7)[:, 0:G * 16].rearrange("p (h n) -> p h n", h=G)
            tps = self.bank(7)[:, 256:512].bitcast(BF16).rearrange("p (h q) -> p h q", h=G)
            g_b, t_b = self.pb[7], P.buf("tps") if g == 0 else t_b
            for t in range(NT if STG >= 3 else 0):
                qb_ = t // 2
                tile = slice(t * 128, (t + 1) * 128)
                for hh in range(G):
                    P.mm(gps[:, hh, :], QA[0:64, hh, tile], kmb[0:64, hh, :], True, True,
                         reads=[QA_b[hh][t // 4], km_b], writes=[g_b])
                P.tt("dve", g2, gps, mc[:, 0, qb_, :].unsqueeze(1).to_broadcast([128, G, 16]), ALU.add,
                     reads=[g_b, k_b], writes=[gt_b])
                SUB = int(os.environ.get("MOBA_SUB", "9"))
                if SUB < 2:
                    continue
                for hh in range(G):
                    P.op("dve", lambda e, hh=hh: e.max(mx[:, hh, :], g2[:, hh, :]), reads=[gt_b], writes=[gt_b])
                P.tt("dve", sel, g2, mx[:, :, 2:3].to_broadcast([128, G, 16]), ALU.is_ge, reads=[gt_b], writes=[gt_b])
                P.tt("dve", sel, sel, mc[:, 1, qb_, :].unsqueeze(1).to_broadcast([128, G, 16]), ALU.mult,
                     reads=[gt_b, k_b], writes=[gt_b])
                P.tt("dve", sel, sel, mc[:, 2, qb_, :].unsqueeze(1).to_broadcast([128, G, 16]), ALU.add,
                     reads=[gt_b, k_b], writes=[gt_b])
                P.ts("dve", mbf[:, :, 64:80], sel, BIG, -BIG, ALU.mult, ALU.add, reads=[gt_b], writes=[mbf_b])
                if SUB < 3:
                    continue
                for hh in range(G):
                    P.tr(tps[0:80, hh, :], mbf[:, hh, :], self.ident[:], reads=[mbf_b, self.c_b], writes=[t_b])
                if SUB < 4:
                    continue
                for hh in range(G):
                    P.copy("dve", mb2[64:80, hh, :], tps[64:80, hh, :], reads=[t_b], writes=[mb2_b])
                for hh in range(G):
                    P.copy("pool", QA[64:80, hh, tile], mb2[64:80, hh, :], reads=[mb2_b], writes=[QA_b[hh][t // 4]])
            for qb in range(8 if STG >= 4 else 0):
                ob_, ob_b, ob_sem = osb[nosb % 2], osb_b[nosb % 2], osb_sem[nosb % 2]
                nosb += 1
                for hh in range(G):
                    O = self.bank(5 + nO % 2)[:, 0:260].rearrange("p (q e) -> p q e", q=4)
                    O_b = self.pb[5 + nO % 2]
                    nO += 1
                    nkt = 4 * qb + 4
                    for kt in range(nkt):
                        sp, sp_b = self.bank(2 + nS % 3), self.pb[2 + nS % 3]
                        pt, pt_b = PT[nS % 3], PT_b[nS % 3]
                        nS += 1
                        P.mm(sp, KA[0:80, hh, kt * 128:(kt + 1) * 128], QA[0:80, hh, qb * 512:(qb + 1) * 512], True, True,
                             reads=[KA_b[hh], ind_b, QA_b[hh][qb]], writes=[sp_b])
                        P.act(pt, sp, AF.Exp, reads=[sp_b], writes=[pt_b])
                        j = kt - 4 * qb
                        if j >= 0:
                            P.tt("pool", pt[:, j * 128:(j + 1) * 128], pt[:, j * 128:(j + 1) * 128], tri, ALU.mult,
                                 reads=[k_b, pt_b], writes=[pt_b])
                        for ql in range(4):
                            qt = 4 * qb + ql
                            if kt <= qt:
                                P.mm(O[:, ql, :], pt[:, ql * 128:(ql + 1) * 128], V[:, kt, hh, :], kt == 0 and ql == 0, kt == qt,
                                     reads=[pt_b, V_b, one_b], writes=[O_b])
                    P.op("dve", lambda e, O=O: e.reciprocal(rec, O[:, :, 64:65]), reads=[O_b], writes=[rec_b])
                    P.tt("dve", ob_[:, :, hh * 64:(hh + 1) * 64], O[:, :, 0:64], rec.to_broadcast([128, 4, 64]), ALU.mult,
                         reads=[O_b, rec_b], writes=[ob_b])
                dstv = self.ob[qb * 512:(qb + 1) * 512, g * G * 64:(g + 1) * G * 64].rearrange("(q p) c -> p q c", p=128)
                P.dma("sp", ob_sem, dstv, ob_, reads=[ob_b], writes=[self.ob_b[4 * qb + i] for i in range(4)])
        self.tm_proj_phase(DC, self.moba_wo, res, res_b, dst, dst_b, "moba")

    def ssd(self, res, res_b, dst, dst_b):
        P, A = self.P, self.A
        A.reset()
        sem = P.dsem("ssd")
        pp = A.alloc([128, 120], F32)
        tc_ = A.alloc([128, 96], F32)
        tri = A.alloc([128, 128], BF16)
        nb4 = A.alloc([128, 4, 128], BF16)
        k_b = P.buf("ssdc")
        P.dma("sp", sem, pp, self.ssd_p, writes=[k_b])
        P.dma("sp", sem, tc_, self.ssd_t, writes=[k_b])
        P.dma("sp", sem, tri, self.tri_d, writes=[k_b])
        P.dma("sp", sem, nb4, self.nb_d, writes=[k_b])
        negones = A.alloc([128, 128], BF16)
        P.memset("pool", negones, -1.0, writes=[k_b])
        cwv = pp[:, 0:96].rearrange("p (c j) -> p c j", j=4)
        cbv = pp[:, 96:120]
        dtk = A.alloc([128, NT, 32], F32)
        atk = A.alloc([128, NT, 32], BF16)
        dt_b = P.buf("dtk")
        aneg = A.alloc([128, 32], F32)
        P.act(aneg, tc_[:, 32:64], AF.Exp, reads=[k_b], writes=[k_b])
        P.ts("dve", aneg, aneg, -1.0, None, ALU.mult, reads=[k_b], writes=[k_b])
        mark = A.off
        win = self.ssd_win.rearrange("(c p) e -> p c e", p=128)
        wch = [A.alloc([128, DC, 128], BF16) for i in range(3)]
        wch_b = P.bufs(3, "swch")
        wch_sem = [P.dsem(f"swch{i}") for i in range(3)]
        dg = [A.alloc([128, 4, 128], BF16) for i in range(2)]
        dg_b = P.bufs(2, "sdg")
        ub = [A.alloc([128, 515], BF16) for i in range(2)]
        ub_b = P.bufs(2, "sub")
        xc = [A.alloc([128, 512], BF16) for i in range(2)]
        xc_b = P.bufs(2, "sxc")
        xc_sem = [P.dsem(f"sxc{i}") for i in range(2)]
        stg = [A.alloc([128, 4, 128], BF16) for i in range(2)]
        stg_b = P.bufs(2, "sstg")
        stg_sem = [P.dsem(f"sstg{i}") for i in range(2)]
        u = 0
        for cc in range(24):
            w, w_b = wch[cc % 3], wch_b[cc % 3]
            P.dma("pool", wch_sem[cc % 3], w, win[:, :, 2 * D + cc * 128:2 * D + (cc + 1) * 128], writes=[w_b])
            for j in range(4):
                P.ts("pool", dg[cc % 2][:, j, :], self.ident[:], cwv[:, cc, j:j + 1], None, ALU.mult,
                     reads=[self.c_b, k_b], writes=[dg_b[cc % 2]])
            for tb in range(8):
                blk = slice(tb * 512, (tb + 1) * 512)
                ps, ps_b = self.bank(u % 2), self.pb[u % 2]
                for kc in range(DC):
                    P.mm(ps, w[:, kc, :], self.hnT[:, kc, blk], kc == 0, kc == DC - 1, reads=[w_b, self.hnT_b[tb]], writes=[ps_b])
                b_, b_b = ub[u % 2], ub_b[u % 2]
                if tb == 0:
                    P.memset("pool", b_[:, 0:3], 0.0, writes=[b_b])
                else:
                    P.copy("pool", b_[:, 0:3], ub[(u - 1) % 2][:, 512:515], reads=[ub_b[(u - 1) % 2]], writes=[b_b])
                P.copy("act", b_[:, 3:515], ps, reads=[ps_b], writes=[b_b])
                cp, cp_b = self.bank(2 + u % 2), self.pb[2 + u % 2]
                for j in range(4):
                    P.mm(cp, dg[cc % 2][:, j, :], b_[:, j:j + 512], j == 0, j == 3, reads=[dg_b[cc % 2], b_b], writes=[cp_b])
                x_, x_b, x_sem = xc[u % 2], xc_b[u % 2], xc_sem[u % 2]
                P.act(x_, cp, AF.Silu, reads=[cp_b, k_b], writes=[x_b], bias=cbv[:, cc:cc + 1])
                if cc >= 16:
                    P.dma("sp", x_sem, self.s_BCT[cc - 16, :, blk], x_, reads=[x_b], writes=[self.s_BCT_b[tb]])
                if cc < 20:
                    tp = self.bank(4 + u % 2).bitcast(BF16)[:, 0:512].rearrange("p (q c) -> p q c", q=4)
                    tp_b = self.pb[4 + u % 2]
                    for q in range(4):
                        P.tr(tp[:, q, :], x_[:, q * 128:(q + 1) * 128], self.ident[:], reads=[x_b, self.c_b], writes=[tp_b])
                    sg_, sg_b, sg_sem = stg[u % 2], stg_b[u % 2], stg_sem[u % 2]
                    P.copy("dve", sg_, tp, reads=[tp_b], writes=[sg_b])
                    dv = self.s_xB[tb * 512:(tb + 1) * 512, cc * 128:(cc + 1) * 128].rearrange("(q p) c -> p q c", p=128)
                    P.dma("sp", sg_sem, dv, sg_, reads=[sg_b], writes=[self.s_xB_b[tb]])
                u += 1
        A.reset(mark)
        wz = A.alloc([128, DC, 2 * D], BF16)
        wz_b = P.buf("wz")
        self.wload(wz, win[:, :, 0:2 * D], sem, wz_b, nsplit=DC)
        wdt = A.alloc([128, DC, 32], BF16)
        P.dma("pool", sem, wdt, win[:, :, 5120:5152], writes=[wz_b])
        zt = [A.alloc([128, 2 * D], BF16) for i in range(2)]
        zt_b = P.bufs(2, "szt")
        zt_sem = [P.dsem(f"szt{i}") for i in range(2)]
        for t in range(NT):
            tile = slice(t * 128, (t + 1) * 128)
            z_, z_b = zt[t % 2], zt_b[t % 2]
            for q in range(4):
                ps, ps_b = self.bank(q % 2), self.pb[q % 2]
                for kc in range(DC):
                    P.mm(ps, self.hnT[:, kc, tile], wz[:, kc, q * 512:(q + 1) * 512], kc == 0, kc == DC - 1,
                         reads=[wz_b, self.hnT_b[t // 4]], writes=[ps_b])
                P.act(z_[:, q * 512:(q + 1) * 512], ps, AF.Silu, reads=[ps_b], writes=[z_b])
            P.dma("sp", zt_sem[t % 2], self.s_z[tile, :], z_, reads=[z_b], writes=[self.s_z_b[t]])
            ps, ps_b = self.bank(2)[:, 0:32], self.pb[2]
            for kc in range(DC):
                P.mm(ps, self.hnT[:, kc, tile], wdt[:, kc, :], kc == 0, kc == DC - 1, reads=[wz_b, self.hnT_b[t // 4]], writes=[ps_b])
            P.tt("dve", dtk[:, t, :], ps, tc_[:, 0:32], ALU.add, reads=[ps_b, k_b], writes=[dt_b])
        P.act(dtk, dtk, AF.Exp, reads=[dt_b], writes=[dt_b])
        P.act(dtk, dtk, AF.Ln, reads=[dt_b], writes=[dt_b], bias=1.0)
        P.tt("dve", atk, dtk, aneg.unsqueeze(1).to_broadcast([128, NT, 32]), ALU.mult, reads=[dt_b, k_b], writes=[dt_b])
        A.reset(mark)
        nw = A.alloc([128, 2 * D], F32)
        P.dma("sp", sem, nw, self.ssd_nw, writes=[k_b])
        xB = [A.alloc([128, 2560], BF16) for i in range(2)]
        xB_b = P.bufs(2, "xB")
        xB_sem = [P.dsem(f"xB{i}") for i in range(2)]
        zz = [A.alloc([128, 2 * D], BF16) for i in range(2)]
        zz_b = P.bufs(2, "zz")
        zz_sem = [P.dsem(f"zz{i}") for i in range(2)]
        bct = [A.alloc([128, 8, 128], BF16) for i in range(2)]
        bct_b = P.bufs(2, "bct")
        bct_sem = [P.dsem(f"bct{i}") for i in range(2)]
        xd = A.alloc([128, 32, 64], BF16)
        xdw = A.alloc([128, 32, 64], BF16)
        xd_b = P.buf("xd")
        acum = A.alloc([128, 32], F32)
        expA = A.alloc([128, 32], F32)
        expLA = A.alloc([128, 32], F32)
        dL = A.alloc([128, 32], F32)
        sc_b = P.buf("ssc")
        R1 = A.alloc([128, 8, 128], BF16)
        R1_b = P.buf("R1")
        E = A.alloc([128, 8, 128], BF16)
        E_b = P.buf("E")
        MT = A.alloc([128, 8, 128], BF16)
        MT_b = P.buf("MT")
        GT = A.alloc([128, 128], BF16)
        GT_b = P.buf("GT")
        yt = A.alloc([128, 2 * D], F32)
        yt_b = P.buf("yt")
        tmp = A.alloc([128, 512], F32)
        tmp_b = P.buf("stmp")
        HT = A.alloc([128, 4, 512], F32)
        HTb = A.alloc([128, 4, 512], BF16)
        HT_b = P.bufs(4, "HT")
        P.memset("pool", HT, 0.0, writes=HT_b)
        P.memset("pool", HTb, 0.0, writes=HT_b)
        ssq = A.alloc([128, 8], F32)
        ssq_b = P.buf("ssq")
        yo = [A.alloc([128, 2 * D], BF16) for i in range(2)]
        yo_b = P.bufs(2, "yo")
        yo_sem = [P.dsem(f"yo{i}") for i in range(2)]
        for c in range(NT):
            sl = c % 2
            tile = slice(c * 128, (c + 1) * 128)
            P.dma("sp", xB_sem[sl], xB[sl], self.s_xB[tile, :], reads=[self.s_xB_b[c // 4]], writes=[xB_b[sl]])
            P.dma("sp", zz_sem[sl], zz[sl], self.s_z[tile, :], reads=[self.s_z_b[c]], writes=[zz_b[sl]])
            P.dma("sp", bct_sem[sl], bct[sl], self.s_BCT[:, :, tile].rearrange("g n t -> n g t"),
                  reads=[self.s_BCT_b[c // 4]], writes=[bct_b[sl]])
            xv = xB[sl][:, 0:2048].rearrange("p (h e) -> p h e", h=32)
            a_c = atk[:, c, :]
            pA, pA_b = self.bank(0)[:, 0:32], self.pb[0]
            pL = self.bank(0)[:, 32:64]
            P.mm(pA, tri, a_c, True, True, reads=[k_b, dt_b], writes=[pA_b])
            P.mm(pL, self.ones[:], a_c, False, True, reads=[self.c_b, dt_b], writes=[pA_b])
            P.copy("dve", acum, pA, reads=[pA_b], writes=[sc_b])
            P.act(expA, pA, AF.Exp, reads=[pA_b], writes=[sc_b])
            P.act(dL, pL, AF.Exp, reads=[pA_b], writes=[sc_b])
            P.tt("dve", expLA, pL, acum, ALU.subtract, reads=[pA_b, sc_b], writes=[sc_b])
            P.act(expLA, expLA, AF.Exp, reads=[sc_b], writes=[sc_b])
            P.tt("dve", xd, xv, dtk[:, c, :].unsqueeze(2).to_broadcast([128, 32, 64]), ALU.mult,
                 reads=[xB_b[sl], dt_b], writes=[xd_b])
            P.tt("pool", xdw, xd, expLA.unsqueeze(2).to_broadcast([128, 32, 64]), ALU.mult, reads=[xd_b, sc_b], writes=[xd_b])
            for g in range(4):
                BTg = bct[sl][:, g, :]
                CTg = bct[sl][:, 4 + g, :]
                Btok = xB[sl][:, 2048 + g * 128:2048 + (g + 1) * 128]
                pG, pG_b = self.bank(1)[:, 0:128], self.pb[1]
                P.mm(pG, BTg, CTg, True, True, reads=[bct_b[sl]], writes=[pG_b])
                P.copy("act", GT, pG, reads=[pG_b], writes=[GT_b])
                P.tt("dve", R1, tri.unsqueeze(1).to_broadcast([128, 8, 128]),
                     a_c[:, g * 8:(g + 1) * 8].unsqueeze(2).to_broadcast([128, 8, 128]), ALU.mult,
                     reads=[k_b, dt_b], writes=[R1_b])
                for hb in range(2):
                    pD, pD_b = self.bank(2 + hb).rearrange("p (h t) -> p h t", h=4), self.pb[2 + hb]
                    P.mm(pD, self.ones[:], R1[:, hb * 4:(hb + 1) * 4, :], True, False, reads=[self.c_b, R1_b], writes=[pD_b])
                    P.mm(pD, self.ident[:], nb4, False, False, reads=[self.c_b, k_b], writes=[pD_b])
                    for h4 in range(4):
                        P.mm(pD[:, h4, :], R1[:, hb * 4 + h4, :], negones, False, h4 == 3, reads=[R1_b, k_b], writes=[pD_b])
                    P.act(E[:, hb * 4:(hb + 1) * 4, :], pD, AF.Exp, reads=[pD_b], writes=[E_b])
                P.tt("dve", MT, E, GT.unsqueeze(1).to_broadcast([128, 8, 128]), ALU.mult, reads=[E_b, GT_b], writes=[MT_b])
                pY, pY_b = self.bank(4).rearrange("p (h e) -> p h e", h=8), self.pb[4]
                for h8 in range(8):
                    P.mm(pY[:, h8, :], MT[:, h8, :], xd[:, g * 8 + h8, :], h8 == 0, h8 == 7, reads=[MT_b, xd_b], writes=[pY_b])
                pO, pO_b = self.bank(5), self.pb[5]
                P.mm(pO, CTg, HTb[:, g, :], True, True, reads=[bct_b[sl], HT_b[g]], writes=[pO_b])
                pS, pS_b = self.bank(6), self.pb[6]
                P.mm(pS, Btok, xdw[:, g * 8:(g + 1) * 8, :], True, True, reads=[xB_b[sl], xd_b], writes=[pS_b])
                P.tt("dve", tmp.rearrange("p (h e) -> p h e", h=8), pO.rearrange("p (h e) -> p h e", h=8),
                     expA[:, g * 8:(g + 1) * 8].unsqueeze(2).to_broadcast([128, 8, 64]), ALU.mult,
                     reads=[pO_b, sc_b], writes=[tmp_b])
                P.tt("dve", yt[:, g * 512:(g + 1) * 512], tmp, self.bank(4), ALU.add, reads=[tmp_b, pY_b], writes=[yt_b])
                Hg = HT[:, g, :]
                P.tt("pool", Hg.rearrange("p (h e) -> p h e", h=8), Hg.rearrange("p (h e) -> p h e", h=8),
                     dL[:, g * 8:(g + 1) * 8].unsqueeze(2).to_broadcast([128, 8, 64]), ALU.mult,
                     reads=[sc_b, HT_b[g]], writes=[HT_b[g]])
                P.tt("dve", Hg, Hg, pS, ALU.add, reads=[pS_b, HT_b[g]], writes=[HT_b[g]])
                P.copy("act", HTb[:, g, :], Hg, reads=[HT_b[g]], writes=[HT_b[g]])
            xs_ = self.xt[0][:, :].bitcast(BF16)
            ytv = yt.rearrange("p (h e) -> p h e", h=32)
            P.tt("pool", xd, xv, tc_[:, 64:96].unsqueeze(2).to_broadcast([128, 32, 64]), ALU.mult,
                 reads=[xB_b[sl], k_b, pY_b, pS_b], writes=[xd_b])
            P.tt("dve", ytv, ytv, xd, ALU.add, reads=[xd_b, yt_b], writes=[yt_b])
            P.tt("dve", yt, yt, zz[sl], ALU.mult, reads=[zz_b[sl], yt_b], writes=[yt_b])
            o_, o_b = yo[sl], yo_b[sl]
            for g in range(4):
                P.act(o_[:, g * 512:(g + 1) * 512], yt[:, g * 512:(g + 1) * 512], AF.Square, reads=[yt_b],
                      writes=[o_b, ssq_b], accum_out=ssq[:, g:g + 1])
            P.ts("dve", ssq[:, 4:8], ssq[:, 0:4], 1.0 / 512, 1e-5, ALU.mult, ALU.add, reads=[ssq_b], writes=[ssq_b])
            P.act(ssq[:, 4:8], ssq[:, 4:8], AF.Sqrt, reads=[ssq_b], writes=[ssq_b])
            P.op("dve", lambda e: e.reciprocal(ssq[:, 4:8], ssq[:, 4:8]), reads=[ssq_b], writes=[ssq_b])
            P.tt("pool", yt, yt, nw, ALU.mult, reads=[yt_b, k_b], writes=[yt_b])
            P.tt("dve", o_.rearrange("p (g e) -> p g e", g=4), yt.rearrange("p (g e) -> p g e", g=4),
                 ssq[:, 4:8].unsqueeze(2).to_broadcast([128, 4, 512]), ALU.mult, reads=[yt_b, ssq_b], writes=[o_b])
            P.dma("sp", yo_sem[sl], self.ob[tile, :], o_, reads=[o_b], writes=[self.ob_b[c]])
        self.tm_proj_phase(16, self.ssd_wout, res, res_b, dst, dst_b, "ssd")

    def rwkv(self, res, res_b, dst, dst_b):
        P, A = self.P, self.A
        A.reset()
        sem = P.dsem("rw")
        k_b = P.buf("rwc")
        rp = A.alloc([128, 88], F32)
        P.dma("sp", sem, rp, self.rw_p, writes=[k_b])
        cc_ = A.alloc([128, 322], BF16)
        P.dma("sp", sem, cc_, self.rw_c, writes=[k_b])
        bones = cc_[:, 0:128]
        hsel = cc_[:, 128:130]
        maskG = cc_[:, 130:258]
        maskA = cc_[0:64, 258:322]
        mu = rp[:, 0:48].rearrange("p (i c) -> p i c", i=6)
        w0, a0, kkp, kap, rkp = (rp[:, 48 + 8 * i:56 + 8 * i] for i in range(5))
        nw0 = A.alloc([128, 8], F32)
        P.ts("dve", nw0, w0, -1.0, None, ALU.mult, reads=[k_b], writes=[k_b])
        mhalf = A.alloc([128, 1], F32)
        P.memset("pool", mhalf, -0.5, writes=[k_b])
        PLx = A.alloc([128, 8, 64], F32)
        PLx_b = P.buf("PLx")
        mark = A.off
        BT = 256
        NQ = BT // 128
        NCB = BT // 64
        W3 = [A.alloc([128, DC, D], BF16) for i in range(3)]
        w_b = P.buf("rww")
        for i in range(3):
            self.wload(W3[i], self.rw_rkv[i].rearrange("(c p) e -> p c e", p=128), sem, w_b, nsplit=DC)
        w1 = A.alloc([128, DC, 64], BF16)
        a1 = A.alloc([128, DC, 64], BF16)
        g1 = A.alloc([128, DC, 160], BF16)
        w2 = A.alloc([64, D], BF16)
        a2 = A.alloc([64, D], BF16)
        g2a = A.alloc([128, D], BF16)
        g2b = A.alloc([32, D], BF16)
        P.dma("pool", sem, w1, self.rw_w1.rearrange("(c p) e -> p c e", p=128), writes=[w_b])
        P.dma("pool", sem, a1, self.rw_a1.rearrange("(c p) e -> p c e", p=128), writes=[w_b])
        P.dma("pool", sem, g1, self.rw_g1.rearrange("(c p) e -> p c e", p=128), writes=[w_b])
        P.dma("pool", sem, w2, self.rw_w2, writes=[w_b])
        P.dma("pool", sem, a2, self.rw_a2, writes=[w_b])
        P.dma("pool", sem, g2a, self.rw_g2[0:128, :], writes=[w_b])
        P.dma("pool", sem, g2b, self.rw_g2[128:160, :], writes=[w_b])
        dT = A.alloc([128, DC, BT], BF16)
        dT_b = P.buf("dT")
        xm = [A.alloc([128, DC, BT], BF16) for i in range(2)]
        xm_b = P.bufs(2, "xm")
        hw = A.alloc([64, BT], BF16)
        ha = A.alloc([64, BT], BF16)
        hga = A.alloc([128, BT], BF16)
        hgb = A.alloc([32, BT], BF16)
        h_b = P.buf("rwh")
        vtok = A.alloc([128, NQ, D], BF16)
        vt_b = P.buf("vtok")
        vt_sem = P.dsem("vtok")
        gtok = A.alloc([128, NQ, D], BF16)
        gt_b = P.buf("gtok")
        gt_sem = P.dsem("gtok")
        F = [A.alloc([128, BT], F32) for i in range(10)]
        F_b = P.bufs(10, "rwF")
        sqb = A.alloc([128, BT], BF16)
        sqb_b = P.buf("sqb")
        ARt = A.alloc([128, NCB, 2, 64], BF16)
        BKt = A.alloc([128, NCB, 2, 64], BF16)
        AB_b = P.buf("ARt")
        AB_sem = P.dsem("ARt")
        bkh = A.alloc([128, 2, BT], BF16)
        bkh_b = P.buf("bkh")
        stg = A.alloc([128, 2, NQ, 128], BF16)
        stg_b = P.buf("rstg")
        stg_sem = P.dsem("rstg")
        rk = A.alloc([128, BT], BF16)
        rk_b = P.buf("rk")
        bon = A.alloc([128, NQ, 16], F32)
        bon_b = P.buf("bon")
        nxm = [0]

        def mk_xm(i, blk):
            sl = nxm[0] % 2
            nxm[0] += 1
            for c in range(DC):
                P.stt("dve", xm[sl][:, c, :], dT[:, c, :], mu[:, i, c:c + 1], self.hnT[:, c, blk],
                      ALU.mult, ALU.add, reads=[dT_b, k_b] + list(self.hnT_b), writes=[xm_b[sl]])
            return xm[sl], xm_b[sl]

        np_ = [0]

        def pbank(n=None):
            np_[0] += 1
            return self.bank(np_[0] % 4)[:, 0:(BT if n is None else n)], self.pb[np_[0] % 4]

        for tb in range(S // BT):
            t0 = tb * BT
            blk = slice(t0, t0 + BT)
            rb_ = self.r1_b[t0 // 512]
            if tb == 0:
                P.ts("dve", dT[:, :, 0:1], self.hnT[:, :, 0:1], -1.0, None, ALU.mult, reads=list(self.hnT_b), writes=[dT_b])
                P.tt("dve", dT[:, :, 1:BT], self.hnT[:, :, 0:BT - 1], self.hnT[:, :, 1:BT], ALU.subtract,
                     reads=list(self.hnT_b), writes=[dT_b])
            else:
                P.tt("dve", dT, self.hnT[:, :, t0 - 1:t0 + BT - 1], self.hnT[:, :, blk], ALU.subtract,
                     reads=list(self.hnT_b), writes=[dT_b])
            x_, x_b = mk_xm(3, blk)
            ps, ps_b = pbank()
            for kc in range(DC):
                P.mm(ps[0:64, :], w1[:, kc, :], x_[:, kc, :], kc == 0, kc == DC - 1, reads=[w_b, x_b], writes=[ps_b])
            P.act(hw, ps[0:64, :], AF.Tanh, reads=[ps_b], writes=[h_b])
            x_, x_b = mk_xm(4, blk)
            ps, ps_b = pbank()
            for kc in range(DC):
                P.mm(ps[0:64, :], a1[:, kc, :], x_[:, kc, :], kc == 0, kc == DC - 1, reads=[w_b, x_b], writes=[ps_b])
            P.copy("act", ha, ps[0:64, :], reads=[ps_b], writes=[h_b])
            x_, x_b = mk_xm(5, blk)
            ps, ps_b = pbank()
            for kc in range(DC):
                P.mm(ps, g1[:, kc, 0:128], x_[:, kc, :], kc == 0, kc == DC - 1, reads=[w_b, x_b], writes=[ps_b])
            P.act(hga, ps, AF.Sigmoid, reads=[ps_b], writes=[h_b])
            ps, ps_b = pbank()
            for kc in range(DC):
                P.mm(ps[0:32, :], g1[:, kc, 128:160], x_[:, kc, :], kc == 0, kc == DC - 1, reads=[w_b, x_b], writes=[ps_b])
            P.act(hgb, ps[0:32, :], AF.Sigmoid, reads=[ps_b], writes=[h_b])
            for q in range(NQ):
                for cb in range(2):
                    ps, ps_b = pbank(512)
                    P.mm(ps, hga[:, q * 128:(q + 1) * 128], g2a[:, cb * 512:(cb + 1) * 512], True, False, reads=[h_b, w_b], writes=[ps_b])
                    P.mm(ps, hgb[:, q * 128:(q + 1) * 128], g2b[:, cb * 512:(cb + 1) * 512], False, True, reads=[h_b, w_b], writes=[ps_b])
                    P.copy("act", gtok[:, q, cb * 512:(cb + 1) * 512], ps, reads=[ps_b], writes=[gt_b])
            P.dma("sp", gt_sem, self.r_g[blk, :].rearrange("(q p) c -> p q c", p=128), gtok, reads=[gt_b], writes=[rb_])
            x_, x_b = mk_xm(2, blk)
            for q in range(NQ):
                for cb in range(2):
                    ps, ps_b = pbank(512)
                    for kc in range(DC):
                        P.mm(ps, x_[:, kc, q * 128:(q + 1) * 128], W3[2][:, kc, cb * 512:(cb + 1) * 512], kc == 0, kc == DC - 1,
                             reads=[w_b, x_b], writes=[ps_b])
                    P.copy("act", vtok[:, q, cb * 512:(cb + 1) * 512], ps, reads=[ps_b], writes=[vt_b])
            P.dma("sp", vt_sem, self.r_v[blk, :].rearrange("(q p) c -> p q c", p=128), vtok, reads=[vt_b], writes=[rb_])
            xr, xr_b = mk_xm(0, blk)
            xk, xk_b = mk_xm(1, blk)
            pbon, pbon_b = self.bank(7)[:, 0:NQ * 16].rearrange("p (q h) -> p q h", q=NQ), self.pb[7]
            for e in range(DC):
                ec = slice(e * 128, (e + 1) * 128)
                r_s, k_s, lw, a_s, kk, t1, t2, lpA, lpB, t3 = F
                (r_sb, k_sb, lw_b, a_sb, kk_b, t1_b, t2_b, lpA_b, lpB_b, t3_b) = F_b
                ps, ps_b = pbank()
                for kc in range(DC):
                    P.mm(ps, W3[0][:, kc, ec], xr[:, kc, :], kc == 0, kc == DC - 1, reads=[w_b, xr_b], writes=[ps_b])
                P.copy("act", r_s, ps, reads=[ps_b], writes=[r_sb])
                ps, ps_b = pbank()
                for kc in range(DC):
                    P.mm(ps, W3[1][:, kc, ec], xk[:, kc, :], kc == 0, kc == DC - 1, reads=[w_b, xk_b], writes=[ps_b])
                P.copy("act", k_s, ps, reads=[ps_b], writes=[k_sb])
                ps, ps_b = pbank()
                P.mm(ps, w2[:, ec], hw, True, True, reads=[w_b, h_b], writes=[ps_b])
                P.act(t1, ps, AF.Exp, reads=[ps_b, k_b], writes=[t1_b], scale=-1.0, bias=nw0[:, e:e + 1])
                P.act(t1, t1, AF.Ln, reads=[t1_b], writes=[t1_b], bias=1.0)
                P.act(t1, t1, AF.Exp, reads=[t1_b, k_b], writes=[t1_b], scale=-1.0, bias=mhalf)
                P.ts("dve", lw, t1, -1.0, None, ALU.mult, reads=[t1_b], writes=[lw_b])
                ps, ps_b = pbank()
                P.mm(ps, a2[:, ec], ha, True, True, reads=[w_b, h_b], writes=[ps_b])
                P.act(a_s, ps, AF.Sigmoid, reads=[ps_b, k_b], writes=[a_sb], bias=a0[:, e:e + 1])
                P.ts("dve", kk, k_s, kkp[:, e:e + 1], None, ALU.mult, reads=[k_sb, k_b], writes=[kk_b])
                P.act(sqb, kk, AF.Square, reads=[kk_b], writes=[sqb_b])
                ps, ps_b = pbank()
                P.mm(ps, bones, sqb, True, True, reads=[k_b, sqb_b], writes=[ps_b])
                P.ts("dve", t2, ps, 1e-24, None, ALU.max, reads=[ps_b], writes=[t2_b])
                P.act(t2, t2, AF.Sqrt, reads=[t2_b], writes=[t2_b])
                P.op("dve", lambda e_, t2=t2: e_.reciprocal(t2, t2), reads=[t2_b], writes=[t2_b])
                P.tt("dve", kk, kk, t2, ALU.mult, reads=[t2_b, kk_b], writes=[kk_b])
                P.ts("pool", t1, a_s, -1.0, kap[:, e:e + 1], ALU.add, ALU.mult, reads=[a_sb, k_b], writes=[t1_b])
                P.stt("dve", k_s, t1, 1.0, k_s, ALU.add, ALU.mult, reads=[t1_b, k_sb], writes=[k_sb])
                P.tt("pool", a_s, kk, a_s, ALU.mult, reads=[kk_b, a_sb], writes=[a_sb])
                v3 = lambda ap: ap.rearrange("p (c t) -> p c t", t=64)
                src_, src_bb = lw, lw_b
                pp_ = [(lpA, lpA_b), (lpB, lpB_b)]
                for si, sft in enumerate((1, 2, 4, 8, 16, 32)):
                    dst_, dst_bb = pp_[si % 2]
                    P.copy("pool", v3(dst_)[:, :, 0:sft], v3(src_)[:, :, 0:sft], reads=[src_bb], writes=[dst_bb])
                    P.tt("dve", v3(dst_)[:, :, sft:64], v3(src_)[:, :, sft:64], v3(src_)[:, :, 0:64 - sft], ALU.add,
                         reads=[src_bb], writes=[dst_bb])
                    src_, src_bb = dst_, dst_bb
                lp, lp_b = src_, src_bb
                P.tt("dve", t2, lp, lw, ALU.subtract, reads=[lp_b, lw_b], writes=[t2_b])
                P.act(t2, t2, AF.Exp, reads=[t2_b], writes=[t2_b])
                P.stt("dve", ARt[:, :, 0, :], v3(kk), -1.0, v3(t2), ALU.mult, ALU.mult, reads=[kk_b, t2_b], writes=[AB_b])
                P.act(t2, lp, AF.Exp, reads=[lp_b], writes=[t2_b])
                P.tt("dve", ARt[:, :, 1, :], v3(r_s), v3(t2), ALU.mult, reads=[r_sb, t2_b], writes=[AB_b])
                P.act(t2, lp, AF.Exp, reads=[lp_b], writes=[t2_b], scale=-1.0)
                P.tt("dve", BKt[:, :, 0, :], v3(a_s), v3(t2), ALU.mult, reads=[a_sb, t2_b], writes=[AB_b])
                P.tt("pool", BKt[:, :, 1, :], v3(k_s), v3(t2), ALU.mult, reads=[k_sb, t2_b], writes=[AB_b])
                P.tt("dve", v3(t3), v3(lp)[:, :, 63:64].to_broadcast([128, NCB, 64]), v3(lp), ALU.subtract, reads=[lp_b], writes=[t3_b])
                P.act(t3, t3, AF.Exp, reads=[t3_b], writes=[t3_b])
                P.act(PLx[:, e, tb * NCB:(tb + 1) * NCB], v3(lp)[:, :, 63], AF.Exp, reads=[lp_b], writes=[PLx_b])
                P.tt("dve", bkh[:, 0, :], a_s, t3, ALU.mult, reads=[a_sb, t3_b], writes=[bkh_b])
                P.tt("pool", bkh[:, 1, :], k_s, t3, ALU.mult, reads=[k_sb, t3_b], writes=[bkh_b])
                tp = self.psum[:, 4 * 512:6 * 512].bitcast(BF16)[:, 0:2 * NQ * 128].rearrange("p (i q c) -> p i q c", i=2, q=NQ)
                tp_b = self.pb[4]
                for i in range(2):
                    for q in range(NQ):
                        P.tr(tp[:, i, q, :], bkh[:, i, q * 128:(q + 1) * 128], self.ident[:], reads=[bkh_b, self.c_b], writes=[tp_b])
                P.copy("act", stg, tp, reads=[tp_b], writes=[stg_b])
                P.dma("sp", stg_sem, self.r_bh[blk, ec].rearrange("(q p) c -> p q c", p=128), stg[:, 0], reads=[stg_b], writes=[rb_])
                P.dma("sp", stg_sem, self.r_kh[blk, ec].rearrange("(q p) c -> p q c", p=128), stg[:, 1], reads=[stg_b], writes=[rb_])
                P.dma("sp", AB_sem, self.r_AR[e, :, tb * NCB:(tb + 1) * NCB, :, :], ARt, reads=[AB_b], writes=[rb_])
                P.dma("sp", AB_sem, self.r_BK[e, :, tb * NCB:(tb + 1) * NCB, :, :], BKt, reads=[AB_b], writes=[rb_])
                P.stt("dve", rk, r_s, rkp[:, e:e + 1], k_s, ALU.mult, ALU.mult, reads=[r_sb, k_sb, k_b], writes=[rk_b])
                for q in range(NQ):
                    P.mm(pbon[:, q, 2 * e:2 * e + 2], rk[:, q * 128:(q + 1) * 128], hsel, e == 0 and q == 0, e == 7 and q == NQ - 1,
                         reads=[rk_b, k_b], writes=[pbon_b])
            P.copy("dve", bon, pbon, reads=[pbon_b], writes=[bon_b])
            for q in range(NQ):
                P.tt("dve", gtok[:, q, :].rearrange("p (h n) -> p h n", h=16), vtok[:, q, :].rearrange("p (h n) -> p h n", h=16),
                     bon[:, q, :].unsqueeze(2).to_broadcast([128, 16, 64]), ALU.mult, reads=[vt_b, bon_b], writes=[gt_b])
            P.dma("sp", gt_sem, self.r_bv[blk, :].rearrange("(q p) c -> p q c", p=128), gtok, reads=[gt_b], writes=[rb_])
        pl_sem = P.dsem("plx")
        plb = P.buf("plxd")
        P.dma("sp", pl_sem, self.r_pl, PLx, reads=[PLx_b], writes=[plb])
        A.reset(mark)
        PL2 = A.alloc([64, 16, 64], F32)
        PL2_b = P.buf("PL2")
        P.dma("sp", pl_sem, PL2, self.r_pl.rearrange("(a n) e c -> n e a c", a=2), reads=[plb], writes=[PL2_b])
        ARc = [A.alloc([64, 16, 2, 64], BF16) for i in range(2)]
        BKc = [A.alloc([64, 16, 2, 64], BF16) for i in range(2)]
        BH = [A.alloc([64, D], BF16) for i in range(2)]
        KH = [A.alloc([64, D], BF16) for i in range(2)]
        Vt = [A.alloc([64, D], BF16) for i in range(2)]
        Ut = [A.alloc([64, D], BF16) for i in range(2)]
        in_b = P.bufs(2, "r2in")
        uv_b = P.bufs(2, "r2uv")
        in_sem = [P.dsem(f"r2in{i}") for i in range(2)]
        Gb = A.alloc([64, 8, 128], BF16)
        Gk = A.alloc([64, 8, 128], BF16)
        Gm_b = P.buf("Gm")
        Ap = [A.alloc([64, 8, 64], BF16) for i in range(2)]
        Mp = [A.alloc([64, 8, 64], BF16) for i in range(2)]
        TT = [A.alloc([64, 8, 64], BF16) for i in range(2)]
        Ap_b, Mp_b, TT_b = P.bufs(2, "Ap"), P.bufs(2, "Mp"), P.bufs(2, "TT")
        Xs = A.alloc([64, 8, 64], BF16)
        Xs_b = P.buf("Xs")
        H = A.alloc([64, 16, 64], F32)
        Hb = A.alloc([64, 16, 64], BF16)
        H_b = P.bufs(2, "H")
        P.memset("pool", H, 0.0, writes=H_b)
        P.memset("pool", Hb, 0.0, writes=H_b)
        ych = [A.alloc([64, D], F32) for i in range(2)]
        ych_b = P.bufs(2, "ych")
        ych_sem = [P.dsem(f"ych{i}") for i in range(2)]
        idb = self.ident[0:64, 0:64].unsqueeze(1).to_broadcast([64, 8, 64])
        mG = maskG[0:64, :].unsqueeze(1).to_broadcast([64, 8, 128])
        ARd = self.r_AR.rearrange("e (a n) c x t -> n (e a) c x t", a=2)
        BKd = self.r_BK.rearrange("e (a n) c x t -> n (e a) c x t", a=2)
        v8 = lambda bk: self.bank(bk)[0:64, :].rearrange("p (h t) -> p h t", h=8)
        for c in range(64):
            sl = c % 2
            rows = slice(c * 64, (c + 1) * 64)
            rb_ = [self.r1_b[c // 8]]
            P.dma("sp", in_sem[sl], ARc[sl], ARd[:, :, c, :, :], reads=rb_, writes=[in_b[sl]])
            P.dma("sp", in_sem[sl], BKc[sl], BKd[:, :, c, :, :], reads=rb_, writes=[in_b[sl]])
            P.dma("sp", in_sem[sl], BH[sl], self.r_bh[rows, :], reads=rb_, writes=[in_b[sl]])
            P.dma("sp", in_sem[sl], KH[sl], self.r_kh[rows, :], reads=rb_, writes=[in_b[sl]])
            P.dma("sp", in_sem[sl], Vt[sl], self.r_v[rows, :], reads=rb_, writes=[in_b[sl]])
            for hf in range(2):
                hds = list(range(8 * hf, 8 * hf + 8))
                pGb = self.psum[0:64, 0:1024].rearrange("p (h t) -> p h t", h=8)
                pGk = self.psum[0:64, 1024:2048].rearrange("p (h t) -> p h t", h=8)
                pAm = v8(4)
                for hi, h in enumerate(hds):
                    P.mm(pGb[:, hi, :], BKc[sl][:, h, 0, :], ARc[sl][:, h, :, :], hi % 4 == 0, hi % 4 == 3, reads=[in_b[sl]], writes=[self.pb[0]])
                for hi, h in enumerate(hds):
                    P.mm(pGk[:, hi, :], BKc[sl][:, h, 1, :], ARc[sl][:, h, :, :], hi % 4 == 0, hi % 4 == 3, reads=[in_b[sl]], writes=[self.pb[2]])
                for hi, h in enumerate(hds):
                    P.mm(pAm[:, hi, :], ARc[sl][:, h, 0, :], BKc[sl][:, h, 0, :], hi == 0, hi == 7, reads=[in_b[sl]], writes=[self.pb[4]])
                P.tt("dve", Gb, pGb, mG, ALU.mult, reads=[self.pb[0], k_b], writes=[Gm_b])
                P.tt("dve", Gk, pGk, mG, ALU.mult, reads=[self.pb[2], k_b], writes=[Gm_b])
                P.copy("pool", Mp[0], Gb[:, :, 0:64], reads=[Gm_b], writes=[Mp_b[0]])
                P.tt("dve", Ap[0], pAm, maskA.unsqueeze(1).to_broadcast([64, 8, 64]), ALU.mult, reads=[self.pb[4], k_b], writes=[Ap_b[0]])
                P.tt("pool", TT[0], Mp[0], idb, ALU.add, reads=[Mp_b[0], self.c_b], writes=[TT_b[0]])
                cur = 0
                for rd in range(5):
                    nxt = 1 - cur
                    pA2, pM2, pT2 = v8(4), v8(5), v8(6)
                    for hi in range(8):
                        P.mm(pA2[:, hi, :], Mp[cur][:, hi, :], Ap[cur][:, hi, :], hi == 0, hi == 7,
                             reads=[Mp_b[cur], Ap_b[cur]], writes=[self.pb[4]])
                    if rd < 4:
                        for hi in range(8):
                            P.mm(pM2[:, hi, :], Ap[cur][:, hi, :], Mp[cur][:, hi, :], hi == 0, hi == 7,
                                 reads=[Mp_b[cur], Ap_b[cur]], writes=[self.pb[5]])
                    P.copy("act", Ap[nxt], pA2, reads=[self.pb[4]], writes=[Ap_b[nxt]])
                    if rd < 4:
                        P.copy("dve", Mp[nxt], pM2, reads=[self.pb[5]], writes=[Mp_b[nxt]])
                    for hi in range(8):
                        P.mm(pT2[:, hi, :], Ap[nxt][:, hi, :], TT[cur][:, hi, :], hi == 0, hi == 7,
                             reads=[Ap_b[nxt], TT_b[cur]], writes=[self.pb[6]])
                    P.tt("dve", TT[nxt], TT[cur], pT2, ALU.add, reads=[self.pb[6], TT_b[cur]], writes=[TT_b[nxt]])
                    cur = nxt
                TTf, TTf_b = TT[cur], TT_b[cur]
                pX, pU, pY, pH = v8(7), v8(4), v8(5), v8(6)
                for hi, h in enumerate(hds):
                    hc = slice(h * 64, h * 64 + 64)
                    P.mm(pX[:, hi, :], ARc[sl][:, h, 0, :], Hb[:, h, :], hi == 0, False, reads=[in_b[sl], H_b[hf]], writes=[self.pb[7]])
                    P.mm(pX[:, hi, :], Gk[:, hi, 0:64], Vt[sl][:, hc], False, hi == 7, reads=[Gm_b, in_b[sl]], writes=[self.pb[7]])
                P.copy("act", Xs, pX, reads=[self.pb[7]], writes=[Xs_b])
                for hi, h in enumerate(hds):
                    P.mm(pU[:, hi, :], TTf[:, hi, :], Xs[:, hi, :], hi == 0, hi == 7, reads=[TTf_b, Xs_b], writes=[self.pb[4]])
                h0 = 8 * hf * 64
                P.copy("act", Ut[sl][:, h0:h0 + 512], self.bank(4)[0:64, :], reads=[self.pb[4]], writes=[uv_b[sl]])
                for hi, h in enumerate(hds):
                    hc = slice(h * 64, h * 64 + 64)
                    P.mm(pY[:, hi, :], ARc[sl][:, h, 1, :], Hb[:, h, :], hi == 0, False, reads=[in_b[sl], H_b[hf]], writes=[self.pb[5]])
                    P.mm(pY[:, hi, :], Gb[:, hi, 64:128], Ut[sl][:, hc], False, False, reads=[Gm_b, uv_b[sl]], writes=[self.pb[5]])
                    P.mm(pY[:, hi, :], Gk[:, hi, 64:128], Vt[sl][:, hc], False, hi == 7, reads=[Gm_b, in_b[sl]], writes=[self.pb[5]])
                P.copy("act", ych[sl][:, h0:h0 + 512], self.bank(5)[0:64, :], reads=[self.pb[5]], writes=[ych_b[sl]])
                for hi, h in enumerate(hds):
                    hc = slice(h * 64, h * 64 + 64)
                    P.mm(pH[:, hi, :], BH[sl][:, hc], Ut[sl][:, hc], hi == 0, False, reads=[in_b[sl], uv_b[sl]], writes=[self.pb[6]])
                    P.mm(pH[:, hi, :], KH[sl][:, hc], Vt[sl][:, hc], False, hi == 7, reads=[in_b[sl]], writes=[self.pb[6]])
                Hh = H[:, 8 * hf:8 * hf + 8, :]
                P.tt("dve", Hh, Hh, PL2[:, 8 * hf:8 * hf + 8, c:c + 1].to_broadcast([64, 8, 64]), ALU.mult,
                     reads=[PL2_b, H_b[hf]], writes=[H_b[hf]])
                P.tt("dve", Hh, Hh, pH, ALU.add, reads=[self.pb[6], H_b[hf]], writes=[H_b[hf]])
                P.copy("pool", Hb[:, 8 * hf:8 * hf + 8, :], Hh, reads=[H_b[hf]], writes=[H_b[hf]])
            P.dma("sp", ych_sem[sl], self.r_y[rows, :], ych[sl], reads=[ych_b[sl]], writes=[self.ry_b[c // 2]])
        A.reset(mark)
        gn = A.alloc([128, 2, D], F32)
        P.dma("sp", sem, gn, self.rw_gn, writes=[k_b])
        yt = [A.alloc([128, D], F32) for i in range(2)]
        bvt = [A.alloc([128, D], BF16) for i in range(2)]
        gtt = [A.alloc([128, D], BF16) for i in range(2)]
        i3_b = P.bufs(2, "r3in")
        i3_sem = [P.dsem(f"r3in{i}") for i in range(2)]
        sqt = A.alloc([128, D], F32)
        sqt_b = P.buf("sqt")
        stt_ = A.alloc([128, 4, 16], F32)
        st_b = P.buf("r3st")
        ot = [A.alloc([128, D], BF16) for i in range(2)]
        ot_b = P.bufs(2, "r3o")
        ot_sem = [P.dsem(f"r3o{i}") for i in range(2)]
        v16 = lambda ap: ap.rearrange("p (h n) -> p h n", h=16)
        for t in range(NT):
            sl = t % 2
            tile = slice(t * 128, (t + 1) * 128)
            P.dma("sp", i3_sem[sl], yt[sl], self.r_y[tile, :], reads=[self.ry_b[t]], writes=[i3_b[sl]])
            P.dma("sp", i3_sem[sl], bvt[sl], self.r_bv[tile, :], reads=[self.r1_b[t // 4]], writes=[i3_b[sl]])
            P.dma("sp", i3_sem[sl], gtt[sl], self.r_g[tile, :], reads=[self.r1_b[t // 4]], writes=[i3_b[sl]])
            y_ = yt[sl]
            mean, ex2, var, rstd = (stt_[:, i, :] for i in range(4))
            P.op("dve", lambda e_, y_=y_, mean=mean: e_.tensor_reduce(mean, v16(y_), AX.X, ALU.add), reads=[i3_b[sl]], writes=[st_b])
            P.act(sqt, y_, AF.Square, reads=[i3_b[sl]], writes=[sqt_b])
            P.op("dve", lambda e_, ex2=ex2: e_.tensor_reduce(ex2, v16(sqt), AX.X, ALU.add), reads=[sqt_b], writes=[st_b])
            P.ts("dve", mean, mean, 1.0 / 64, None, ALU.mult, reads=[st_b], writes=[st_b])
            P.stt("dve", var, mean, -1.0, mean, ALU.mult, ALU.mult, reads=[st_b], writes=[st_b])
            P.stt("dve", var, ex2, 1.0 / 64, var, ALU.mult, ALU.add, reads=[st_b], writes=[st_b])
            P.ts("dve", var, var, 64e-5, None, ALU.add, reads=[st_b], writes=[st_b])
            P.act(rstd, var, AF.Sqrt, reads=[st_b], writes=[st_b])
            P.op("dve", lambda e_, rstd=rstd: e_.reciprocal(rstd, rstd), reads=[st_b], writes=[st_b])
            P.tt("dve", v16(y_), v16(y_), mean.unsqueeze(2).to_broadcast([128, 16, 64]), ALU.subtract, reads=[st_b, i3_b[sl]], writes=[i3_b[sl]])
            P.tt("dve", v16(y_), v16(y_), rstd.unsqueeze(2).to_broadcast([128, 16, 64]), ALU.mult, reads=[st_b, i3_b[sl]], writes=[i3_b[sl]])
            P.tt("pool", y_, y_, gn[:, 0, :], ALU.mult, reads=[k_b, i3_b[sl]], writes=[i3_b[sl]])
            P.tt("pool", y_, y_, gn[:, 1, :], ALU.add, reads=[k_b, i3_b[sl]], writes=[i3_b[sl]])
            P.tt("dve", y_, y_, bvt[sl], ALU.add, reads=[i3_b[sl]], writes=[i3_b[sl]])
            P.tt("dve", ot[sl], y_, gtt[sl], ALU.mult, reads=[i3_b[sl]], writes=[ot_b[sl]])
            P.dma("sp", ot_sem[sl], self.ob[tile, 0:D], ot[sl], reads=[ot_b[sl]], writes=[self.ob_b[t]])
        self.tm_proj_phase(DC, self.rw_wo, res, res_b, dst, dst_b, "rwkv")

    def build(self):
        P = self.P
        src, src_b = self.x_in, self.xin_b
        pp = [(self.xa, self.xa_b), (self.xb, self.xb_b)]
        ip = 0
        for l in self.layers:
            if self.do_mix:
                self.norm_phase(src, src_b, self.gmix[:, l, :])
                dst, dst_b = pp[ip]
                ip ^= 1
                if l == 0:
                    self.moba(src, src_b, dst, dst_b)
                if l == 1:
                    self.rwkv(src, src_b, dst, dst_b)
                if l == 2:
                    self.ssd(src, src_b, dst, dst_b)
                if l == 3:
                    self.conformer(src, src_b, dst, dst_b)
                src, src_b = dst, dst_b
            if self.do_ffn:
                self.norm_phase(src, src_b, self.gffn[:, l, :])
                dst, dst_b = pp[ip]
                ip ^= 1
                self.ffn_phase(l, src, src_b, dst, dst_b)
                src, src_b = dst, dst_b
        self.A.reset()
        gf = self.A.alloc([128, D], F32)
        gf_b = P.buf("gf")
        P.dma("sp", self.c_sem, gf, self.norm_final, writes=[gf_b])
        for t in range(NT):
            xt, xt_b, sl, k = self.load_x(src, src_b, t)
            rstd, ss_b = self.rms_tile(xt, xt_b, k)
            P.act(xt[:], xt[:], AF.Copy, reads=[xt_b, ss_b], writes=[xt_b], scale=rstd)
            P.tt("dve", xt[:], xt[:], gf, ALU.mult, reads=[xt_b, gf_b], writes=[xt_b])
            self.store_x(self.out, self.out_b, t, xt, xt_b, sl)
        return P.finish()


def _fm(v, nch):
    v = np.asarray(v, np.float32)
    lead = v.shape[:-1]
    a = v.reshape(lead + (nch, 128))
    a = np.moveaxis(a, -1, 0)
    return np.ascontiguousarray(a)


def _bc(v, n=128):
    v = np.asarray(v, np.float32).reshape(1, -1)
    return np.ascontiguousarray(np.broadcast_to(v, (n, v.shape[1])))


def make_shared(inp):
    m = {}
    f = lambda k: np.ascontiguousarray(np.asarray(inp[k], np.float32))
    m["ident"] = np.eye(128, dtype=np.float32).astype(ml_dtypes.bfloat16)
    m["norm_mix"] = _fm(inp["norm_mix"], DC)
    m["norm_ffn"] = _fm(inp["norm_ffn"], DC)
    m["norm_final"] = _bc(inp["norm_final"])
    m["ffn_w_up"] = f("ffn_w_up")
    m["ffn_w_down"] = f("ffn_w_down")
    cw = np.asarray(inp["ffn_conv_w"], np.float32)
    cw = cw.transpose(0, 2, 1).reshape(4, 2 * FC, 128, 3)
    m["ffn_cw"] = np.ascontiguousarray(cw.transpose(2, 0, 1, 3))
    m["ffn_cb"] = _fm(inp["ffn_conv_b"], 2 * FC)
    m["moba_w_qkv"] = f("moba_w_qkv")[0]
    m["moba_w_o"] = f("moba_w_o")[0]
    kk = np.arange(S) // 256
    m["blkind"] = (kk[None, :] == np.arange(16)[:, None]).astype(np.float32).astype(ml_dtypes.bfloat16)
    qb = np.arange(16)[:, None]
    nn = np.arange(16)[None, :]
    mcst = np.stack([np.where(nn < qb, 0.0, -1e30), (nn < qb).astype(np.float32), (nn == qb).astype(np.float32)]).astype(np.float32)
    m["mconst"] = np.ascontiguousarray(np.broadcast_to(mcst[None], (128, 3, 16, 16)))
    m["tri"] = (np.arange(128)[None, :] >= np.arange(128)[:, None]).astype(np.float32).astype(ml_dtypes.bfloat16)
    m["rwkv_w_rkv"] = f("rwkv_w_rkv")[0]
    m["rwkv_w_o"] = f("rwkv_w_o")[0]
    for k_ in ("w1", "a1", "g1", "w2", "a2", "g2"):
        m["rwkv_" + k_] = f("rwkv_" + k_)[0]
    mu_ = _fm(inp["rwkv_mu"][0], 8).reshape(128, 48)
    m["rw_p"] = np.ascontiguousarray(np.concatenate(
        [mu_] + [_fm(np.asarray(inp["rwkv_" + k_][0]).reshape(-1), 8) for k_ in ("w0", "a0", "k_k", "k_a", "r_k")], axis=1))
    m["rw_gn"] = np.ascontiguousarray(np.stack([_bc(inp["rwkv_gn_w"][0]), _bc(inp["rwkv_gn_b"][0])], axis=1))
    i128 = np.arange(128)
    bones = (i128[:, None] // 64 == i128[None, :] // 64).astype(np.float32)
    hsel = (i128[:, None] // 64 == np.arange(2)[None, :]).astype(np.float32)
    s64 = i128[:, None] % 64
    t64 = i128[None, :] % 64
    maskG = np.where(i128[None, :] < 64, s64 < t64, s64 <= t64).astype(np.float32)
    maskA = np.zeros((128, 64), np.float32)
    maskA[:64] = (np.arange(64)[None, :] < np.arange(64)[:, None]).astype(np.float32)
    m["rw_c"] = np.ascontiguousarray(np.concatenate([bones, hsel, maskG, maskA], axis=1)).astype(ml_dtypes.bfloat16)
    m["ssd_w_in"] = f("ssd_w_in")[0]
    m["ssd_w_out"] = f("ssd_w_out")[0]
    scw = np.asarray(inp["ssd_conv_w"], np.float32)[0]
    scw = scw.T.reshape(24, 128, 4).transpose(1, 0, 2).reshape(128, 96)
    m["ssd_p"] = np.ascontiguousarray(np.concatenate([scw, _fm(inp["ssd_conv_b"][0], 24)], axis=1))
    m["ssd_t"] = np.ascontiguousarray(np.concatenate([_bc(inp["ssd_dt_bias"][0]), _bc(inp["ssd_a_log"][0]), _bc(inp["ssd_d"][0])], axis=1))
    m["ssd_nw"] = _bc(inp["ssd_norm_w"][0])
    nbm = np.where(np.arange(128)[:, None] > np.arange(128)[None, :], -30000.0, 0.0).astype(np.float32)
    m["nbmask"] = np.ascontiguousarray(np.broadcast_to(nbm[:, None, :], (128, 4, 128))).astype(ml_dtypes.bfloat16)
    m["conf_w_pw1"] = f("conf_w_pw1")[0]
    m["conf_w_pw2"] = f("conf_w_pw2")[0]
    dww = np.asarray(inp["conf_dw_w"], np.float32)[0]
    dww = dww.T.reshape(8, 128, 31).transpose(1, 0, 2).reshape(128, 8 * 31)
    m["conf_p"] = np.ascontiguousarray(np.concatenate([
        _fm(inp["conf_b_pw1"][0], 16), dww, _fm(inp["conf_dw_b"][0], 8),
        _fm(inp["conf_ln_w"][0], 8), _fm(inp["conf_ln_b"][0], 8)], axis=1))
    m["conf_b2"] = _bc(inp["conf_b_pw2"][0])
    return m


def make_inputs(inp, b, shared=None):
    m = dict(shared if shared is not None else make_shared(inp))
    m["x"] = np.ascontiguousarray(inp["x"][b], dtype=np.float32)
    return m


_NC = {}


def kernel(**inputs):
    if "nc" not in _NC:
        _NC["nc"] = Model().build()
    nc = _NC["nc"]
    shared = make_shared(inputs)
    in_maps = [make_inputs(inputs, b, shared) for b in range(8)]
    res = run_bass_kernel_spmd(nc, in_maps, core_ids=list(range(8)))
    return np.stack([np.asarray(r["out"], np.float32) for r in res.results], axis=0)
```

```python
import numpy as np
from contextlib import ExitStack
import ml_dtypes
import concourse.bass as bass
import concourse.mybir as mybir
from concourse.bass_utils import run_bass_kernel_spmd

F32 = mybir.dt.float32
BF16 = mybir.dt.bfloat16
AF = mybir.ActivationFunctionType
ALU = mybir.AluOpType
AX = mybir.AxisListType

S = 4096
D = 1024
DFF = 2816
NT = S // 128
DC = D // 128
FC = DFF // 128
EPS = 1e-6
ENGS = ("pe", "act", "dve", "pool", "sp")
STRICT = ("act", "dve", "pool")
EPOCH = 16384


class Buf:
    __slots__ = ("w", "r", "name")

    def __init__(self, name=""):
        self.w = None
        self.r = {}
        self.name = name


class Prog:
    def __init__(self):
        self.nc = bass.Bass("TRN2", target_bir_lowering=False)
        self.es = ExitStack()
        self.streams = {e: [] for e in ENGS}
        self.cnt = {e: 0 for e in ENGS}
        self.seen = {e: {} for e in ENGS}
        self.sems = {}
        self.epochs = set()
        self.nbuf = 0

    def sb(self, name, shape, dt):
        return self.es.enter_context(self.nc.sbuf_tensor(name, list(shape), dt))

    def ps(self, name, shape, dt):
        return self.es.enter_context(self.nc.psum_tensor(name, list(shape), dt))

    def dram(self, name, shape, dt, kind="Internal"):
        return self.nc.dram_tensor(name, list(shape), dt, kind=kind).ap()

    def dsem(self, name):
        k = ("d", name)
        assert k not in self.cnt
        self.cnt[k] = 0
        return k

    def buf(self, name=""):
        return Buf(name)

    def bufs(self, n, name=""):
        return [Buf(name + str(i)) for i in range(n)]

    def _dep(self, eng, dep):
        if dep is None:
            return
        k, v = dep
        if k == eng and eng not in STRICT:
            return
        if self.seen[eng].get(k, 0) >= v:
            return
        self.seen[eng][k] = v
        if k in ENGS:
            ep = (v - 1) // EPOCH
            self.epochs.add((k, ep))
            self.streams[eng].append(("w", (k, ep), (v - 1) % EPOCH + 1))
        else:
            self.streams[eng].append(("w", k, v))

    def _deps(self, eng, reads, writes):
        for b in reads:
            self._dep(eng, b.w)
        for b in writes:
            self._dep(eng, b.w)
            for k, v in b.r.items():
                self._dep(eng, (k, v))

    def op(self, eng, fn, reads=(), writes=()):
        self._deps(eng, reads, writes)
        self.cnt[eng] += 1
        n = self.cnt[eng]
        self.epochs.add((eng, (n - 1) // EPOCH))
        self.streams[eng].append(("o", fn, (eng, (n - 1) // EPOCH), 1))
        for b in reads:
            b.r[eng] = n
        for b in writes:
            b.w = (eng, n)
            b.r = {}

    def dma(self, q, sem, out, in_, reads=(), writes=(), **kw):
        self._deps(q, reads, writes)
        self.cnt[sem] += 16
        n = self.cnt[sem]
        self.streams[q].append(("o", lambda e: e.dma_start(out=out, in_=in_, **kw), sem, 16))
        for b in reads:
            b.r[sem] = n
        for b in writes:
            b.w = (sem, n)
            b.r = {}

    def mm(self, out, lhsT, rhs, start, stop, reads=(), writes=()):
        self.op("pe", lambda e: e.matmul(out, lhsT, rhs, start=start, stop=stop), reads, writes)

    def tr(self, out, in_, ident, reads=(), writes=()):
        self.op("pe", lambda e: e.transpose(out, in_, ident), reads, writes)

    def act(self, out, in_, func, reads=(), writes=(), **kw):
        self.op("act", lambda e: e.activation(out, in_, func, **kw), reads, writes)

    def tt(self, eng, out, in0, in1, op, reads=(), writes=()):
        self.op(eng, lambda e: e.tensor_tensor(out, in0, in1, op), reads, writes)

    def ts(self, eng, out, in0, s1, s2, op0, op1=None, reads=(), writes=()):
        if op1 is None:
            self.op(eng, lambda e: e.tensor_scalar(out, in0, s1, None, op0), reads, writes)
        else:
            self.op(eng, lambda e: e.tensor_scalar(out, in0, s1, s2, op0, op1), reads, writes)

    def stt(self, eng, out, in0, scalar, in1, op0, op1, reads=(), writes=()):
        self.op(eng, lambda e: e.scalar_tensor_tensor(out, in0, scalar, in1, op0, op1), reads, writes)

    def copy(self, eng, out, in_, reads=(), writes=()):
        if eng == "act":
            self.op(eng, lambda e: e.copy(out, in_), reads, writes)
        else:
            self.op(eng, lambda e: e.tensor_copy(out, in_), reads, writes)

    def memset(self, eng, ap, val, writes=()):
        self.op(eng, lambda e: e.memset(ap, val), (), writes)

    def barrier(self):
        for e in ENGS:
            for k, v in self.cnt.items():
                if v > 0 and k != e:
                    self._dep(e, (k, v))

    def finish(self):
        nc = self.nc
        for k, v in self.cnt.items():
            if v > 0 and k != "sp":
                self._dep("sp", (k, v))
        for k in self.cnt:
            if k not in ENGS:
                self.sems[k] = self.es.enter_context(nc.semaphore("s_" + k[1]))
        for (k, ep) in sorted(self.epochs):
            self.sems[(k, ep)] = self.es.enter_context(nc.semaphore(f"e_{k}_{ep}"))
        streams, sems = self.streams, self.sems

        def replay(name, e):
            for it in streams[name]:
                if it[0] == "w":
                    e.wait_ge(sems[it[1]], it[2])
                else:
                    it[1](e).then_inc(sems[it[2]], it[3])

        with nc.Block() as block:
            @block.tensor
            def _(e):
                replay("pe", e)

            @block.scalar
            def _(e):
                replay("act", e)

            @block.vector
            def _(e):
                replay("dve", e)

            @block.gpsimd
            def _(e):
                replay("pool", e)

            @block.sync
            def _(e):
                replay("sp", e)
        self.es.close()
        return nc


class Arena:
    def __init__(self, P, name, nbytes):
        self.P = P
        self.t = P.sb(name, [128, nbytes // 2], BF16)
        self.n = nbytes // 2
        self.off = 0

    def reset(self, to=0):
        self.P.barrier()
        self.off = to

    def alloc(self, shape, dt):
        ne = 1
        for d in shape[1:]:
            ne *= d
        if dt == F32:
            ne *= 2
        ne = (ne + 15) // 16 * 16
        assert self.off + ne <= self.n, (self.off, ne, self.n)
        ap = self.t[0:shape[0], self.off:self.off + ne]
        self.off += ne
        if dt == F32:
            ap = ap.bitcast(F32)
        nfree = 1
        for d in shape[1:]:
            nfree *= d
        ap = ap[:, 0:nfree]
        if len(shape) == 3:
            ap = ap.rearrange("p (a b) -> p a b", a=shape[1])
        elif len(shape) == 4:
            ap = ap.rearrange("p (a b c) -> p a b c", a=shape[1], b=shape[2])
        return ap


class Model:
    def __init__(self, layers=(0, 1, 2, 3), do_mix=True, do_ffn=True):
        self.P = P = Prog()
        self.layers = layers
        self.do_mix = do_mix
        self.do_ffn = do_ffn
        d = lambda n, sh, dt=F32: P.dram(n, sh, dt, "ExternalInput")
        self.x_in = d("x", [S, D])
        self.out = P.dram("out", [S, D], F32, "ExternalOutput")
        self.ident_d = d("ident", [128, 128], BF16)
        self.norm_mix = d("norm_mix", [128, 4, DC])
        self.norm_ffn = d("norm_ffn", [128, 4, DC])
        self.norm_final = d("norm_final", [128, D])
        self.ffn_w_up = d("ffn_w_up", [4, FC, 128, DC * 2 * 128])
        self.ffn_w_down = d("ffn_w_down", [4, 128, FC, D])
        self.ffn_cw = d("ffn_cw", [128, 4, 2 * FC, 3])
        self.ffn_cb = d("ffn_cb", [128, 4, 2 * FC])
        self.conf_w1 = d("conf_w_pw1", [D, 2 * D])
        self.conf_w2 = d("conf_w_pw2", [D, D])
        self.conf_p = d("conf_p", [128, 16 + 8 * 31 + 8 + 8 + 8])
        self.conf_b2 = d("conf_b2", [128, D])
        self.moba_wqkv = d("moba_w_qkv", [D, 3 * D])
        self.moba_wo = d("moba_w_o", [D, D])
        self.blkind = d("blkind", [16, S], BF16)
        self.mconst = d("mconst", [128, 3, 16, 16])
        self.tri_d = d("tri", [128, 128], BF16)
        self.ssd_win = d("ssd_w_in", [D, 5152])
        self.ssd_wout = d("ssd_w_out", [2 * D, D])
        self.ssd_p = d("ssd_p", [128, 24 * 5])
        self.ssd_t = d("ssd_t", [128, 96])
        self.ssd_nw = d("ssd_nw", [128, 2 * D])
        self.nb_d = d("nbmask", [128, 4, 128], BF16)
        self.s_xB = P.dram("s_xB", [S, 2560], BF16)
        self.s_xB_b = P.bufs(8, "sxB")
        self.s_BCT = P.dram("s_BCT", [8, 128, S], BF16)
        self.s_BCT_b = P.bufs(8, "sBCT")
        self.s_z = P.dram("s_z", [S, 2 * D], BF16)
        self.s_z_b = P.bufs(NT, "sz")
        self.rw_rkv = d("rwkv_w_rkv", [3, D, D])
        self.rw_wo = d("rwkv_w_o", [D, D])
        self.rw_w1 = d("rwkv_w1", [D, 64])
        self.rw_a1 = d("rwkv_a1", [D, 64])
        self.rw_g1 = d("rwkv_g1", [D, 160])
        self.rw_w2 = d("rwkv_w2", [64, D])
        self.rw_a2 = d("rwkv_a2", [64, D])
        self.rw_g2 = d("rwkv_g2", [160, D])
        self.rw_p = d("rw_p", [128, 88])
        self.rw_gn = d("rw_gn", [128, 2, D])
        self.rw_c = d("rw_c", [128, 128 + 2 + 128 + 64], BF16)
        self.r_AR = P.dram("r_AR", [8, 128, 64, 2, 64], BF16)
        self.r_BK = P.dram("r_BK", [8, 128, 64, 2, 64], BF16)
        self.r_bh = P.dram("r_bh", [S, D], BF16)
        self.r_kh = P.dram("r_kh", [S, D], BF16)
        self.r_v = P.dram("r_v", [S, D], BF16)
        self.r_g = P.dram("r_g", [S, D], BF16)
        self.r_bv = P.dram("r_bv", [S, D], BF16)
        self.r_y = P.dram("r_y", [S, D], F32)
        self.r_pl = P.dram("r_pl", [128, 8, 64], F32)
        self.r1_b = P.bufs(8, "r1")
        self.ry_b = P.bufs(NT, "ry")
        self.ob = P.dram("ob", [S, 2 * D], BF16)
        self.ob_b = P.bufs(NT, "ob")
        self.xa = P.dram("xa", [S, D], F32)
        self.xa_b = P.bufs(NT, "xa")
        self.xb = P.dram("xb", [S, D], F32)
        self.xb_b = P.bufs(NT, "xb")
        self.xin_b = P.bufs(NT, "xin")
        self.out_b = P.bufs(NT, "out")
        self.c_sem = P.dsem("const")
        self.c_b = P.buf("consts")
        self.ident = P.sb("ident_sb", [128, 128], BF16)
        self.ident_b = self.c_b
        P.dma("sp", self.c_sem, self.ident[:], self.ident_d, writes=[self.c_b])
        self.gmix = P.sb("gmix", [128, 4, DC], F32)
        self.gffn = P.sb("gffn", [128, 4, DC], F32)
        self.g_b = self.c_b
        P.dma("sp", self.c_sem, self.gmix[:], self.norm_mix, writes=[self.c_b])
        P.dma("sp", self.c_sem, self.gffn[:], self.norm_ffn, writes=[self.c_b])
        self.cw = P.sb("ffn_cw_sb", [128, 4, 2 * FC, 3], F32)
        self.cb = P.sb("ffn_cb_sb", [128, 4, 2 * FC], F32)
        P.dma("sp", self.c_sem, self.cw[:], self.ffn_cw, writes=[self.c_b])
        P.dma("sp", self.c_sem, self.cb[:], self.ffn_cb, writes=[self.c_b])
        self.ones = P.sb("ones_bf", [128, 128], BF16)
        P.memset("pool", self.ones[:], 1.0, writes=[self.c_b])
        self.hnT = P.sb("hnT", [128, DC, S], BF16)
        self.hnT_b = P.bufs(S // 512, "hnT")
        self.xt = [P.sb(f"xt{i}", [128, D], F32) for i in range(2)]
        self.xt_b = P.bufs(2, "xt")
        self.xt_sem = [P.dsem(f"xt{i}") for i in range(2)]
        self.xs = [P.sb(f"xs{i}", [128, D], BF16) for i in range(2)]
        self.xs_b = P.bufs(2, "xs")
        self.sq = P.sb("sqjunk", [128, D], BF16)
        self.sq_b = P.buf("sq")
        self.ss = [P.sb(f"ss{i}", [128, 2], F32) for i in range(2)]
        self.ss_b = P.bufs(2, "ss")
        self.nk = 0
        self.A = Arena(P, "arena", 122 * 1024)
        self.psum = P.ps("psum", [128, 4096], F32)
        self.pb = P.bufs(8, "psb")

    def bank(self, i):
        return self.psum[:, i * 512:(i + 1) * 512]

    def wload(self, dst_sb, src_ap, sem, b, nsplit=1):
        P = self.P
        n = dst_sb.shape[1]
        step = n // nsplit
        for i in range(nsplit):
            P.dma("pool", sem, dst_sb[:, i * step:(i + 1) * step], src_ap[:, i * step:(i + 1) * step], writes=[b])

    def rms_tile(self, xt, xt_b, k):
        P = self.P
        ss, ss_b = self.ss[k % 2], self.ss_b[k % 2]
        P.act(self.sq[:], xt[:], AF.Square, reads=[xt_b], writes=[self.sq_b, ss_b], accum_out=ss[:, 0:1])
        P.ts("dve", ss[:, 1:2], ss[:, 0:1], 1.0 / D, EPS, ALU.mult, ALU.add, reads=[ss_b], writes=[ss_b])
        P.act(ss[:, 1:2], ss[:, 1:2], AF.Sqrt, reads=[ss_b], writes=[ss_b])
        P.op("dve", lambda e: e.reciprocal(ss[:, 1:2], ss[:, 1:2]), reads=[ss_b], writes=[ss_b])
        return ss[:, 1:2], ss_b

    def load_x(self, src, src_b, t):
        P = self.P
        k = self.nk
        self.nk += 1
        sl = k % 2
        xt, xt_b = self.xt[sl], self.xt_b[sl]
        P.dma("sp", self.xt_sem[sl], xt[:], src[t * 128:(t + 1) * 128, :], reads=[src_b[t]], writes=[xt_b])
        return xt, xt_b, sl, k

    def store_x(self, dst, dst_b, t, xt, xt_b, sl):
        self.P.dma("sp", self.xt_sem[sl], dst[t * 128:(t + 1) * 128, :], xt[:], reads=[xt_b], writes=[dst_b[t]])

    def norm_phase(self, src, src_b, g):
        P = self.P
        psT = [self.bank(6 + i).bitcast(BF16).rearrange("p (c t) -> p c t", c=DC) for i in range(2)]
        for t in range(NT):
            xt, xt_b, sl, k = self.load_x(src, src_b, t)
            rstd, ss_b = self.rms_tile(xt, xt_b, k)
            xs, xs_b = self.xs[sl], self.xs_b[sl]
            P.act(xs[:], xt[:], AF.Copy, reads=[xt_b, ss_b], writes=[xs_b], scale=rstd)
            pt, pt_b = psT[sl], self.pb[6 + sl]
            for c in range(DC):
                P.tr(pt[:, c, :], xs[:, c * 128:(c + 1) * 128], self.ident[:], reads=[xs_b, self.c_b], writes=[pt_b])
            hb = self.hnT_b[t // 4]
            gb = g.unsqueeze(2).to_broadcast([128, DC, 128])
            P.tt("dve", self.hnT[:, :, t * 128:(t + 1) * 128], pt, gb, ALU.mult,
                 reads=[pt_b, self.c_b], writes=[hb])

    def res_store(self, res, res_b, dst, dst_b, t, dp, dp_b, bias=None):
        P = self.P
        xt, xt_b, sl, k = self.load_x(res, res_b, t)
        P.tt("dve", xt[:], xt[:], dp, ALU.add, reads=[dp_b, xt_b], writes=[xt_b])
        if bias is not None:
            P.tt("pool", xt[:], xt[:], bias[0], ALU.add, reads=[bias[1], xt_b], writes=[xt_b])
        self.store_x(dst, dst_b, t, xt, xt_b, sl)

    def ffn_phase(self, l, res, res_b, dst, dst_b):
        P = self.P
        A = self.A
        A.reset()
        R = {}
        R["raw"] = [[self.bank(h * 2 + s) for s in range(2)] for h in range(2)]
        R["raw_b"] = [[self.pb[h * 2 + s] for s in range(2)] for h in range(2)]
        R["cps"] = [self.bank(4 + h) for h in range(2)]
        R["cps_b"] = [self.pb[4 + h] for h in range(2)]
        R["dps"] = self.psum[:, 6 * 512:8 * 512]
        R["dps_b"] = self.pb[6]
        R["wd"] = A.alloc([128, FC, D], BF16)
        R["wd_b"] = P.buf("wd")
        R["wd_sem"] = P.dsem(f"wd{l}")
        R["wu"] = [A.alloc([128, DC, 2, 128], BF16) for s in range(3)]
        R["wu_b"] = P.bufs(3, "wu")
        R["wu_sem"] = [P.dsem(f"wu{l}_{s}") for s in range(3)]
        R["dg"] = [A.alloc([128, 2, 3, 128], BF16) for s in range(2)]
        R["dg_b"] = P.bufs(2, "dg")
        R["ub"] = [[A.alloc([128, 514], BF16) for s in range(2)] for h in range(2)]
        R["ub_b"] = [P.bufs(2, f"ub{h}") for h in range(2)]
        R["halo"] = A.alloc([128, 2 * FC, 2], BF16)
        R["halo_b"] = P.bufs(2 * FC, "halo")
        R["sg"] = [A.alloc([128, 512], F32) for s in range(2)]
        R["sg_b"] = P.bufs(2, "sg")
        R["actT"] = A.alloc([128, FC, 1024], BF16)
        R["actT_b"] = P.bufs(2, "actT")
        wdn = self.ffn_w_down[l]
        self.wload(R["wd"], wdn, R["wd_sem"], R["wd_b"], nsplit=2)
        P.memset("pool", R["halo"], 0.0, writes=R["halo_b"])
        SB = 1024
        units = []
        for sbk in range(S // SB):
            for i in range(FC):
                for tb in range(SB // 512):
                    units.append((sbk, i, tb))
        nU = len(units)

        def stage_load(sbk, i):
            j = (sbk * FC + i)
            sl = j % 3
            w, w_b, w_sem = R["wu"][sl], R["wu_b"][sl], R["wu_sem"][sl]
            P.dma("pool", w_sem, w.rearrange("p c h e -> p (c h e)"), self.ffn_w_up[l, i], writes=[w_b])
            dg, dg_b = R["dg"][j % 2], R["dg_b"][j % 2]
            for h in range(2):
                ch = h * FC + i
                for tap in range(3):
                    P.ts("dve", dg[:, h, tap, :], self.ident[:], self.cw[:, l, ch, tap:tap + 1], None, ALU.mult,
                         reads=[self.c_b], writes=[dg_b])

        def stage_A(u):
            sbk, i, tb = units[u]
            j = sbk * FC + i
            w, w_b = R["wu"][j % 3], R["wu_b"][j % 3]
            t0 = sbk * SB + tb * 512
            for h in range(2):
                ps, ps_b = R["raw"][h][u % 2], R["raw_b"][h][u % 2]
                for c in range(DC):
                    P.mm(ps, w[:, c, h, :], self.hnT[:, c, t0:t0 + 512], c == 0, c == DC - 1,
                         reads=[w_b, self.hnT_b[t0 // 512]], writes=[ps_b])

        def stage_B(u):
            sbk, i, tb = units[u]
            j = sbk * FC + i
            dg, dg_b = R["dg"][j % 2], R["dg_b"][j % 2]
            for h in range(2):
                ch = h * FC + i
                ps, ps_b = R["raw"][h][u % 2], R["raw_b"][h][u % 2]
                ub, ub_b = R["ub"][h][u % 2], R["ub_b"][h][u % 2]
                P.copy("dve", ub[:, 0:2], R["halo"][:, ch, :], reads=[R["halo_b"][ch]], writes=[ub_b])
                P.copy("act", ub[:, 2:514], ps, reads=[ps_b], writes=[ub_b])
                P.copy("dve", R["halo"][:, ch, :], ub[:, 512:514], reads=[ub_b], writes=[R["halo_b"][ch]])
                cp, cp_b = R["cps"][h], R["cps_b"][h]
                for tap in range(3):
                    P.mm(cp, dg[:, h, tap, :], ub[:, tap:tap + 512], tap == 0, tap == 2,
                         reads=[dg_b, ub_b], writes=[cp_b])

        def stage_C(u):
            sbk, i, tb = units[u]
            sg, sg_b = R["sg"][u % 2], R["sg_b"][u % 2]
            P.act(sg, R["cps"][0], AF.Silu, reads=[R["cps_b"][0], self.c_b], writes=[sg_b],
                  bias=self.cb[:, l, i:i + 1])
            P.stt("dve", R["actT"][:, i, tb * 512:(tb + 1) * 512], R["cps"][1], self.cb[:, l, FC + i:FC + i + 1],
                  sg, ALU.add, ALU.mult, reads=[R["cps_b"][1], sg_b, self.c_b], writes=[R["actT_b"][tb]])

        def down(sbk):
            for tt in range(SB // 128):
                t = sbk * (SB // 128) + tt
                dp, dp_b = R["dps"], R["dps_b"]
                for hh in range(2):
                    for c in range(FC):
                        P.mm(dp[:, hh * 512:(hh + 1) * 512], R["actT"][:, c, tt * 128:(tt + 1) * 128],
                             R["wd"][:, c, hh * 512:(hh + 1) * 512], c == 0, c == FC - 1,
                             reads=[R["actT_b"][tt // 4], R["wd_b"]], writes=[dp_b])
                self.res_store(res, res_b, dst, dst_b, t, dp, dp_b)

        upb = SB // 512 * FC
        for u in range(nU + 1):
            if u < nU:
                sbk, i, tb = units[u]
                if tb == 0:
                    stage_load(sbk, i)
                stage_A(u)
            if u >= 1:
                stage_B(u - 1)
                stage_C(u - 1)
                if (u % upb) == 0:
                    down(u // upb - 1)

    def conformer(self, res, res_b, dst, dst_b):
        P = self.P
        A = self.A
        A.reset()
        NP = 16 + 8 * 31 + 24
        cp = A.alloc([128, NP], F32)
        cp_b = P.buf("confp")
        sem = P.dsem("conf")
        P.dma("sp", sem, cp, self.conf_p, writes=[cp_b])
        b1 = cp[:, 0:16]
        dww = cp[:, 16:16 + 248].rearrange("p (c j) -> p c j", c=8)
        dwb = cp[:, 264:272]
        lnw = cp[:, 272:280]
        lnb = cp[:, 280:288]
        gT = A.alloc([128, DC, 30 + S], BF16)
        gT_b = P.bufs(S // 512, "gT")
        gz_b = P.buf("gTpad")
        P.memset("pool", gT[:, :, 0:30], 0.0, writes=[gz_b])
        mark = A.off
        w1 = A.alloc([128, DC, 2 * D], BF16)
        w1_b = P.buf("w1")
        self.wload(w1, self.conf_w1.rearrange("(c p) e -> p c e", p=128), sem, w1_b, nsplit=DC)
        sg = [A.alloc([128, 512], F32) for i in range(2)]
        sg_b = P.bufs(2, "csg")
        k = 0
        for c in range(DC):
            for tb in range(8):
                psa, psa_b = self.bank(2 * (k % 2)), self.pb[2 * (k % 2)]
                psb, psb_b = self.bank(2 * (k % 2) + 1), self.pb[2 * (k % 2) + 1]
                for kc in range(DC):
                    P.mm(psa, w1[:, kc, c * 128:(c + 1) * 128], self.hnT[:, kc, tb * 512:(tb + 1) * 512],
                         kc == 0, kc == DC - 1, reads=[w1_b, self.hnT_b[tb]], writes=[psa_b])
                for kc in range(DC):
                    P.mm(psb, w1[:, kc, D + c * 128:D + (c + 1) * 128], self.hnT[:, kc, tb * 512:(tb + 1) * 512],
                         kc == 0, kc == DC - 1, reads=[w1_b, self.hnT_b[tb]], writes=[psb_b])
                P.act(sg[k % 2], psb, AF.Sigmoid, reads=[psb_b, cp_b], writes=[sg_b[k % 2]], bias=b1[:, 8 + c:9 + c])
                P.stt("dve", gT[:, c, 30 + tb * 512:30 + (tb + 1) * 512], psa, b1[:, c:c + 1], sg[k % 2],
                      ALU.add, ALU.mult, reads=[psa_b, sg_b[k % 2], cp_b], writes=[gT_b[tb]])
                k += 1
        A.reset(mark)
        w2 = A.alloc([128, DC, D], BF16)
        w2_b = P.buf("w2")
        self.wload(w2, self.conf_w2.rearrange("(c p) e -> p c e", p=128), sem, w2_b, nsplit=DC)
        b2 = A.alloc([128, D], F32)
        b2_b = P.buf("b2")
        P.dma("sp", sem, b2, self.conf_b2, writes=[b2_b])
        mark2 = A.off
        dg = [A.alloc([128, 31, 128], BF16) for i in range(2)]
        dg_b = P.bufs(2, "cdg")
        k = 0
        for c in range(DC):
            for j in range(31):
                P.ts("pool", dg[c % 2][:, j, :], self.ident[:], dww[:, c, j:j + 1], None, ALU.mult,
                     reads=[self.c_b, cp_b], writes=[dg_b[c % 2]])
            for tb in range(8):
                ps, ps_b = self.bank(k % 2), self.pb[k % 2]
                rb = [gz_b, gT_b[tb]] + ([gT_b[tb - 1]] if tb > 0 else [])
                for j in range(31):
                    P.mm(ps, dg[c % 2][:, j, :], gT[:, c, tb * 512 + j:tb * 512 + j + 512], j == 0, j == 30,
                         reads=[dg_b[c % 2]] + rb, writes=[ps_b])
                P.act(self.hnT[:, c, tb * 512:(tb + 1) * 512], ps, AF.Identity, reads=[ps_b, cp_b],
                      writes=[self.hnT_b[tb]], bias=dwb[:, c:c + 1])
                k += 1
        A.reset(mark2)
        sqv = A.alloc([128, DC, 512], BF16)
        sqv_b = P.buf("sqv")
        st = [A.alloc([128, 512], F32) for i in range(3)]
        st_b = P.buf("st")
        tmp = [A.alloc([128, 512], F32) for i in range(2)]
        tmp_b = P.bufs(2, "ctmp")
        zT = A.alloc([128, DC, 512], BF16)
        zT_b = P.buf("zT")
        vT = self.hnT
        for tb in range(8):
            blk = slice(tb * 512, (tb + 1) * 512)
            P.act(sqv, vT[:, :, blk], AF.Square, reads=[self.hnT_b[tb]], writes=[sqv_b])
            pS, pS_b = self.bank(2), self.pb[2]
            pQ, pQ_b = self.bank(3), self.pb[3]
            for c in range(DC):
                P.mm(pS, self.ones[:], vT[:, c, blk], c == 0, c == DC - 1, reads=[self.c_b, self.hnT_b[tb]], writes=[pS_b])
            for c in range(DC):
                P.mm(pQ, self.ones[:], sqv[:, c, :], c == 0, c == DC - 1, reads=[self.c_b, sqv_b], writes=[pQ_b])
            mean, var, rstd = st
            P.ts("dve", mean, pS, 1.0 / D, None, ALU.mult, reads=[pS_b], writes=[st_b])
            P.stt("dve", var, mean, -1.0, mean, ALU.mult, ALU.mult, reads=[st_b], writes=[st_b])
            P.stt("dve", var, pQ, 1.0 / D, var, ALU.mult, ALU.add, reads=[pQ_b, st_b], writes=[st_b])
            P.ts("dve", var, var, 1e-5, None, ALU.add, reads=[st_b], writes=[st_b])
            P.act(rstd, var, AF.Sqrt, reads=[st_b], writes=[st_b])
            P.op("dve", lambda e, rstd=rstd: e.reciprocal(rstd, rstd), reads=[st_b], writes=[st_b])
            for c in range(DC):
                tm, tm_b = tmp[c % 2], tmp_b[c % 2]
                P.tt("pool", tm, vT[:, c, blk], mean, ALU.subtract, reads=[self.hnT_b[tb], st_b], writes=[tm_b])
                P.tt("dve", tm, tm, rstd, ALU.mult, reads=[st_b, tm_b], writes=[tm_b])
                P.act(zT[:, c, :], tm, AF.Silu, reads=[tm_b, cp_b], writes=[zT_b],
                      scale=lnw[:, c:c + 1], bias=lnb[:, c:c + 1])
            for tq in range(4):
                t = tb * 4 + tq
                dp, dp_b = self.psum[:, 6 * 512:8 * 512], self.pb[6]
                for hh in range(2):
                    for c in range(DC):
                        P.mm(dp[:, hh * 512:(hh + 1) * 512], zT[:, c, tq * 128:(tq + 1) * 128],
                             w2[:, c, hh * 512:(hh + 1) * 512], c == 0, c == DC - 1,
                             reads=[zT_b, w2_b], writes=[dp_b])
                self.res_store(res, res_b, dst, dst_b, t, dp, dp_b, bias=(b2, b2_b))


    def tm_proj_phase(self, KC, W_dram, res, res_b, dst, dst_b, tag):
        P, A = self.P, self.A
        A.reset()
        sem = P.dsem("tmp" + tag)
        W = A.alloc([128, KC, D], BF16)
        W_b = P.buf("tmW")
        self.wload(W, W_dram.rearrange("(c p) e -> p c e", p=128), sem, W_b, nsplit=KC)
        yt = [A.alloc([128, KC * 128], BF16) for i in range(2)]
        yt_b = P.bufs(2, "tmy")
        yt_sem = [P.dsem(f"tmy{tag}{i}") for i in range(2)]
        yT = [A.alloc([128, KC, 128], BF16) for i in range(2)]
        yT_b = P.bufs(2, "tmyT")
        for t in range(NT):
            sl = t % 2
            P.dma("sp", yt_sem[sl], yt[sl], self.ob[t * 128:(t + 1) * 128, 0:KC * 128], reads=[self.ob_b[t]], writes=[yt_b[sl]])
            pt = self.psum[:, sl * 1024:(sl + 1) * 1024].bitcast(BF16)[:, 0:KC * 128].rearrange("p (c t) -> p c t", c=KC)
            pt_b = self.pb[2 * sl]
            for c in range(KC):
                P.tr(pt[:, c, :], yt[sl][:, c * 128:(c + 1) * 128], self.ident[:], reads=[yt_b[sl], self.c_b], writes=[pt_b])
            P.copy("act", yT[sl], pt, reads=[pt_b], writes=[yT_b[sl]])
            dp, dp_b = self.psum[:, (4 + 2 * sl) * 512:(6 + 2 * sl) * 512], self.pb[4 + 2 * sl]
            for hh in range(2):
                for c in range(KC):
                    P.mm(dp[:, hh * 512:(hh + 1) * 512], yT[sl][:, c, :], W[:, c, hh * 512:(hh + 1) * 512],
                         c == 0, c == KC - 1, reads=[yT_b[sl], W_b], writes=[dp_b])
            self.res_store(res, res_b, dst, dst_b, t, dp, dp_b)

    def moba(self, res, res_b, dst, dst_b):
        P, A = self.P, self.A
        A.reset()
        G = 4
        BIG = 30000.0
        sem = P.dsem("moba")
        mc = A.alloc([128, 3, 16, 16], F32)
        tri = A.alloc([128, 128], BF16)
        k_b = P.buf("mobac")
        P.dma("sp", sem, mc, self.mconst, writes=[k_b])
        P.dma("sp", sem, tri, self.tri_d, writes=[k_b])
        QA = A.alloc([128, G, S], BF16)
        KA = A.alloc([128, G, S], BF16)
        QA_b = [P.bufs(8, f"QA{h}") for h in range(G)]
        KA_b = P.bufs(G, "KA")
        ind_b = P.buf("ind")
        V = A.alloc([128, NT, G, 65], BF16)
        V_b = P.buf("V")
        one_b = P.buf("Vone")
        P.memset("pool", V[:, :, :, 64:65], 1.0, writes=[one_b])
        for hh in range(G):
            P.dma("sp", sem, KA[64:80, hh, :], self.blkind, writes=[ind_b])
        w3 = A.alloc([128, DC, 3, G * 64], BF16)
        w3_b = P.buf("w3")
        w3_sem = P.dsem("mobaw")
        km = A.alloc([128, G, 16], F32)
        kmb = A.alloc([128, G, 16], BF16)
        km_b = P.buf("km")
        g2 = A.alloc([128, G, 16], F32)
        sel = A.alloc([128, G, 16], F32)
        mx = A.alloc([128, G, 8], F32)
        gt_b = P.buf("gate")
        mbf = A.alloc([128, G, 80], BF16)
        mbf_b = P.buf("mbf")
        P.memset("pool", mbf, 0.0, writes=[mbf_b])
        mb2 = A.alloc([128, G, 128], BF16)
        mb2_b = P.buf("mb2")
        PT = [A.alloc([128, 512], BF16) for i in range(3)]
        PT_b = P.bufs(3, "PT")
        osb = [A.alloc([128, 4, G * 64], BF16) for i in range(2)]
        osb_b = P.bufs(2, "osb")
        osb_sem = [P.dsem(f"osb{i}") for i in range(2)]
        rec = A.alloc([128, 4, 1], F32)
        rec_b = P.buf("rec")
        wqkv = self.moba_wqkv.rearrange("(c p) e -> p c e", p=128)
        nps = 0
        nS = 0
        nO = 0
        nosb = 0
        for g in range(D // 64 // G):
            for i in range(3):
                P.dma("pool", w3_sem, w3[:, :, i, :], wqkv[:, :, i * D + g * G * 64:i * D + (g + 1) * G * 64], writes=[w3_b])
            for hh in range(G):
                for tb in range(8):
                    blk = slice(tb * 512, (tb + 1) * 512)
                    for i in range(2):
                        ps, ps_b = self.bank(nps % 2)[0:64, :], self.pb[nps % 2]
                        nps += 1
                        for kc in range(DC):
                            P.mm(ps, w3[:, kc, i, hh * 64:(hh + 1) * 64], self.hnT[:, kc, blk], kc == 0, kc == DC - 1,
                                 reads=[w3_b, self.hnT_b[tb]], writes=[ps_b])
                        if i == 0:
                            P.act(QA[0:64, hh, blk], ps, AF.Copy, reads=[ps_b], writes=[QA_b[hh][tb]], scale=0.125)
                        else:
                            P.copy("dve", KA[0:64, hh, blk], ps, reads=[ps_b], writes=[KA_b[hh]])
                P.op("dve", lambda e, hh=hh: e.tensor_reduce(km[0:64, hh, :], KA[0:64, hh, :].rearrange("p (n k) -> p n k", k=256),
                                                             AX.X, ALU.add), reads=[KA_b[hh]], writes=[km_b])
            P.ts("dve", kmb[0:64], km[0:64], 1.0 / 256, None, ALU.mult, reads=[km_b], writes=[km_b])
            for t in range(NT):
                ps, ps_b = self.bank(nps % 2)[:, 0:G * 64], self.pb[nps % 2]
                nps += 1
                for kc in range(DC):
                    P.mm(ps, self.hnT[:, kc, t * 128:(t + 1) * 128], w3[:, kc, 2, :], kc == 0, kc == DC - 1,
                         reads=[w3_b, self.hnT_b[t // 4]], writes=[ps_b])
                P.copy("act", V[:, t, :, 0:64], ps.rearrange("p (h e) -> p h e", h=G), reads=[ps_b], writes=[V_b])
            gps = self.bank(7)[:, 0:G * 16].rearrange("p (h n) -> p h n", h=G)
            tps = self.bank(7)[:, 256:512].bitcast(BF16).rearrange("p (h q) -> p h q", h=G)
            g_b, t_b = self.pb[7], P.buf("tps") if g == 0 else t_b
            for t in range(NT):
                qb_ = t // 2
                tile = slice(t * 128, (t + 1) * 128)
                for hh in range(G):
                    P.mm(gps[:, hh, :], QA[0:64, hh, tile], kmb[0:64, hh, :], True, True,
                         reads=[QA_b[hh][t // 4], km_b], writes=[g_b])
                P.tt("dve", g2, gps, mc[:, 0, qb_, :].unsqueeze(1).to_broadcast([128, G, 16]), ALU.add,
                     reads=[g_b, k_b], writes=[gt_b])
                for hh in range(G):
                    P.op("dve", lambda e, hh=hh: e.max(mx[:, hh, :], g2[:, hh, :]), reads=[gt_b], writes=[gt_b])
                P.tt("dve", sel, g2, mx[:, :, 2:3].to_broadcast([128, G, 16]), ALU.is_ge, reads=[gt_b], writes=[gt_b])
                P.tt("dve", sel, sel, mc[:, 1, qb_, :].unsqueeze(1).to_broadcast([128, G, 16]), ALU.mult,
                     reads=[gt_b, k_b], writes=[gt_b])
                P.tt("dve", sel, sel, mc[:, 2, qb_, :].unsqueeze(1).to_broadcast([128, G, 16]), ALU.add,
                     reads=[gt_b, k_b], writes=[gt_b])
                P.ts("dve", mbf[:, :, 64:80], sel, BIG, -BIG, ALU.mult, ALU.add, reads=[gt_b], writes=[mbf_b])
                for hh in range(G):
                    P.tr(tps[0:80, hh, :], mbf[:, hh, :], self.ident[:], reads=[mbf_b, self.c_b], writes=[t_b])
                for hh in range(G):
                    P.copy("dve", mb2[64:80, hh, :], tps[64:80, hh, :], reads=[t_b], writes=[mb2_b])
                for hh in range(G):
                    P.copy("pool", QA[64:80, hh, tile], mb2[64:80, hh, :], reads=[mb2_b], writes=[QA_b[hh][t // 4]])
            for qb in range(8):
                ob_, ob_b, ob_sem = osb[nosb % 2], osb_b[nosb % 2], osb_sem[nosb % 2]
                nosb += 1
                for hh in range(G):
                    O = self.bank(5 + nO % 2)[:, 0:260].rearrange("p (q e) -> p q e", q=4)
                    O_b = self.pb[5 + nO % 2]
                    nO += 1
                    nkt = 4 * qb + 4
                    slots = {}
                    import os
                    LAG = 0 if os.environ.get("MOBA_ATT") == "old" else 1
                    for kt in range(nkt + LAG):
                        if kt < nkt:
                            sp, sp_b = self.bank(2 + nS % 3), self.pb[2 + nS % 3]
                            pt, pt_b = PT[nS % 3], PT_b[nS % 3]
                            slots[kt] = (pt, pt_b)
                            nS += 1
                            P.mm(sp, KA[0:80, hh, kt * 128:(kt + 1) * 128], QA[0:80, hh, qb * 512:(qb + 1) * 512], True, True,
                                 reads=[KA_b[hh], ind_b, QA_b[hh][qb]], writes=[sp_b])
                            P.act(pt, sp, AF.Exp, reads=[sp_b], writes=[pt_b])
                            j = kt - 4 * qb
                            if j >= 0:
                                P.tt("pool", pt[:, j * 128:(j + 1) * 128], pt[:, j * 128:(j + 1) * 128], tri, ALU.mult,
                                     reads=[k_b, pt_b], writes=[pt_b])
                        if kt >= LAG:
                            k2 = kt - LAG
                            pt, pt_b = slots.pop(k2)
                            for ql in range(4):
                                qt = 4 * qb + ql
                                if k2 <= qt:
                                    P.mm(O[:, ql, :], pt[:, ql * 128:(ql + 1) * 128], V[:, k2, hh, :], k2 == 0 and ql == 0, k2 == qt,
                                         reads=[pt_b, V_b, one_b], writes=[O_b])
                    P.op("dve", lambda e, O=O: e.reciprocal(rec, O[:, :, 64:65]), reads=[O_b], writes=[rec_b])
                    P.tt("dve", ob_[:, :, hh * 64:(hh + 1) * 64], O[:, :, 0:64], rec.to_broadcast([128, 4, 64]), ALU.mult,
                         reads=[O_b, rec_b], writes=[ob_b])
                dstv = self.ob[qb * 512:(qb + 1) * 512, g * G * 64:(g + 1) * G * 64].rearrange("(q p) c -> p q c", p=128)
                P.dma("sp", ob_sem, dstv, ob_, reads=[ob_b], writes=[self.ob_b[4 * qb + i] for i in range(4)])
        self.tm_proj_phase(DC, self.moba_wo, res, res_b, dst, dst_b, "moba")

    def ssd(self, res, res_b, dst, dst_b):
        P, A = self.P, self.A
        A.reset()
        sem = P.dsem("ssd")
        pp = A.alloc([128, 120], F32)
        tc_ = A.alloc([128, 96], F32)
        tri = A.alloc([128, 128], BF16)
        nb4 = A.alloc([128, 4, 128], BF16)
        k_b = P.buf("ssdc")
        P.dma("sp", sem, pp, self.ssd_p, writes=[k_b])
        P.dma("sp", sem, tc_, self.ssd_t, writes=[k_b])
        P.dma("sp", sem, tri, self.tri_d, writes=[k_b])
        P.dma("sp", sem, nb4, self.nb_d, writes=[k_b])
        negones = A.alloc([128, 128], BF16)
        P.memset("pool", negones, -1.0, writes=[k_b])
        cwv = pp[:, 0:96].rearrange("p (c j) -> p c j", j=4)
        cbv = pp[:, 96:120]
        dtk = A.alloc([128, NT, 32], F32)
        atk = A.alloc([128, NT, 32], BF16)
        dt_b = P.buf("dtk")
        aneg = A.alloc([128, 32], F32)
        P.act(aneg, tc_[:, 32:64], AF.Exp, reads=[k_b], writes=[k_b])
        P.ts("dve", aneg, aneg, -1.0, None, ALU.mult, reads=[k_b], writes=[k_b])
        mark = A.off
        win = self.ssd_win.rearrange("(c p) e -> p c e", p=128)
        wch = [A.alloc([128, DC, 128], BF16) for i in range(3)]
        wch_b = P.bufs(3, "swch")
        wch_sem = [P.dsem(f"swch{i}") for i in range(3)]
        dg = [A.alloc([128, 4, 128], BF16) for i in range(2)]
        dg_b = P.bufs(2, "sdg")
        ub = [A.alloc([128, 515], BF16) for i in range(2)]
        ub_b = P.bufs(2, "sub")
        xc = [A.alloc([128, 512], BF16) for i in range(2)]
        xc_b = P.bufs(2, "sxc")
        xc_sem = [P.dsem(f"sxc{i}") for i in range(2)]
        stg = [A.alloc([128, 4, 128], BF16) for i in range(2)]
        stg_b = P.bufs(2, "sstg")
        stg_sem = [P.dsem(f"sstg{i}") for i in range(2)]
        u = 0
        for cc in range(24):
            w, w_b = wch[cc % 3], wch_b[cc % 3]
            P.dma("pool", wch_sem[cc % 3], w, win[:, :, 2 * D + cc * 128:2 * D + (cc + 1) * 128], writes=[w_b])
            for j in range(4):
                P.ts("pool", dg[cc % 2][:, j, :], self.ident[:], cwv[:, cc, j:j + 1], None, ALU.mult,
                     reads=[self.c_b, k_b], writes=[dg_b[cc % 2]])
            for tb in range(8):
                blk = slice(tb * 512, (tb + 1) * 512)
                ps, ps_b = self.bank(u % 2), self.pb[u % 2]
                for kc in range(DC):
                    P.mm(ps, w[:, kc, :], self.hnT[:, kc, blk], kc == 0, kc == DC - 1, reads=[w_b, self.hnT_b[tb]], writes=[ps_b])
                b_, b_b = ub[u % 2], ub_b[u % 2]
                if tb == 0:
                    P.memset("pool", b_[:, 0:3], 0.0, writes=[b_b])
                else:
                    P.copy("pool", b_[:, 0:3], ub[(u - 1) % 2][:, 512:515], reads=[ub_b[(u - 1) % 2]], writes=[b_b])
                P.copy("act", b_[:, 3:515], ps, reads=[ps_b], writes=[b_b])
                cp, cp_b = self.bank(2 + u % 2), self.pb[2 + u % 2]
                for j in range(4):
                    P.mm(cp, dg[cc % 2][:, j, :], b_[:, j:j + 512], j == 0, j == 3, reads=[dg_b[cc % 2], b_b], writes=[cp_b])
                x_, x_b, x_sem = xc[u % 2], xc_b[u % 2], xc_sem[u % 2]
                P.act(x_, cp, AF.Silu, reads=[cp_b, k_b], writes=[x_b], bias=cbv[:, cc:cc + 1])
                if cc >= 16:
                    P.dma("sp", x_sem, self.s_BCT[cc - 16, :, blk], x_, reads=[x_b], writes=[self.s_BCT_b[tb]])
                if cc < 20:
                    tp = self.bank(4 + u % 2).bitcast(BF16)[:, 0:512].rearrange("p (q c) -> p q c", q=4)
                    tp_b = self.pb[4 + u % 2]
                    for q in range(4):
                        P.tr(tp[:, q, :], x_[:, q * 128:(q + 1) * 128], self.ident[:], reads=[x_b, self.c_b], writes=[tp_b])
                    sg_, sg_b, sg_sem = stg[u % 2], stg_b[u % 2], stg_sem[u % 2]
                    P.copy("dve", sg_, tp, reads=[tp_b], writes=[sg_b])
                    dv = self.s_xB[tb * 512:(tb + 1) * 512, cc * 128:(cc + 1) * 128].rearrange("(q p) c -> p q c", p=128)
                    P.dma("sp", sg_sem, dv, sg_, reads=[sg_b], writes=[self.s_xB_b[tb]])
                u += 1
        A.reset(mark)
        wz = A.alloc([128, DC, 2 * D], BF16)
        wz_b = P.buf("wz")
        self.wload(wz, win[:, :, 0:2 * D], sem, wz_b, nsplit=DC)
        wdt = A.alloc([128, DC, 32], BF16)
        P.dma("pool", sem, wdt, win[:, :, 5120:5152], writes=[wz_b])
        zt = [A.alloc([128, 2 * D], BF16) for i in range(2)]
        zt_b = P.bufs(2, "szt")
        zt_sem = [P.dsem(f"szt{i}") for i in range(2)]
        for t in range(NT):
            tile = slice(t * 128, (t + 1) * 128)
            z_, z_b = zt[t % 2], zt_b[t % 2]
            for q in range(4):
                ps, ps_b = self.bank(q % 2), self.pb[q % 2]
                for kc in range(DC):
                    P.mm(ps, self.hnT[:, kc, tile], wz[:, kc, q * 512:(q + 1) * 512], kc == 0, kc == DC - 1,
                         reads=[wz_b, self.hnT_b[t // 4]], writes=[ps_b])
                P.act(z_[:, q * 512:(q + 1) * 512], ps, AF.Silu, reads=[ps_b], writes=[z_b])
            P.dma("sp", zt_sem[t % 2], self.s_z[tile, :], z_, reads=[z_b], writes=[self.s_z_b[t]])
            ps, ps_b = self.bank(2)[:, 0:32], self.pb[2]
            for kc in range(DC):
                P.mm(ps, self.hnT[:, kc, tile], wdt[:, kc, :], kc == 0, kc == DC - 1, reads=[wz_b, self.hnT_b[t // 4]], writes=[ps_b])
            P.tt("dve", dtk[:, t, :], ps, tc_[:, 0:32], ALU.add, reads=[ps_b, k_b], writes=[dt_b])
        P.act(dtk, dtk, AF.Exp, reads=[dt_b], writes=[dt_b])
        P.act(dtk, dtk, AF.Ln, reads=[dt_b], writes=[dt_b], bias=1.0)
        P.tt("dve", atk, dtk, aneg.unsqueeze(1).to_broadcast([128, NT, 32]), ALU.mult, reads=[dt_b, k_b], writes=[dt_b])
        A.reset(mark)
        nw = A.alloc([128, 2 * D], F32)
        P.dma("sp", sem, nw, self.ssd_nw, writes=[k_b])
        xB = [A.alloc([128, 2560], BF16) for i in range(2)]
        xB_b = P.bufs(2, "xB")
        xB_sem = [P.dsem(f"xB{i}") for i in range(2)]
        zz = [A.alloc([128, 2 * D], BF16) for i in range(2)]
        zz_b = P.bufs(2, "zz")
        zz_sem = [P.dsem(f"zz{i}") for i in range(2)]
        bct = [A.alloc([128, 8, 128], BF16) for i in range(2)]
        bct_b = P.bufs(2, "bct")
        bct_sem = [P.dsem(f"bct{i}") for i in range(2)]
        xd = A.alloc([128, 32, 64], BF16)
        xdw = A.alloc([128, 32, 64], BF16)
        xd_b = P.buf("xd")
        acum = A.alloc([128, 32], F32)
        expA = A.alloc([128, 32], F32)
        expLA = A.alloc([128, 32], F32)
        dL = A.alloc([128, 32], F32)
        sc_b = P.buf("ssc")
        R1 = A.alloc([128, 8, 128], BF16)
        R1_b = P.buf("R1")
        E = A.alloc([128, 8, 128], BF16)
        E_b = P.buf("E")
        MT = A.alloc([128, 8, 128], BF16)
        MT_b = P.buf("MT")
        GT = A.alloc([128, 128], BF16)
        GT_b = P.buf("GT")
        yt = A.alloc([128, 2 * D], F32)
        yt_b = P.buf("yt")
        tmp = A.alloc([128, 512], F32)
        tmp_b = P.buf("stmp")
        HT = A.alloc([128, 4, 512], F32)
        HTb = A.alloc([128, 4, 512], BF16)
        HT_b = P.bufs(4, "HT")
        P.memset("pool", HT, 0.0, writes=HT_b)
        P.memset("pool", HTb, 0.0, writes=HT_b)
        ssq = A.alloc([128, 8], F32)
        ssq_b = P.buf("ssq")
        yo = [A.alloc([128, 2 * D], BF16) for i in range(2)]
        yo_b = P.bufs(2, "yo")
        yo_sem = [P.dsem(f"yo{i}") for i in range(2)]
        for c in range(NT):
            sl = c % 2
            tile = slice(c * 128, (c + 1) * 128)
            P.dma("sp", xB_sem[sl], xB[sl], self.s_xB[tile, :], reads=[self.s_xB_b[c // 4]], writes=[xB_b[sl]])
            P.dma("sp", zz_sem[sl], zz[sl], self.s_z[tile, :], reads=[self.s_z_b[c]], writes=[zz_b[sl]])
            P.dma("sp", bct_sem[sl], bct[sl], self.s_BCT[:, :, tile].rearrange("g n t -> n g t"),
                  reads=[self.s_BCT_b[c // 4]], writes=[bct_b[sl]])
            xv = xB[sl][:, 0:2048].rearrange("p (h e) -> p h e", h=32)
            a_c = atk[:, c, :]
            pA, pA_b = self.bank(0)[:, 0:32], self.pb[0]
            pL = self.bank(0)[:, 32:64]
            P.mm(pA, tri, a_c, True, True, reads=[k_b, dt_b], writes=[pA_b])
            P.mm(pL, self.ones[:], a_c, False, True, reads=[self.c_b, dt_b], writes=[pA_b])
            P.copy("dve", acum, pA, reads=[pA_b], writes=[sc_b])
            P.act(expA, pA, AF.Exp, reads=[pA_b], writes=[sc_b])
            P.act(dL, pL, AF.Exp, reads=[pA_b], writes=[sc_b])
            P.tt("dve", expLA, pL, acum, ALU.subtract, reads=[pA_b, sc_b], writes=[sc_b])
            P.act(expLA, expLA, AF.Exp, reads=[sc_b], writes=[sc_b])
            P.tt("dve", xd, xv, dtk[:, c, :].unsqueeze(2).to_broadcast([128, 32, 64]), ALU.mult,
                 reads=[xB_b[sl], dt_b], writes=[xd_b])
            P.tt("pool", xdw, xd, expLA.unsqueeze(2).to_broadcast([128, 32, 64]), ALU.mult, reads=[xd_b, sc_b], writes=[xd_b])
            for g in range(4):
                BTg = bct[sl][:, g, :]
                CTg = bct[sl][:, 4 + g, :]
                Btok = xB[sl][:, 2048 + g * 128:2048 + (g + 1) * 128]
                pG, pG_b = self.bank(1)[:, 0:128], self.pb[1]
                P.mm(pG, BTg, CTg, True, True, reads=[bct_b[sl]], writes=[pG_b])
                P.copy("act", GT, pG, reads=[pG_b], writes=[GT_b])
                P.tt("dve", R1, tri.unsqueeze(1).to_broadcast([128, 8, 128]),
                     a_c[:, g * 8:(g + 1) * 8].unsqueeze(2).to_broadcast([128, 8, 128]), ALU.mult,
                     reads=[k_b, dt_b], writes=[R1_b])
                for hb in range(2):
                    pD, pD_b = self.bank(2 + hb).rearrange("p (h t) -> p h t", h=4), self.pb[2 + hb]
                    P.mm(pD, self.ones[:], R1[:, hb * 4:(hb + 1) * 4, :], True, False, reads=[self.c_b, R1_b], writes=[pD_b])
                    P.mm(pD, self.ident[:], nb4, False, False, reads=[self.c_b, k_b], writes=[pD_b])
                    for h4 in range(4):
                        P.mm(pD[:, h4, :], R1[:, hb * 4 + h4, :], negones, False, h4 == 3, reads=[R1_b, k_b], writes=[pD_b])
                    P.act(E[:, hb * 4:(hb + 1) * 4, :], pD, AF.Exp, reads=[pD_b], writes=[E_b])
                P.tt("dve", MT, E, GT.unsqueeze(1).to_broadcast([128, 8, 128]), ALU.mult, reads=[E_b, GT_b], writes=[MT_b])
                pY, pY_b = self.bank(4).rearrange("p (h e) -> p h e", h=8), self.pb[4]
                for h8 in range(8):
                    P.mm(pY[:, h8, :], MT[:, h8, :], xd[:, g * 8 + h8, :], h8 == 0, h8 == 7, reads=[MT_b, xd_b], writes=[pY_b])
                pO, pO_b = self.bank(5), self.pb[5]
                P.mm(pO, CTg, HTb[:, g, :], True, True, reads=[bct_b[sl], HT_b[g]], writes=[pO_b])
                pS, pS_b = self.bank(6), self.pb[6]
                P.mm(pS, Btok, xdw[:, g * 8:(g + 1) * 8, :], True, True, reads=[xB_b[sl], xd_b], writes=[pS_b])
                P.tt("dve", tmp.rearrange("p (h e) -> p h e", h=8), pO.rearrange("p (h e) -> p h e", h=8),
                     expA[:, g * 8:(g + 1) * 8].unsqueeze(2).to_broadcast([128, 8, 64]), ALU.mult,
                     reads=[pO_b, sc_b], writes=[tmp_b])
                P.tt("dve", yt[:, g * 512:(g + 1) * 512], tmp, self.bank(4), ALU.add, reads=[tmp_b, pY_b], writes=[yt_b])
                Hg = HT[:, g, :]
                P.tt("pool", Hg.rearrange("p (h e) -> p h e", h=8), Hg.rearrange("p (h e) -> p h e", h=8),
                     dL[:, g * 8:(g + 1) * 8].unsqueeze(2).to_broadcast([128, 8, 64]), ALU.mult,
                     reads=[sc_b, HT_b[g]], writes=[HT_b[g]])
                P.tt("dve", Hg, Hg, pS, ALU.add, reads=[pS_b, HT_b[g]], writes=[HT_b[g]])
                P.copy("act", HTb[:, g, :], Hg, reads=[HT_b[g]], writes=[HT_b[g]])
            xs_ = self.xt[0][:, :].bitcast(BF16)
            ytv = yt.rearrange("p (h e) -> p h e", h=32)
            P.tt("pool", xd, xv, tc_[:, 64:96].unsqueeze(2).to_broadcast([128, 32, 64]), ALU.mult,
                 reads=[xB_b[sl], k_b, pY_b, pS_b], writes=[xd_b])
            P.tt("dve", ytv, ytv, xd, ALU.add, reads=[xd_b, yt_b], writes=[yt_b])
            P.tt("dve", yt, yt, zz[sl], ALU.mult, reads=[zz_b[sl], yt_b], writes=[yt_b])
            o_, o_b = yo[sl], yo_b[sl]
            for g in range(4):
                P.act(o_[:, g * 512:(g + 1) * 512], yt[:, g * 512:(g + 1) * 512], AF.Square, reads=[yt_b],
                      writes=[o_b, ssq_b], accum_out=ssq[:, g:g + 1])
            P.ts("dve", ssq[:, 4:8], ssq[:, 0:4], 1.0 / 512, 1e-5, ALU.mult, ALU.add, reads=[ssq_b], writes=[ssq_b])
            P.act(ssq[:, 4:8], ssq[:, 4:8], AF.Sqrt, reads=[ssq_b], writes=[ssq_b])
            P.op("dve", lambda e: e.reciprocal(ssq[:, 4:8], ssq[:, 4:8]), reads=[ssq_b], writes=[ssq_b])
            P.tt("pool", yt, yt, nw, ALU.mult, reads=[yt_b, k_b], writes=[yt_b])
            P.tt("dve", o_.rearrange("p (g e) -> p g e", g=4), yt.rearrange("p (g e) -> p g e", g=4),
                 ssq[:, 4:8].unsqueeze(2).to_broadcast([128, 4, 512]), ALU.mult, reads=[yt_b, ssq_b], writes=[o_b])
            P.dma("sp", yo_sem[sl], self.ob[tile, :], o_, reads=[o_b], writes=[self.ob_b[c]])
        self.tm_proj_phase(16, self.ssd_wout, res, res_b, dst, dst_b, "ssd")

    def rwkv(self, res, res_b, dst, dst_b):
        P, A = self.P, self.A
        A.reset()
        sem = P.dsem("rw")
        k_b = P.buf("rwc")
        rp = A.alloc([128, 88], F32)
        P.dma("sp", sem, rp, self.rw_p, writes=[k_b])
        cc_ = A.alloc([128, 322], BF16)
        P.dma("sp", sem, cc_, self.rw_c, writes=[k_b])
        bones = cc_[:, 0:128]
        hsel = cc_[:, 128:130]
        maskG = cc_[:, 130:258]
        maskA = cc_[0:64, 258:322]
        mu = rp[:, 0:48].rearrange("p (i c) -> p i c", i=6)
        w0, a0, kkp, kap, rkp = (rp[:, 48 + 8 * i:56 + 8 * i] for i in range(5))
        nw0 = A.alloc([128, 8], F32)
        P.ts("dve", nw0, w0, -1.0, None, ALU.mult, reads=[k_b], writes=[k_b])
        mhalf = A.alloc([128, 1], F32)
        P.memset("pool", mhalf, -0.5, writes=[k_b])
        PLx = A.alloc([128, 8, 64], F32)
        PLx_b = P.buf("PLx")
        mark = A.off
        BT = 256
        NQ = BT // 128
        NCB = BT // 64
        W3 = [A.alloc([128, DC, D], BF16) for i in range(3)]
        w_b = P.buf("rww")
        for i in range(3):
            self.wload(W3[i], self.rw_rkv[i].rearrange("(c p) e -> p c e", p=128), sem, w_b, nsplit=DC)
        w1 = A.alloc([128, DC, 64], BF16)
        a1 = A.alloc([128, DC, 64], BF16)
        g1 = A.alloc([128, DC, 160], BF16)
        w2 = A.alloc([64, D], BF16)
        a2 = A.alloc([64, D], BF16)
        g2a = A.alloc([128, D], BF16)
        g2b = A.alloc([32, D], BF16)
        P.dma("pool", sem, w1, self.rw_w1.rearrange("(c p) e -> p c e", p=128), writes=[w_b])
        P.dma("pool", sem, a1, self.rw_a1.rearrange("(c p) e -> p c e", p=128), writes=[w_b])
        P.dma("pool", sem, g1, self.rw_g1.rearrange("(c p) e -> p c e", p=128), writes=[w_b])
        P.dma("pool", sem, w2, self.rw_w2, writes=[w_b])
        P.dma("pool", sem, a2, self.rw_a2, writes=[w_b])
        P.dma("pool", sem, g2a, self.rw_g2[0:128, :], writes=[w_b])
        P.dma("pool", sem, g2b, self.rw_g2[128:160, :], writes=[w_b])
        dT = A.alloc([128, DC, BT], BF16)
        dT_b = P.buf("dT")
        xm = [A.alloc([128, DC, BT], BF16) for i in range(2)]
        xm_b = P.bufs(2, "xm")
        hw = A.alloc([64, BT], BF16)
        ha = A.alloc([64, BT], BF16)
        hga = A.alloc([128, BT], BF16)
        hgb = A.alloc([32, BT], BF16)
        h_b = P.buf("rwh")
        vtok = A.alloc([128, NQ, D], BF16)
        vt_b = P.buf("vtok")
        vt_sem = P.dsem("vtok")
        gtok = A.alloc([128, NQ, D], BF16)
        gt_b = P.buf("gtok")
        gt_sem = P.dsem("gtok")
        F = [A.alloc([128, BT], F32) for i in range(10)]
        F_b = P.bufs(10, "rwF")
        sqb = A.alloc([128, BT], BF16)
        sqb_b = P.buf("sqb")
        ARt = A.alloc([128, NCB, 2, 64], BF16)
        BKt = A.alloc([128, NCB, 2, 64], BF16)
        AB_b = P.buf("ARt")
        AB_sem = P.dsem("ARt")
        bkh = A.alloc([128, 2, BT], BF16)
        bkh_b = P.buf("bkh")
        stg = A.alloc([128, 2, NQ, 128], BF16)
        stg_b = P.buf("rstg")
        stg_sem = P.dsem("rstg")
        rk = A.alloc([128, BT], BF16)
        rk_b = P.buf("rk")
        bon = A.alloc([128, NQ, 16], F32)
        bon_b = P.buf("bon")
        nxm = [0]

        def mk_xm(i, blk):
            sl = nxm[0] % 2
            nxm[0] += 1
            for c in range(DC):
                P.stt("dve", xm[sl][:, c, :], dT[:, c, :], mu[:, i, c:c + 1], self.hnT[:, c, blk],
                      ALU.mult, ALU.add, reads=[dT_b, k_b] + list(self.hnT_b), writes=[xm_b[sl]])
            return xm[sl], xm_b[sl]

        np_ = [0]

        def pbank(n=None):
            np_[0] += 1
            return self.bank(np_[0] % 4)[:, 0:(BT if n is None else n)], self.pb[np_[0] % 4]

        for tb in range(S // BT):
            t0 = tb * BT
            blk = slice(t0, t0 + BT)
            rb_ = self.r1_b[t0 // 512]
            if tb == 0:
                P.ts("dve", dT[:, :, 0:1], self.hnT[:, :, 0:1], -1.0, None, ALU.mult, reads=list(self.hnT_b), writes=[dT_b])
                P.tt("dve", dT[:, :, 1:BT], self.hnT[:, :, 0:BT - 1], self.hnT[:, :, 1:BT], ALU.subtract,
                     reads=list(self.hnT_b), writes=[dT_b])
            else:
                P.tt("dve", dT, self.hnT[:, :, t0 - 1:t0 + BT - 1], self.hnT[:, :, blk], ALU.subtract,
                     reads=list(self.hnT_b), writes=[dT_b])
            x_, x_b = mk_xm(3, blk)
            ps, ps_b = pbank()
            for kc in range(DC):
                P.mm(ps[0:64, :], w1[:, kc, :], x_[:, kc, :], kc == 0, kc == DC - 1, reads=[w_b, x_b], writes=[ps_b])
            P.act(hw, ps[0:64, :], AF.Tanh, reads=[ps_b], writes=[h_b])
            x_, x_b = mk_xm(4, blk)
            ps, ps_b = pbank()
            for kc in range(DC):
                P.mm(ps[0:64, :], a1[:, kc, :], x_[:, kc, :], kc == 0, kc == DC - 1, reads=[w_b, x_b], writes=[ps_b])
            P.copy("act", ha, ps[0:64, :], reads=[ps_b], writes=[h_b])
            x_, x_b = mk_xm(5, blk)
            ps, ps_b = pbank()
            for kc in range(DC):
                P.mm(ps, g1[:, kc, 0:128], x_[:, kc, :], kc == 0, kc == DC - 1, reads=[w_b, x_b], writes=[ps_b])
            P.act(hga, ps, AF.Sigmoid, reads=[ps_b], writes=[h_b])
            ps, ps_b = pbank()
            for kc in range(DC):
                P.mm(ps[0:32, :], g1[:, kc, 128:160], x_[:, kc, :], kc == 0, kc == DC - 1, reads=[w_b, x_b], writes=[ps_b])
            P.act(hgb, ps[0:32, :], AF.Sigmoid, reads=[ps_b], writes=[h_b])
            for q in range(NQ):
                for cb in range(2):
                    ps, ps_b = pbank(512)
                    P.mm(ps, hga[:, q * 128:(q + 1) * 128], g2a[:, cb * 512:(cb + 1) * 512], True, False, reads=[h_b, w_b], writes=[ps_b])
                    P.mm(ps, hgb[:, q * 128:(q + 1) * 128], g2b[:, cb * 512:(cb + 1) * 512], False, True, reads=[h_b, w_b], writes=[ps_b])
                    P.copy("act", gtok[:, q, cb * 512:(cb + 1) * 512], ps, reads=[ps_b], writes=[gt_b])
            P.dma("sp", gt_sem, self.r_g[blk, :].rearrange("(q p) c -> p q c", p=128), gtok, reads=[gt_b], writes=[rb_])
            x_, x_b = mk_xm(2, blk)
            for q in range(NQ):
                for cb in range(2):
                    ps, ps_b = pbank(512)
                    for kc in range(DC):
                        P.mm(ps, x_[:, kc, q * 128:(q + 1) * 128], W3[2][:, kc, cb * 512:(cb + 1) * 512], kc == 0, kc == DC - 1,
                             reads=[w_b, x_b], writes=[ps_b])
                    P.copy("act", vtok[:, q, cb * 512:(cb + 1) * 512], ps, reads=[ps_b], writes=[vt_b])
            P.dma("sp", vt_sem, self.r_v[blk, :].rearrange("(q p) c -> p q c", p=128), vtok, reads=[vt_b], writes=[rb_])
            xr, xr_b = mk_xm(0, blk)
            xk, xk_b = mk_xm(1, blk)
            pbon, pbon_b = self.bank(7)[:, 0:NQ * 16].rearrange("p (q h) -> p q h", q=NQ), self.pb[7]
            for e in range(DC):
                ec = slice(e * 128, (e + 1) * 128)
                r_s, k_s, lw, a_s, kk, t1, t2, lpA, lpB, t3 = F
                (r_sb, k_sb, lw_b, a_sb, kk_b, t1_b, t2_b, lpA_b, lpB_b, t3_b) = F_b
                ps, ps_b = pbank()
                for kc in range(DC):
                    P.mm(ps, W3[0][:, kc, ec], xr[:, kc, :], kc == 0, kc == DC - 1, reads=[w_b, xr_b], writes=[ps_b])
                P.copy("act", r_s, ps, reads=[ps_b], writes=[r_sb])
                ps, ps_b = pbank()
                for kc in range(DC):
                    P.mm(ps, W3[1][:, kc, ec], xk[:, kc, :], kc == 0, kc == DC - 1, reads=[w_b, xk_b], writes=[ps_b])
                P.copy("act", k_s, ps, reads=[ps_b], writes=[k_sb])
                ps, ps_b = pbank()
                P.mm(ps, w2[:, ec], hw, True, True, reads=[w_b, h_b], writes=[ps_b])
                P.act(t1, ps, AF.Exp, reads=[ps_b, k_b], writes=[t1_b], scale=-1.0, bias=nw0[:, e:e + 1])
                P.act(t1, t1, AF.Ln, reads=[t1_b], writes=[t1_b], bias=1.0)
                P.act(t1, t1, AF.Exp, reads=[t1_b, k_b], writes=[t1_b], scale=-1.0, bias=mhalf)
                P.ts("dve", lw, t1, -1.0, None, ALU.mult, reads=[t1_b], writes=[lw_b])
                ps, ps_b = pbank()
                P.mm(ps, a2[:, ec], ha, True, True, reads=[w_b, h_b], writes=[ps_b])
                P.act(a_s, ps, AF.Sigmoid, reads=[ps_b, k_b], writes=[a_sb], bias=a0[:, e:e + 1])
                P.ts("dve", kk, k_s, kkp[:, e:e + 1], None, ALU.mult, reads=[k_sb, k_b], writes=[kk_b])
                P.act(sqb, kk, AF.Square, reads=[kk_b], writes=[sqb_b])
                ps, ps_b = pbank()
                P.mm(ps, bones, sqb, True, True, reads=[k_b, sqb_b], writes=[ps_b])
                P.ts("dve", t2, ps, 1e-24, None, ALU.max, reads=[ps_b], writes=[t2_b])
                P.act(t2, t2, AF.Sqrt, reads=[t2_b], writes=[t2_b])
                P.op("dve", lambda e_, t2=t2: e_.reciprocal(t2, t2), reads=[t2_b], writes=[t2_b])
                P.tt("dve", kk, kk, t2, ALU.mult, reads=[t2_b, kk_b], writes=[kk_b])
                P.ts("pool", t1, a_s, -1.0, kap[:, e:e + 1], ALU.add, ALU.mult, reads=[a_sb, k_b], writes=[t1_b])
                P.stt("dve", k_s, t1, 1.0, k_s, ALU.add, ALU.mult, reads=[t1_b, k_sb], writes=[k_sb])
                P.tt("pool", a_s, kk, a_s, ALU.mult, reads=[kk_b, a_sb], writes=[a_sb])
                v3 = lambda ap: ap.rearrange("p (c t) -> p c t", t=64)
                src_, src_bb = lw, lw_b
                pp_ = [(lpA, lpA_b), (lpB, lpB_b)]
                for si, sft in enumerate((1, 2, 4, 8, 16, 32)):
                    dst_, dst_bb = pp_[si % 2]
                    P.copy("pool", v3(dst_)[:, :, 0:sft], v3(src_)[:, :, 0:sft], reads=[src_bb], writes=[dst_bb])
                    P.tt("dve", v3(dst_)[:, :, sft:64], v3(src_)[:, :, sft:64], v3(src_)[:, :, 0:64 - sft], ALU.add,
                         reads=[src_bb], writes=[dst_bb])
                    src_, src_bb = dst_, dst_bb
                lp, lp_b = src_, src_bb
                P.tt("dve", t2, lp, lw, ALU.subtract, reads=[lp_b, lw_b], writes=[t2_b])
                P.act(t2, t2, AF.Exp, reads=[t2_b], writes=[t2_b])
                P.stt("dve", ARt[:, :, 0, :], v3(kk), -1.0, v3(t2), ALU.mult, ALU.mult, reads=[kk_b, t2_b], writes=[AB_b])
                P.act(t2, lp, AF.Exp, reads=[lp_b], writes=[t2_b])
                P.tt("dve", ARt[:, :, 1, :], v3(r_s), v3(t2), ALU.mult, reads=[r_sb, t2_b], writes=[AB_b])
                P.act(t2, lp, AF.Exp, reads=[lp_b], writes=[t2_b], scale=-1.0)
                P.tt("dve", BKt[:, :, 0, :], v3(a_s), v3(t2), ALU.mult, reads=[a_sb, t2_b], writes=[AB_b])
                P.tt("pool", BKt[:, :, 1, :], v3(k_s), v3(t2), ALU.mult, reads=[k_sb, t2_b], writes=[AB_b])
                P.tt("dve", v3(t3), v3(lp)[:, :, 63:64].to_broadcast([128, NCB, 64]), v3(lp), ALU.subtract, reads=[lp_b], writes=[t3_b])
                P.act(t3, t3, AF.Exp, reads=[t3_b], writes=[t3_b])
                P.act(PLx[:, e, tb * NCB:(tb + 1) * NCB], v3(lp)[:, :, 63], AF.Exp, reads=[lp_b], writes=[PLx_b])
                P.tt("dve", bkh[:, 0, :], a_s, t3, ALU.mult, reads=[a_sb, t3_b], writes=[bkh_b])
                P.tt("pool", bkh[:, 1, :], k_s, t3, ALU.mult, reads=[k_sb, t3_b], writes=[bkh_b])
                tp = self.psum[:, 4 * 512:6 * 512].bitcast(BF16)[:, 0:2 * NQ * 128].rearrange("p (i q c) -> p i q c", i=2, q=NQ)
                tp_b = self.pb[4]
                for i in range(2):
                    for q in range(NQ):
                        P.tr(tp[:, i, q, :], bkh[:, i, q * 128:(q + 1) * 128], self.ident[:], reads=[bkh_b, self.c_b], writes=[tp_b])
                P.copy("act", stg, tp, reads=[tp_b], writes=[stg_b])
                P.dma("sp", stg_sem, self.r_bh[blk, ec].rearrange("(q p) c -> p q c", p=128), stg[:, 0], reads=[stg_b], writes=[rb_])
                P.dma("sp", stg_sem, self.r_kh[blk, ec].rearrange("(q p) c -> p q c", p=128), stg[:, 1], reads=[stg_b], writes=[rb_])
                P.dma("sp", AB_sem, self.r_AR[e, :, tb * NCB:(tb + 1) * NCB, :, :], ARt, reads=[AB_b], writes=[rb_])
                P.dma("sp", AB_sem, self.r_BK[e, :, tb * NCB:(tb + 1) * NCB, :, :], BKt, reads=[AB_b], writes=[rb_])
                P.stt("dve", rk, r_s, rkp[:, e:e + 1], k_s, ALU.mult, ALU.mult, reads=[r_sb, k_sb, k_b], writes=[rk_b])
                for q in range(NQ):
                    P.mm(pbon[:, q, 2 * e:2 * e + 2], rk[:, q * 128:(q + 1) * 128], hsel, e == 0 and q == 0, e == 7 and q == NQ - 1,
                         reads=[rk_b, k_b], writes=[pbon_b])
            P.copy("dve", bon, pbon, reads=[pbon_b], writes=[bon_b])
            for q in range(NQ):
                P.tt("dve", gtok[:, q, :].rearrange("p (h n) -> p h n", h=16), vtok[:, q, :].rearrange("p (h n) -> p h n", h=16),
                     bon[:, q, :].unsqueeze(2).to_broadcast([128, 16, 64]), ALU.mult, reads=[vt_b, bon_b], writes=[gt_b])
            P.dma("sp", gt_sem, self.r_bv[blk, :].rearrange("(q p) c -> p q c", p=128), gtok, reads=[gt_b], writes=[rb_])
        pl_sem = P.dsem("plx")
        plb = P.buf("plxd")
        P.dma("sp", pl_sem, self.r_pl, PLx, reads=[PLx_b], writes=[plb])
        A.reset(mark)
        PL2 = A.alloc([64, 16, 64], F32)
        PL2_b = P.buf("PL2")
        P.dma("sp", pl_sem, PL2, self.r_pl.rearrange("(a n) e c -> n e a c", a=2), reads=[plb], writes=[PL2_b])
        ARc = [A.alloc([64, 16, 2, 64], BF16) for i in range(2)]
        BKc = [A.alloc([64, 16, 2, 64], BF16) for i in range(2)]
        BH = [A.alloc([64, D], BF16) for i in range(2)]
        KH = [A.alloc([64, D], BF16) for i in range(2)]
        Vt = [A.alloc([64, D], BF16) for i in range(2)]
        Ut = [A.alloc([64, D], BF16) for i in range(2)]
        in_b = P.bufs(2, "r2in")
        uv_b = P.bufs(2, "r2uv")
        in_sem = [P.dsem(f"r2in{i}") for i in range(2)]
        Gb = A.alloc([64, 8, 128], BF16)
        Gk = A.alloc([64, 8, 128], BF16)
        Gm_b = P.buf("Gm")
        Ap = [A.alloc([64, 8, 64], BF16) for i in range(2)]
        Mp = [A.alloc([64, 8, 64], BF16) for i in range(2)]
        TT = [A.alloc([64, 8, 64], BF16) for i in range(2)]
        Ap_b, Mp_b, TT_b = P.bufs(2, "Ap"), P.bufs(2, "Mp"), P.bufs(2, "TT")
        Xs = A.alloc([64, 8, 64], BF16)
        Xs_b = P.buf("Xs")
        H = A.alloc([64, 16, 64], F32)
        Hb = A.alloc([64, 16, 64], BF16)
        H_b = P.bufs(2, "H")
        P.memset("pool", H, 0.0, writes=H_b)
        P.memset("pool", Hb, 0.0, writes=H_b)
        ych = [A.alloc([64, D], F32) for i in range(2)]
        ych_b = P.bufs(2, "ych")
        ych_sem = [P.dsem(f"ych{i}") for i in range(2)]
        idb = self.ident[0:64, 0:64].unsqueeze(1).to_broadcast([64, 8, 64])
        mG = maskG[0:64, :].unsqueeze(1).to_broadcast([64, 8, 128])
        ARd = self.r_AR.rearrange("e (a n) c x t -> n (e a) c x t", a=2)
        BKd = self.r_BK.rearrange("e (a n) c x t -> n (e a) c x t", a=2)
        v8 = lambda bk: self.bank(bk)[0:64, :].rearrange("p (h t) -> p h t", h=8)
        for c in range(64):
            sl = c % 2
            rows = slice(c * 64, (c + 1) * 64)
            rb_ = [self.r1_b[c // 8]]
            P.dma("sp", in_sem[sl], ARc[sl], ARd[:, :, c, :, :], reads=rb_, writes=[in_b[sl]])
            P.dma("sp", in_sem[sl], BKc[sl], BKd[:, :, c, :, :], reads=rb_, writes=[in_b[sl]])
            P.dma("sp", in_sem[sl], BH[sl], self.r_bh[rows, :], reads=rb_, writes=[in_b[sl]])
            P.dma("sp", in_sem[sl], KH[sl], self.r_kh[rows, :], reads=rb_, writes=[in_b[sl]])
            P.dma("sp", in_sem[sl], Vt[sl], self.r_v[rows, :], reads=rb_, writes=[in_b[sl]])
            for hf in range(2):
                hds = list(range(8 * hf, 8 * hf + 8))
                pGb = self.psum[0:64, 0:1024].rearrange("p (h t) -> p h t", h=8)
                pGk = self.psum[0:64, 1024:2048].rearrange("p (h t) -> p h t", h=8)
                pAm = v8(4)
                for hi, h in enumerate(hds):
                    P.mm(pGb[:, hi, :], BKc[sl][:, h, 0, :], ARc[sl][:, h, :, :], hi % 4 == 0, hi % 4 == 3, reads=[in_b[sl]], writes=[self.pb[0]])
                for hi, h in enumerate(hds):
                    P.mm(pGk[:, hi, :], BKc[sl][:, h, 1, :], ARc[sl][:, h, :, :], hi % 4 == 0, hi % 4 == 3, reads=[in_b[sl]], writes=[self.pb[2]])
                for hi, h in enumerate(hds):
                    P.mm(pAm[:, hi, :], ARc[sl][:, h, 0, :], BKc[sl][:, h, 0, :], hi == 0, hi == 7, reads=[in_b[sl]], writes=[self.pb[4]])
                P.tt("dve", Gb, pGb, mG, ALU.mult, reads=[self.pb[0], k_b], writes=[Gm_b])
                P.tt("dve", Gk, pGk, mG, ALU.mult, reads=[self.pb[2], k_b], writes=[Gm_b])
                P.copy("pool", Mp[0], Gb[:, :, 0:64], reads=[Gm_b], writes=[Mp_b[0]])
                P.tt("dve", Ap[0], pAm, maskA.unsqueeze(1).to_broadcast([64, 8, 64]), ALU.mult, reads=[self.pb[4], k_b], writes=[Ap_b[0]])
                P.tt("pool", TT[0], Mp[0], idb, ALU.add, reads=[Mp_b[0], self.c_b], writes=[TT_b[0]])
                cur = 0
                for rd in range(5):
                    nxt = 1 - cur
                    pA2, pM2, pT2 = v8(4), v8(5), v8(6)
                    for hi in range(8):
                        P.mm(pA2[:, hi, :], Mp[cur][:, hi, :], Ap[cur][:, hi, :], hi == 0, hi == 7,
                             reads=[Mp_b[cur], Ap_b[cur]], writes=[self.pb[4]])
                    if rd < 4:
                        for hi in range(8):
                            P.mm(pM2[:, hi, :], Ap[cur][:, hi, :], Mp[cur][:, hi, :], hi == 0, hi == 7,
                                 reads=[Mp_b[cur], Ap_b[cur]], writes=[self.pb[5]])
                    P.copy("act", Ap[nxt], pA2, reads=[self.pb[4]], writes=[Ap_b[nxt]])
                    if rd < 4:
                        P.copy("dve", Mp[nxt], pM2, reads=[self.pb[5]], writes=[Mp_b[nxt]])
                    for hi in range(8):
                        P.mm(pT2[:, hi, :], Ap[nxt][:, hi, :], TT[cur][:, hi, :], hi == 0, hi == 7,
                             reads=[Ap_b[nxt], TT_b[cur]], writes=[self.pb[6]])
                    P.tt("dve", TT[nxt], TT[cur], pT2, ALU.add, reads=[self.pb[6], TT_b[cur]], writes=[TT_b[nxt]])
                    cur = nxt
                TTf, TTf_b = TT[cur], TT_b[cur]
                pX, pU, pY, pH = v8(7), v8(4), v8(5), v8(6)
                for hi, h in enumerate(hds):
                    hc = slice(h * 64, h * 64 + 64)
                    P.mm(pX[:, hi, :], ARc[sl][:, h, 0, :], Hb[:, h, :], hi == 0, False, reads=[in_b[sl], H_b[hf]], writes=[self.pb[7]])
                    P.mm(pX[:, hi, :], Gk[:, hi, 0:64], Vt[sl][:, hc], False, hi == 7, reads=[Gm_b, in_b[sl]], writes=[self.pb[7]])
                P.copy("act", Xs, pX, reads=[self.pb[7]], writes=[Xs_b])
                for hi, h in enumerate(hds):
                    P.mm(pU[:, hi, :], TTf[:, hi, :], Xs[:, hi, :], hi == 0, hi == 7, reads=[TTf_b, Xs_b], writes=[self.pb[4]])
                h0 = 8 * hf * 64
                P.copy("act", Ut[sl][:, h0:h0 + 512], self.bank(4)[0:64, :], reads=[self.pb[4]], writes=[uv_b[sl]])
                for hi, h in enumerate(hds):
                    hc = slice(h * 64, h * 64 + 64)
                    P.mm(pY[:, hi, :], ARc[sl][:, h, 1, :], Hb[:, h, :], hi == 0, False, reads=[in_b[sl], H_b[hf]], writes=[self.pb[5]])
                    P.mm(pY[:, hi, :], Gb[:, hi, 64:128], Ut[sl][:, hc], False, False, reads=[Gm_b, uv_b[sl]], writes=[self.pb[5]])
                    P.mm(pY[:, hi, :], Gk[:, hi, 64:128], Vt[sl][:, hc], False, hi == 7, reads=[Gm_b, in_b[sl]], writes=[self.pb[5]])
                P.copy("act", ych[sl][:, h0:h0 + 512], self.bank(5)[0:64, :], reads=[self.pb[5]], writes=[ych_b[sl]])
                for hi, h in enumerate(hds):
                    hc = slice(h * 64, h * 64 + 64)
                    P.mm(pH[:, hi, :], BH[sl][:, hc], Ut[sl][:, hc], hi == 0, False, reads=[in_b[sl], uv_b[sl]], writes=[self.pb[6]])
                    P.mm(pH[:, hi, :], KH[sl][:, hc], Vt[sl][:, hc], False, hi == 7, reads=[in_b[sl]], writes=[self.pb[6]])
                Hh = H[:, 8 * hf:8 * hf + 8, :]
                P.tt("dve", Hh, Hh, PL2[:, 8 * hf:8 * hf + 8, c:c + 1].to_broadcast([64, 8, 64]), ALU.mult,
                     reads=[PL2_b, H_b[hf]], writes=[H_b[hf]])
                P.tt("dve", Hh, Hh, pH, ALU.add, reads=[self.pb[6], H_b[hf]], writes=[H_b[hf]])
                P.copy("pool", Hb[:, 8 * hf:8 * hf + 8, :], Hh, reads=[H_b[hf]], writes=[H_b[hf]])
            P.dma("sp", ych_sem[sl], self.r_y[rows, :], ych[sl], reads=[ych_b[sl]], writes=[self.ry_b[c // 2]])
        A.reset(mark)
        gn = A.alloc([128, 2, D], F32)
        P.dma("sp", sem, gn, self.rw_gn, writes=[k_b])
        yt = [A.alloc([128, D], F32) for i in range(2)]
        bvt = [A.alloc([128, D], BF16) for i in range(2)]
        gtt = [A.alloc([128, D], BF16) for i in range(2)]
        i3_b = P.bufs(2, "r3in")
        i3_sem = [P.dsem(f"r3in{i}") for i in range(2)]
        sqt = A.alloc([128, D], F32)
        sqt_b = P.buf("sqt")
        stt_ = A.alloc([128, 4, 16], F32)
        st_b = P.buf("r3st")
        ot = [A.alloc([128, D], BF16) for i in range(2)]
        ot_b = P.bufs(2, "r3o")
        ot_sem = [P.dsem(f"r3o{i}") for i in range(2)]
        v16 = lambda ap: ap.rearrange("p (h n) -> p h n", h=16)
        for t in range(NT):
            sl = t % 2
            tile = slice(t * 128, (t + 1) * 128)
            P.dma("sp", i3_sem[sl], yt[sl], self.r_y[tile, :], reads=[self.ry_b[t]], writes=[i3_b[sl]])
            P.dma("sp", i3_sem[sl], bvt[sl], self.r_bv[tile, :], reads=[self.r1_b[t // 4]], writes=[i3_b[sl]])
            P.dma("sp", i3_sem[sl], gtt[sl], self.r_g[tile, :], reads=[self.r1_b[t // 4]], writes=[i3_b[sl]])
            y_ = yt[sl]
            mean, ex2, var, rstd = (stt_[:, i, :] for i in range(4))
            P.op("dve", lambda e_, y_=y_, mean=mean: e_.tensor_reduce(mean, v16(y_), AX.X, ALU.add), reads=[i3_b[sl]], writes=[st_b])
            P.act(sqt, y_, AF.Square, reads=[i3_b[sl]], writes=[sqt_b])
            P.op("dve", lambda e_, ex2=ex2: e_.tensor_reduce(ex2, v16(sqt), AX.X, ALU.add), reads=[sqt_b], writes=[st_b])
            P.ts("dve", mean, mean, 1.0 / 64, None, ALU.mult, reads=[st_b], writes=[st_b])
            P.stt("dve", var, mean, -1.0, mean, ALU.mult, ALU.mult, reads=[st_b], writes=[st_b])
            P.stt("dve", var, ex2, 1.0 / 64, var, ALU.mult, ALU.add, reads=[st_b], writes=[st_b])
            P.ts("dve", var, var, 64e-5, None, ALU.add, reads=[st_b], writes=[st_b])
            P.act(rstd, var, AF.Sqrt, reads=[st_b], writes=[st_b])
            P.op("dve", lambda e_, rstd=rstd: e_.reciprocal(rstd, rstd), reads=[st_b], writes=[st_b])
            P.tt("dve", v16(y_), v16(y_), mean.unsqueeze(2).to_broadcast([128, 16, 64]), ALU.subtract, reads=[st_b, i3_b[sl]], writes=[i3_b[sl]])
            P.tt("dve", v16(y_), v16(y_), rstd.unsqueeze(2).to_broadcast([128, 16, 64]), ALU.mult, reads=[st_b, i3_b[sl]], writes=[i3_b[sl]])
            P.tt("pool", y_, y_, gn[:, 0, :], ALU.mult, reads=[k_b, i3_b[sl]], writes=[i3_b[sl]])
            P.tt("pool", y_, y_, gn[:, 1, :], ALU.add, reads=[k_b, i3_b[sl]], writes=[i3_b[sl]])
            P.tt("dve", y_, y_, bvt[sl], ALU.add, reads=[i3_b[sl]], writes=[i3_b[sl]])
            P.tt("dve", ot[sl], y_, gtt[sl], ALU.mult, reads=[i3_b[sl]], writes=[ot_b[sl]])
            P.dma("sp", ot_sem[sl], self.ob[tile, 0:D], ot[sl], reads=[ot_b[sl]], writes=[self.ob_b[t]])
        self.tm_proj_phase(DC, self.rw_wo, res, res_b, dst, dst_b, "rwkv")

    def build(self):
        P = self.P
        src, src_b = self.x_in, self.xin_b
        pp = [(self.xa, self.xa_b), (self.xb, self.xb_b)]
        ip = 0
        for l in self.layers:
            if self.do_mix:
                self.norm_phase(src, src_b, self.gmix[:, l, :])
                dst, dst_b = pp[ip]
                ip ^= 1
                if l == 0:
                    self.moba(src, src_b, dst, dst_b)
                if l == 1:
                    self.rwkv(src, src_b, dst, dst_b)
                if l == 2:
                    self.ssd(src, src_b, dst, dst_b)
                if l == 3:
                    self.conformer(src, src_b, dst, dst_b)
                src, src_b = dst, dst_b
            if self.do_ffn:
                self.norm_phase(src, src_b, self.gffn[:, l, :])
                dst, dst_b = pp[ip]
                ip ^= 1
                self.ffn_phase(l, src, src_b, dst, dst_b)
                src, src_b = dst, dst_b
        self.A.reset()
        gf = self.A.alloc([128, D], F32)
        gf_b = P.buf("gf")
        P.dma("sp", self.c_sem, gf, self.norm_final, writes=[gf_b])
        for t in range(NT):
            xt, xt_b, sl, k = self.load_x(src, src_b, t)
            rstd, ss_b = self.rms_tile(xt, xt_b, k)
            P.act(xt[:], xt[:], AF.Copy, reads=[xt_b, ss_b], writes=[xt_b], scale=rstd)
            P.tt("dve", xt[:], xt[:], gf, ALU.mult, reads=[xt_b, gf_b], writes=[xt_b])
            self.store_x(self.out, self.out_b, t, xt, xt_b, sl)
        return P.finish()


def _fm(v, nch):
    v = np.asarray(v, np.float32)
    lead = v.shape[:-1]
    a = v.reshape(lead + (nch, 128))
    a = np.moveaxis(a, -1, 0)
    return np.ascontiguousarray(a)


def _bc(v, n=128):
    v = np.asarray(v, np.float32).reshape(1, -1)
    return np.ascontiguousarray(np.broadcast_to(v, (n, v.shape[1])))


def make_shared(inp):
    m = {}
    f = lambda k: np.ascontiguousarray(np.asarray(inp[k], np.float32))
    m["ident"] = np.eye(128, dtype=np.float32).astype(ml_dtypes.bfloat16)
    m["norm_mix"] = _fm(inp["norm_mix"], DC)
    m["norm_ffn"] = _fm(inp["norm_ffn"], DC)
    m["norm_final"] = _bc(inp["norm_final"])
    wu = np.asarray(inp["ffn_w_up"], np.float32).reshape(4, DC, 128, 2, FC, 128)
    m["ffn_w_up"] = np.ascontiguousarray(wu.transpose(0, 4, 2, 1, 3, 5)).reshape(4, FC, 128, DC * 2 * 128)
    wdn = np.asarray(inp["ffn_w_down"], np.float32).reshape(4, FC, 128, D)
    m["ffn_w_down"] = np.ascontiguousarray(wdn.transpose(0, 2, 1, 3))
    cw = np.asarray(inp["ffn_conv_w"], np.float32)
    cw = cw.transpose(0, 2, 1).reshape(4, 2 * FC, 128, 3)
    m["ffn_cw"] = np.ascontiguousarray(cw.transpose(2, 0, 1, 3))
    m["ffn_cb"] = _fm(inp["ffn_conv_b"], 2 * FC)
    m["moba_w_qkv"] = f("moba_w_qkv")[0]
    m["moba_w_o"] = f("moba_w_o")[0]
    kk = np.arange(S) // 256
    m["blkind"] = (kk[None, :] == np.arange(16)[:, None]).astype(np.float32).astype(ml_dtypes.bfloat16)
    qb = np.arange(16)[:, None]
    nn = np.arange(16)[None, :]
    mcst = np.stack([np.where(nn < qb, 0.0, -1e30), (nn < qb).astype(np.float32), (nn == qb).astype(np.float32)]).astype(np.float32)
    m["mconst"] = np.ascontiguousarray(np.broadcast_to(mcst[None], (128, 3, 16, 16)))
    m["tri"] = (np.arange(128)[None, :] >= np.arange(128)[:, None]).astype(np.float32).astype(ml_dtypes.bfloat16)
    m["rwkv_w_rkv"] = f("rwkv_w_rkv")[0]
    m["rwkv_w_o"] = f("rwkv_w_o")[0]
    for k_ in ("w1", "a1", "g1", "w2", "a2", "g2"):
        m["rwkv_" + k_] = f("rwkv_" + k_)[0]
    mu_ = _fm(inp["rwkv_mu"][0], 8).reshape(128, 48)
    m["rw_p"] = np.ascontiguousarray(np.concatenate(
        [mu_] + [_fm(np.asarray(inp["rwkv_" + k_][0]).reshape(-1), 8) for k_ in ("w0", "a0", "k_k", "k_a", "r_k")], axis=1))
    m["rw_gn"] = np.ascontiguousarray(np.stack([_bc(inp["rwkv_gn_w"][0]), _bc(inp["rwkv_gn_b"][0])], axis=1))
    i128 = np.arange(128)
    bones = (i128[:, None] // 64 == i128[None, :] // 64).astype(np.float32)
    hsel = (i128[:, None] // 64 == np.arange(2)[None, :]).astype(np.float32)
    s64 = i128[:, None] % 64
    t64 = i128[None, :] % 64
    maskG = np.where(i128[None, :] < 64, s64 < t64, s64 <= t64).astype(np.float32)
    maskA = np.zeros((128, 64), np.float32)
    maskA[:64] = (np.arange(64)[None, :] < np.arange(64)[:, None]).astype(np.float32)
    m["rw_c"] = np.ascontiguousarray(np.concatenate([bones, hsel, maskG, maskA], axis=1)).astype(ml_dtypes.bfloat16)
    m["ssd_w_in"] = f("ssd_w_in")[0]
    m["ssd_w_out"] = f("ssd_w_out")[0]
    scw = np.asarray(inp["ssd_conv_w"], np.float32)[0]
    scw = scw.T.reshape(24, 128, 4).transpose(1, 0, 2).reshape(128, 96)
    m["ssd_p"] = np.ascontiguousarray(np.concatenate([scw, _fm(inp["ssd_conv_b"][0], 24)], axis=1))
    m["ssd_t"] = np.ascontiguousarray(np.concatenate([_bc(inp["ssd_dt_bias"][0]), _bc(inp["ssd_a_log"][0]), _bc(inp["ssd_d"][0])], axis=1))
    m["ssd_nw"] = _bc(inp["ssd_norm_w"][0])
    nbm = np.where(np.arange(128)[:, None] > np.arange(128)[None, :], -30000.0, 0.0).astype(np.float32)
    m["nbmask"] = np.ascontiguousarray(np.broadcast_to(nbm[:, None, :], (128, 4, 128))).astype(ml_dtypes.bfloat16)
    m["conf_w_pw1"] = f("conf_w_pw1")[0]
    m["conf_w_pw2"] = f("conf_w_pw2")[0]
    dww = np.asarray(inp["conf_dw_w"], np.float32)[0]
    dww = dww.T.reshape(8, 128, 31).transpose(1, 0, 2).reshape(128, 8 * 31)
    m["conf_p"] = np.ascontiguousarray(np.concatenate([
        _fm(inp["conf_b_pw1"][0], 16), dww, _fm(inp["conf_dw_b"][0], 8),
        _fm(inp["conf_ln_w"][0], 8), _fm(inp["conf_ln_b"][0], 8)], axis=1))
    m["conf_b2"] = _bc(inp["conf_b_pw2"][0])
    return m


def make_inputs(inp, b, shared=None):
    m = dict(shared if shared is not None else make_shared(inp))
    m["x"] = np.ascontiguousarray(inp["x"][b], dtype=np.float32)
    return m


_NC = {}


def kernel(**inputs):
    if "nc" not in _NC:
        _NC["nc"] = Model().build()
    nc = _NC["nc"]
    shared = make_shared(inputs)
    in_maps = [make_inputs(inputs, b, shared) for b in range(8)]
    res = run_bass_kernel_spmd(nc, in_maps, core_ids=list(range(8)))
    return np.stack([np.asarray(r["out"], np.float32) for r in res.results], axis=0)
```

```python
import numpy as np
from contextlib import ExitStack
import ml_dtypes
import concourse.bass as bass
import concourse.mybir as mybir
from concourse.bass_utils import run_bass_kernel_spmd

F32 = mybir.dt.float32
BF16 = mybir.dt.bfloat16
AF = mybir.ActivationFunctionType
ALU = mybir.AluOpType
AX = mybir.AxisListType

S = 4096
D = 1024
DFF = 2816
NT = S // 128
DC = D // 128
FC = DFF // 128
EPS = 1e-6
ENGS = ("pe", "act", "dve", "pool", "sp")
STRICT = ("act", "dve", "pool")
EPOCH = 16384


class Buf:
    __slots__ = ("w", "r", "name")

    def __init__(self, name=""):
        self.w = None
        self.r = {}
        self.name = name


class Prog:
    def __init__(self):
        self.nc = bass.Bass("TRN2", target_bir_lowering=False)
        self.es = ExitStack()
        self.streams = {e: [] for e in ENGS}
        self.cnt = {e: 0 for e in ENGS}
        self.seen = {e: {} for e in ENGS}
        self.sems = {}
        self.epochs = set()
        self.nbuf = 0

    def sb(self, name, shape, dt):
        return self.es.enter_context(self.nc.sbuf_tensor(name, list(shape), dt))

    def ps(self, name, shape, dt):
        return self.es.enter_context(self.nc.psum_tensor(name, list(shape), dt))

    def dram(self, name, shape, dt, kind="Internal"):
        return self.nc.dram_tensor(name, list(shape), dt, kind=kind).ap()

    def dsem(self, name):
        k = ("d", name)
        assert k not in self.cnt
        self.cnt[k] = 0
        return k

    def buf(self, name=""):
        return Buf(name)

    def bufs(self, n, name=""):
        return [Buf(name + str(i)) for i in range(n)]

    def _dep(self, eng, dep):
        if dep is None:
            return
        k, v = dep
        if k == eng and eng not in STRICT:
            return
        if self.seen[eng].get(k, 0) >= v:
            return
        self.seen[eng][k] = v
        if k in ENGS:
            ep = (v - 1) // EPOCH
            self.epochs.add((k, ep))
            self.streams[eng].append(("w", (k, ep), (v - 1) % EPOCH + 1))
        else:
            self.streams[eng].append(("w", k, v))

    def _deps(self, eng, reads, writes):
        for b in reads:
            self._dep(eng, b.w)
        for b in writes:
            self._dep(eng, b.w)
            for k, v in b.r.items():
                self._dep(eng, (k, v))

    def op(self, eng, fn, reads=(), writes=()):
        self._deps(eng, reads, writes)
        self.cnt[eng] += 1
        n = self.cnt[eng]
        self.epochs.add((eng, (n - 1) // EPOCH))
        self.streams[eng].append(("o", fn, (eng, (n - 1) // EPOCH), 1))
        for b in reads:
            b.r[eng] = n
        for b in writes:
            b.w = (eng, n)
            b.r = {}

    def dma(self, q, sem, out, in_, reads=(), writes=(), **kw):
        self._deps(q, reads, writes)
        self.cnt[sem] += 16
        n = self.cnt[sem]
        self.streams[q].append(("o", lambda e: e.dma_start(out=out, in_=in_, **kw), sem, 16))
        for b in reads:
            b.r[sem] = n
        for b in writes:
            b.w = (sem, n)
            b.r = {}

    def mm(self, out, lhsT, rhs, start, stop, reads=(), writes=()):
        self.op("pe", lambda e: e.matmul(out, lhsT, rhs, start=start, stop=stop), reads, writes)

    def tr(self, out, in_, ident, reads=(), writes=()):
        self.op("pe", lambda e: e.transpose(out, in_, ident), reads, writes)

    def act(self, out, in_, func, reads=(), writes=(), **kw):
        self.op("act", lambda e: e.activation(out, in_, func, **kw), reads, writes)

    def tt(self, eng, out, in0, in1, op, reads=(), writes=()):
        self.op(eng, lambda e: e.tensor_tensor(out, in0, in1, op), reads, writes)

    def ts(self, eng, out, in0, s1, s2, op0, op1=None, reads=(), writes=()):
        if op1 is None:
            self.op(eng, lambda e: e.tensor_scalar(out, in0, s1, None, op0), reads, writes)
        else:
            self.op(eng, lambda e: e.tensor_scalar(out, in0, s1, s2, op0, op1), reads, writes)

    def stt(self, eng, out, in0, scalar, in1, op0, op1, reads=(), writes=()):
        self.op(eng, lambda e: e.scalar_tensor_tensor(out, in0, scalar, in1, op0, op1), reads, writes)

    def copy(self, eng, out, in_, reads=(), writes=()):
        if eng == "act":
            self.op(eng, lambda e: e.copy(out, in_), reads, writes)
        else:
            self.op(eng, lambda e: e.tensor_copy(out, in_), reads, writes)

    def memset(self, eng, ap, val, writes=()):
        self.op(eng, lambda e: e.memset(ap, val), (), writes)

    def barrier(self):
        for e in ENGS:
            for k, v in self.cnt.items():
                if v > 0 and k != e:
                    self._dep(e, (k, v))

    def finish(self):
        nc = self.nc
        for k, v in self.cnt.items():
            if v > 0 and k != "sp":
                self._dep("sp", (k, v))
        for k in self.cnt:
            if k not in ENGS:
                self.sems[k] = self.es.enter_context(nc.semaphore("s_" + k[1]))
        for (k, ep) in sorted(self.epochs):
            self.sems[(k, ep)] = self.es.enter_context(nc.semaphore(f"e_{k}_{ep}"))
        streams, sems = self.streams, self.sems

        def replay(name, e):
            for it in streams[name]:
                if it[0] == "w":
                    e.wait_ge(sems[it[1]], it[2])
                else:
                    it[1](e).then_inc(sems[it[2]], it[3])

        with nc.Block() as block:
            @block.tensor
            def _(e):
                replay("pe", e)

            @block.scalar
            def _(e):
                replay("act", e)

            @block.vector
            def _(e):
                replay("dve", e)

            @block.gpsimd
            def _(e):
                replay("pool", e)

            @block.sync
            def _(e):
                replay("sp", e)
        self.es.close()
        return nc


class Arena:
    def __init__(self, P, name, nbytes):
        self.P = P
        self.t = P.sb(name, [128, nbytes // 2], BF16)
        self.n = nbytes // 2
        self.off = 0

    def reset(self, to=0):
        self.P.barrier()
        self.off = to

    def alloc(self, shape, dt):
        ne = 1
        for d in shape[1:]:
            ne *= d
        if dt == F32:
            ne *= 2
        ne = (ne + 15) // 16 * 16
        assert self.off + ne <= self.n, (self.off, ne, self.n)
        ap = self.t[0:shape[0], self.off:self.off + ne]
        self.off += ne
        if dt == F32:
            ap = ap.bitcast(F32)
        nfree = 1
        for d in shape[1:]:
            nfree *= d
        ap = ap[:, 0:nfree]
        if len(shape) == 3:
            ap = ap.rearrange("p (a b) -> p a b", a=shape[1])
        elif len(shape) == 4:
            ap = ap.rearrange("p (a b c) -> p a b c", a=shape[1], b=shape[2])
        return ap


class Model:
    def __init__(self, layers=(0, 1, 2, 3), do_mix=True, do_ffn=True):
        self.P = P = Prog()
        self.layers = layers
        self.do_mix = do_mix
        self.do_ffn = do_ffn
        d = lambda n, sh, dt=F32: P.dram(n, sh, dt, "ExternalInput")
        self.x_in = d("x", [S, D])
        self.out = P.dram("out", [S, D], F32, "ExternalOutput")
        self.ident_d = d("ident", [128, 128], BF16)
        self.norm_mix = d("norm_mix", [128, 4, DC])
        self.norm_ffn = d("norm_ffn", [128, 4, DC])
        self.norm_final = d("norm_final", [128, D])
        self.ffn_w_up = d("ffn_w_up", [4, FC, 128, DC * 2 * 128])
        self.ffn_w_down = d("ffn_w_down", [4, 128, FC, D])
        self.ffn_cw = d("ffn_cw", [128, 4, 2 * FC, 3])
        self.ffn_cb = d("ffn_cb", [128, 4, 2 * FC])
        self.conf_w1 = d("conf_w_pw1", [D, 2 * D])
        self.conf_w2 = d("conf_w_pw2", [D, D])
        self.conf_p = d("conf_p", [128, 16 + 8 * 31 + 8 + 8 + 8])
        self.conf_b2 = d("conf_b2", [128, D])
        self.moba_wqkv = d("moba_w_qkv", [D, 3 * D])
        self.moba_wo = d("moba_w_o", [D, D])
        self.blkind = d("blkind", [16, S], BF16)
        self.mconst = d("mconst", [128, 3, 16, 16])
        self.tri_d = d("tri", [128, 128], BF16)
        self.ssd_win = d("ssd_w_in", [D, 5152])
        self.ssd_wout = d("ssd_w_out", [2 * D, D])
        self.ssd_wx = d("ssd_wx", [24, 128, DC * 128])
        self.ssd_p = d("ssd_p", [128, 24 * 5])
        self.ssd_t = d("ssd_t", [128, 96])
        self.ssd_nw = d("ssd_nw", [128, 2 * D])
        self.nb_d = d("nbmask", [128, 4, 128], BF16)
        self.s_xB = P.dram("s_xB", [S, 2560], BF16)
        self.s_xB_b = P.bufs(8, "sxB")
        self.s_BCT = P.dram("s_BCT", [8, 128, S], BF16)
        self.s_BCT_b = P.bufs(8, "sBCT")
        self.s_z = P.dram("s_z", [S, 2 * D], BF16)
        self.s_z_b = P.bufs(NT, "sz")
        self.rw_rkv = d("rwkv_w_rkv", [3, D, D])
        self.rw_wo = d("rwkv_w_o", [D, D])
        self.rw_w1 = d("rwkv_w1", [D, 64])
        self.rw_a1 = d("rwkv_a1", [D, 64])
        self.rw_g1 = d("rwkv_g1", [D, 160])
        self.rw_w2 = d("rwkv_w2", [64, D])
        self.rw_a2 = d("rwkv_a2", [64, D])
        self.rw_g2 = d("rwkv_g2", [160, D])
        self.rw_p = d("rw_p", [128, 88])
        self.rw_gn = d("rw_gn", [128, 2, D])
        self.rw_c = d("rw_c", [128, 128 + 2 + 128 + 64], BF16)
        self.r_AR = P.dram("r_AR", [8, 128, 64, 2, 64], BF16)
        self.r_BK = P.dram("r_BK", [8, 128, 64, 2, 64], BF16)
        self.r_bh = P.dram("r_bh", [S, D], BF16)
        self.r_kh = P.dram("r_kh", [S, D], BF16)
        self.r_v = P.dram("r_v", [S, D], BF16)
        self.r_g = P.dram("r_g", [S, D], BF16)
        self.r_bv = P.dram("r_bv", [S, D], BF16)
        self.r_y = P.dram("r_y", [S, D], F32)
        self.r_pl = P.dram("r_pl", [128, 8, 64], F32)
        self.r1_b = P.bufs(8, "r1")
        self.ry_b = P.bufs(NT, "ry")
        self.ob = P.dram("ob", [S, 2 * D], BF16)
        self.ob_b = P.bufs(NT, "ob")
        self.xa = P.dram("xa", [S, D], F32)
        self.xa_b = P.bufs(NT, "xa")
        self.xb = P.dram("xb", [S, D], F32)
        self.xb_b = P.bufs(NT, "xb")
        self.xin_b = P.bufs(NT, "xin")
        self.out_b = P.bufs(NT, "out")
        self.c_sem = P.dsem("const")
        self.c_b = P.buf("consts")
        self.ident = P.sb("ident_sb", [128, 128], BF16)
        self.ident_b = self.c_b
        P.dma("sp", self.c_sem, self.ident[:], self.ident_d, writes=[self.c_b])
        self.gmix = P.sb("gmix", [128, 4, DC], F32)
        self.gffn = P.sb("gffn", [128, 4, DC], F32)
        self.g_b = self.c_b
        P.dma("sp", self.c_sem, self.gmix[:], self.norm_mix, writes=[self.c_b])
        P.dma("sp", self.c_sem, self.gffn[:], self.norm_ffn, writes=[self.c_b])
        self.cw = P.sb("ffn_cw_sb", [128, 4, 2 * FC, 3], F32)
        self.cb = P.sb("ffn_cb_sb", [128, 4, 2 * FC], F32)
        P.dma("sp", self.c_sem, self.cw[:], self.ffn_cw, writes=[self.c_b])
        P.dma("sp", self.c_sem, self.cb[:], self.ffn_cb, writes=[self.c_b])
        self.ones = P.sb("ones_bf", [128, 128], BF16)
        P.memset("pool", self.ones[:], 1.0, writes=[self.c_b])
        self.hnT = P.sb("hnT", [128, DC, S], BF16)
        self.hnT_b = P.bufs(S // 512, "hnT")
        self.xt = [P.sb(f"xt{i}", [128, D], F32) for i in range(2)]
        self.xt_b = P.bufs(2, "xt")
        self.xt_sem = [P.dsem(f"xt{i}") for i in range(2)]
        self.xs = [P.sb(f"xs{i}", [128, D], BF16) for i in range(2)]
        self.xs_b = P.bufs(2, "xs")
        self.sq = P.sb("sqjunk", [128, D], BF16)
        self.sq_b = P.buf("sq")
        self.ss = [P.sb(f"ss{i}", [128, 2], F32) for i in range(2)]
        self.ss_b = P.bufs(2, "ss")
        self.nk = 0
        self.A = Arena(P, "arena", 122 * 1024)
        self.psum = P.ps("psum", [128, 4096], F32)
        self.pb = P.bufs(8, "psb")

    def bank(self, i):
        return self.psum[:, i * 512:(i + 1) * 512]

    def wload(self, dst_sb, src_ap, sem, b, nsplit=1):
        P = self.P
        n = dst_sb.shape[1]
        step = n // nsplit
        for i in range(nsplit):
            P.dma("pool", sem, dst_sb[:, i * step:(i + 1) * step], src_ap[:, i * step:(i + 1) * step], writes=[b])

    def rms_tile(self, xt, xt_b, k):
        P = self.P
        ss, ss_b = self.ss[k % 2], self.ss_b[k % 2]
        P.act(self.sq[:], xt[:], AF.Square, reads=[xt_b], writes=[self.sq_b, ss_b], accum_out=ss[:, 0:1])
        P.ts("dve", ss[:, 1:2], ss[:, 0:1], 1.0 / D, EPS, ALU.mult, ALU.add, reads=[ss_b], writes=[ss_b])
        P.act(ss[:, 1:2], ss[:, 1:2], AF.Sqrt, reads=[ss_b], writes=[ss_b])
        P.op("dve", lambda e: e.reciprocal(ss[:, 1:2], ss[:, 1:2]), reads=[ss_b], writes=[ss_b])
        return ss[:, 1:2], ss_b

    def load_x(self, src, src_b, t):
        P = self.P
        k = self.nk
        self.nk += 1
        sl = k % 2
        xt, xt_b = self.xt[sl], self.xt_b[sl]
        P.dma("sp", self.xt_sem[sl], xt[:], src[t * 128:(t + 1) * 128, :], reads=[src_b[t]], writes=[xt_b])
        return xt, xt_b, sl, k

    def store_x(self, dst, dst_b, t, xt, xt_b, sl):
        self.P.dma("sp", self.xt_sem[sl], dst[t * 128:(t + 1) * 128, :], xt[:], reads=[xt_b], writes=[dst_b[t]])

    def norm_phase(self, src, src_b, g):
        P = self.P
        psT = [self.bank(6 + i).bitcast(BF16).rearrange("p (c t) -> p c t", c=DC) for i in range(2)]
        for t in range(NT):
            xt, xt_b, sl, k = self.load_x(src, src_b, t)
            rstd, ss_b = self.rms_tile(xt, xt_b, k)
            xs, xs_b = self.xs[sl], self.xs_b[sl]
            P.act(xs[:], xt[:], AF.Copy, reads=[xt_b, ss_b], writes=[xs_b], scale=rstd)
            pt, pt_b = psT[sl], self.pb[6 + sl]
            for c in range(DC):
                P.tr(pt[:, c, :], xs[:, c * 128:(c + 1) * 128], self.ident[:], reads=[xs_b, self.c_b], writes=[pt_b])
            hb = self.hnT_b[t // 4]
            gb = g.unsqueeze(2).to_broadcast([128, DC, 128])
            P.tt("dve", self.hnT[:, :, t * 128:(t + 1) * 128], pt, gb, ALU.mult,
                 reads=[pt_b, self.c_b], writes=[hb])

    def res_store(self, res, res_b, dst, dst_b, t, dp, dp_b, bias=None):
        P = self.P
        xt, xt_b, sl, k = self.load_x(res, res_b, t)
        P.tt("dve", xt[:], xt[:], dp, ALU.add, reads=[dp_b, xt_b], writes=[xt_b])
        if bias is not None:
            P.tt("pool", xt[:], xt[:], bias[0], ALU.add, reads=[bias[1], xt_b], writes=[xt_b])
        self.store_x(dst, dst_b, t, xt, xt_b, sl)

    def ffn_phase(self, l, res, res_b, dst, dst_b):
        P = self.P
        A = self.A
        A.reset()
        R = {}
        R["raw"] = [[self.bank(h * 2 + s) for s in range(2)] for h in range(2)]
        R["raw_b"] = [[self.pb[h * 2 + s] for s in range(2)] for h in range(2)]
        R["cps"] = [self.bank(4 + h) for h in range(2)]
        R["cps_b"] = [self.pb[4 + h] for h in range(2)]
        R["dps"] = self.psum[:, 6 * 512:8 * 512]
        R["dps_b"] = self.pb[6]
        R["wd"] = A.alloc([128, FC, D], BF16)
        R["wd_b"] = P.buf("wd")
        R["wd_sem"] = P.dsem(f"wd{l}")
        R["wu"] = [A.alloc([128, DC, 2, 128], BF16) for s in range(3)]
        R["wu_b"] = P.bufs(3, "wu")
        R["wu_sem"] = [P.dsem(f"wu{l}_{s}") for s in range(3)]
        R["dg"] = [A.alloc([128, 2, 3, 128], BF16) for s in range(2)]
        R["dg_b"] = P.bufs(2, "dg")
        R["ub"] = [[A.alloc([128, 514], BF16) for s in range(2)] for h in range(2)]
        R["ub_b"] = [P.bufs(2, f"ub{h}") for h in range(2)]
        R["halo"] = A.alloc([128, 2 * FC, 2], BF16)
        R["halo_b"] = P.bufs(2 * FC, "halo")
        R["sg"] = [A.alloc([128, 512], F32) for s in range(2)]
        R["sg_b"] = P.bufs(2, "sg")
        R["actT"] = A.alloc([128, FC, 1024], BF16)
        R["actT_b"] = P.bufs(2, "actT")
        wdn = self.ffn_w_down[l]
        self.wload(R["wd"], wdn, R["wd_sem"], R["wd_b"], nsplit=2)
        P.memset("pool", R["halo"], 0.0, writes=R["halo_b"])
        SB = 1024
        units = []
        for sbk in range(S // SB):
            for i in range(FC):
                for tb in range(SB // 512):
                    units.append((sbk, i, tb))
        nU = len(units)

        def stage_load(sbk, i):
            j = (sbk * FC + i)
            sl = j % 3
            w, w_b, w_sem = R["wu"][sl], R["wu_b"][sl], R["wu_sem"][sl]
            P.dma("pool", w_sem, w.rearrange("p c h e -> p (c h e)"), self.ffn_w_up[l, i], writes=[w_b])
            dg, dg_b = R["dg"][j % 2], R["dg_b"][j % 2]
            for h in range(2):
                ch = h * FC + i
                for tap in range(3):
                    P.ts("dve", dg[:, h, tap, :], self.ident[:], self.cw[:, l, ch, tap:tap + 1], None, ALU.mult,
                         reads=[self.c_b], writes=[dg_b])

        def stage_A(u):
            sbk, i, tb = units[u]
            j = sbk * FC + i
            w, w_b = R["wu"][j % 3], R["wu_b"][j % 3]
            t0 = sbk * SB + tb * 512
            for h in range(2):
                ps, ps_b = R["raw"][h][u % 2], R["raw_b"][h][u % 2]
                for c in range(DC):
                    P.mm(ps, w[:, c, h, :], self.hnT[:, c, t0:t0 + 512], c == 0, c == DC - 1,
                         reads=[w_b, self.hnT_b[t0 // 512]], writes=[ps_b])

        def stage_B(u):
            sbk, i, tb = units[u]
            j = sbk * FC + i
            dg, dg_b = R["dg"][j % 2], R["dg_b"][j % 2]
            for h in range(2):
                ch = h * FC + i
                ps, ps_b = R["raw"][h][u % 2], R["raw_b"][h][u % 2]
                ub, ub_b = R["ub"][h][u % 2], R["ub_b"][h][u % 2]
                P.copy("dve", ub[:, 0:2], R["halo"][:, ch, :], reads=[R["halo_b"][ch]], writes=[ub_b])
                P.copy("act", ub[:, 2:514], ps, reads=[ps_b], writes=[ub_b])
                P.copy("dve", R["halo"][:, ch, :], ub[:, 512:514], reads=[ub_b], writes=[R["halo_b"][ch]])
                cp, cp_b = R["cps"][h], R["cps_b"][h]
                for tap in range(3):
                    P.mm(cp, dg[:, h, tap, :], ub[:, tap:tap + 512], tap == 0, tap == 2,
                         reads=[dg_b, ub_b], writes=[cp_b])

        def stage_C(u):
            sbk, i, tb = units[u]
            sg, sg_b = R["sg"][u % 2], R["sg_b"][u % 2]
            P.act(sg, R["cps"][0], AF.Silu, reads=[R["cps_b"][0], self.c_b], writes=[sg_b],
                  bias=self.cb[:, l, i:i + 1])
            P.stt("dve", R["actT"][:, i, tb * 512:(tb + 1) * 512], R["cps"][1], self.cb[:, l, FC + i:FC + i + 1],
                  sg, ALU.add, ALU.mult, reads=[R["cps_b"][1], sg_b, self.c_b], writes=[R["actT_b"][tb]])

        def down(sbk):
            for tt in range(SB // 128):
                t = sbk * (SB // 128) + tt
                dp, dp_b = R["dps"], R["dps_b"]
                for hh in range(2):
                    for c in range(FC):
                        P.mm(dp[:, hh * 512:(hh + 1) * 512], R["actT"][:, c, tt * 128:(tt + 1) * 128],
                             R["wd"][:, c, hh * 512:(hh + 1) * 512], c == 0, c == FC - 1,
                             reads=[R["actT_b"][tt // 4], R["wd_b"]], writes=[dp_b])
                self.res_store(res, res_b, dst, dst_b, t, dp, dp_b)

        upb = SB // 512 * FC
        for u in range(nU + 1):
            if u < nU:
                sbk, i, tb = units[u]
                if tb == 0:
                    stage_load(sbk, i)
                stage_A(u)
            if u >= 1:
                stage_B(u - 1)
                stage_C(u - 1)
                if (u % upb) == 0:
                    down(u // upb - 1)

    def conformer(self, res, res_b, dst, dst_b):
        P = self.P
        A = self.A
        A.reset()
        NP = 16 + 8 * 31 + 24
        cp = A.alloc([128, NP], F32)
        cp_b = P.buf("confp")
        sem = P.dsem("conf")
        P.dma("sp", sem, cp, self.conf_p, writes=[cp_b])
        b1 = cp[:, 0:16]
        dww = cp[:, 16:16 + 248].rearrange("p (c j) -> p c j", c=8)
        dwb = cp[:, 264:272]
        lnw = cp[:, 272:280]
        lnb = cp[:, 280:288]
        gT = A.alloc([128, DC, 30 + S], BF16)
        gT_b = P.bufs(S // 512, "gT")
        gz_b = P.buf("gTpad")
        P.memset("pool", gT[:, :, 0:30], 0.0, writes=[gz_b])
        mark = A.off
        w1 = A.alloc([128, DC, 2 * D], BF16)
        w1_b = P.buf("w1")
        self.wload(w1, self.conf_w1.rearrange("(c p) e -> p c e", p=128), sem, w1_b, nsplit=DC)
        sg = [A.alloc([128, 512], F32) for i in range(2)]
        sg_b = P.bufs(2, "csg")
        k = 0
        for c in range(DC):
            for tb in range(8):
                psa, psa_b = self.bank(2 * (k % 2)), self.pb[2 * (k % 2)]
                psb, psb_b = self.bank(2 * (k % 2) + 1), self.pb[2 * (k % 2) + 1]
                for kc in range(DC):
                    P.mm(psa, w1[:, kc, c * 128:(c + 1) * 128], self.hnT[:, kc, tb * 512:(tb + 1) * 512],
                         kc == 0, kc == DC - 1, reads=[w1_b, self.hnT_b[tb]], writes=[psa_b])
                for kc in range(DC):
                    P.mm(psb, w1[:, kc, D + c * 128:D + (c + 1) * 128], self.hnT[:, kc, tb * 512:(tb + 1) * 512],
                         kc == 0, kc == DC - 1, reads=[w1_b, self.hnT_b[tb]], writes=[psb_b])
                P.act(sg[k % 2], psb, AF.Sigmoid, reads=[psb_b, cp_b], writes=[sg_b[k % 2]], bias=b1[:, 8 + c:9 + c])
                P.stt("dve", gT[:, c, 30 + tb * 512:30 + (tb + 1) * 512], psa, b1[:, c:c + 1], sg[k % 2],
                      ALU.add, ALU.mult, reads=[psa_b, sg_b[k % 2], cp_b], writes=[gT_b[tb]])
                k += 1
        A.reset(mark)
        w2 = A.alloc([128, DC, D], BF16)
        w2_b = P.buf("w2")
        self.wload(w2, self.conf_w2.rearrange("(c p) e -> p c e", p=128), sem, w2_b, nsplit=DC)
        b2 = A.alloc([128, D], F32)
        b2_b = P.buf("b2")
        P.dma("sp", sem, b2, self.conf_b2, writes=[b2_b])
        mark2 = A.off
        dg = [A.alloc([128, 31, 128], BF16) for i in range(2)]
        dg_b = P.bufs(2, "cdg")
        k = 0
        for c in range(DC):
            for j in range(31):
                P.ts("pool", dg[c % 2][:, j, :], self.ident[:], dww[:, c, j:j + 1], None, ALU.mult,
                     reads=[self.c_b, cp_b], writes=[dg_b[c % 2]])
            for tb in range(8):
                ps, ps_b = self.bank(k % 2), self.pb[k % 2]
                rb = [gz_b, gT_b[tb]] + ([gT_b[tb - 1]] if tb > 0 else [])
                for j in range(31):
                    P.mm(ps, dg[c % 2][:, j, :], gT[:, c, tb * 512 + j:tb * 512 + j + 512], j == 0, j == 30,
                         reads=[dg_b[c % 2]] + rb, writes=[ps_b])
                P.act(self.hnT[:, c, tb * 512:(tb + 1) * 512], ps, AF.Identity, reads=[ps_b, cp_b],
                      writes=[self.hnT_b[tb]], bias=dwb[:, c:c + 1])
                k += 1
        A.reset(mark2)
        sqv = A.alloc([128, DC, 512], BF16)
        sqv_b = P.buf("sqv")
        st = [A.alloc([128, 512], F32) for i in range(3)]
        st_b = P.buf("st")
        tmp = [A.alloc([128, 512], F32) for i in range(2)]
        tmp_b = P.bufs(2, "ctmp")
        zT = A.alloc([128, DC, 512], BF16)
        zT_b = P.buf("zT")
        vT = self.hnT
        for tb in range(8):
            blk = slice(tb * 512, (tb + 1) * 512)
            P.act(sqv, vT[:, :, blk], AF.Square, reads=[self.hnT_b[tb]], writes=[sqv_b])
            pS, pS_b = self.bank(2), self.pb[2]
            pQ, pQ_b = self.bank(3), self.pb[3]
            for c in range(DC):
                P.mm(pS, self.ones[:], vT[:, c, blk], c == 0, c == DC - 1, reads=[self.c_b, self.hnT_b[tb]], writes=[pS_b])
            for c in range(DC):
                P.mm(pQ, self.ones[:], sqv[:, c, :], c == 0, c == DC - 1, reads=[self.c_b, sqv_b], writes=[pQ_b])
            mean, var, rstd = st
            P.ts("dve", mean, pS, 1.0 / D, None, ALU.mult, reads=[pS_b], writes=[st_b])
            P.stt("dve", var, mean, -1.0, mean, ALU.mult, ALU.mult, reads=[st_b], writes=[st_b])
            P.stt("dve", var, pQ, 1.0 / D, var, ALU.mult, ALU.add, reads=[pQ_b, st_b], writes=[st_b])
            P.ts("dve", var, var, 1e-5, None, ALU.add, reads=[st_b], writes=[st_b])
            P.act(rstd, var, AF.Sqrt, reads=[st_b], writes=[st_b])
            P.op("dve", lambda e, rstd=rstd: e.reciprocal(rstd, rstd), reads=[st_b], writes=[st_b])
            for c in range(DC):
                tm, tm_b = tmp[c % 2], tmp_b[c % 2]
                P.tt("pool", tm, vT[:, c, blk], mean, ALU.subtract, reads=[self.hnT_b[tb], st_b], writes=[tm_b])
                P.tt("dve", tm, tm, rstd, ALU.mult, reads=[st_b, tm_b], writes=[tm_b])
                P.act(zT[:, c, :], tm, AF.Silu, reads=[tm_b, cp_b], writes=[zT_b],
                      scale=lnw[:, c:c + 1], bias=lnb[:, c:c + 1])
            for tq in range(4):
                t = tb * 4 + tq
                dp, dp_b = self.psum[:, 6 * 512:8 * 512], self.pb[6]
                for hh in range(2):
                    for c in range(DC):
                        P.mm(dp[:, hh * 512:(hh + 1) * 512], zT[:, c, tq * 128:(tq + 1) * 128],
                             w2[:, c, hh * 512:(hh + 1) * 512], c == 0, c == DC - 1,
                             reads=[zT_b, w2_b], writes=[dp_b])
                self.res_store(res, res_b, dst, dst_b, t, dp, dp_b, bias=(b2, b2_b))


    def tm_proj_phase(self, KC, W_dram, res, res_b, dst, dst_b, tag):
        P, A = self.P, self.A
        A.reset()
        sem = P.dsem("tmp" + tag)
        W = A.alloc([128, KC, D], BF16)
        W_b = P.buf("tmW")
        self.wload(W, W_dram.rearrange("(c p) e -> p c e", p=128), sem, W_b, nsplit=KC)
        yt = [A.alloc([128, KC * 128], BF16) for i in range(2)]
        yt_b = P.bufs(2, "tmy")
        yt_sem = [P.dsem(f"tmy{tag}{i}") for i in range(2)]
        yT = [A.alloc([128, KC, 128], BF16) for i in range(2)]
        yT_b = P.bufs(2, "tmyT")
        for t in range(NT):
            sl = t % 2
            P.dma("sp", yt_sem[sl], yt[sl], self.ob[t * 128:(t + 1) * 128, 0:KC * 128], reads=[self.ob_b[t]], writes=[yt_b[sl]])
            pt = self.psum[:, sl * 1024:(sl + 1) * 1024].bitcast(BF16)[:, 0:KC * 128].rearrange("p (c t) -> p c t", c=KC)
            pt_b = self.pb[2 * sl]
            for c in range(KC):
                P.tr(pt[:, c, :], yt[sl][:, c * 128:(c + 1) * 128], self.ident[:], reads=[yt_b[sl], self.c_b], writes=[pt_b])
            P.copy("act", yT[sl], pt, reads=[pt_b], writes=[yT_b[sl]])
            dp, dp_b = self.psum[:, (4 + 2 * sl) * 512:(6 + 2 * sl) * 512], self.pb[4 + 2 * sl]
            for hh in range(2):
                for c in range(KC):
                    P.mm(dp[:, hh * 512:(hh + 1) * 512], yT[sl][:, c, :], W[:, c, hh * 512:(hh + 1) * 512],
                         c == 0, c == KC - 1, reads=[yT_b[sl], W_b], writes=[dp_b])
            self.res_store(res, res_b, dst, dst_b, t, dp, dp_b)

    def moba(self, res, res_b, dst, dst_b):
        P, A = self.P, self.A
        A.reset()
        G = 4
        BIG = 30000.0
        sem = P.dsem("moba")
        mc = A.alloc([128, 3, 16, 16], F32)
        tri = A.alloc([128, 128], BF16)
        k_b = P.buf("mobac")
        P.dma("sp", sem, mc, self.mconst, writes=[k_b])
        P.dma("sp", sem, tri, self.tri_d, writes=[k_b])
        QA = A.alloc([128, G, S], BF16)
        KA = A.alloc([128, G, S], BF16)
        QA_b = [P.bufs(8, f"QA{h}") for h in range(G)]
        KA_b = P.bufs(G, "KA")
        ind_b = P.buf("ind")
        V = A.alloc([128, NT, G, 65], BF16)
        V_b = P.buf("V")
        one_b = P.buf("Vone")
        P.memset("pool", V[:, :, :, 64:65], 1.0, writes=[one_b])
        for hh in range(G):
            P.dma("sp", sem, KA[64:80, hh, :], self.blkind, writes=[ind_b])
        w3 = A.alloc([128, DC, 3, G * 64], BF16)
        w3_b = P.buf("w3")
        w3_sem = P.dsem("mobaw")
        km = A.alloc([128, G, 16], F32)
        kmb = A.alloc([128, G, 16], BF16)
        km_b = P.buf("km")
        g2 = [A.alloc([128, G, 16], F32) for i in range(2)]
        sel = [A.alloc([128, G, 16], F32) for i in range(2)]
        mx = [A.alloc([128, G, 8], F32) for i in range(2)]
        g2_b, sel_b = P.bufs(2, "g2"), P.bufs(2, "sel")
        mx_b = [P.bufs(G, "mx") for i in range(2)]
        mbf = [A.alloc([128, G, 80], BF16) for i in range(2)]
        mbf_b = P.bufs(2, "mbf")
        for i in range(2):
            P.memset("pool", mbf[i], 0.0, writes=[mbf_b[i]])
        mb2 = [A.alloc([128, G, 128], BF16) for i in range(2)]
        mb2_b = P.bufs(2, "mb2")
        PT = [A.alloc([128, 512], BF16) for i in range(3)]
        PT_b = P.bufs(3, "PT")
        osb = [A.alloc([128, 4, G * 64], BF16) for i in range(2)]
        osb_b = P.bufs(2, "osb")
        osb_sem = [P.dsem(f"osb{i}") for i in range(2)]
        rec = A.alloc([128, 4, 1], F32)
        rec_b = P.buf("rec")
        wqkv = self.moba_wqkv.rearrange("(c p) e -> p c e", p=128)
        nps = 0
        nS = 0
        nO = 0
        nosb = 0
        for g in range(D // 64 // G):
            for i in range(3):
                P.dma("pool", w3_sem, w3[:, :, i, :], wqkv[:, :, i * D + g * G * 64:i * D + (g + 1) * G * 64], writes=[w3_b])
            for hh in range(G):
                for tb in range(8):
                    blk = slice(tb * 512, (tb + 1) * 512)
                    for i in range(2):
                        ps, ps_b = self.bank(nps % 2)[0:64, :], self.pb[nps % 2]
                        nps += 1
                        for kc in range(DC):
                            P.mm(ps, w3[:, kc, i, hh * 64:(hh + 1) * 64], self.hnT[:, kc, blk], kc == 0, kc == DC - 1,
                                 reads=[w3_b, self.hnT_b[tb]], writes=[ps_b])
                        if i == 0:
                            P.act(QA[0:64, hh, blk], ps, AF.Copy, reads=[ps_b], writes=[QA_b[hh][tb]], scale=0.125)
                        else:
                            P.copy("dve", KA[0:64, hh, blk], ps, reads=[ps_b], writes=[KA_b[hh]])
                P.op("dve", lambda e, hh=hh: e.tensor_reduce(km[0:64, hh, :], KA[0:64, hh, :].rearrange("p (n k) -> p n k", k=256),
                                                             AX.X, ALU.add), reads=[KA_b[hh]], writes=[km_b])
            P.ts("dve", kmb[0:64], km[0:64], 1.0 / 256, None, ALU.mult, reads=[km_b], writes=[km_b])
            for t in range(NT):
                ps, ps_b = self.bank(nps % 2)[:, 0:G * 64], self.pb[nps % 2]
                nps += 1
                for kc in range(DC):
                    P.mm(ps, self.hnT[:, kc, t * 128:(t + 1) * 128], w3[:, kc, 2, :], kc == 0, kc == DC - 1,
                         reads=[w3_b, self.hnT_b[t // 4]], writes=[ps_b])
                P.copy("act", V[:, t, :, 0:64], ps.rearrange("p (h e) -> p h e", h=G), reads=[ps_b], writes=[V_b])
            def g_stage1(t):
                sl = t % 2
                tile = slice(t * 128, (t + 1) * 128)
                gps = self.bank(2 + sl)[:, 0:G * 16].rearrange("p (h n) -> p h n", h=G)
                for hh in range(G):
                    P.mm(gps[:, hh, :], QA[0:64, hh, tile], kmb[0:64, hh, :], True, True,
                         reads=[QA_b[hh][t // 4], km_b], writes=[self.pb[2 + sl]])

            def g_stage2(t):
                sl = t % 2
                qb_ = t // 2
                gps = self.bank(2 + sl)[:, 0:G * 16].rearrange("p (h n) -> p h n", h=G)
                g2_, sel_, mx_, mbf_ = g2[sl], sel[sl], mx[sl], mbf[sl]
                P.tt("dve", g2_, gps, mc[:, 0, qb_, :].unsqueeze(1).to_broadcast([128, G, 16]), ALU.add,
                     reads=[self.pb[2 + sl], k_b], writes=[g2_b[sl]])
                for hh in range(G):
                    P.op("dve", lambda e, hh=hh: e.max(mx_[:, hh, :], g2_[:, hh, :]), reads=[g2_b[sl]], writes=[mx_b[sl][hh]])
                P.tt("dve", sel_, g2_, mx_[:, :, 2:3].to_broadcast([128, G, 16]), ALU.is_ge, reads=[g2_b[sl]] + mx_b[sl], writes=[sel_b[sl]])
                P.tt("dve", sel_, sel_, mc[:, 1, qb_, :].unsqueeze(1).to_broadcast([128, G, 16]), ALU.mult,
                     reads=[sel_b[sl], k_b], writes=[sel_b[sl]])
                P.tt("dve", sel_, sel_, mc[:, 2, qb_, :].unsqueeze(1).to_broadcast([128, G, 16]), ALU.add,
                     reads=[sel_b[sl], k_b], writes=[sel_b[sl]])
                P.ts("dve", mbf_[:, :, 64:80], sel_, BIG, -BIG, ALU.mult, ALU.add, reads=[sel_b[sl]], writes=[mbf_b[sl]])

            def g_stage3(t):
                sl = t % 2
                tps = self.bank(6 + sl)[:, 0:256].bitcast(BF16).rearrange("p (h q) -> p h q", h=G)
                for hh in range(G):
                    P.tr(tps[0:80, hh, :], mbf[sl][:, hh, :], self.ident[:], reads=[mbf_b[sl], self.c_b], writes=[self.pb[6 + sl]])

            def g_stage4(t):
                sl = t % 2
                tile = slice(t * 128, (t + 1) * 128)
                tps = self.bank(6 + sl)[:, 0:256].bitcast(BF16).rearrange("p (h q) -> p h q", h=G)
                for hh in range(G):
                    P.copy("dve", mb2[sl][64:80, hh, :], tps[64:80, hh, :], reads=[self.pb[6 + sl]], writes=[mb2_b[sl]])
                for hh in range(G):
                    P.copy("pool", QA[64:80, hh, tile], mb2[sl][64:80, hh, :], reads=[mb2_b[sl]], writes=[QA_b[hh][t // 4]])

            for t in range(NT + 2):
                if t < NT:
                    g_stage1(t)
                if 1 <= t <= NT:
                    g_stage2(t - 1)
                    g_stage3(t - 1)
                if t >= 2:
                    g_stage4(t - 2)
            for qb in range(8):
                ob_, ob_b, ob_sem = osb[nosb % 2], osb_b[nosb % 2], osb_sem[nosb % 2]
                nosb += 1
                for hh in range(G):
                    O = self.bank(5 + nO % 2)[:, 0:260].rearrange("p (q e) -> p q e", q=4)
                    O_b = self.pb[5 + nO % 2]
                    nO += 1
                    nkt = 4 * qb + 4
                    slots = {}
                    import os
                    LAG = 0 if os.environ.get("MOBA_ATT") == "old" else 1
                    for kt in range(nkt + LAG):
                        if kt < nkt:
                            sp, sp_b = self.bank(2 + nS % 3), self.pb[2 + nS % 3]
                            pt, pt_b = PT[nS % 3], PT_b[nS % 3]
                            slots[kt] = (pt, pt_b)
                            nS += 1
                            P.mm(sp, KA[0:80, hh, kt * 128:(kt + 1) * 128], QA[0:80, hh, qb * 512:(qb + 1) * 512], True, True,
                                 reads=[KA_b[hh], ind_b, QA_b[hh][qb]], writes=[sp_b])
                            P.act(pt, sp, AF.Exp, reads=[sp_b], writes=[pt_b])
                            j = kt - 4 * qb
                            if j >= 0:
                                P.tt("pool", pt[:, j * 128:(j + 1) * 128], pt[:, j * 128:(j + 1) * 128], tri, ALU.mult,
                                     reads=[k_b, pt_b], writes=[pt_b])
                        if kt >= LAG:
                            k2 = kt - LAG
                            pt, pt_b = slots.pop(k2)
                            for ql in range(4):
                                qt = 4 * qb + ql
                                if k2 <= qt:
                                    P.mm(O[:, ql, :], pt[:, ql * 128:(ql + 1) * 128], V[:, k2, hh, :], k2 == 0 and ql == 0, k2 == qt,
                                         reads=[pt_b, V_b, one_b], writes=[O_b])
                    P.op("dve", lambda e, O=O: e.reciprocal(rec, O[:, :, 64:65]), reads=[O_b], writes=[rec_b])
                    P.tt("dve", ob_[:, :, hh * 64:(hh + 1) * 64], O[:, :, 0:64], rec.to_broadcast([128, 4, 64]), ALU.mult,
                         reads=[O_b, rec_b], writes=[ob_b])
                dstv = self.ob[qb * 512:(qb + 1) * 512, g * G * 64:(g + 1) * G * 64].rearrange("(q p) c -> p q c", p=128)
                P.dma("sp", ob_sem, dstv, ob_, reads=[ob_b], writes=[self.ob_b[4 * qb + i] for i in range(4)])
        self.tm_proj_phase(DC, self.moba_wo, res, res_b, dst, dst_b, "moba")

    def ssd(self, res, res_b, dst, dst_b):
        P, A = self.P, self.A
        A.reset()
        sem = P.dsem("ssd")
        pp = A.alloc([128, 120], F32)
        tc_ = A.alloc([128, 96], F32)
        tri = A.alloc([128, 128], BF16)
        nb4 = A.alloc([128, 4, 128], BF16)
        k_b = P.buf("ssdc")
        P.dma("sp", sem, pp, self.ssd_p, writes=[k_b])
        P.dma("sp", sem, tc_, self.ssd_t, writes=[k_b])
        P.dma("sp", sem, tri, self.tri_d, writes=[k_b])
        P.dma("sp", sem, nb4, self.nb_d, writes=[k_b])
        negones = A.alloc([128, 128], BF16)
        P.memset("pool", negones, -1.0, writes=[k_b])
        cwv = pp[:, 0:96].rearrange("p (c j) -> p c j", j=4)
        cbv = pp[:, 96:120]
        dtk = A.alloc([128, NT, 32], F32)
        atk = A.alloc([128, NT, 32], BF16)
        dt_b = P.buf("dtk")
        aneg = A.alloc([128, 32], F32)
        P.act(aneg, tc_[:, 32:64], AF.Exp, reads=[k_b], writes=[k_b])
        P.ts("dve", aneg, aneg, -1.0, None, ALU.mult, reads=[k_b], writes=[k_b])
        mark = A.off
        win = self.ssd_win.rearrange("(c p) e -> p c e", p=128)
        wch = [A.alloc([128, DC, 128], BF16) for i in range(3)]
        wch_b = P.bufs(3, "swch")
        wch_sem = [P.dsem(f"swch{i}") for i in range(3)]
        dg = [A.alloc([128, 4, 128], BF16) for i in range(2)]
        dg_b = P.bufs(2, "sdg")
        ub = [A.alloc([128, 515], BF16) for i in range(2)]
        ub_b = P.bufs(2, "sub")
        xc = [A.alloc([128, 512], BF16) for i in range(2)]
        xc_b = P.bufs(2, "sxc")
        xc_sem = [P.dsem(f"sxc{i}") for i in range(2)]
        stg = [A.alloc([128, 4, 128], BF16) for i in range(2)]
        stg_b = P.bufs(2, "sstg")
        stg_sem = [P.dsem(f"sstg{i}") for i in range(2)]
        u = 0
        for cc in range(24):
            w, w_b = wch[cc % 3], wch_b[cc % 3]
            P.dma("pool", wch_sem[cc % 3], w.rearrange("p c e -> p (c e)"), self.ssd_wx[cc], writes=[w_b])
            for j in range(4):
                P.ts("dve", dg[cc % 2][:, j, :], self.ident[:], cwv[:, cc, j:j + 1], None, ALU.mult,
                     reads=[self.c_b, k_b], writes=[dg_b[cc % 2]])
            for tb in range(8):
                blk = slice(tb * 512, (tb + 1) * 512)
                ps, ps_b = self.bank(u % 2), self.pb[u % 2]
                for kc in range(DC):
                    P.mm(ps, w[:, kc, :], self.hnT[:, kc, blk], kc == 0, kc == DC - 1, reads=[w_b, self.hnT_b[tb]], writes=[ps_b])
                b_, b_b = ub[u % 2], ub_b[u % 2]
                if tb == 0:
                    P.memset("dve", b_[:, 0:3], 0.0, writes=[b_b])
                else:
                    P.copy("dve", b_[:, 0:3], ub[(u - 1) % 2][:, 512:515], reads=[ub_b[(u - 1) % 2]], writes=[b_b])
                P.copy("act", b_[:, 3:515], ps, reads=[ps_b], writes=[b_b])
                cp, cp_b = self.bank(2 + u % 2), self.pb[2 + u % 2]
                for j in range(4):
                    P.mm(cp, dg[cc % 2][:, j, :], b_[:, j:j + 512], j == 0, j == 3, reads=[dg_b[cc % 2], b_b], writes=[cp_b])
                x_, x_b, x_sem = xc[u % 2], xc_b[u % 2], xc_sem[u % 2]
                P.act(x_, cp, AF.Silu, reads=[cp_b, k_b], writes=[x_b], bias=cbv[:, cc:cc + 1])
                if cc >= 16:
                    P.dma("sp", x_sem, self.s_BCT[cc - 16, :, blk], x_, reads=[x_b], writes=[self.s_BCT_b[tb]])
                if cc < 20:
                    tp = self.bank(4 + u % 2).bitcast(BF16)[:, 0:512].rearrange("p (q c) -> p q c", q=4)
                    tp_b = self.pb[4 + u % 2]
                    for q in range(4):
                        P.tr(tp[:, q, :], x_[:, q * 128:(q + 1) * 128], self.ident[:], reads=[x_b, self.c_b], writes=[tp_b])
                    sg_, sg_b, sg_sem = stg[u % 2], stg_b[u % 2], stg_sem[u % 2]
                    P.copy("dve", sg_, tp, reads=[tp_b], writes=[sg_b])
                    dv = self.s_xB[tb * 512:(tb + 1) * 512, cc * 128:(cc + 1) * 128].rearrange("(q p) c -> p q c", p=128)
                    P.dma("sp", sg_sem, dv, sg_, reads=[sg_b], writes=[self.s_xB_b[tb]])
                u += 1
        A.reset(mark)
        wz = A.alloc([128, DC, 2 * D], BF16)
        wz_b = P.buf("wz")
        self.wload(wz, win[:, :, 0:2 * D], sem, wz_b, nsplit=DC)
        wdt = A.alloc([128, DC, 32], BF16)
        P.dma("pool", sem, wdt, win[:, :, 5120:5152], writes=[wz_b])
        zt = [A.alloc([128, 2 * D], BF16) for i in range(2)]
        zt_b = P.bufs(2, "szt")
        zt_sem = [P.dsem(f"szt{i}") for i in range(2)]
        for t in range(NT):
            tile = slice(t * 128, (t + 1) * 128)
            z_, z_b = zt[t % 2], zt_b[t % 2]
            for q in range(4):
                ps, ps_b = self.bank(q % 2), self.pb[q % 2]
                for kc in range(DC):
                    P.mm(ps, self.hnT[:, kc, tile], wz[:, kc, q * 512:(q + 1) * 512], kc == 0, kc == DC - 1,
                         reads=[wz_b, self.hnT_b[t // 4]], writes=[ps_b])
                P.act(z_[:, q * 512:(q + 1) * 512], ps, AF.Silu, reads=[ps_b], writes=[z_b])
            P.dma("sp", zt_sem[t % 2], self.s_z[tile, :], z_, reads=[z_b], writes=[self.s_z_b[t]])
            ps, ps_b = self.bank(2)[:, 0:32], self.pb[2]
            for kc in range(DC):
                P.mm(ps, self.hnT[:, kc, tile], wdt[:, kc, :], kc == 0, kc == DC - 1, reads=[wz_b, self.hnT_b[t // 4]], writes=[ps_b])
            P.tt("dve", dtk[:, t, :], ps, tc_[:, 0:32], ALU.add, reads=[ps_b, k_b], writes=[dt_b])
        P.act(dtk, dtk, AF.Exp, reads=[dt_b], writes=[dt_b])
        P.act(dtk, dtk, AF.Ln, reads=[dt_b], writes=[dt_b], bias=1.0)
        P.tt("dve", atk, dtk, aneg.unsqueeze(1).to_broadcast([128, NT, 32]), ALU.mult, reads=[dt_b, k_b], writes=[dt_b])
        A.reset(mark)
        nw = A.alloc([128, 2 * D], F32)
        P.dma("sp", sem, nw, self.ssd_nw, writes=[k_b])
        xB = [A.alloc([128, 2560], BF16) for i in range(2)]
        xB_b = P.bufs(2, "xB")
        xB_sem = [P.dsem(f"xB{i}") for i in range(2)]
        zz = [A.alloc([128, 2 * D], BF16) for i in range(2)]
        zz_b = P.bufs(2, "zz")
        zz_sem = [P.dsem(f"zz{i}") for i in range(2)]
        bct = [A.alloc([128, 8, 128], BF16) for i in range(2)]
        bct_b = P.bufs(2, "bct")
        bct_sem = [P.dsem(f"bct{i}") for i in range(2)]
        xd = A.alloc([128, 32, 64], BF16)
        xdw = A.alloc([128, 32, 64], BF16)
        xd_b = P.buf("xd")
        acum = A.alloc([128, 32], F32)
        expA = A.alloc([128, 32], F32)
        expLA = A.alloc([128, 32], F32)
        dL = A.alloc([128, 32], F32)
        sc_b = P.buf("ssc")
        R1 = A.alloc([128, 8, 128], BF16)
        R1_b = P.buf("R1")
        E = A.alloc([128, 8, 128], BF16)
        E_b = P.buf("E")
        MT = A.alloc([128, 8, 128], BF16)
        MT_b = P.buf("MT")
        GT = A.alloc([128, 128], BF16)
        GT_b = P.buf("GT")
        yt = A.alloc([128, 2 * D], F32)
        yt_b = P.buf("yt")
        tmp = A.alloc([128, 512], F32)
        tmp_b = P.buf("stmp")
        HT = A.alloc([128, 4, 512], F32)
        HTb = A.alloc([128, 4, 512], BF16)
        HT_b = P.bufs(4, "HT")
        P.memset("pool", HT, 0.0, writes=HT_b)
        P.memset("pool", HTb, 0.0, writes=HT_b)
        ssq = A.alloc([128, 8], F32)
        ssq_b = P.buf("ssq")
        yo = [A.alloc([128, 2 * D], BF16) for i in range(2)]
        yo_b = P.bufs(2, "yo")
        yo_sem = [P.dsem(f"yo{i}") for i in range(2)]
        for c in range(NT):
            sl = c % 2
            tile = slice(c * 128, (c + 1) * 128)
            P.dma("sp", xB_sem[sl], xB[sl], self.s_xB[tile, :], reads=[self.s_xB_b[c // 4]], writes=[xB_b[sl]])
            P.dma("sp", zz_sem[sl], zz[sl], self.s_z[tile, :], reads=[self.s_z_b[c]], writes=[zz_b[sl]])
            P.dma("sp", bct_sem[sl], bct[sl], self.s_BCT[:, :, tile].rearrange("g n t -> n g t"),
                  reads=[self.s_BCT_b[c // 4]], writes=[bct_b[sl]])
            xv = xB[sl][:, 0:2048].rearrange("p (h e) -> p h e", h=32)
            a_c = atk[:, c, :]
            pA, pA_b = self.bank(0)[:, 0:32], self.pb[0]
            pL = self.bank(0)[:, 32:64]
            P.mm(pA, tri, a_c, True, True, reads=[k_b, dt_b], writes=[pA_b])
            P.mm(pL, self.ones[:], a_c, False, True, reads=[self.c_b, dt_b], writes=[pA_b])
            P.copy("dve", acum, pA, reads=[pA_b], writes=[sc_b])
            P.act(expA, pA, AF.Exp, reads=[pA_b], writes=[sc_b])
            P.act(dL, pL, AF.Exp, reads=[pA_b], writes=[sc_b])
            P.tt("dve", expLA, pL, acum, ALU.subtract, reads=[pA_b, sc_b], writes=[sc_b])
            P.act(expLA, expLA, AF.Exp, reads=[sc_b], writes=[sc_b])
            P.tt("dve", xd, xv, dtk[:, c, :].unsqueeze(2).to_broadcast([128, 32, 64]), ALU.mult,
                 reads=[xB_b[sl], dt_b], writes=[xd_b])
            P.tt("pool", xdw, xd, expLA.unsqueeze(2).to_broadcast([128, 32, 64]), ALU.mult, reads=[xd_b, sc_b], writes=[xd_b])
            for g in range(4):
                BTg = bct[sl][:, g, :]
                CTg = bct[sl][:, 4 + g, :]
                Btok = xB[sl][:, 2048 + g * 128:2048 + (g + 1) * 128]
                pG, pG_b = self.bank(1)[:, 0:128], self.pb[1]
                P.mm(pG, BTg, CTg, True, True, reads=[bct_b[sl]], writes=[pG_b])
                P.copy("act", GT, pG, reads=[pG_b], writes=[GT_b])
                P.tt("dve", R1, tri.unsqueeze(1).to_broadcast([128, 8, 128]),
                     a_c[:, g * 8:(g + 1) * 8].unsqueeze(2).to_broadcast([128, 8, 128]), ALU.mult,
                     reads=[k_b, dt_b], writes=[R1_b])
                for hb in range(2):
                    pD, pD_b = self.bank(2 + hb).rearrange("p (h t) -> p h t", h=4), self.pb[2 + hb]
                    P.mm(pD, self.ones[:], R1[:, hb * 4:(hb + 1) * 4, :], True, False, reads=[self.c_b, R1_b], writes=[pD_b])
                    P.mm(pD, self.ident[:], nb4, False, False, reads=[self.c_b, k_b], writes=[pD_b])
                    for h4 in range(4):
                        P.mm(pD[:, h4, :], R1[:, hb * 4 + h4, :], negones, False, h4 == 3, reads=[R1_b, k_b], writes=[pD_b])
                    P.act(E[:, hb * 4:(hb + 1) * 4, :], pD, AF.Exp, reads=[pD_b], writes=[E_b])
                P.tt("dve", MT, E, GT.unsqueeze(1).to_broadcast([128, 8, 128]), ALU.mult, reads=[E_b, GT_b], writes=[MT_b])
                pY, pY_b = self.bank(4).rearrange("p (h e) -> p h e", h=8), self.pb[4]
                for h8 in range(8):
                    P.mm(pY[:, h8, :], MT[:, h8, :], xd[:, g * 8 + h8, :], h8 == 0, h8 == 7, reads=[MT_b, xd_b], writes=[pY_b])
                pO, pO_b = self.bank(5), self.pb[5]
                P.mm(pO, CTg, HTb[:, g, :], True, True, reads=[bct_b[sl], HT_b[g]], writes=[pO_b])
                pS, pS_b = self.bank(6), self.pb[6]
                P.mm(pS, Btok, xdw[:, g * 8:(g + 1) * 8, :], True, True, reads=[xB_b[sl], xd_b], writes=[pS_b])
                P.tt("dve", tmp.rearrange("p (h e) -> p h e", h=8), pO.rearrange("p (h e) -> p h e", h=8),
                     expA[:, g * 8:(g + 1) * 8].unsqueeze(2).to_broadcast([128, 8, 64]), ALU.mult,
                     reads=[pO_b, sc_b], writes=[tmp_b])
                P.tt("dve", yt[:, g * 512:(g + 1) * 512], tmp, self.bank(4), ALU.add, reads=[tmp_b, pY_b], writes=[yt_b])
                Hg = HT[:, g, :]
                P.tt("pool", Hg.rearrange("p (h e) -> p h e", h=8), Hg.rearrange("p (h e) -> p h e", h=8),
                     dL[:, g * 8:(g + 1) * 8].unsqueeze(2).to_broadcast([128, 8, 64]), ALU.mult,
                     reads=[sc_b, HT_b[g]], writes=[HT_b[g]])
                P.tt("dve", Hg, Hg, pS, ALU.add, reads=[pS_b, HT_b[g]], writes=[HT_b[g]])
                P.copy("act", HTb[:, g, :], Hg, reads=[HT_b[g]], writes=[HT_b[g]])
            xs_ = self.xt[0][:, :].bitcast(BF16)
            ytv = yt.rearrange("p (h e) -> p h e", h=32)
            P.tt("pool", xd, xv, tc_[:, 64:96].unsqueeze(2).to_broadcast([128, 32, 64]), ALU.mult,
                 reads=[xB_b[sl], k_b, pY_b, pS_b], writes=[xd_b])
            P.tt("dve", ytv, ytv, xd, ALU.add, reads=[xd_b, yt_b], writes=[yt_b])
            P.tt("dve", yt, yt, zz[sl], ALU.mult, reads=[zz_b[sl], yt_b], writes=[yt_b])
            o_, o_b = yo[sl], yo_b[sl]
            for g in range(4):
                P.act(o_[:, g * 512:(g + 1) * 512], yt[:, g * 512:(g + 1) * 512], AF.Square, reads=[yt_b],
                      writes=[o_b, ssq_b], accum_out=ssq[:, g:g + 1])
            P.ts("dve", ssq[:, 4:8], ssq[:, 0:4], 1.0 / 512, 1e-5, ALU.mult, ALU.add, reads=[ssq_b], writes=[ssq_b])
            P.act(ssq[:, 4:8], ssq[:, 4:8], AF.Sqrt, reads=[ssq_b], writes=[ssq_b])
            P.op("dve", lambda e: e.reciprocal(ssq[:, 4:8], ssq[:, 4:8]), reads=[ssq_b], writes=[ssq_b])
            P.tt("pool", yt, yt, nw, ALU.mult, reads=[yt_b, k_b], writes=[yt_b])
            P.tt("dve", o_.rearrange("p (g e) -> p g e", g=4), yt.rearrange("p (g e) -> p g e", g=4),
                 ssq[:, 4:8].unsqueeze(2).to_broadcast([128, 4, 512]), ALU.mult, reads=[yt_b, ssq_b], writes=[o_b])
            P.dma("sp", yo_sem[sl], self.ob[tile, :], o_, reads=[o_b], writes=[self.ob_b[c]])
        self.tm_proj_phase(16, self.ssd_wout, res, res_b, dst, dst_b, "ssd")

    def rwkv(self, res, res_b, dst, dst_b):
        P, A = self.P, self.A
        A.reset()
        sem = P.dsem("rw")
        k_b = P.buf("rwc")
        rp = A.alloc([128, 88], F32)
        P.dma("sp", sem, rp, self.rw_p, writes=[k_b])
        cc_ = A.alloc([128, 322], BF16)
        P.dma("sp", sem, cc_, self.rw_c, writes=[k_b])
        bones = cc_[:, 0:128]
        hsel = cc_[:, 128:130]
        maskG = cc_[:, 130:258]
        maskA = cc_[0:64, 258:322]
        mu = rp[:, 0:48].rearrange("p (i c) -> p i c", i=6)
        w0, a0, kkp, kap, rkp = (rp[:, 48 + 8 * i:56 + 8 * i] for i in range(5))
        nw0 = A.alloc([128, 8], F32)
        P.ts("dve", nw0, w0, -1.0, None, ALU.mult, reads=[k_b], writes=[k_b])
        mhalf = A.alloc([128, 1], F32)
        P.memset("pool", mhalf, -0.5, writes=[k_b])
        PLx = A.alloc([128, 8, 64], F32)
        PLx_b = P.buf("PLx")
        mark = A.off
        BT = 256
        NQ = BT // 128
        NCB = BT // 64
        W3 = [A.alloc([128, DC, D], BF16) for i in range(3)]
        w_b = P.buf("rww")
        for i in range(3):
            self.wload(W3[i], self.rw_rkv[i].rearrange("(c p) e -> p c e", p=128), sem, w_b, nsplit=DC)
        w1 = A.alloc([128, DC, 64], BF16)
        a1 = A.alloc([128, DC, 64], BF16)
        g1 = A.alloc([128, DC, 160], BF16)
        w2 = A.alloc([64, D], BF16)
        a2 = A.alloc([64, D], BF16)
        g2a = A.alloc([128, D], BF16)
        g2b = A.alloc([32, D], BF16)
        P.dma("pool", sem, w1, self.rw_w1.rearrange("(c p) e -> p c e", p=128), writes=[w_b])
        P.dma("pool", sem, a1, self.rw_a1.rearrange("(c p) e -> p c e", p=128), writes=[w_b])
        P.dma("pool", sem, g1, self.rw_g1.rearrange("(c p) e -> p c e", p=128), writes=[w_b])
        P.dma("pool", sem, w2, self.rw_w2, writes=[w_b])
        P.dma("pool", sem, a2, self.rw_a2, writes=[w_b])
        P.dma("pool", sem, g2a, self.rw_g2[0:128, :], writes=[w_b])
        P.dma("pool", sem, g2b, self.rw_g2[128:160, :], writes=[w_b])
        dT = A.alloc([128, DC, BT], BF16)
        dT_b = P.buf("dT")
        xm = [A.alloc([128, DC, BT], BF16) for i in range(2)]
        xm_b = P.bufs(2, "xm")
        hw = A.alloc([64, BT], BF16)
        ha = A.alloc([64, BT], BF16)
        hga = A.alloc([128, BT], BF16)
        hgb = A.alloc([32, BT], BF16)
        h_b = P.buf("rwh")
        vtok = A.alloc([128, NQ, D], BF16)
        vt_b = P.buf("vtok")
        vt_sem = P.dsem("vtok")
        gtok = A.alloc([128, NQ, D], BF16)
        gt_b = P.buf("gtok")
        gt_sem = P.dsem("gtok")
        F = [A.alloc([128, BT], F32) for i in range(10)]
        F_b = P.bufs(10, "rwF")
        sqb = A.alloc([128, BT], BF16)
        sqb_b = P.buf("sqb")
        ARt = A.alloc([128, NCB, 2, 64], BF16)
        BKt = A.alloc([128, NCB, 2, 64], BF16)
        AB_b = P.buf("ARt")
        AB_sem = P.dsem("ARt")
        bkh = A.alloc([128, 2, BT], BF16)
        bkh_b = P.buf("bkh")
        stg = A.alloc([128, 2, NQ, 128], BF16)
        stg_b = P.buf("rstg")
        stg_sem = P.dsem("rstg")
        rk = A.alloc([128, BT], BF16)
        rk_b = P.buf("rk")
        bon = A.alloc([128, NQ, 16], F32)
        bon_b = P.buf("bon")
        nxm = [0]

        def mk_xm(i, blk):
            sl = nxm[0] % 2
            nxm[0] += 1
            for c in range(DC):
                P.stt("dve", xm[sl][:, c, :], dT[:, c, :], mu[:, i, c:c + 1], self.hnT[:, c, blk],
                      ALU.mult, ALU.add, reads=[dT_b, k_b] + list(self.hnT_b), writes=[xm_b[sl]])
            return xm[sl], xm_b[sl]

        np_ = [0]

        def pbank(n=None):
            np_[0] += 1
            return self.bank(np_[0] % 4)[:, 0:(BT if n is None else n)], self.pb[np_[0] % 4]

        for tb in range(S // BT):
            t0 = tb * BT
            blk = slice(t0, t0 + BT)
            rb_ = self.r1_b[t0 // 512]
            if tb == 0:
                P.ts("dve", dT[:, :, 0:1], self.hnT[:, :, 0:1], -1.0, None, ALU.mult, reads=list(self.hnT_b), writes=[dT_b])
                P.tt("dve", dT[:, :, 1:BT], self.hnT[:, :, 0:BT - 1], self.hnT[:, :, 1:BT], ALU.subtract,
                     reads=list(self.hnT_b), writes=[dT_b])
            else:
                P.tt("dve", dT, self.hnT[:, :, t0 - 1:t0 + BT - 1], self.hnT[:, :, blk], ALU.subtract,
                     reads=list(self.hnT_b), writes=[dT_b])
            x_, x_b = mk_xm(3, blk)
            ps, ps_b = pbank()
            for kc in range(DC):
                P.mm(ps[0:64, :], w1[:, kc, :], x_[:, kc, :], kc == 0, kc == DC - 1, reads=[w_b, x_b], writes=[ps_b])
            P.act(hw, ps[0:64, :], AF.Tanh, reads=[ps_b], writes=[h_b])
            x_, x_b = mk_xm(4, blk)
            ps, ps_b = pbank()
            for kc in range(DC):
                P.mm(ps[0:64, :], a1[:, kc, :], x_[:, kc, :], kc == 0, kc == DC - 1, reads=[w_b, x_b], writes=[ps_b])
            P.copy("act", ha, ps[0:64, :], reads=[ps_b], writes=[h_b])
            x_, x_b = mk_xm(5, blk)
            ps, ps_b = pbank()
            for kc in range(DC):
                P.mm(ps, g1[:, kc, 0:128], x_[:, kc, :], kc == 0, kc == DC - 1, reads=[w_b, x_b], writes=[ps_b])
            P.act(hga, ps, AF.Sigmoid, reads=[ps_b], writes=[h_b])
            ps, ps_b = pbank()
            for kc in range(DC):
                P.mm(ps[0:32, :], g1[:, kc, 128:160], x_[:, kc, :], kc == 0, kc == DC - 1, reads=[w_b, x_b], writes=[ps_b])
            P.act(hgb, ps[0:32, :], AF.Sigmoid, reads=[ps_b], writes=[h_b])
            for q in range(NQ):
                for cb in range(2):
                    ps, ps_b = pbank(512)
                    P.mm(ps, hga[:, q * 128:(q + 1) * 128], g2a[:, cb * 512:(cb + 1) * 512], True, False, reads=[h_b, w_b], writes=[ps_b])
                    P.mm(ps, hgb[:, q * 128:(q + 1) * 128], g2b[:, cb * 512:(cb + 1) * 512], False, True, reads=[h_b, w_b], writes=[ps_b])
                    P.copy("act", gtok[:, q, cb * 512:(cb + 1) * 512], ps, reads=[ps_b], writes=[gt_b])
            P.dma("sp", gt_sem, self.r_g[blk, :].rearrange("(q p) c -> p q c", p=128), gtok, reads=[gt_b], writes=[rb_])
            x_, x_b = mk_xm(2, blk)
            for q in range(NQ):
                for cb in range(2):
                    ps, ps_b = pbank(512)
                    for kc in range(DC):
                        P.mm(ps, x_[:, kc, q * 128:(q + 1) * 128], W3[2][:, kc, cb * 512:(cb + 1) * 512], kc == 0, kc == DC - 1,
                             reads=[w_b, x_b], writes=[ps_b])
                    P.copy("act", vtok[:, q, cb * 512:(cb + 1) * 512], ps, reads=[ps_b], writes=[vt_b])
            P.dma("sp", vt_sem, self.r_v[blk, :].rearrange("(q p) c -> p q c", p=128), vtok, reads=[vt_b], writes=[rb_])
            xr, xr_b = mk_xm(0, blk)
            xk, xk_b = mk_xm(1, blk)
            pbon, pbon_b = self.bank(7)[:, 0:NQ * 16].rearrange("p (q h) -> p q h", q=NQ), self.pb[7]
            for e in range(DC):
                ec = slice(e * 128, (e + 1) * 128)
                r_s, k_s, lw, a_s, kk, t1, t2, lpA, lpB, t3 = F
                (r_sb, k_sb, lw_b, a_sb, kk_b, t1_b, t2_b, lpA_b, lpB_b, t3_b) = F_b
                ps, ps_b = pbank()
                for kc in range(DC):
                    P.mm(ps, W3[0][:, kc, ec], xr[:, kc, :], kc == 0, kc == DC - 1, reads=[w_b, xr_b], writes=[ps_b])
                P.copy("act", r_s, ps, reads=[ps_b], writes=[r_sb])
                ps, ps_b = pbank()
                for kc in range(DC):
                    P.mm(ps, W3[1][:, kc, ec], xk[:, kc, :], kc == 0, kc == DC - 1, reads=[w_b, xk_b], writes=[ps_b])
                P.copy("act", k_s, ps, reads=[ps_b], writes=[k_sb])
                ps, ps_b = pbank()
                P.mm(ps, w2[:, ec], hw, True, True, reads=[w_b, h_b], writes=[ps_b])
                P.act(t1, ps, AF.Exp, reads=[ps_b, k_b], writes=[t1_b], scale=-1.0, bias=nw0[:, e:e + 1])
                P.act(t1, t1, AF.Ln, reads=[t1_b], writes=[t1_b], bias=1.0)
                P.act(t1, t1, AF.Exp, reads=[t1_b, k_b], writes=[t1_b], scale=-1.0, bias=mhalf)
                P.ts("dve", lw, t1, -1.0, None, ALU.mult, reads=[t1_b], writes=[lw_b])
                ps, ps_b = pbank()
                P.mm(ps, a2[:, ec], ha, True, True, reads=[w_b, h_b], writes=[ps_b])
                P.act(a_s, ps, AF.Sigmoid, reads=[ps_b, k_b], writes=[a_sb], bias=a0[:, e:e + 1])
                P.ts("dve", kk, k_s, kkp[:, e:e + 1], None, ALU.mult, reads=[k_sb, k_b], writes=[kk_b])
                P.act(sqb, kk, AF.Square, reads=[kk_b], writes=[sqb_b])
                ps, ps_b = pbank()
                P.mm(ps, bones, sqb, True, True, reads=[k_b, sqb_b], writes=[ps_b])
                P.ts("dve", t2, ps, 1e-24, None, ALU.max, reads=[ps_b], writes=[t2_b])
                P.act(t2, t2, AF.Sqrt, reads=[t2_b], writes=[t2_b])
                P.op("dve", lambda e_, t2=t2: e_.reciprocal(t2, t2), reads=[t2_b], writes=[t2_b])
                P.tt("dve", kk, kk, t2, ALU.mult, reads=[t2_b, kk_b], writes=[kk_b])
                P.ts("pool", t1, a_s, -1.0, kap[:, e:e + 1], ALU.add, ALU.mult, reads=[a_sb, k_b], writes=[t1_b])
                P.stt("dve", k_s, t1, 1.0, k_s, ALU.add, ALU.mult, reads=[t1_b, k_sb], writes=[k_sb])
                P.tt("pool", a_s, kk, a_s, ALU.mult, reads=[kk_b, a_sb], writes=[a_sb])
                v3 = lambda ap: ap.rearrange("p (c t) -> p c t", t=64)
                src_, src_bb = lw, lw_b
                pp_ = [(lpA, lpA_b), (lpB, lpB_b)]
                for si, sft in enumerate((1, 2, 4, 8, 16, 32)):
                    dst_, dst_bb = pp_[si % 2]
                    P.copy("pool", v3(dst_)[:, :, 0:sft], v3(src_)[:, :, 0:sft], reads=[src_bb], writes=[dst_bb])
                    P.tt("dve", v3(dst_)[:, :, sft:64], v3(src_)[:, :, sft:64], v3(src_)[:, :, 0:64 - sft], ALU.add,
                         reads=[src_bb], writes=[dst_bb])
                    src_, src_bb = dst_, dst_bb
                lp, lp_b = src_, src_bb
                P.tt("dve", t2, lp, lw, ALU.subtract, reads=[lp_b, lw_b], writes=[t2_b])
                P.act(t2, t2, AF.Exp, reads=[t2_b], writes=[t2_b])
                P.stt("dve", ARt[:, :, 0, :], v3(kk), -1.0, v3(t2), ALU.mult, ALU.mult, reads=[kk_b, t2_b], writes=[AB_b])
                P.act(t2, lp, AF.Exp, reads=[lp_b], writes=[t2_b])
                P.tt("dve", ARt[:, :, 1, :], v3(r_s), v3(t2), ALU.mult, reads=[r_sb, t2_b], writes=[AB_b])
                P.act(t2, lp, AF.Exp, reads=[lp_b], writes=[t2_b], scale=-1.0)
                P.tt("dve", BKt[:, :, 0, :], v3(a_s), v3(t2), ALU.mult, reads=[a_sb, t2_b], writes=[AB_b])
                P.tt("pool", BKt[:, :, 1, :], v3(k_s), v3(t2), ALU.mult, reads=[k_sb, t2_b], writes=[AB_b])
                P.tt("dve", v3(t3), v3(lp)[:, :, 63:64].to_broadcast([128, NCB, 64]), v3(lp), ALU.subtract, reads=[lp_b], writes=[t3_b])
                P.act(t3, t3, AF.Exp, reads=[t3_b], writes=[t3_b])
                P.act(PLx[:, e, tb * NCB:(tb + 1) * NCB], v3(lp)[:, :, 63], AF.Exp, reads=[lp_b], writes=[PLx_b])
                P.tt("dve", bkh[:, 0, :], a_s, t3, ALU.mult, reads=[a_sb, t3_b], writes=[bkh_b])
                P.tt("pool", bkh[:, 1, :], k_s, t3, ALU.mult, reads=[k_sb, t3_b], writes=[bkh_b])
                tp = self.psum[:, 4 * 512:6 * 512].bitcast(BF16)[:, 0:2 * NQ * 128].rearrange("p (i q c) -> p i q c", i=2, q=NQ)
                tp_b = self.pb[4]
                for i in range(2):
                    for q in range(NQ):
                        P.tr(tp[:, i, q, :], bkh[:, i, q * 128:(q + 1) * 128], self.ident[:], reads=[bkh_b, self.c_b], writes=[tp_b])
                P.copy("act", stg, tp, reads=[tp_b], writes=[stg_b])
                P.dma("sp", stg_sem, self.r_bh[blk, ec].rearrange("(q p) c -> p q c", p=128), stg[:, 0], reads=[stg_b], writes=[rb_])
                P.dma("sp", stg_sem, self.r_kh[blk, ec].rearrange("(q p) c -> p q c", p=128), stg[:, 1], reads=[stg_b], writes=[rb_])
                P.dma("sp", AB_sem, self.r_AR[e, :, tb * NCB:(tb + 1) * NCB, :, :], ARt, reads=[AB_b], writes=[rb_])
                P.dma("sp", AB_sem, self.r_BK[e, :, tb * NCB:(tb + 1) * NCB, :, :], BKt, reads=[AB_b], writes=[rb_])
                P.stt("dve", rk, r_s, rkp[:, e:e + 1], k_s, ALU.mult, ALU.mult, reads=[r_sb, k_sb, k_b], writes=[rk_b])
                for q in range(NQ):
                    P.mm(pbon[:, q, 2 * e:2 * e + 2], rk[:, q * 128:(q + 1) * 128], hsel, e == 0 and q == 0, e == 7 and q == NQ - 1,
                         reads=[rk_b, k_b], writes=[pbon_b])
            P.copy("dve", bon, pbon, reads=[pbon_b], writes=[bon_b])
            for q in range(NQ):
                P.tt("dve", gtok[:, q, :].rearrange("p (h n) -> p h n", h=16), vtok[:, q, :].rearrange("p (h n) -> p h n", h=16),
                     bon[:, q, :].unsqueeze(2).to_broadcast([128, 16, 64]), ALU.mult, reads=[vt_b, bon_b], writes=[gt_b])
            P.dma("sp", gt_sem, self.r_bv[blk, :].rearrange("(q p) c -> p q c", p=128), gtok, reads=[gt_b], writes=[rb_])
        pl_sem = P.dsem("plx")
        plb = P.buf("plxd")
        P.dma("sp", pl_sem, self.r_pl, PLx, reads=[PLx_b], writes=[plb])
        A.reset(mark)
        PL2 = A.alloc([64, 16, 64], F32)
        PL2_b = P.buf("PL2")
        P.dma("sp", pl_sem, PL2, self.r_pl.rearrange("(a n) e c -> n e a c", a=2), reads=[plb], writes=[PL2_b])
        ARc = [A.alloc([64, 16, 2, 64], BF16) for i in range(2)]
        BKc = [A.alloc([64, 16, 2, 64], BF16) for i in range(2)]
        BH = [A.alloc([64, D], BF16) for i in range(2)]
        KH = [A.alloc([64, D], BF16) for i in range(2)]
        Vt = [A.alloc([64, D], BF16) for i in range(2)]
        Ut = [A.alloc([64, D], BF16) for i in range(2)]
        in_b = P.bufs(2, "r2in")
        uv_b = P.bufs(2, "r2uv")
        in_sem = [P.dsem(f"r2in{i}") for i in range(2)]
        Gb = [A.alloc([64, 8, 128], BF16) for i in range(2)]
        Gk = [A.alloc([64, 8, 128], BF16) for i in range(2)]
        Gm_b = P.bufs(2, "Gm")
        Ap = [A.alloc([64, 8, 64], BF16) for i in range(2)]
        Mp = [A.alloc([64, 8, 64], BF16) for i in range(2)]
        Ap_b, Mp_b = P.bufs(2, "Ap"), P.bufs(2, "Mp")
        TT = [[A.alloc([64, 8, 64], BF16) for i in range(2)] for j in range(2)]
        TT_b = [P.bufs(2, "TT") for j in range(2)]
        Xs = A.alloc([64, 8, 64], BF16)
        Xs_b = P.buf("Xs")
        H = A.alloc([64, 16, 64], F32)
        Hb = A.alloc([64, 16, 64], BF16)
        H_b = P.bufs(2, "H")
        P.memset("pool", H, 0.0, writes=H_b)
        P.memset("pool", Hb, 0.0, writes=H_b)
        ych = [A.alloc([64, D], F32) for i in range(2)]
        ych_b = P.bufs(2, "ych")
        ych_sem = [P.dsem(f"ych{i}") for i in range(2)]
        idb = self.ident[0:64, 0:64].unsqueeze(1).to_broadcast([64, 8, 64])
        mG = maskG[0:64, :].unsqueeze(1).to_broadcast([64, 8, 128])
        ARd = self.r_AR.rearrange("e (a n) c x t -> n (e a) c x t", a=2)
        BKd = self.r_BK.rearrange("e (a n) c x t -> n (e a) c x t", a=2)
        v8 = lambda bk: self.bank(bk)[0:64, :].rearrange("p (h t) -> p h t", h=8)
        TTfin = {}

        def gen_A(u):
            c, hf = u // 2, u % 2
            sl = c % 2
            up = u % 2
            rows = slice(c * 64, (c + 1) * 64)
            if hf == 0:
                rb_ = [self.r1_b[c // 8]]
                P.dma("sp", in_sem[sl], ARc[sl], ARd[:, :, c, :, :], reads=rb_, writes=[in_b[sl]])
                P.dma("sp", in_sem[sl], BKc[sl], BKd[:, :, c, :, :], reads=rb_, writes=[in_b[sl]])
                P.dma("sp", in_sem[sl], BH[sl], self.r_bh[rows, :], reads=rb_, writes=[in_b[sl]])
                P.dma("sp", in_sem[sl], KH[sl], self.r_kh[rows, :], reads=rb_, writes=[in_b[sl]])
                P.dma("sp", in_sem[sl], Vt[sl], self.r_v[rows, :], reads=rb_, writes=[in_b[sl]])
            hds = list(range(8 * hf, 8 * hf + 8))
            pGb = self.psum[0:64, 0:1024].rearrange("p (h t) -> p h t", h=8)
            pGk = self.psum[0:64, 1024:2048].rearrange("p (h t) -> p h t", h=8)
            pAm = v8(4)
            for hi, h in enumerate(hds):
                P.mm(pGb[:, hi, :], BKc[sl][:, h, 0, :], ARc[sl][:, h, :, :], hi % 4 == 0, hi % 4 == 3, reads=[in_b[sl]], writes=[self.pb[0]])
            for hi, h in enumerate(hds):
                P.mm(pGk[:, hi, :], BKc[sl][:, h, 1, :], ARc[sl][:, h, :, :], hi % 4 == 0, hi % 4 == 3, reads=[in_b[sl]], writes=[self.pb[2]])
            for hi, h in enumerate(hds):
                P.mm(pAm[:, hi, :], ARc[sl][:, h, 0, :], BKc[sl][:, h, 0, :], hi == 0, hi == 7, reads=[in_b[sl]], writes=[self.pb[4]])
            P.tt("dve", Gb[up], pGb, mG, ALU.mult, reads=[self.pb[0], k_b], writes=[Gm_b[up]])
            P.tt("dve", Gk[up], pGk, mG, ALU.mult, reads=[self.pb[2], k_b], writes=[Gm_b[up]])
            P.copy("pool", Mp[0], Gb[up][:, :, 0:64], reads=[Gm_b[up]], writes=[Mp_b[0]])
            P.tt("dve", Ap[0], pAm, maskA.unsqueeze(1).to_broadcast([64, 8, 64]), ALU.mult, reads=[self.pb[4], k_b], writes=[Ap_b[0]])
            P.tt("pool", TT[up][0], Mp[0], idb, ALU.add, reads=[Mp_b[0], self.c_b], writes=[TT_b[up][0]])
            yield
            cur = 0
            for rd in range(5):
                nxt = 1 - cur
                pA2, pM2, pT2 = v8(4), v8(5), v8(6)
                for hi in range(8):
                    P.mm(pA2[:, hi, :], Mp[cur][:, hi, :], Ap[cur][:, hi, :], hi == 0, hi == 7,
                         reads=[Mp_b[cur], Ap_b[cur]], writes=[self.pb[4]])
                if rd < 4:
                    for hi in range(8):
                        P.mm(pM2[:, hi, :], Ap[cur][:, hi, :], Mp[cur][:, hi, :], hi == 0, hi == 7,
                             reads=[Mp_b[cur], Ap_b[cur]], writes=[self.pb[5]])
                P.copy("act", Ap[nxt], pA2, reads=[self.pb[4]], writes=[Ap_b[nxt]])
                if rd < 4:
                    P.copy("dve", Mp[nxt], pM2, reads=[self.pb[5]], writes=[Mp_b[nxt]])
                yield
                for hi in range(8):
                    P.mm(pT2[:, hi, :], Ap[nxt][:, hi, :], TT[up][cur][:, hi, :], hi == 0, hi == 7,
                         reads=[Ap_b[nxt], TT_b[up][cur]], writes=[self.pb[6]])
                P.tt("dve", TT[up][nxt], TT[up][cur], pT2, ALU.add, reads=[self.pb[6], TT_b[up][cur]], writes=[TT_b[up][nxt]])
                cur = nxt
                yield
            TTfin[u] = (TT[up][cur], TT_b[up][cur])

        def gen_B(u):
            c, hf = u // 2, u % 2
            sl = c % 2
            up = u % 2
            rows = slice(c * 64, (c + 1) * 64)
            hds = list(range(8 * hf, 8 * hf + 8))
            TTf, TTf_b = TTfin[u]
            pX = v8(7)
            b7 = self.pb[7]
            for hi, h in enumerate(hds):
                hc = slice(h * 64, h * 64 + 64)
                P.mm(pX[:, hi, :], ARc[sl][:, h, 0, :], Hb[:, h, :], hi == 0, False, reads=[in_b[sl], H_b[hf]], writes=[b7])
                P.mm(pX[:, hi, :], Gk[up][:, hi, 0:64], Vt[sl][:, hc], False, hi == 7, reads=[Gm_b[up], in_b[sl]], writes=[b7])
            P.copy("act", Xs, pX, reads=[b7], writes=[Xs_b])
            yield
            for hi, h in enumerate(hds):
                P.mm(pX[:, hi, :], TTf[:, hi, :], Xs[:, hi, :], hi == 0, hi == 7, reads=[TTf_b, Xs_b], writes=[b7])
            h0 = 8 * hf * 64
            P.copy("act", Ut[sl][:, h0:h0 + 512], self.bank(7)[0:64, :], reads=[b7], writes=[uv_b[sl]])
            yield
            for hi, h in enumerate(hds):
                hc = slice(h * 64, h * 64 + 64)
                P.mm(pX[:, hi, :], ARc[sl][:, h, 1, :], Hb[:, h, :], hi == 0, False, reads=[in_b[sl], H_b[hf]], writes=[b7])
                P.mm(pX[:, hi, :], Gb[up][:, hi, 64:128], Ut[sl][:, hc], False, False, reads=[Gm_b[up], uv_b[sl]], writes=[b7])
                P.mm(pX[:, hi, :], Gk[up][:, hi, 64:128], Vt[sl][:, hc], False, hi == 7, reads=[Gm_b[up], in_b[sl]], writes=[b7])
            P.copy("act", ych[sl][:, h0:h0 + 512], self.bank(7)[0:64, :], reads=[b7], writes=[ych_b[sl]])
            yield
            for hi, h in enumerate(hds):
                hc = slice(h * 64, h * 64 + 64)
                P.mm(pX[:, hi, :], BH[sl][:, hc], Ut[sl][:, hc], hi == 0, False, reads=[in_b[sl], uv_b[sl]], writes=[b7])
                P.mm(pX[:, hi, :], KH[sl][:, hc], Vt[sl][:, hc], False, hi == 7, reads=[in_b[sl]], writes=[b7])
            Hh = H[:, 8 * hf:8 * hf + 8, :]
            P.tt("dve", Hh, Hh, PL2[:, 8 * hf:8 * hf + 8, c:c + 1].to_broadcast([64, 8, 64]), ALU.mult,
                 reads=[PL2_b, H_b[hf]], writes=[H_b[hf]])
            P.tt("dve", Hh, Hh, pX, ALU.add, reads=[b7, H_b[hf]], writes=[H_b[hf]])
            P.copy("pool", Hb[:, 8 * hf:8 * hf + 8, :], Hh, reads=[H_b[hf]], writes=[H_b[hf]])
            if hf == 1:
                P.dma("sp", ych_sem[sl], self.r_y[rows, :], ych[sl], reads=[ych_b[sl]], writes=[self.ry_b[c // 2]])
            yield

        NU = 128
        for _ in gen_A(0):
            pass
        for u in range(1, NU + 1):
            ga = gen_A(u) if u < NU else iter(())
            gb = gen_B(u - 1)
            da = db = False
            while not (da and db):
                if not da:
                    try:
                        next(ga)
                    except StopIteration:
                        da = True
                if not db:
                    try:
                        next(gb)
                    except StopIteration:
                        db = True
        A.reset(mark)
        gn = A.alloc([128, 2, D], F32)
        P.dma("sp", sem, gn, self.rw_gn, writes=[k_b])
        yt = [A.alloc([128, D], F32) for i in range(2)]
        bvt = [A.alloc([128, D], BF16) for i in range(2)]
        gtt = [A.alloc([128, D], BF16) for i in range(2)]
        i3_b = P.bufs(2, "r3in")
        i3_sem = [P.dsem(f"r3in{i}") for i in range(2)]
        sqt = A.alloc([128, D], F32)
        sqt_b = P.buf("sqt")
        stt_ = A.alloc([128, 4, 16], F32)
        st_b = P.buf("r3st")
        ot = [A.alloc([128, D], BF16) for i in range(2)]
        ot_b = P.bufs(2, "r3o")
        ot_sem = [P.dsem(f"r3o{i}") for i in range(2)]
        v16 = lambda ap: ap.rearrange("p (h n) -> p h n", h=16)
        for t in range(NT):
            sl = t % 2
            tile = slice(t * 128, (t + 1) * 128)
            P.dma("sp", i3_sem[sl], yt[sl], self.r_y[tile, :], reads=[self.ry_b[t]], writes=[i3_b[sl]])
            P.dma("sp", i3_sem[sl], bvt[sl], self.r_bv[tile, :], reads=[self.r1_b[t // 4]], writes=[i3_b[sl]])
            P.dma("sp", i3_sem[sl], gtt[sl], self.r_g[tile, :], reads=[self.r1_b[t // 4]], writes=[i3_b[sl]])
            y_ = yt[sl]
            mean, ex2, var, rstd = (stt_[:, i, :] for i in range(4))
            P.op("dve", lambda e_, y_=y_, mean=mean: e_.tensor_reduce(mean, v16(y_), AX.X, ALU.add), reads=[i3_b[sl]], writes=[st_b])
            P.act(sqt, y_, AF.Square, reads=[i3_b[sl]], writes=[sqt_b])
            P.op("dve", lambda e_, ex2=ex2: e_.tensor_reduce(ex2, v16(sqt), AX.X, ALU.add), reads=[sqt_b], writes=[st_b])
            P.ts("dve", mean, mean, 1.0 / 64, None, ALU.mult, reads=[st_b], writes=[st_b])
            P.stt("dve", var, mean, -1.0, mean, ALU.mult, ALU.mult, reads=[st_b], writes=[st_b])
            P.stt("dve", var, ex2, 1.0 / 64, var, ALU.mult, ALU.add, reads=[st_b], writes=[st_b])
            P.ts("dve", var, var, 64e-5, None, ALU.add, reads=[st_b], writes=[st_b])
            P.act(rstd, var, AF.Sqrt, reads=[st_b], writes=[st_b])
            P.op("dve", lambda e_, rstd=rstd: e_.reciprocal(rstd, rstd), reads=[st_b], writes=[st_b])
            P.tt("dve", v16(y_), v16(y_), mean.unsqueeze(2).to_broadcast([128, 16, 64]), ALU.subtract, reads=[st_b, i3_b[sl]], writes=[i3_b[sl]])
            P.tt("dve", v16(y_), v16(y_), rstd.unsqueeze(2).to_broadcast([128, 16, 64]), ALU.mult, reads=[st_b, i3_b[sl]], writes=[i3_b[sl]])
            P.tt("pool", y_, y_, gn[:, 0, :], ALU.mult, reads=[k_b, i3_b[sl]], writes=[i3_b[sl]])
            P.tt("pool", y_, y_, gn[:, 1, :], ALU.add, reads=[k_b, i3_b[sl]], writes=[i3_b[sl]])
            P.tt("dve", y_, y_, bvt[sl], ALU.add, reads=[i3_b[sl]], writes=[i3_b[sl]])
            P.tt("dve", ot[sl], y_, gtt[sl], ALU.mult, reads=[i3_b[sl]], writes=[ot_b[sl]])
            P.dma("sp", ot_sem[sl], self.ob[tile, 0:D], ot[sl], reads=[ot_b[sl]], writes=[self.ob_b[t]])
        self.tm_proj_phase(DC, self.rw_wo, res, res_b, dst, dst_b, "rwkv")

    def build(self):
        P = self.P
        src, src_b = self.x_in, self.xin_b
        pp = [(self.xa, self.xa_b), (self.xb, self.xb_b)]
        ip = 0
        for l in self.layers:
            if self.do_mix:
                self.norm_phase(src, src_b, self.gmix[:, l, :])
                dst, dst_b = pp[ip]
                ip ^= 1
                if l == 0:
                    self.moba(src, src_b, dst, dst_b)
                if l == 1:
                    self.rwkv(src, src_b, dst, dst_b)
                if l == 2:
                    self.ssd(src, src_b, dst, dst_b)
                if l == 3:
                    self.conformer(src, src_b, dst, dst_b)
                src, src_b = dst, dst_b
            if self.do_ffn:
                self.norm_phase(src, src_b, self.gffn[:, l, :])
                dst, dst_b = pp[ip]
                ip ^= 1
                self.ffn_phase(l, src, src_b, dst, dst_b)
                src, src_b = dst, dst_b
        self.A.reset()
        gf = self.A.alloc([128, D], F32)
        gf_b = P.buf("gf")
        P.dma("sp", self.c_sem, gf, self.norm_final, writes=[gf_b])
        for t in range(NT):
            xt, xt_b, sl, k = self.load_x(src, src_b, t)
            rstd, ss_b = self.rms_tile(xt, xt_b, k)
            P.act(xt[:], xt[:], AF.Copy, reads=[xt_b, ss_b], writes=[xt_b], scale=rstd)
            P.tt("dve", xt[:], xt[:], gf, ALU.mult, reads=[xt_b, gf_b], writes=[xt_b])
            self.store_x(self.out, self.out_b, t, xt, xt_b, sl)
        return P.finish()


def _fm(v, nch):
    v = np.asarray(v, np.float32)
    lead = v.shape[:-1]
    a = v.reshape(lead + (nch, 128))
    a = np.moveaxis(a, -1, 0)
    return np.ascontiguousarray(a)


def _bc(v, n=128):
    v = np.asarray(v, np.float32).reshape(1, -1)
    return np.ascontiguousarray(np.broadcast_to(v, (n, v.shape[1])))


def make_shared(inp):
    m = {}
    f = lambda k: np.ascontiguousarray(np.asarray(inp[k], np.float32))
    m["ident"] = np.eye(128, dtype=np.float32).astype(ml_dtypes.bfloat16)
    m["norm_mix"] = _fm(inp["norm_mix"], DC)
    m["norm_ffn"] = _fm(inp["norm_ffn"], DC)
    m["norm_final"] = _bc(inp["norm_final"])
    wu = np.asarray(inp["ffn_w_up"], np.float32).reshape(4, DC, 128, 2, FC, 128)
    m["ffn_w_up"] = np.ascontiguousarray(wu.transpose(0, 4, 2, 1, 3, 5)).reshape(4, FC, 128, DC * 2 * 128)
    wdn = np.asarray(inp["ffn_w_down"], np.float32).reshape(4, FC, 128, D)
    m["ffn_w_down"] = np.ascontiguousarray(wdn.transpose(0, 2, 1, 3))
    cw = np.asarray(inp["ffn_conv_w"], np.float32)
    cw = cw.transpose(0, 2, 1).reshape(4, 2 * FC, 128, 3)
    m["ffn_cw"] = np.ascontiguousarray(cw.transpose(2, 0, 1, 3))
    m["ffn_cb"] = _fm(inp["ffn_conv_b"], 2 * FC)
    m["moba_w_qkv"] = f("moba_w_qkv")[0]
    m["moba_w_o"] = f("moba_w_o")[0]
    kk = np.arange(S) // 256
    m["blkind"] = (kk[None, :] == np.arange(16)[:, None]).astype(np.float32).astype(ml_dtypes.bfloat16)
    qb = np.arange(16)[:, None]
    nn = np.arange(16)[None, :]
    mcst = np.stack([np.where(nn < qb, 0.0, -1e30), (nn < qb).astype(np.float32), (nn == qb).astype(np.float32)]).astype(np.float32)
    m["mconst"] = np.ascontiguousarray(np.broadcast_to(mcst[None], (128, 3, 16, 16)))
    m["tri"] = (np.arange(128)[None, :] >= np.arange(128)[:, None]).astype(np.float32).astype(ml_dtypes.bfloat16)
    m["rwkv_w_rkv"] = f("rwkv_w_rkv")[0]
    m["rwkv_w_o"] = f("rwkv_w_o")[0]
    for k_ in ("w1", "a1", "g1", "w2", "a2", "g2"):
        m["rwkv_" + k_] = f("rwkv_" + k_)[0]
    mu_ = _fm(inp["rwkv_mu"][0], 8).reshape(128, 48)
    m["rw_p"] = np.ascontiguousarray(np.concatenate(
        [mu_] + [_fm(np.asarray(inp["rwkv_" + k_][0]).reshape(-1), 8) for k_ in ("w0", "a0", "k_k", "k_a", "r_k")], axis=1))
    m["rw_gn"] = np.ascontiguousarray(np.stack([_bc(inp["rwkv_gn_w"][0]), _bc(inp["rwkv_gn_b"][0])], axis=1))
    i128 = np.arange(128)
    bones = (i128[:, None] // 64 == i128[None, :] // 64).astype(np.float32)
    hsel = (i128[:, None] // 64 == np.arange(2)[None, :]).astype(np.float32)
    s64 = i128[:, None] % 64
    t64 = i128[None, :] % 64
    maskG = np.where(i128[None, :] < 64, s64 < t64, s64 <= t64).astype(np.float32)
    maskA = np.zeros((128, 64), np.float32)
    maskA[:64] = (np.arange(64)[None, :] < np.arange(64)[:, None]).astype(np.float32)
    m["rw_c"] = np.ascontiguousarray(np.concatenate([bones, hsel, maskG, maskA], axis=1)).astype(ml_dtypes.bfloat16)
    m["ssd_w_in"] = f("ssd_w_in")[0]
    m["ssd_w_out"] = f("ssd_w_out")[0]
    wx = np.asarray(inp["ssd_w_in"], np.float32)[0][:, 2 * D:2 * D + 3072].reshape(DC, 128, 24, 128)
    m["ssd_wx"] = np.ascontiguousarray(wx.transpose(2, 1, 0, 3)).reshape(24, 128, DC * 128)
    scw = np.asarray(inp["ssd_conv_w"], np.float32)[0]
    scw = scw.T.reshape(24, 128, 4).transpose(1, 0, 2).reshape(128, 96)
    m["ssd_p"] = np.ascontiguousarray(np.concatenate([scw, _fm(inp["ssd_conv_b"][0], 24)], axis=1))
    m["ssd_t"] = np.ascontiguousarray(np.concatenate([_bc(inp["ssd_dt_bias"][0]), _bc(inp["ssd_a_log"][0]), _bc(inp["ssd_d"][0])], axis=1))
    m["ssd_nw"] = _bc(inp["ssd_norm_w"][0])
    nbm = np.where(np.arange(128)[:, None] > np.arange(128)[None, :], -30000.0, 0.0).astype(np.float32)
    m["nbmask"] = np.ascontiguousarray(np.broadcast_to(nbm[:, None, :], (128, 4, 128))).astype(ml_dtypes.bfloat16)
    m["conf_w_pw1"] = f("conf_w_pw1")[0]
    m["conf_w_pw2"] = f("conf_w_pw2")[0]
    dww = np.asarray(inp["conf_dw_w"], np.float32)[0]
    dww = dww.T.reshape(8, 128, 31).transpose(1, 0, 2).reshape(128, 8 * 31)
    m["conf_p"] = np.ascontiguousarray(np.concatenate([
        _fm(inp["conf_b_pw1"][0], 16), dww, _fm(inp["conf_dw_b"][0], 8),
        _fm(inp["conf_ln_w"][0], 8), _fm(inp["conf_ln_b"][0], 8)], axis=1))
    m["conf_b2"] = _bc(inp["conf_b_pw2"][0])
    return m


def make_inputs(inp, b, shared=None):
    m = dict(shared if shared is not None else make_shared(inp))
    m["x"] = np.ascontiguousarray(inp["x"][b], dtype=np.float32)
    return m


_NC = {}


def kernel(**inputs):
    if "nc" not in _NC:
        _NC["nc"] = Model().build()
    nc = _NC["nc"]
    shared = make_shared(inputs)
    in_maps = [make_inputs(inputs, b, shared) for b in range(8)]
    res = run_bass_kernel_spmd(nc, in_maps, core_ids=list(range(8)))
    return np.stack([np.asarray(r["out"], np.float32) for r in res.results], axis=0)
```

```python
import numpy as np
from contextlib import ExitStack
import ml_dtypes
import concourse.bass as bass
import concourse.mybir as mybir
from concourse.bass_utils import run_bass_kernel_spmd

F32 = mybir.dt.float32
BF16 = mybir.dt.bfloat16
AF = mybir.ActivationFunctionType
ALU = mybir.AluOpType
AX = mybir.AxisListType

S = 4096
D = 1024
DFF = 2816
NT = S // 128
DC = D // 128
FC = DFF // 128
EPS = 1e-6
ENGS = ("pe", "act", "dve", "pool", "sp")
STRICT = ("act", "dve", "pool")
EPOCH = 16384


class Buf:
    __slots__ = ("w", "r", "name")

    def __init__(self, name=""):
        self.w = None
        self.r = {}
        self.name = name


class Prog:
    def __init__(self):
        self.nc = bass.Bass("TRN2", target_bir_lowering=False)
        self.es = ExitStack()
        self.streams = {e: [] for e in ENGS}
        self.cnt = {e: 0 for e in ENGS}
        self.seen = {e: {} for e in ENGS}
        self.sems = {}
        self.epochs = set()
        self.nbuf = 0

    def sb(self, name, shape, dt):
        return self.es.enter_context(self.nc.sbuf_tensor(name, list(shape), dt))

    def ps(self, name, shape, dt):
        return self.es.enter_context(self.nc.psum_tensor(name, list(shape), dt))

    def dram(self, name, shape, dt, kind="Internal"):
        return self.nc.dram_tensor(name, list(shape), dt, kind=kind).ap()

    def dsem(self, name):
        k = ("d", name)
        assert k not in self.cnt
        self.cnt[k] = 0
        return k

    def buf(self, name=""):
        return Buf(name)

    def bufs(self, n, name=""):
        return [Buf(name + str(i)) for i in range(n)]

    def _dep(self, eng, dep):
        if dep is None:
            return
        k, v = dep
        if k == eng and eng not in STRICT:
            return
        if self.seen[eng].get(k, 0) >= v:
            return
        self.seen[eng][k] = v
        if k in ENGS:
            ep = (v - 1) // EPOCH
            self.epochs.add((k, ep))
            self.streams[eng].append(("w", (k, ep), (v - 1) % EPOCH + 1))
        else:
            self.streams[eng].append(("w", k, v))

    def _deps(self, eng, reads, writes):
        for b in reads:
            self._dep(eng, b.w)
        for b in writes:
            self._dep(eng, b.w)
            for k, v in b.r.items():
                self._dep(eng, (k, v))

    def op(self, eng, fn, reads=(), writes=()):
        self._deps(eng, reads, writes)
        self.cnt[eng] += 1
        n = self.cnt[eng]
        self.epochs.add((eng, (n - 1) // EPOCH))
        self.streams[eng].append(("o", fn, (eng, (n - 1) // EPOCH), 1))
        for b in reads:
            b.r[eng] = n
        for b in writes:
            b.w = (eng, n)
            b.r = {}

    def dma(self, q, sem, out, in_, reads=(), writes=(), **kw):
        self._deps(q, reads, writes)
        self.cnt[sem] += 16
        n = self.cnt[sem]
        self.streams[q].append(("o", lambda e: e.dma_start(out=out, in_=in_, **kw), sem, 16))
        for b in reads:
            b.r[sem] = n
        for b in writes:
            b.w = (sem, n)
            b.r = {}

    def mm(self, out, lhsT, rhs, start, stop, reads=(), writes=()):
        self.op("pe", lambda e: e.matmul(out, lhsT, rhs, start=start, stop=stop), reads, writes)

    def tr(self, out, in_, ident, reads=(), writes=()):
        self.op("pe", lambda e: e.transpose(out, in_, ident), reads, writes)

    def act(self, out, in_, func, reads=(), writes=(), **kw):
        self.op("act", lambda e: e.activation(out, in_, func, **kw), reads, writes)

    def tt(self, eng, out, in0, in1, op, reads=(), writes=()):
        self.op(eng, lambda e: e.tensor_tensor(out, in0, in1, op), reads, writes)

    def ts(self, eng, out, in0, s1, s2, op0, op1=None, reads=(), writes=()):
        if op1 is None:
            self.op(eng, lambda e: e.tensor_scalar(out, in0, s1, None, op0), reads, writes)
        else:
            self.op(eng, lambda e: e.tensor_scalar(out, in0, s1, s2, op0, op1), reads, writes)

    def stt(self, eng, out, in0, scalar, in1, op0, op1, reads=(), writes=()):
        self.op(eng, lambda e: e.scalar_tensor_tensor(out, in0, scalar, in1, op0, op1), reads, writes)

    def copy(self, eng, out, in_, reads=(), writes=()):
        if eng == "act":
            self.op(eng, lambda e: e.copy(out, in_), reads, writes)
        else:
            self.op(eng, lambda e: e.tensor_copy(out, in_), reads, writes)

    def memset(self, eng, ap, val, writes=()):
        self.op(eng, lambda e: e.memset(ap, val), (), writes)

    def barrier(self):
        for e in ENGS:
            for k, v in self.cnt.items():
                if v > 0 and k != e:
                    self._dep(e, (k, v))

    def finish(self):
        nc = self.nc
        for k, v in self.cnt.items():
            if v > 0 and k != "sp":
                self._dep("sp", (k, v))
        for k in self.cnt:
            if k not in ENGS:
                self.sems[k] = self.es.enter_context(nc.semaphore("s_" + k[1]))
        for (k, ep) in sorted(self.epochs):
            self.sems[(k, ep)] = self.es.enter_context(nc.semaphore(f"e_{k}_{ep}"))
        streams, sems = self.streams, self.sems

        def replay(name, e):
            for it in streams[name]:
                if it[0] == "w":
                    e.wait_ge(sems[it[1]], it[2])
                else:
                    it[1](e).then_inc(sems[it[2]], it[3])

        with nc.Block() as block:
            @block.tensor
            def _(e):
                replay("pe", e)

            @block.scalar
            def _(e):
                replay("act", e)

            @block.vector
            def _(e):
                replay("dve", e)

            @block.gpsimd
            def _(e):
                replay("pool", e)

            @block.sync
            def _(e):
                replay("sp", e)
        self.es.close()
        return nc


class Arena:
    def __init__(self, P, name, nbytes):
        self.P = P
        self.t = P.sb(name, [128, nbytes // 2], BF16)
        self.n = nbytes // 2
        self.off = 0

    def reset(self, to=0):
        self.P.barrier()
        self.off = to

    def alloc(self, shape, dt):
        ne = 1
        for d in shape[1:]:
            ne *= d
        if dt == F32:
            ne *= 2
        ne = (ne + 15) // 16 * 16
        assert self.off + ne <= self.n, (self.off, ne, self.n)
        ap = self.t[0:shape[0], self.off:self.off + ne]
        self.off += ne
        if dt == F32:
            ap = ap.bitcast(F32)
        nfree = 1
        for d in shape[1:]:
            nfree *= d
        ap = ap[:, 0:nfree]
        if len(shape) == 3:
            ap = ap.rearrange("p (a b) -> p a b", a=shape[1])
        elif len(shape) == 4:
            ap = ap.rearrange("p (a b c) -> p a b c", a=shape[1], b=shape[2])
        return ap


class Model:
    def __init__(self, layers=(0, 1, 2, 3), do_mix=True, do_ffn=True):
        self.P = P = Prog()
        self.layers = layers
        self.do_mix = do_mix
        self.do_ffn = do_ffn
        d = lambda n, sh, dt=F32: P.dram(n, sh, dt, "ExternalInput")
        self.x_in = d("x", [S, D])
        self.out = P.dram("out", [S, D], F32, "ExternalOutput")
        self.ident_d = d("ident", [128, 128], BF16)
        self.norm_mix = d("norm_mix", [128, 4, DC])
        self.norm_ffn = d("norm_ffn", [128, 4, DC])
        self.norm_final = d("norm_final", [128, D])
        self.ffn_w_up = d("ffn_w_up", [4, FC, 128, DC * 2 * 128])
        self.ffn_w_down = d("ffn_w_down", [4, 128, FC, D])
        self.ffn_cw = d("ffn_cw", [128, 4, 2 * FC, 3])
        self.ffn_cb = d("ffn_cb", [128, 4, 2 * FC])
        self.conf_w1 = d("conf_w_pw1", [D, 2 * D])
        self.conf_w2 = d("conf_w_pw2", [D, D])
        self.conf_p = d("conf_p", [128, 16 + 8 * 31 + 8 + 8 + 8])
        self.conf_b2 = d("conf_b2", [128, D])
        self.moba_wqkv = d("moba_w_qkv", [D, 3 * D])
        self.moba_wo = d("moba_w_o", [D, D])
        self.blkind = d("blkind", [16, S], BF16)
        self.mconst = d("mconst", [128, 3, 16, 16])
        self.tri_d = d("tri", [128, 128], BF16)
        self.ssd_win = d("ssd_w_in", [D, 5152])
        self.ssd_wout = d("ssd_w_out", [2 * D, D])
        self.ssd_wx = d("ssd_wx", [24, 128, DC * 128])
        self.ssd_p = d("ssd_p", [128, 24 * 5])
        self.ssd_t = d("ssd_t", [128, 96])
        self.ssd_nw = d("ssd_nw", [128, 2 * D])
        self.nb_d = d("nbmask", [128, 4, 128], BF16)
        self.s_xB = P.dram("s_xB", [S, 2560], BF16)
        self.s_xB_b = P.bufs(8, "sxB")
        self.s_BCT = P.dram("s_BCT", [8, 128, S], BF16)
        self.s_BCT_b = P.bufs(8, "sBCT")
        self.s_z = P.dram("s_z", [S, 2 * D], BF16)
        self.s_z_b = P.bufs(NT, "sz")
        self.rw_rkv = d("rwkv_w_rkv", [3, D, D])
        self.rw_wo = d("rwkv_w_o", [D, D])
        self.rw_w1 = d("rwkv_w1", [D, 64])
        self.rw_a1 = d("rwkv_a1", [D, 64])
        self.rw_g1 = d("rwkv_g1", [D, 160])
        self.rw_w2 = d("rwkv_w2", [64, D])
        self.rw_a2 = d("rwkv_a2", [64, D])
        self.rw_g2 = d("rwkv_g2", [160, D])
        self.rw_p = d("rw_p", [128, 88])
        self.rw_gn = d("rw_gn", [128, 2, D])
        self.rw_c = d("rw_c", [128, 128 + 2 + 128 + 64], BF16)
        self.r_AR = P.dram("r_AR", [8, 128, 64, 2, 64], BF16)
        self.r_BK = P.dram("r_BK", [8, 128, 64, 2, 64], BF16)
        self.r_bh = P.dram("r_bh", [S, D], BF16)
        self.r_kh = P.dram("r_kh", [S, D], BF16)
        self.r_v = P.dram("r_v", [S, D], BF16)
        self.r_g = P.dram("r_g", [S, D], BF16)
        self.r_bv = P.dram("r_bv", [S, D], BF16)
        self.r_y = P.dram("r_y", [S, D], F32)
        self.r_pl = P.dram("r_pl", [128, 8, 64], F32)
        self.r1_b = P.bufs(8, "r1")
        self.ry_b = P.bufs(NT, "ry")
        self.ob = P.dram("ob", [S, 2 * D], BF16)
        self.ob_b = P.bufs(NT, "ob")
        self.xa = P.dram("xa", [S, D], F32)
        self.xa_b = P.bufs(NT, "xa")
        self.xb = P.dram("xb", [S, D], F32)
        self.xb_b = P.bufs(NT, "xb")
        self.xin_b = P.bufs(NT, "xin")
        self.out_b = P.bufs(NT, "out")
        self.c_sem = P.dsem("const")
        self.c_b = P.buf("consts")
        self.ident = P.sb("ident_sb", [128, 128], BF16)
        self.ident_b = self.c_b
        P.dma("sp", self.c_sem, self.ident[:], self.ident_d, writes=[self.c_b])
        self.gmix = P.sb("gmix", [128, 4, DC], F32)
        self.gffn = P.sb("gffn", [128, 4, DC], F32)
        self.g_b = self.c_b
        P.dma("sp", self.c_sem, self.gmix[:], self.norm_mix, writes=[self.c_b])
        P.dma("sp", self.c_sem, self.gffn[:], self.norm_ffn, writes=[self.c_b])
        self.cw = P.sb("ffn_cw_sb", [128, 4, 2 * FC, 3], F32)
        self.cb = P.sb("ffn_cb_sb", [128, 4, 2 * FC], F32)
        P.dma("sp", self.c_sem, self.cw[:], self.ffn_cw, writes=[self.c_b])
        P.dma("sp", self.c_sem, self.cb[:], self.ffn_cb, writes=[self.c_b])
        self.ones = P.sb("ones_bf", [128, 128], BF16)
        P.memset("pool", self.ones[:], 1.0, writes=[self.c_b])
        self.hnT = P.sb("hnT", [128, DC, S], BF16)
        self.hnT_b = P.bufs(S // 512, "hnT")
        self.xt = [P.sb(f"xt{i}", [128, D], F32) for i in range(2)]
        self.xt_b = P.bufs(2, "xt")
        self.xt_sem = [P.dsem(f"xt{i}") for i in range(2)]
        self.xs = [P.sb(f"xs{i}", [128, D], BF16) for i in range(2)]
        self.xs_b = P.bufs(2, "xs")
        self.sq = P.sb("sqjunk", [128, D], BF16)
        self.sq_b = P.buf("sq")
        self.ss = [P.sb(f"ss{i}", [128, 2], F32) for i in range(2)]
        self.ss_b = P.bufs(2, "ss")
        self.nk = 0
        self.A = Arena(P, "arena", 122 * 1024)
        self.psum = P.ps("psum", [128, 4096], F32)
        self.pb = P.bufs(8, "psb")

    def bank(self, i):
        return self.psum[:, i * 512:(i + 1) * 512]

    def wload(self, dst_sb, src_ap, sem, b, nsplit=1):
        P = self.P
        n = dst_sb.shape[1]
        step = n // nsplit
        for i in range(nsplit):
            P.dma("pool", sem, dst_sb[:, i * step:(i + 1) * step], src_ap[:, i * step:(i + 1) * step], writes=[b])

    def rms_tile(self, xt, xt_b, k):
        P = self.P
        ss, ss_b = self.ss[k % 2], self.ss_b[k % 2]
        P.act(self.sq[:], xt[:], AF.Square, reads=[xt_b], writes=[self.sq_b, ss_b], accum_out=ss[:, 0:1])
        P.ts("dve", ss[:, 1:2], ss[:, 0:1], 1.0 / D, EPS, ALU.mult, ALU.add, reads=[ss_b], writes=[ss_b])
        P.act(ss[:, 1:2], ss[:, 1:2], AF.Sqrt, reads=[ss_b], writes=[ss_b])
        P.op("dve", lambda e: e.reciprocal(ss[:, 1:2], ss[:, 1:2]), reads=[ss_b], writes=[ss_b])
        return ss[:, 1:2], ss_b

    def load_x(self, src, src_b, t):
        P = self.P
        k = self.nk
        self.nk += 1
        sl = k % 2
        xt, xt_b = self.xt[sl], self.xt_b[sl]
        P.dma("sp", self.xt_sem[sl], xt[:], src[t * 128:(t + 1) * 128, :], reads=[src_b[t]], writes=[xt_b])
        return xt, xt_b, sl, k

    def store_x(self, dst, dst_b, t, xt, xt_b, sl):
        self.P.dma("sp", self.xt_sem[sl], dst[t * 128:(t + 1) * 128, :], xt[:], reads=[xt_b], writes=[dst_b[t]])

    def norm_phase(self, src, src_b, g):
        P = self.P
        psT = [self.bank(6 + i).bitcast(BF16).rearrange("p (c t) -> p c t", c=DC) for i in range(2)]
        for t in range(NT):
            xt, xt_b, sl, k = self.load_x(src, src_b, t)
            rstd, ss_b = self.rms_tile(xt, xt_b, k)
            xs, xs_b = self.xs[sl], self.xs_b[sl]
            P.act(xs[:], xt[:], AF.Copy, reads=[xt_b, ss_b], writes=[xs_b], scale=rstd)
            pt, pt_b = psT[sl], self.pb[6 + sl]
            for c in range(DC):
                P.tr(pt[:, c, :], xs[:, c * 128:(c + 1) * 128], self.ident[:], reads=[xs_b, self.c_b], writes=[pt_b])
            hb = self.hnT_b[t // 4]
            gb = g.unsqueeze(2).to_broadcast([128, DC, 128])
            P.tt("dve", self.hnT[:, :, t * 128:(t + 1) * 128], pt, gb, ALU.mult,
                 reads=[pt_b, self.c_b], writes=[hb])

    def res_store(self, res, res_b, dst, dst_b, t, dp, dp_b, bias=None, pre=None):
        P = self.P
        xt, xt_b, sl, k = pre if pre is not None else self.load_x(res, res_b, t)
        P.tt("dve", xt[:], xt[:], dp, ALU.add, reads=[dp_b, xt_b], writes=[xt_b])
        if bias is not None:
            P.tt("pool", xt[:], xt[:], bias[0], ALU.add, reads=[bias[1], xt_b], writes=[xt_b])
        self.store_x(dst, dst_b, t, xt, xt_b, sl)

    def ffn_phase(self, l, res, res_b, dst, dst_b):
        P = self.P
        A = self.A
        A.reset()
        R = {}
        R["raw"] = [[self.bank(h * 2 + s) for s in range(2)] for h in range(2)]
        R["raw_b"] = [[self.pb[h * 2 + s] for s in range(2)] for h in range(2)]
        R["cps"] = [self.bank(4 + h) for h in range(2)]
        R["cps_b"] = [self.pb[4 + h] for h in range(2)]
        R["dps"] = self.psum[:, 6 * 512:8 * 512]
        R["dps_b"] = self.pb[6]
        R["wd"] = A.alloc([128, FC, D], BF16)
        R["wd_b"] = P.buf("wd")
        R["wd_sem"] = P.dsem(f"wd{l}")
        R["wu"] = [A.alloc([128, DC, 2, 128], BF16) for s in range(3)]
        R["wu_b"] = P.bufs(3, "wu")
        R["wu_sem"] = [P.dsem(f"wu{l}_{s}") for s in range(3)]
        R["dg"] = [A.alloc([128, 2, 3, 128], BF16) for s in range(2)]
        R["dg_b"] = P.bufs(2, "dg")
        R["ub"] = [[A.alloc([128, 514], BF16) for s in range(2)] for h in range(2)]
        R["ub_b"] = [P.bufs(2, f"ub{h}") for h in range(2)]
        R["halo"] = A.alloc([128, 2 * FC, 2], BF16)
        R["halo_b"] = P.bufs(2 * FC, "halo")
        R["sg"] = [A.alloc([128, 512], F32) for s in range(2)]
        R["sg_b"] = P.bufs(2, "sg")
        R["actT"] = A.alloc([128, FC, 1024], BF16)
        R["actT_b"] = P.bufs(2, "actT")
        wdn = self.ffn_w_down[l]
        self.wload(R["wd"], wdn, R["wd_sem"], R["wd_b"], nsplit=2)
        P.memset("pool", R["halo"], 0.0, writes=R["halo_b"])
        SB = 1024
        units = []
        for sbk in range(S // SB):
            for i in range(FC):
                for tb in range(SB // 512):
                    units.append((sbk, i, tb))
        nU = len(units)

        def stage_load(sbk, i):
            j = (sbk * FC + i)
            sl = j % 3
            w, w_b, w_sem = R["wu"][sl], R["wu_b"][sl], R["wu_sem"][sl]
            P.dma("pool", w_sem, w.rearrange("p c h e -> p (c h e)"), self.ffn_w_up[l, i], writes=[w_b])
            dg, dg_b = R["dg"][j % 2], R["dg_b"][j % 2]
            for h in range(2):
                ch = h * FC + i
                for tap in range(3):
                    P.ts("dve", dg[:, h, tap, :], self.ident[:], self.cw[:, l, ch, tap:tap + 1], None, ALU.mult,
                         reads=[self.c_b], writes=[dg_b])

        def stage_A(u):
            sbk, i, tb = units[u]
            j = sbk * FC + i
            w, w_b = R["wu"][j % 3], R["wu_b"][j % 3]
            t0 = sbk * SB + tb * 512
            for h in range(2):
                ps, ps_b = R["raw"][h][u % 2], R["raw_b"][h][u % 2]
                for c in range(DC):
                    P.mm(ps, w[:, c, h, :], self.hnT[:, c, t0:t0 + 512], c == 0, c == DC - 1,
                         reads=[w_b, self.hnT_b[t0 // 512]], writes=[ps_b])

        def stage_B(u):
            sbk, i, tb = units[u]
            j = sbk * FC + i
            dg, dg_b = R["dg"][j % 2], R["dg_b"][j % 2]
            for h in range(2):
                ch = h * FC + i
                ps, ps_b = R["raw"][h][u % 2], R["raw_b"][h][u % 2]
                ub, ub_b = R["ub"][h][u % 2], R["ub_b"][h][u % 2]
                P.copy("dve", ub[:, 0:2], R["halo"][:, ch, :], reads=[R["halo_b"][ch]], writes=[ub_b])
                P.copy("act", ub[:, 2:514], ps, reads=[ps_b], writes=[ub_b])
                P.copy("dve", R["halo"][:, ch, :], ub[:, 512:514], reads=[ub_b], writes=[R["halo_b"][ch]])
                cp, cp_b = R["cps"][h], R["cps_b"][h]
                for tap in range(3):
                    P.mm(cp, dg[:, h, tap, :], ub[:, tap:tap + 512], tap == 0, tap == 2,
                         reads=[dg_b, ub_b], writes=[cp_b])

        def stage_C(u):
            sbk, i, tb = units[u]
            sg, sg_b = R["sg"][u % 2], R["sg_b"][u % 2]
            P.act(sg, R["cps"][0], AF.Silu, reads=[R["cps_b"][0], self.c_b], writes=[sg_b],
                  bias=self.cb[:, l, i:i + 1])
            P.stt("dve", R["actT"][:, i, tb * 512:(tb + 1) * 512], R["cps"][1], self.cb[:, l, FC + i:FC + i + 1],
                  sg, ALU.add, ALU.mult, reads=[R["cps_b"][1], sg_b, self.c_b], writes=[R["actT_b"][tb]])

        def down(sbk):
            nxt = self.load_x(res, res_b, sbk * (SB // 128))
            for tt in range(SB // 128):
                t = sbk * (SB // 128) + tt
                cur = nxt
                dp, dp_b = R["dps"], R["dps_b"]
                for hh in range(2):
                    for c in range(FC):
                        P.mm(dp[:, hh * 512:(hh + 1) * 512], R["actT"][:, c, tt * 128:(tt + 1) * 128],
                             R["wd"][:, c, hh * 512:(hh + 1) * 512], c == 0, c == FC - 1,
                             reads=[R["actT_b"][tt // 4], R["wd_b"]], writes=[dp_b])
                if tt + 1 < SB // 128:
                    nxt = self.load_x(res, res_b, t + 1)
                self.res_store(res, res_b, dst, dst_b, t, dp, dp_b, pre=cur)

        upb = SB // 512 * FC
        for u in range(nU + 1):
            if u < nU:
                sbk, i, tb = units[u]
                if tb == 0:
                    stage_load(sbk, i)
                stage_A(u)
            if u >= 1:
                stage_B(u - 1)
                stage_C(u - 1)
                if (u % upb) == 0:
                    down(u // upb - 1)

    def conformer(self, res, res_b, dst, dst_b):
        P = self.P
        A = self.A
        A.reset()
        NP = 16 + 8 * 31 + 24
        cp = A.alloc([128, NP], F32)
        cp_b = P.buf("confp")
        sem = P.dsem("conf")
        P.dma("sp", sem, cp, self.conf_p, writes=[cp_b])
        b1 = cp[:, 0:16]
        dww = cp[:, 16:16 + 248].rearrange("p (c j) -> p c j", c=8)
        dwb = cp[:, 264:272]
        lnw = cp[:, 272:280]
        lnb = cp[:, 280:288]
        gT = A.alloc([128, DC, 30 + S], BF16)
        gT_b = P.bufs(S // 512, "gT")
        gz_b = P.buf("gTpad")
        P.memset("pool", gT[:, :, 0:30], 0.0, writes=[gz_b])
        mark = A.off
        w1 = A.alloc([128, DC, 2 * D], BF16)
        w1_b = P.buf("w1")
        self.wload(w1, self.conf_w1.rearrange("(c p) e -> p c e", p=128), sem, w1_b, nsplit=DC)
        sg = [A.alloc([128, 512], F32) for i in range(2)]
        sg_b = P.bufs(2, "csg")
        k = 0
        for c in range(DC):
            for tb in range(8):
                psa, psa_b = self.bank(2 * (k % 2)), self.pb[2 * (k % 2)]
                psb, psb_b = self.bank(2 * (k % 2) + 1), self.pb[2 * (k % 2) + 1]
                for kc in range(DC):
                    P.mm(psa, w1[:, kc, c * 128:(c + 1) * 128], self.hnT[:, kc, tb * 512:(tb + 1) * 512],
                         kc == 0, kc == DC - 1, reads=[w1_b, self.hnT_b[tb]], writes=[psa_b])
                for kc in range(DC):
                    P.mm(psb, w1[:, kc, D + c * 128:D + (c + 1) * 128], self.hnT[:, kc, tb * 512:(tb + 1) * 512],
                         kc == 0, kc == DC - 1, reads=[w1_b, self.hnT_b[tb]], writes=[psb_b])
                P.act(sg[k % 2], psb, AF.Sigmoid, reads=[psb_b, cp_b], writes=[sg_b[k % 2]], bias=b1[:, 8 + c:9 + c])
                P.stt("dve", gT[:, c, 30 + tb * 512:30 + (tb + 1) * 512], psa, b1[:, c:c + 1], sg[k % 2],
                      ALU.add, ALU.mult, reads=[psa_b, sg_b[k % 2], cp_b], writes=[gT_b[tb]])
                k += 1
        A.reset(mark)
        w2 = A.alloc([128, DC, D], BF16)
        w2_b = P.buf("w2")
        self.wload(w2, self.conf_w2.rearrange("(c p) e -> p c e", p=128), sem, w2_b, nsplit=DC)
        b2 = A.alloc([128, D], F32)
        b2_b = P.buf("b2")
        P.dma("sp", sem, b2, self.conf_b2, writes=[b2_b])
        mark2 = A.off
        dg = [A.alloc([128, 31, 128], BF16) for i in range(2)]
        dg_b = P.bufs(2, "cdg")
        k = 0
        for c in range(DC):
            for j in range(31):
                P.ts("pool", dg[c % 2][:, j, :], self.ident[:], dww[:, c, j:j + 1], None, ALU.mult,
                     reads=[self.c_b, cp_b], writes=[dg_b[c % 2]])
            for tb in range(8):
                ps, ps_b = self.bank(k % 2), self.pb[k % 2]
                rb = [gz_b, gT_b[tb]] + ([gT_b[tb - 1]] if tb > 0 else [])
                for j in range(31):
                    P.mm(ps, dg[c % 2][:, j, :], gT[:, c, tb * 512 + j:tb * 512 + j + 512], j == 0, j == 30,
                         reads=[dg_b[c % 2]] + rb, writes=[ps_b])
                P.act(self.hnT[:, c, tb * 512:(tb + 1) * 512], ps, AF.Identity, reads=[ps_b, cp_b],
                      writes=[self.hnT_b[tb]], bias=dwb[:, c:c + 1])
                k += 1
        A.reset(mark2)
        sqv = A.alloc([128, DC, 512], BF16)
        sqv_b = P.buf("sqv")
        st = [A.alloc([128, 512], F32) for i in range(3)]
        st_b = P.buf("st")
        tmp = [A.alloc([128, 512], F32) for i in range(2)]
        tmp_b = P.bufs(2, "ctmp")
        zT = A.alloc([128, DC, 512], BF16)
        zT_b = P.buf("zT")
        vT = self.hnT
        for tb in range(8):
            blk = slice(tb * 512, (tb + 1) * 512)
            P.act(sqv, vT[:, :, blk], AF.Square, reads=[self.hnT_b[tb]], writes=[sqv_b])
            pS, pS_b = self.bank(2), self.pb[2]
            pQ, pQ_b = self.bank(3), self.pb[3]
            for c in range(DC):
                P.mm(pS, self.ones[:], vT[:, c, blk], c == 0, c == DC - 1, reads=[self.c_b, self.hnT_b[tb]], writes=[pS_b])
            for c in range(DC):
                P.mm(pQ, self.ones[:], sqv[:, c, :], c == 0, c == DC - 1, reads=[self.c_b, sqv_b], writes=[pQ_b])
            mean, var, rstd = st
            P.ts("dve", mean, pS, 1.0 / D, None, ALU.mult, reads=[pS_b], writes=[st_b])
            P.stt("dve", var, mean, -1.0, mean, ALU.mult, ALU.mult, reads=[st_b], writes=[st_b])
            P.stt("dve", var, pQ, 1.0 / D, var, ALU.mult, ALU.add, reads=[pQ_b, st_b], writes=[st_b])
            P.ts("dve", var, var, 1e-5, None, ALU.add, reads=[st_b], writes=[st_b])
            P.act(rstd, var, AF.Sqrt, reads=[st_b], writes=[st_b])
            P.op("dve", lambda e, rstd=rstd: e.reciprocal(rstd, rstd), reads=[st_b], writes=[st_b])
            for c in range(DC):
                tm, tm_b = tmp[c % 2], tmp_b[c % 2]
                P.tt("pool", tm, vT[:, c, blk], mean, ALU.subtract, reads=[self.hnT_b[tb], st_b], writes=[tm_b])
                P.tt("dve", tm, tm, rstd, ALU.mult, reads=[st_b, tm_b], writes=[tm_b])
                P.act(zT[:, c, :], tm, AF.Silu, reads=[tm_b, cp_b], writes=[zT_b],
                      scale=lnw[:, c:c + 1], bias=lnb[:, c:c + 1])
            for tq in range(4):
                t = tb * 4 + tq
                dp, dp_b = self.psum[:, 6 * 512:8 * 512], self.pb[6]
                for hh in range(2):
                    for c in range(DC):
                        P.mm(dp[:, hh * 512:(hh + 1) * 512], zT[:, c, tq * 128:(tq + 1) * 128],
                             w2[:, c, hh * 512:(hh + 1) * 512], c == 0, c == DC - 1,
                             reads=[zT_b, w2_b], writes=[dp_b])
                self.res_store(res, res_b, dst, dst_b, t, dp, dp_b, bias=(b2, b2_b))


    def tm_proj_phase(self, KC, W_dram, res, res_b, dst, dst_b, tag):
        P, A = self.P, self.A
        A.reset()
        sem = P.dsem("tmp" + tag)
        W = A.alloc([128, KC, D], BF16)
        W_b = P.buf("tmW")
        self.wload(W, W_dram.rearrange("(c p) e -> p c e", p=128), sem, W_b, nsplit=KC)
        yt = [A.alloc([128, KC * 128], BF16) for i in range(2)]
        yt_b = P.bufs(2, "tmy")
        yt_sem = [P.dsem(f"tmy{tag}{i}") for i in range(2)]
        yT = [A.alloc([128, KC, 128], BF16) for i in range(2)]
        yT_b = P.bufs(2, "tmyT")
        def ld(t):
            P.dma("sp", yt_sem[t % 2], yt[t % 2], self.ob[t * 128:(t + 1) * 128, 0:KC * 128], reads=[self.ob_b[t]], writes=[yt_b[t % 2]])
            return self.load_x(res, res_b, t)

        nxt = ld(0)
        for t in range(NT):
            sl = t % 2
            cur = nxt
            if t + 1 < NT:
                nxt = ld(t + 1)
            pt = self.psum[:, sl * 1024:(sl + 1) * 1024].bitcast(BF16)[:, 0:KC * 128].rearrange("p (c t) -> p c t", c=KC)
            pt_b = self.pb[2 * sl]
            for c in range(KC):
                P.tr(pt[:, c, :], yt[sl][:, c * 128:(c + 1) * 128], self.ident[:], reads=[yt_b[sl], self.c_b], writes=[pt_b])
            P.copy("act", yT[sl], pt, reads=[pt_b], writes=[yT_b[sl]])
            dp, dp_b = self.psum[:, (4 + 2 * sl) * 512:(6 + 2 * sl) * 512], self.pb[4 + 2 * sl]
            for hh in range(2):
                for c in range(KC):
                    P.mm(dp[:, hh * 512:(hh + 1) * 512], yT[sl][:, c, :], W[:, c, hh * 512:(hh + 1) * 512],
                         c == 0, c == KC - 1, reads=[yT_b[sl], W_b], writes=[dp_b])
            self.res_store(res, res_b, dst, dst_b, t, dp, dp_b, pre=cur)

    def moba(self, res, res_b, dst, dst_b):
        P, A = self.P, self.A
        A.reset()
        G = 4
        BIG = 30000.0
        sem = P.dsem("moba")
        mc = A.alloc([128, 3, 16, 16], F32)
        tri = A.alloc([128, 128], BF16)
        k_b = P.buf("mobac")
        P.dma("sp", sem, mc, self.mconst, writes=[k_b])
        P.dma("sp", sem, tri, self.tri_d, writes=[k_b])
        QA = A.alloc([128, G, S], BF16)
        KA = A.alloc([128, G, S], BF16)
        QA_b = [P.bufs(8, f"QA{h}") for h in range(G)]
        KA_b = P.bufs(G, "KA")
        ind_b = P.buf("ind")
        V = A.alloc([128, NT, G, 65], BF16)
        V_b = P.buf("V")
        one_b = P.buf("Vone")
        P.memset("pool", V[:, :, :, 64:65], 1.0, writes=[one_b])
        for hh in range(G):
            P.dma("sp", sem, KA[64:80, hh, :], self.blkind, writes=[ind_b])
        w3 = A.alloc([128, DC, 3, G * 64], BF16)
        w3_b = P.buf("w3")
        w3_sem = P.dsem("mobaw")
        km = A.alloc([128, G, 16], F32)
        kmb = A.alloc([128, G, 16], BF16)
        km_b = P.buf("km")
        g2 = [A.alloc([128, G, 16], F32) for i in range(2)]
        sel = [A.alloc([128, G, 16], F32) for i in range(2)]
        mx = [A.alloc([128, G, 8], F32) for i in range(2)]
        g2_b, sel_b = P.bufs(2, "g2"), P.bufs(2, "sel")
        mx_b = [P.bufs(G, "mx") for i in range(2)]
        mbf = [A.alloc([128, G, 80], BF16) for i in range(2)]
        mbf_b = P.bufs(2, "mbf")
        for i in range(2):
            P.memset("pool", mbf[i], 0.0, writes=[mbf_b[i]])
        mb2 = [A.alloc([128, G, 128], BF16) for i in range(2)]
        mb2_b = P.bufs(2, "mb2")
        PT = [A.alloc([128, 512], BF16) for i in range(3)]
        PT_b = P.bufs(3, "PT")
        osb = [A.alloc([128, 4, G * 64], BF16) for i in range(2)]
        osb_b = P.bufs(2, "osb")
        osb_sem = [P.dsem(f"osb{i}") for i in range(2)]
        rec = A.alloc([128, 4, 1], F32)
        rec_b = P.buf("rec")
        wqkv = self.moba_wqkv.rearrange("(c p) e -> p c e", p=128)
        nps = 0
        nS = 0
        nO = 0
        nosb = 0
        for g in range(D // 64 // G):
            for i in range(3):
                P.dma("pool", w3_sem, w3[:, :, i, :], wqkv[:, :, i * D + g * G * 64:i * D + (g + 1) * G * 64], writes=[w3_b])
            for hh in range(G):
                for tb in range(8):
                    blk = slice(tb * 512, (tb + 1) * 512)
                    for i in range(2):
                        ps, ps_b = self.bank(nps % 2)[0:64, :], self.pb[nps % 2]
                        nps += 1
                        for kc in range(DC):
                            P.mm(ps, w3[:, kc, i, hh * 64:(hh + 1) * 64], self.hnT[:, kc, blk], kc == 0, kc == DC - 1,
                                 reads=[w3_b, self.hnT_b[tb]], writes=[ps_b])
                        if i == 0:
                            P.act(QA[0:64, hh, blk], ps, AF.Copy, reads=[ps_b], writes=[QA_b[hh][tb]], scale=0.125)
                        else:
                            P.copy("dve", KA[0:64, hh, blk], ps, reads=[ps_b], writes=[KA_b[hh]])
                P.op("dve", lambda e, hh=hh: e.tensor_reduce(km[0:64, hh, :], KA[0:64, hh, :].rearrange("p (n k) -> p n k", k=256),
                                                             AX.X, ALU.add), reads=[KA_b[hh]], writes=[km_b])
            P.ts("dve", kmb[0:64], km[0:64], 1.0 / 256, None, ALU.mult, reads=[km_b], writes=[km_b])
            for t in range(NT):
                ps, ps_b = self.bank(nps % 2)[:, 0:G * 64], self.pb[nps % 2]
                nps += 1
                for kc in range(DC):
                    P.mm(ps, self.hnT[:, kc, t * 128:(t + 1) * 128], w3[:, kc, 2, :], kc == 0, kc == DC - 1,
                         reads=[w3_b, self.hnT_b[t // 4]], writes=[ps_b])
                P.copy("act", V[:, t, :, 0:64], ps.rearrange("p (h e) -> p h e", h=G), reads=[ps_b], writes=[V_b])
            def g_stage1(t):
                sl = t % 2
                tile = slice(t * 128, (t + 1) * 128)
                gps = self.bank(2 + sl)[:, 0:G * 16].rearrange("p (h n) -> p h n", h=G)
                for hh in range(G):
                    P.mm(gps[:, hh, :], QA[0:64, hh, tile], kmb[0:64, hh, :], True, True,
                         reads=[QA_b[hh][t // 4], km_b], writes=[self.pb[2 + sl]])

            def g_stage2(t):
                sl = t % 2
                qb_ = t // 2
                gps = self.bank(2 + sl)[:, 0:G * 16].rearrange("p (h n) -> p h n", h=G)
                g2_, sel_, mx_, mbf_ = g2[sl], sel[sl], mx[sl], mbf[sl]
                P.tt("dve", g2_, gps, mc[:, 0, qb_, :].unsqueeze(1).to_broadcast([128, G, 16]), ALU.add,
                     reads=[self.pb[2 + sl], k_b], writes=[g2_b[sl]])
                for hh in range(G):
                    P.op("dve", lambda e, hh=hh: e.max(mx_[:, hh, :], g2_[:, hh, :]), reads=[g2_b[sl]], writes=[mx_b[sl][hh]])
                P.tt("dve", sel_, g2_, mx_[:, :, 2:3].to_broadcast([128, G, 16]), ALU.is_ge, reads=[g2_b[sl]] + mx_b[sl], writes=[sel_b[sl]])
                P.tt("dve", sel_, sel_, mc[:, 1, qb_, :].unsqueeze(1).to_broadcast([128, G, 16]), ALU.mult,
                     reads=[sel_b[sl], k_b], writes=[sel_b[sl]])
                P.tt("dve", sel_, sel_, mc[:, 2, qb_, :].unsqueeze(1).to_broadcast([128, G, 16]), ALU.add,
                     reads=[sel_b[sl], k_b], writes=[sel_b[sl]])
                P.ts("dve", mbf_[:, :, 64:80], sel_, BIG, -BIG, ALU.mult, ALU.add, reads=[sel_b[sl]], writes=[mbf_b[sl]])

            def g_stage3(t):
                sl = t % 2
                tps = self.bank(6 + sl)[:, 0:256].bitcast(BF16).rearrange("p (h q) -> p h q", h=G)
                for hh in range(G):
                    P.tr(tps[0:80, hh, :], mbf[sl][:, hh, :], self.ident[:], reads=[mbf_b[sl], self.c_b], writes=[self.pb[6 + sl]])

            def g_stage4(t):
                sl = t % 2
                tile = slice(t * 128, (t + 1) * 128)
                tps = self.bank(6 + sl)[:, 0:256].bitcast(BF16).rearrange("p (h q) -> p h q", h=G)
                for hh in range(G):
                    P.copy("dve", mb2[sl][64:80, hh, :], tps[64:80, hh, :], reads=[self.pb[6 + sl]], writes=[mb2_b[sl]])
                for hh in range(G):
                    P.copy("pool", QA[64:80, hh, tile], mb2[sl][64:80, hh, :], reads=[mb2_b[sl]], writes=[QA_b[hh][t // 4]])

            for t in range(NT + 2):
                if t < NT:
                    g_stage1(t)
                if 1 <= t <= NT:
                    g_stage2(t - 1)
                    g_stage3(t - 1)
                if t >= 2:
                    g_stage4(t - 2)
            for qb in range(8):
                ob_, ob_b, ob_sem = osb[nosb % 2], osb_b[nosb % 2], osb_sem[nosb % 2]
                nosb += 1
                for hh in range(G):
                    O = self.bank(5 + nO % 2)[:, 0:260].rearrange("p (q e) -> p q e", q=4)
                    O_b = self.pb[5 + nO % 2]
                    nO += 1
                    nkt = 4 * qb + 4
                    slots = {}
                    import os
                    LAG = 0 if os.environ.get("MOBA_ATT") == "old" else 1
                    for kt in range(nkt + LAG):
                        if kt < nkt:
                            sp, sp_b = self.bank(2 + nS % 3), self.pb[2 + nS % 3]
                            pt, pt_b = PT[nS % 3], PT_b[nS % 3]
                            slots[kt] = (pt, pt_b)
                            nS += 1
                            P.mm(sp, KA[0:80, hh, kt * 128:(kt + 1) * 128], QA[0:80, hh, qb * 512:(qb + 1) * 512], True, True,
                                 reads=[KA_b[hh], ind_b, QA_b[hh][qb]], writes=[sp_b])
                            P.act(pt, sp, AF.Exp, reads=[sp_b], writes=[pt_b])
                            j = kt - 4 * qb
                            if j >= 0:
                                P.tt("pool", pt[:, j * 128:(j + 1) * 128], pt[:, j * 128:(j + 1) * 128], tri, ALU.mult,
                                     reads=[k_b, pt_b], writes=[pt_b])
                        if kt >= LAG:
                            k2 = kt - LAG
                            pt, pt_b = slots.pop(k2)
                            for ql in range(4):
                                qt = 4 * qb + ql
                                if k2 <= qt:
                                    P.mm(O[:, ql, :], pt[:, ql * 128:(ql + 1) * 128], V[:, k2, hh, :], k2 == 0 and ql == 0, k2 == qt,
                                         reads=[pt_b, V_b, one_b], writes=[O_b])
                    P.op("dve", lambda e, O=O: e.reciprocal(rec, O[:, :, 64:65]), reads=[O_b], writes=[rec_b])
                    P.tt("dve", ob_[:, :, hh * 64:(hh + 1) * 64], O[:, :, 0:64], rec.to_broadcast([128, 4, 64]), ALU.mult,
                         reads=[O_b, rec_b], writes=[ob_b])
                dstv = self.ob[qb * 512:(qb + 1) * 512, g * G * 64:(g + 1) * G * 64].rearrange("(q p) c -> p q c", p=128)
                P.dma("sp", ob_sem, dstv, ob_, reads=[ob_b], writes=[self.ob_b[4 * qb + i] for i in range(4)])
        self.tm_proj_phase(DC, self.moba_wo, res, res_b, dst, dst_b, "moba")

    def ssd(self, res, res_b, dst, dst_b):
        P, A = self.P, self.A
        A.reset()
        sem = P.dsem("ssd")
        pp = A.alloc([128, 120], F32)
        tc_ = A.alloc([128, 96], F32)
        tri = A.alloc([128, 128], BF16)
        nb4 = A.alloc([128, 4, 128], BF16)
        k_b = P.buf("ssdc")
        P.dma("sp", sem, pp, self.ssd_p, writes=[k_b])
        P.dma("sp", sem, tc_, self.ssd_t, writes=[k_b])
        P.dma("sp", sem, tri, self.tri_d, writes=[k_b])
        P.dma("sp", sem, nb4, self.nb_d, writes=[k_b])
        negones = A.alloc([128, 128], BF16)
        P.memset("pool", negones, -1.0, writes=[k_b])
        cwv = pp[:, 0:96].rearrange("p (c j) -> p c j", j=4)
        cbv = pp[:, 96:120]
        dtk = A.alloc([128, NT, 32], F32)
        atk = A.alloc([128, NT, 32], BF16)
        dt_b = P.buf("dtk")
        aneg = A.alloc([128, 32], F32)
        P.act(aneg, tc_[:, 32:64], AF.Exp, reads=[k_b], writes=[k_b])
        P.ts("dve", aneg, aneg, -1.0, None, ALU.mult, reads=[k_b], writes=[k_b])
        mark = A.off
        win = self.ssd_win.rearrange("(c p) e -> p c e", p=128)
        wch = [A.alloc([128, DC, 128], BF16) for i in range(3)]
        wch_b = P.bufs(3, "swch")
        wch_sem = [P.dsem(f"swch{i}") for i in range(3)]
        dg = [A.alloc([128, 4, 128], BF16) for i in range(2)]
        dg_b = P.bufs(2, "sdg")
        ub = [A.alloc([128, 515], BF16) for i in range(2)]
        ub_b = P.bufs(2, "sub")
        xc = [A.alloc([128, 512], BF16) for i in range(2)]
        xc_b = P.bufs(2, "sxc")
        xc_sem = [P.dsem(f"sxc{i}") for i in range(2)]
        stg = [A.alloc([128, 4, 128], BF16) for i in range(2)]
        stg_b = P.bufs(2, "sstg")
        stg_sem = [P.dsem(f"sstg{i}") for i in range(2)]
        u = 0
        for cc in range(24):
            w, w_b = wch[cc % 3], wch_b[cc % 3]
            P.dma("pool", wch_sem[cc % 3], w.rearrange("p c e -> p (c e)"), self.ssd_wx[cc], writes=[w_b])
            for j in range(4):
                P.ts("dve", dg[cc % 2][:, j, :], self.ident[:], cwv[:, cc, j:j + 1], None, ALU.mult,
                     reads=[self.c_b, k_b], writes=[dg_b[cc % 2]])
            for tb in range(8):
                blk = slice(tb * 512, (tb + 1) * 512)
                ps, ps_b = self.bank(u % 2), self.pb[u % 2]
                for kc in range(DC):
                    P.mm(ps, w[:, kc, :], self.hnT[:, kc, blk], kc == 0, kc == DC - 1, reads=[w_b, self.hnT_b[tb]], writes=[ps_b])
                b_, b_b = ub[u % 2], ub_b[u % 2]
                if tb == 0:
                    P.memset("dve", b_[:, 0:3], 0.0, writes=[b_b])
                else:
                    P.copy("dve", b_[:, 0:3], ub[(u - 1) % 2][:, 512:515], reads=[ub_b[(u - 1) % 2]], writes=[b_b])
                P.copy("act", b_[:, 3:515], ps, reads=[ps_b], writes=[b_b])
                cp, cp_b = self.bank(2 + u % 2), self.pb[2 + u % 2]
                for j in range(4):
                    P.mm(cp, dg[cc % 2][:, j, :], b_[:, j:j + 512], j == 0, j == 3, reads=[dg_b[cc % 2], b_b], writes=[cp_b])
                x_, x_b, x_sem = xc[u % 2], xc_b[u % 2], xc_sem[u % 2]
                P.act(x_, cp, AF.Silu, reads=[cp_b, k_b], writes=[x_b], bias=cbv[:, cc:cc + 1])
                if cc >= 16:
                    P.dma("sp", x_sem, self.s_BCT[cc - 16, :, blk], x_, reads=[x_b], writes=[self.s_BCT_b[tb]])
                if cc < 20:
                    tp = self.bank(4 + u % 2).bitcast(BF16)[:, 0:512].rearrange("p (q c) -> p q c", q=4)
                    tp_b = self.pb[4 + u % 2]
                    for q in range(4):
                        P.tr(tp[:, q, :], x_[:, q * 128:(q + 1) * 128], self.ident[:], reads=[x_b, self.c_b], writes=[tp_b])
                    sg_, sg_b, sg_sem = stg[u % 2], stg_b[u % 2], stg_sem[u % 2]
                    P.copy("dve", sg_, tp, reads=[tp_b], writes=[sg_b])
                    dv = self.s_xB[tb * 512:(tb + 1) * 512, cc * 128:(cc + 1) * 128].rearrange("(q p) c -> p q c", p=128)
                    P.dma("sp", sg_sem, dv, sg_, reads=[sg_b], writes=[self.s_xB_b[tb]])
                u += 1
        A.reset(mark)
        wz = A.alloc([128, DC, 2 * D], BF16)
        wz_b = P.buf("wz")
        self.wload(wz, win[:, :, 0:2 * D], sem, wz_b, nsplit=DC)
        wdt = A.alloc([128, DC, 32], BF16)
        P.dma("pool", sem, wdt, win[:, :, 5120:5152], writes=[wz_b])
        zt = [A.alloc([128, 2 * D], BF16) for i in range(2)]
        zt_b = P.bufs(2, "szt")
        zt_sem = [P.dsem(f"szt{i}") for i in range(2)]
        for t in range(NT):
            tile = slice(t * 128, (t + 1) * 128)
            z_, z_b = zt[t % 2], zt_b[t % 2]
            for q in range(4):
                ps, ps_b = self.bank(q % 2), self.pb[q % 2]
                for kc in range(DC):
                    P.mm(ps, self.hnT[:, kc, tile], wz[:, kc, q * 512:(q + 1) * 512], kc == 0, kc == DC - 1,
                         reads=[wz_b, self.hnT_b[t // 4]], writes=[ps_b])
                P.act(z_[:, q * 512:(q + 1) * 512], ps, AF.Silu, reads=[ps_b], writes=[z_b])
            P.dma("sp", zt_sem[t % 2], self.s_z[tile, :], z_, reads=[z_b], writes=[self.s_z_b[t]])
            ps, ps_b = self.bank(2)[:, 0:32], self.pb[2]
            for kc in range(DC):
                P.mm(ps, self.hnT[:, kc, tile], wdt[:, kc, :], kc == 0, kc == DC - 1, reads=[wz_b, self.hnT_b[t // 4]], writes=[ps_b])
            P.tt("dve", dtk[:, t, :], ps, tc_[:, 0:32], ALU.add, reads=[ps_b, k_b], writes=[dt_b])
        P.act(dtk, dtk, AF.Exp, reads=[dt_b], writes=[dt_b])
        P.act(dtk, dtk, AF.Ln, reads=[dt_b], writes=[dt_b], bias=1.0)
        P.tt("dve", atk, dtk, aneg.unsqueeze(1).to_broadcast([128, NT, 32]), ALU.mult, reads=[dt_b, k_b], writes=[dt_b])
        A.reset(mark)
        nw = A.alloc([128, 2 * D], F32)
        P.dma("sp", sem, nw, self.ssd_nw, writes=[k_b])
        xB = [A.alloc([128, 2560], BF16) for i in range(2)]
        xB_b = P.bufs(2, "xB")
        xB_sem = [P.dsem(f"xB{i}") for i in range(2)]
        zz = [A.alloc([128, 2 * D], BF16) for i in range(2)]
        zz_b = P.bufs(2, "zz")
        zz_sem = [P.dsem(f"zz{i}") for i in range(2)]
        bct = [A.alloc([128, 8, 128], BF16) for i in range(2)]
        bct_b = P.bufs(2, "bct")
        bct_sem = [P.dsem(f"bct{i}") for i in range(2)]
        xd = A.alloc([128, 32, 64], BF16)
        xdw = A.alloc([128, 32, 64], BF16)
        xd_b = P.buf("xd")
        acum = A.alloc([128, 32], F32)
        expA = A.alloc([128, 32], F32)
        expLA = A.alloc([128, 32], F32)
        dL = A.alloc([128, 32], F32)
        sc_b = P.buf("ssc")
        R1 = A.alloc([128, 8, 128], BF16)
        R1_b = P.buf("R1")
        E = A.alloc([128, 8, 128], BF16)
        E_b = P.buf("E")
        MT = A.alloc([128, 8, 128], BF16)
        MT_b = P.buf("MT")
        GT = A.alloc([128, 128], BF16)
        GT_b = P.buf("GT")
        yt = A.alloc([128, 2 * D], F32)
        yt_b = P.buf("yt")
        tmp = A.alloc([128, 512], F32)
        tmp_b = P.buf("stmp")
        HT = A.alloc([128, 4, 512], F32)
        HTb = A.alloc([128, 4, 512], BF16)
        HT_b = P.bufs(4, "HT")
        P.memset("pool", HT, 0.0, writes=HT_b)
        P.memset("pool", HTb, 0.0, writes=HT_b)
        ssq = A.alloc([128, 8], F32)
        ssq_b = P.buf("ssq")
        yo = [A.alloc([128, 2 * D], BF16) for i in range(2)]
        yo_b = P.bufs(2, "yo")
        yo_sem = [P.dsem(f"yo{i}") for i in range(2)]
        def s2_load(c):
            sl = c % 2
            tile = slice(c * 128, (c + 1) * 128)
            P.dma("sp", xB_sem[sl], xB[sl], self.s_xB[tile, :], reads=[self.s_xB_b[c // 4]], writes=[xB_b[sl]])
            P.dma("sp", zz_sem[sl], zz[sl], self.s_z[tile, :], reads=[self.s_z_b[c]], writes=[zz_b[sl]])
            P.dma("sp", bct_sem[sl], bct[sl], self.s_BCT[:, :, tile].rearrange("g n t -> n g t"),
                  reads=[self.s_BCT_b[c // 4]], writes=[bct_b[sl]])

        s2_load(0)
        for c in range(NT):
            sl = c % 2
            tile = slice(c * 128, (c + 1) * 128)
            if c + 1 < NT:
                s2_load(c + 1)
            xv = xB[sl][:, 0:2048].rearrange("p (h e) -> p h e", h=32)
            a_c = atk[:, c, :]
            pA, pA_b = self.bank(0)[:, 0:32], self.pb[0]
            pL = self.bank(0)[:, 32:64]
            P.mm(pA, tri, a_c, True, True, reads=[k_b, dt_b], writes=[pA_b])
            P.mm(pL, self.ones[:], a_c, False, True, reads=[self.c_b, dt_b], writes=[pA_b])
            P.copy("dve", acum, pA, reads=[pA_b], writes=[sc_b])
            P.act(expA, pA, AF.Exp, reads=[pA_b], writes=[sc_b])
            P.act(dL, pL, AF.Exp, reads=[pA_b], writes=[sc_b])
            P.tt("dve", expLA, pL, acum, ALU.subtract, reads=[pA_b, sc_b], writes=[sc_b])
            P.act(expLA, expLA, AF.Exp, reads=[sc_b], writes=[sc_b])
            P.tt("dve", xd, xv, dtk[:, c, :].unsqueeze(2).to_broadcast([128, 32, 64]), ALU.mult,
                 reads=[xB_b[sl], dt_b], writes=[xd_b])
            P.tt("pool", xdw, xd, expLA.unsqueeze(2).to_broadcast([128, 32, 64]), ALU.mult, reads=[xd_b, sc_b], writes=[xd_b])
            for g in range(4):
                BTg = bct[sl][:, g, :]
                CTg = bct[sl][:, 4 + g, :]
                Btok = xB[sl][:, 2048 + g * 128:2048 + (g + 1) * 128]
                pG, pG_b = self.bank(1)[:, 0:128], self.pb[1]
                P.mm(pG, BTg, CTg, True, True, reads=[bct_b[sl]], writes=[pG_b])
                P.copy("act", GT, pG, reads=[pG_b], writes=[GT_b])
                P.tt("dve", R1, tri.unsqueeze(1).to_broadcast([128, 8, 128]),
                     a_c[:, g * 8:(g + 1) * 8].unsqueeze(2).to_broadcast([128, 8, 128]), ALU.mult,
                     reads=[k_b, dt_b], writes=[R1_b])
                for hb in range(2):
                    pD, pD_b = self.bank(2 + hb).rearrange("p (h t) -> p h t", h=4), self.pb[2 + hb]
                    P.mm(pD, self.ones[:], R1[:, hb * 4:(hb + 1) * 4, :], True, False, reads=[self.c_b, R1_b], writes=[pD_b])
                    P.mm(pD, self.ident[:], nb4, False, False, reads=[self.c_b, k_b], writes=[pD_b])
                    for h4 in range(4):
                        P.mm(pD[:, h4, :], R1[:, hb * 4 + h4, :], negones, False, h4 == 3, reads=[R1_b, k_b], writes=[pD_b])
                    P.act(E[:, hb * 4:(hb + 1) * 4, :], pD, AF.Exp, reads=[pD_b], writes=[E_b])
                P.tt("dve", MT, E, GT.unsqueeze(1).to_broadcast([128, 8, 128]), ALU.mult, reads=[E_b, GT_b], writes=[MT_b])
                pY, pY_b = self.bank(4).rearrange("p (h e) -> p h e", h=8), self.pb[4]
                for h8 in range(8):
                    P.mm(pY[:, h8, :], MT[:, h8, :], xd[:, g * 8 + h8, :], h8 == 0, h8 == 7, reads=[MT_b, xd_b], writes=[pY_b])
                pO, pO_b = self.bank(5), self.pb[5]
                P.mm(pO, CTg, HTb[:, g, :], True, True, reads=[bct_b[sl], HT_b[g]], writes=[pO_b])
                pS, pS_b = self.bank(6), self.pb[6]
                P.mm(pS, Btok, xdw[:, g * 8:(g + 1) * 8, :], True, True, reads=[xB_b[sl], xd_b], writes=[pS_b])
                P.tt("dve", tmp.rearrange("p (h e) -> p h e", h=8), pO.rearrange("p (h e) -> p h e", h=8),
                     expA[:, g * 8:(g + 1) * 8].unsqueeze(2).to_broadcast([128, 8, 64]), ALU.mult,
                     reads=[pO_b, sc_b], writes=[tmp_b])
                P.tt("dve", yt[:, g * 512:(g + 1) * 512], tmp, self.bank(4), ALU.add, reads=[tmp_b, pY_b], writes=[yt_b])
                Hg = HT[:, g, :]
                P.tt("pool", Hg.rearrange("p (h e) -> p h e", h=8), Hg.rearrange("p (h e) -> p h e", h=8),
                     dL[:, g * 8:(g + 1) * 8].unsqueeze(2).to_broadcast([128, 8, 64]), ALU.mult,
                     reads=[sc_b, HT_b[g]], writes=[HT_b[g]])
                P.tt("dve", Hg, Hg, pS, ALU.add, reads=[pS_b, HT_b[g]], writes=[HT_b[g]])
                P.copy("act", HTb[:, g, :], Hg, reads=[HT_b[g]], writes=[HT_b[g]])
            xs_ = self.xt[0][:, :].bitcast(BF16)
            ytv = yt.rearrange("p (h e) -> p h e", h=32)
            P.tt("pool", xd, xv, tc_[:, 64:96].unsqueeze(2).to_broadcast([128, 32, 64]), ALU.mult,
                 reads=[xB_b[sl], k_b, pY_b, pS_b], writes=[xd_b])
            P.tt("dve", ytv, ytv, xd, ALU.add, reads=[xd_b, yt_b], writes=[yt_b])
            P.tt("dve", yt, yt, zz[sl], ALU.mult, reads=[zz_b[sl], yt_b], writes=[yt_b])
            o_, o_b = yo[sl], yo_b[sl]
            for g in range(4):
                P.act(o_[:, g * 512:(g + 1) * 512], yt[:, g * 512:(g + 1) * 512], AF.Square, reads=[yt_b],
                      writes=[o_b, ssq_b], accum_out=ssq[:, g:g + 1])
            P.ts("dve", ssq[:, 4:8], ssq[:, 0:4], 1.0 / 512, 1e-5, ALU.mult, ALU.add, reads=[ssq_b], writes=[ssq_b])
            P.act(ssq[:, 4:8], ssq[:, 4:8], AF.Sqrt, reads=[ssq_b], writes=[ssq_b])
            P.op("dve", lambda e: e.reciprocal(ssq[:, 4:8], ssq[:, 4:8]), reads=[ssq_b], writes=[ssq_b])
            P.tt("pool", yt, yt, nw, ALU.mult, reads=[yt_b, k_b], writes=[yt_b])
            P.tt("dve", o_.rearrange("p (g e) -> p g e", g=4), yt.rearrange("p (g e) -> p g e", g=4),
                 ssq[:, 4:8].unsqueeze(2).to_broadcast([128, 4, 512]), ALU.mult, reads=[yt_b, ssq_b], writes=[o_b])
            P.dma("sp", yo_sem[sl], self.ob[tile, :], o_, reads=[o_b], writes=[self.ob_b[c]])
        self.tm_proj_phase(16, self.ssd_wout, res, res_b, dst, dst_b, "ssd")

    def rwkv(self, res, res_b, dst, dst_b):
        P, A = self.P, self.A
        A.reset()
        sem = P.dsem("rw")
        k_b = P.buf("rwc")
        rp = A.alloc([128, 88], F32)
        P.dma("sp", sem, rp, self.rw_p, writes=[k_b])
        cc_ = A.alloc([128, 322], BF16)
        P.dma("sp", sem, cc_, self.rw_c, writes=[k_b])
        bones = cc_[:, 0:128]
        hsel = cc_[:, 128:130]
        maskG = cc_[:, 130:258]
        maskA = cc_[0:64, 258:322]
        mu = rp[:, 0:48].rearrange("p (i c) -> p i c", i=6)
        w0, a0, kkp, kap, rkp = (rp[:, 48 + 8 * i:56 + 8 * i] for i in range(5))
        nw0 = A.alloc([128, 8], F32)
        P.ts("dve", nw0, w0, -1.0, None, ALU.mult, reads=[k_b], writes=[k_b])
        mhalf = A.alloc([128, 1], F32)
        P.memset("pool", mhalf, -0.5, writes=[k_b])
        PLx = A.alloc([128, 8, 64], F32)
        PLx_b = P.buf("PLx")
        mark = A.off
        BT = 256
        NQ = BT // 128
        NCB = BT // 64
        W3 = [A.alloc([128, DC, D], BF16) for i in range(3)]
        w_b = P.buf("rww")
        for i in range(3):
            self.wload(W3[i], self.rw_rkv[i].rearrange("(c p) e -> p c e", p=128), sem, w_b, nsplit=DC)
        w1 = A.alloc([128, DC, 64], BF16)
        a1 = A.alloc([128, DC, 64], BF16)
        g1 = A.alloc([128, DC, 160], BF16)
        w2 = A.alloc([64, D], BF16)
        a2 = A.alloc([64, D], BF16)
        g2a = A.alloc([128, D], BF16)
        g2b = A.alloc([32, D], BF16)
        P.dma("pool", sem, w1, self.rw_w1.rearrange("(c p) e -> p c e", p=128), writes=[w_b])
        P.dma("pool", sem, a1, self.rw_a1.rearrange("(c p) e -> p c e", p=128), writes=[w_b])
        P.dma("pool", sem, g1, self.rw_g1.rearrange("(c p) e -> p c e", p=128), writes=[w_b])
        P.dma("pool", sem, w2, self.rw_w2, writes=[w_b])
        P.dma("pool", sem, a2, self.rw_a2, writes=[w_b])
        P.dma("pool", sem, g2a, self.rw_g2[0:128, :], writes=[w_b])
        P.dma("pool", sem, g2b, self.rw_g2[128:160, :], writes=[w_b])
        dT = A.alloc([128, DC, BT], BF16)
        dT_b = P.buf("dT")
        xm = [A.alloc([128, DC, BT], BF16) for i in range(2)]
        xm_b = P.bufs(2, "xm")
        hw = A.alloc([64, BT], BF16)
        ha = A.alloc([64, BT], BF16)
        hga = A.alloc([128, BT], BF16)
        hgb = A.alloc([32, BT], BF16)
        h_b = P.buf("rwh")
        vtok = A.alloc([128, NQ, D], BF16)
        vt_b = P.buf("vtok")
        vt_sem = P.dsem("vtok")
        gtok = A.alloc([128, NQ, D], BF16)
        gt_b = P.buf("gtok")
        gt_sem = P.dsem("gtok")
        F = [A.alloc([128, BT], F32) for i in range(10)]
        F_b = P.bufs(10, "rwF")
        sqb = A.alloc([128, BT], BF16)
        sqb_b = P.buf("sqb")
        ARt = A.alloc([128, NCB, 2, 64], BF16)
        BKt = A.alloc([128, NCB, 2, 64], BF16)
        AB_b = P.buf("ARt")
        AB_sem = P.dsem("ARt")
        bkh = A.alloc([128, 2, BT], BF16)
        bkh_b = P.buf("bkh")
        stg = A.alloc([128, 2, NQ, 128], BF16)
        stg_b = P.buf("rstg")
        stg_sem = P.dsem("rstg")
        rk = A.alloc([128, BT], BF16)
        rk_b = P.buf("rk")
        bon = A.alloc([128, NQ, 16], F32)
        bon_b = P.buf("bon")
        nxm = [0]

        def mk_xm(i, blk):
            sl = nxm[0] % 2
            nxm[0] += 1
            for c in range(DC):
                P.stt("dve", xm[sl][:, c, :], dT[:, c, :], mu[:, i, c:c + 1], self.hnT[:, c, blk],
                      ALU.mult, ALU.add, reads=[dT_b, k_b] + list(self.hnT_b), writes=[xm_b[sl]])
            return xm[sl], xm_b[sl]

        np_ = [0]

        def pbank(n=None):
            np_[0] += 1
            return self.bank(np_[0] % 4)[:, 0:(BT if n is None else n)], self.pb[np_[0] % 4]

        for tb in range(S // BT):
            t0 = tb * BT
            blk = slice(t0, t0 + BT)
            rb_ = self.r1_b[t0 // 512]
            if tb == 0:
                P.ts("dve", dT[:, :, 0:1], self.hnT[:, :, 0:1], -1.0, None, ALU.mult, reads=list(self.hnT_b), writes=[dT_b])
                P.tt("dve", dT[:, :, 1:BT], self.hnT[:, :, 0:BT - 1], self.hnT[:, :, 1:BT], ALU.subtract,
                     reads=list(self.hnT_b), writes=[dT_b])
            else:
                P.tt("dve", dT, self.hnT[:, :, t0 - 1:t0 + BT - 1], self.hnT[:, :, blk], ALU.subtract,
                     reads=list(self.hnT_b), writes=[dT_b])
            x_, x_b = mk_xm(3, blk)
            ps, ps_b = pbank()
            for kc in range(DC):
                P.mm(ps[0:64, :], w1[:, kc, :], x_[:, kc, :], kc == 0, kc == DC - 1, reads=[w_b, x_b], writes=[ps_b])
            P.act(hw, ps[0:64, :], AF.Tanh, reads=[ps_b], writes=[h_b])
            x_, x_b = mk_xm(4, blk)
            ps, ps_b = pbank()
            for kc in range(DC):
                P.mm(ps[0:64, :], a1[:, kc, :], x_[:, kc, :], kc == 0, kc == DC - 1, reads=[w_b, x_b], writes=[ps_b])
            P.copy("act", ha, ps[0:64, :], reads=[ps_b], writes=[h_b])
            x_, x_b = mk_xm(5, blk)
            ps, ps_b = pbank()
            for kc in range(DC):
                P.mm(ps, g1[:, kc, 0:128], x_[:, kc, :], kc == 0, kc == DC - 1, reads=[w_b, x_b], writes=[ps_b])
            P.act(hga, ps, AF.Sigmoid, reads=[ps_b], writes=[h_b])
            ps, ps_b = pbank()
            for kc in range(DC):
                P.mm(ps[0:32, :], g1[:, kc, 128:160], x_[:, kc, :], kc == 0, kc == DC - 1, reads=[w_b, x_b], writes=[ps_b])
            P.act(hgb, ps[0:32, :], AF.Sigmoid, reads=[ps_b], writes=[h_b])
            for q in range(NQ):
                for cb in range(2):
                    ps, ps_b = pbank(512)
                    P.mm(ps, hga[:, q * 128:(q + 1) * 128], g2a[:, cb * 512:(cb + 1) * 512], True, False, reads=[h_b, w_b], writes=[ps_b])
                    P.mm(ps, hgb[:, q * 128:(q + 1) * 128], g2b[:, cb * 512:(cb + 1) * 512], False, True, reads=[h_b, w_b], writes=[ps_b])
                    P.copy("act", gtok[:, q, cb * 512:(cb + 1) * 512], ps, reads=[ps_b], writes=[gt_b])
            P.dma("sp", gt_sem, self.r_g[blk, :].rearrange("(q p) c -> p q c", p=128), gtok, reads=[gt_b], writes=[rb_])
            x_, x_b = mk_xm(2, blk)
            for q in range(NQ):
                for cb in range(2):
                    ps, ps_b = pbank(512)
                    for kc in range(DC):
                        P.mm(ps, x_[:, kc, q * 128:(q + 1) * 128], W3[2][:, kc, cb * 512:(cb + 1) * 512], kc == 0, kc == DC - 1,
                             reads=[w_b, x_b], writes=[ps_b])
                    P.copy("act", vtok[:, q, cb * 512:(cb + 1) * 512], ps, reads=[ps_b], writes=[vt_b])
            P.dma("sp", vt_sem, self.r_v[blk, :].rearrange("(q p) c -> p q c", p=128), vtok, reads=[vt_b], writes=[rb_])
            xr, xr_b = mk_xm(0, blk)
            xk, xk_b = mk_xm(1, blk)
            pbon, pbon_b = self.bank(7)[:, 0:NQ * 16].rearrange("p (q h) -> p q h", q=NQ), self.pb[7]
            for e in range(DC):
                ec = slice(e * 128, (e + 1) * 128)
                r_s, k_s, lw, a_s, kk, t1, t2, lpA, lpB, t3 = F
                (r_sb, k_sb, lw_b, a_sb, kk_b, t1_b, t2_b, lpA_b, lpB_b, t3_b) = F_b
                ps, ps_b = pbank()
                for kc in range(DC):
                    P.mm(ps, W3[0][:, kc, ec], xr[:, kc, :], kc == 0, kc == DC - 1, reads=[w_b, xr_b], writes=[ps_b])
                P.copy("act", r_s, ps, reads=[ps_b], writes=[r_sb])
                ps, ps_b = pbank()
                for kc in range(DC):
                    P.mm(ps, W3[1][:, kc, ec], xk[:, kc, :], kc == 0, kc == DC - 1, reads=[w_b, xk_b], writes=[ps_b])
                P.copy("act", k_s, ps, reads=[ps_b], writes=[k_sb])
                ps, ps_b = pbank()
                P.mm(ps, w2[:, ec], hw, True, True, reads=[w_b, h_b], writes=[ps_b])
                P.act(t1, ps, AF.Exp, reads=[ps_b, k_b], writes=[t1_b], scale=-1.0, bias=nw0[:, e:e + 1])
                P.act(t1, t1, AF.Ln, reads=[t1_b], writes=[t1_b], bias=1.0)
                P.act(t1, t1, AF.Exp, reads=[t1_b, k_b], writes=[t1_b], scale=-1.0, bias=mhalf)
                P.ts("dve", lw, t1, -1.0, None, ALU.mult, reads=[t1_b], writes=[lw_b])
                ps, ps_b = pbank()
                P.mm(ps, a2[:, ec], ha, True, True, reads=[w_b, h_b], writes=[ps_b])
                P.act(a_s, ps, AF.Sigmoid, reads=[ps_b, k_b], writes=[a_sb], bias=a0[:, e:e + 1])
                P.ts("dve", kk, k_s, kkp[:, e:e + 1], None, ALU.mult, reads=[k_sb, k_b], writes=[kk_b])
                P.act(sqb, kk, AF.Square, reads=[kk_b], writes=[sqb_b])
                ps, ps_b = pbank()
                P.mm(ps, bones, sqb, True, True, reads=[k_b, sqb_b], writes=[ps_b])
                P.ts("dve", t2, ps, 1e-24, None, ALU.max, reads=[ps_b], writes=[t2_b])
                P.act(t2, t2, AF.Sqrt, reads=[t2_b], writes=[t2_b])
                P.op("dve", lambda e_, t2=t2: e_.reciprocal(t2, t2), reads=[t2_b], writes=[t2_b])
                P.tt("dve", kk, kk, t2, ALU.mult, reads=[t2_b, kk_b], writes=[kk_b])
                P.ts("pool", t1, a_s, -1.0, kap[:, e:e + 1], ALU.add, ALU.mult, reads=[a_sb, k_b], writes=[t1_b])
                P.stt("dve", k_s, t1, 1.0, k_s, ALU.add, ALU.mult, reads=[t1_b, k_sb], writes=[k_sb])
                P.tt("pool", a_s, kk, a_s, ALU.mult, reads=[kk_b, a_sb], writes=[a_sb])
                v3 = lambda ap: ap.rearrange("p (c t) -> p c t", t=64)
                src_, src_bb = lw, lw_b
                pp_ = [(lpA, lpA_b), (lpB, lpB_b)]
                for si, sft in enumerate((1, 2, 4, 8, 16, 32)):
                    dst_, dst_bb = pp_[si % 2]
                    P.copy("pool", v3(dst_)[:, :, 0:sft], v3(src_)[:, :, 0:sft], reads=[src_bb], writes=[dst_bb])
                    P.tt("dve", v3(dst_)[:, :, sft:64], v3(src_)[:, :, sft:64], v3(src_)[:, :, 0:64 - sft], ALU.add,
                         reads=[src_bb], writes=[dst_bb])
                    src_, src_bb = dst_, dst_bb
                lp, lp_b = src_, src_bb
                P.tt("dve", t2, lp, lw, ALU.subtract, reads=[lp_b, lw_b], writes=[t2_b])
                P.act(t2, t2, AF.Exp, reads=[t2_b], writes=[t2_b])
                P.stt("dve", ARt[:, :, 0, :], v3(kk), -1.0, v3(t2), ALU.mult, ALU.mult, reads=[kk_b, t2_b], writes=[AB_b])
                P.act(t2, lp, AF.Exp, reads=[lp_b], writes=[t2_b])
                P.tt("dve", ARt[:, :, 1, :], v3(r_s), v3(t2), ALU.mult, reads=[r_sb, t2_b], writes=[AB_b])
                P.act(t2, lp, AF.Exp, reads=[lp_b], writes=[t2_b], scale=-1.0)
                P.tt("dve", BKt[:, :, 0, :], v3(a_s), v3(t2), ALU.mult, reads=[a_sb, t2_b], writes=[AB_b])
                P.tt("pool", BKt[:, :, 1, :], v3(k_s), v3(t2), ALU.mult, reads=[k_sb, t2_b], writes=[AB_b])
                P.tt("dve", v3(t3), v3(lp)[:, :, 63:64].to_broadcast([128, NCB, 64]), v3(lp), ALU.subtract, reads=[lp_b], writes=[t3_b])
                P.act(t3, t3, AF.Exp, reads=[t3_b], writes=[t3_b])
                P.act(PLx[:, e, tb * NCB:(tb + 1) * NCB], v3(lp)[:, :, 63], AF.Exp, reads=[lp_b], writes=[PLx_b])
                P.tt("dve", bkh[:, 0, :], a_s, t3, ALU.mult, reads=[a_sb, t3_b], writes=[bkh_b])
                P.tt("pool", bkh[:, 1, :], k_s, t3, ALU.mult, reads=[k_sb, t3_b], writes=[bkh_b])
                tp = self.psum[:, 4 * 512:6 * 512].bitcast(BF16)[:, 0:2 * NQ * 128].rearrange("p (i q c) -> p i q c", i=2, q=NQ)
                tp_b = self.pb[4]
                for i in range(2):
                    for q in range(NQ):
                        P.tr(tp[:, i, q, :], bkh[:, i, q * 128:(q + 1) * 128], self.ident[:], reads=[bkh_b, self.c_b], writes=[tp_b])
                P.copy("act", stg, tp, reads=[tp_b], writes=[stg_b])
                P.dma("sp", stg_sem, self.r_bh[blk, ec].rearrange("(q p) c -> p q c", p=128), stg[:, 0], reads=[stg_b], writes=[rb_])
                P.dma("sp", stg_sem, self.r_kh[blk, ec].rearrange("(q p) c -> p q c", p=128), stg[:, 1], reads=[stg_b], writes=[rb_])
                P.dma("sp", AB_sem, self.r_AR[e, :, tb * NCB:(tb + 1) * NCB, :, :], ARt, reads=[AB_b], writes=[rb_])
                P.dma("sp", AB_sem, self.r_BK[e, :, tb * NCB:(tb + 1) * NCB, :, :], BKt, reads=[AB_b], writes=[rb_])
                P.stt("dve", rk, r_s, rkp[:, e:e + 1], k_s, ALU.mult, ALU.mult, reads=[r_sb, k_sb, k_b], writes=[rk_b])
                for q in range(NQ):
                    P.mm(pbon[:, q, 2 * e:2 * e + 2], rk[:, q * 128:(q + 1) * 128], hsel, e == 0 and q == 0, e == 7 and q == NQ - 1,
                         reads=[rk_b, k_b], writes=[pbon_b])
            P.copy("dve", bon, pbon, reads=[pbon_b], writes=[bon_b])
            for q in range(NQ):
                P.tt("dve", gtok[:, q, :].rearrange("p (h n) -> p h n", h=16), vtok[:, q, :].rearrange("p (h n) -> p h n", h=16),
                     bon[:, q, :].unsqueeze(2).to_broadcast([128, 16, 64]), ALU.mult, reads=[vt_b, bon_b], writes=[gt_b])
            P.dma("sp", gt_sem, self.r_bv[blk, :].rearrange("(q p) c -> p q c", p=128), gtok, reads=[gt_b], writes=[rb_])
        pl_sem = P.dsem("plx")
        plb = P.buf("plxd")
        P.dma("sp", pl_sem, self.r_pl, PLx, reads=[PLx_b], writes=[plb])
        A.reset(mark)
        PL2 = A.alloc([64, 16, 64], F32)
        PL2_b = P.buf("PL2")
        P.dma("sp", pl_sem, PL2, self.r_pl.rearrange("(a n) e c -> n e a c", a=2), reads=[plb], writes=[PL2_b])
        ARc = [A.alloc([64, 16, 2, 64], BF16) for i in range(2)]
        BKc = [A.alloc([64, 16, 2, 64], BF16) for i in range(2)]
        BH = [A.alloc([64, D], BF16) for i in range(2)]
        KH = [A.alloc([64, D], BF16) for i in range(2)]
        Vt = [A.alloc([64, D], BF16) for i in range(2)]
        Ut = [A.alloc([64, D], BF16) for i in range(2)]
        in_b = P.bufs(2, "r2in")
        uv_b = P.bufs(2, "r2uv")
        in_sem = [P.dsem(f"r2in{i}") for i in range(2)]
        Gb = [A.alloc([64, 8, 128], BF16) for i in range(2)]
        Gk = [A.alloc([64, 8, 128], BF16) for i in range(2)]
        Gm_b = P.bufs(2, "Gm")
        Ap = [A.alloc([64, 8, 64], BF16) for i in range(2)]
        Mp = [A.alloc([64, 8, 64], BF16) for i in range(2)]
        Ap_b, Mp_b = P.bufs(2, "Ap"), P.bufs(2, "Mp")
        TT = [[A.alloc([64, 8, 64], BF16) for i in range(2)] for j in range(2)]
        TT_b = [P.bufs(2, "TT") for j in range(2)]
        Xs = A.alloc([64, 8, 64], BF16)
        Xs_b = P.buf("Xs")
        H = A.alloc([64, 16, 64], F32)
        Hb = A.alloc([64, 16, 64], BF16)
        H_b = P.bufs(2, "H")
        P.memset("pool", H, 0.0, writes=H_b)
        P.memset("pool", Hb, 0.0, writes=H_b)
        ych = [A.alloc([64, D], F32) for i in range(2)]
        ych_b = P.bufs(2, "ych")
        ych_sem = [P.dsem(f"ych{i}") for i in range(2)]
        idb = self.ident[0:64, 0:64].unsqueeze(1).to_broadcast([64, 8, 64])
        mG = maskG[0:64, :].unsqueeze(1).to_broadcast([64, 8, 128])
        ARd = self.r_AR.rearrange("e (a n) c x t -> n (e a) c x t", a=2)
        BKd = self.r_BK.rearrange("e (a n) c x t -> n (e a) c x t", a=2)
        v8 = lambda bk: self.bank(bk)[0:64, :].rearrange("p (h t) -> p h t", h=8)
        TTfin = {}

        def gen_A(u):
            c, hf = u // 2, u % 2
            sl = c % 2
            up = u % 2
            rows = slice(c * 64, (c + 1) * 64)
            if hf == 0:
                rb_ = [self.r1_b[c // 8]]
                P.dma("sp", in_sem[sl], ARc[sl], ARd[:, :, c, :, :], reads=rb_, writes=[in_b[sl]])
                P.dma("sp", in_sem[sl], BKc[sl], BKd[:, :, c, :, :], reads=rb_, writes=[in_b[sl]])
                P.dma("sp", in_sem[sl], BH[sl], self.r_bh[rows, :], reads=rb_, writes=[in_b[sl]])
                P.dma("sp", in_sem[sl], KH[sl], self.r_kh[rows, :], reads=rb_, writes=[in_b[sl]])
                P.dma("sp", in_sem[sl], Vt[sl], self.r_v[rows, :], reads=rb_, writes=[in_b[sl]])
            hds = list(range(8 * hf, 8 * hf + 8))
            pGb = self.psum[0:64, 0:1024].rearrange("p (h t) -> p h t", h=8)
            pGk = self.psum[0:64, 1024:2048].rearrange("p (h t) -> p h t", h=8)
            pAm = v8(4)
            for hi, h in enumerate(hds):
                P.mm(pGb[:, hi, :], BKc[sl][:, h, 0, :], ARc[sl][:, h, :, :], hi % 4 == 0, hi % 4 == 3, reads=[in_b[sl]], writes=[self.pb[0]])
            for hi, h in enumerate(hds):
                P.mm(pGk[:, hi, :], BKc[sl][:, h, 1, :], ARc[sl][:, h, :, :], hi % 4 == 0, hi % 4 == 3, reads=[in_b[sl]], writes=[self.pb[2]])
            for hi, h in enumerate(hds):
                P.mm(pAm[:, hi, :], ARc[sl][:, h, 0, :], BKc[sl][:, h, 0, :], hi == 0, hi == 7, reads=[in_b[sl]], writes=[self.pb[4]])
            P.tt("dve", Gb[up], pGb, mG, ALU.mult, reads=[self.pb[0], k_b], writes=[Gm_b[up]])
            P.tt("dve", Gk[up], pGk, mG, ALU.mult, reads=[self.pb[2], k_b], writes=[Gm_b[up]])
            P.copy("pool", Mp[0], Gb[up][:, :, 0:64], reads=[Gm_b[up]], writes=[Mp_b[0]])
            P.tt("dve", Ap[0], pAm, maskA.unsqueeze(1).to_broadcast([64, 8, 64]), ALU.mult, reads=[self.pb[4], k_b], writes=[Ap_b[0]])
            P.tt("pool", TT[up][0], Mp[0], idb, ALU.add, reads=[Mp_b[0], self.c_b], writes=[TT_b[up][0]])
            yield
            cur = 0
            for rd in range(5):
                nxt = 1 - cur
                pA2, pM2, pT2 = v8(4), v8(5), v8(6)
                for hi in range(8):
                    P.mm(pA2[:, hi, :], Mp[cur][:, hi, :], Ap[cur][:, hi, :], hi == 0, hi == 7,
                         reads=[Mp_b[cur], Ap_b[cur]], writes=[self.pb[4]])
                if rd < 4:
                    for hi in range(8):
                        P.mm(pM2[:, hi, :], Ap[cur][:, hi, :], Mp[cur][:, hi, :], hi == 0, hi == 7,
                             reads=[Mp_b[cur], Ap_b[cur]], writes=[self.pb[5]])
                P.copy("act", Ap[nxt], pA2, reads=[self.pb[4]], writes=[Ap_b[nxt]])
                if rd < 4:
                    P.copy("dve", Mp[nxt], pM2, reads=[self.pb[5]], writes=[Mp_b[nxt]])
                yield
                for hi in range(8):
                    P.mm(pT2[:, hi, :], Ap[nxt][:, hi, :], TT[up][cur][:, hi, :], hi == 0, hi == 7,
                         reads=[Ap_b[nxt], TT_b[up][cur]], writes=[self.pb[6]])
                P.tt("dve", TT[up][nxt], TT[up][cur], pT2, ALU.add, reads=[self.pb[6], TT_b[up][cur]], writes=[TT_b[up][nxt]])
                cur = nxt
                yield
            TTfin[u] = (TT[up][cur], TT_b[up][cur])

        def gen_B(u):
            c, hf = u // 2, u % 2
            sl = c % 2
            up = u % 2
            rows = slice(c * 64, (c + 1) * 64)
            hds = list(range(8 * hf, 8 * hf + 8))
            TTf, TTf_b = TTfin[u]
            pX = v8(7)
            b7 = self.pb[7]
            for hi, h in enumerate(hds):
                hc = slice(h * 64, h * 64 + 64)
                P.mm(pX[:, hi, :], ARc[sl][:, h, 0, :], Hb[:, h, :], hi == 0, False, reads=[in_b[sl], H_b[hf]], writes=[b7])
                P.mm(pX[:, hi, :], Gk[up][:, hi, 0:64], Vt[sl][:, hc], False, hi == 7, reads=[Gm_b[up], in_b[sl]], writes=[b7])
            P.copy("act", Xs, pX, reads=[b7], writes=[Xs_b])
            yield
            for hi, h in enumerate(hds):
                P.mm(pX[:, hi, :], TTf[:, hi, :], Xs[:, hi, :], hi == 0, hi == 7, reads=[TTf_b, Xs_b], writes=[b7])
            h0 = 8 * hf * 64
            P.copy("act", Ut[sl][:, h0:h0 + 512], self.bank(7)[0:64, :], reads=[b7], writes=[uv_b[sl]])
            yield
            for hi, h in enumerate(hds):
                hc = slice(h * 64, h * 64 + 64)
                P.mm(pX[:, hi, :], ARc[sl][:, h, 1, :], Hb[:, h, :], hi == 0, False, reads=[in_b[sl], H_b[hf]], writes=[b7])
                P.mm(pX[:, hi, :], Gb[up][:, hi, 64:128], Ut[sl][:, hc], False, False, reads=[Gm_b[up], uv_b[sl]], writes=[b7])
                P.mm(pX[:, hi, :], Gk[up][:, hi, 64:128], Vt[sl][:, hc], False, hi == 7, reads=[Gm_b[up], in_b[sl]], writes=[b7])
            P.copy("act", ych[sl][:, h0:h0 + 512], self.bank(7)[0:64, :], reads=[b7], writes=[ych_b[sl]])
            yield
            for hi, h in enumerate(hds):
                hc = slice(h * 64, h * 64 + 64)
                P.mm(pX[:, hi, :], BH[sl][:, hc], Ut[sl][:, hc], hi == 0, False, reads=[in_b[sl], uv_b[sl]], writes=[b7])
                P.mm(pX[:, hi, :], KH[sl][:, hc], Vt[sl][:, hc], False, hi == 7, reads=[in_b[sl]], writes=[b7])
            Hh = H[:, 8 * hf:8 * hf + 8, :]
            P.tt("dve", Hh, Hh, PL2[:, 8 * hf:8 * hf + 8, c:c + 1].to_broadcast([64, 8, 64]), ALU.mult,
                 reads=[PL2_b, H_b[hf]], writes=[H_b[hf]])
            P.tt("dve", Hh, Hh, pX, ALU.add, reads=[b7, H_b[hf]], writes=[H_b[hf]])
            P.copy("pool", Hb[:, 8 * hf:8 * hf + 8, :], Hh, reads=[H_b[hf]], writes=[H_b[hf]])
            if hf == 1:
                P.dma("sp", ych_sem[sl], self.r_y[rows, :], ych[sl], reads=[ych_b[sl]], writes=[self.ry_b[c // 2]])
            yield

        NU = 128
        for _ in gen_A(0):
            pass
        for u in range(1, NU + 1):
            ga = gen_A(u) if u < NU else iter(())
            gb = gen_B(u - 1)
            da = db = False
            while not (da and db):
                if not da:
                    try:
                        next(ga)
                    except StopIteration:
                        da = True
                if not db:
                    try:
                        next(gb)
                    except StopIteration:
                        db = True
        A.reset(mark)
        gn = A.alloc([128, 2, D], F32)
        P.dma("sp", sem, gn, self.rw_gn, writes=[k_b])
        yt = [A.alloc([128, D], F32) for i in range(2)]
        bvt = [A.alloc([128, D], BF16) for i in range(2)]
        gtt = [A.alloc([128, D], BF16) for i in range(2)]
        i3_b = P.bufs(2, "r3in")
        i3_sem = [P.dsem(f"r3in{i}") for i in range(2)]
        sqt = A.alloc([128, D], F32)
        sqt_b = P.buf("sqt")
        stt_ = A.alloc([128, 4, 16], F32)
        st_b = P.buf("r3st")
        ot = [A.alloc([128, D], BF16) for i in range(2)]
        ot_b = P.bufs(2, "r3o")
        ot_sem = [P.dsem(f"r3o{i}") for i in range(2)]
        v16 = lambda ap: ap.rearrange("p (h n) -> p h n", h=16)
        def r3_load(t):
            sl = t % 2
            tile = slice(t * 128, (t + 1) * 128)
            P.dma("sp", i3_sem[sl], yt[sl], self.r_y[tile, :], reads=[self.ry_b[t]], writes=[i3_b[sl]])
            P.dma("sp", i3_sem[sl], bvt[sl], self.r_bv[tile, :], reads=[self.r1_b[t // 4]], writes=[i3_b[sl]])
            P.dma("sp", i3_sem[sl], gtt[sl], self.r_g[tile, :], reads=[self.r1_b[t // 4]], writes=[i3_b[sl]])

        r3_load(0)
        for t in range(NT):
            sl = t % 2
            tile = slice(t * 128, (t + 1) * 128)
            if t + 1 < NT:
                r3_load(t + 1)
            y_ = yt[sl]
            mean, ex2, var, rstd = (stt_[:, i, :] for i in range(4))
            P.op("dve", lambda e_, y_=y_, mean=mean: e_.tensor_reduce(mean, v16(y_), AX.X, ALU.add), reads=[i3_b[sl]], writes=[st_b])
            P.act(sqt, y_, AF.Square, reads=[i3_b[sl]], writes=[sqt_b])
            P.op("dve", lambda e_, ex2=ex2: e_.tensor_reduce(ex2, v16(sqt), AX.X, ALU.add), reads=[sqt_b], writes=[st_b])
            P.ts("dve", mean, mean, 1.0 / 64, None, ALU.mult, reads=[st_b], writes=[st_b])
            P.stt("dve", var, mean, -1.0, mean, ALU.mult, ALU.mult, reads=[st_b], writes=[st_b])
            P.stt("dve", var, ex2, 1.0 / 64, var, ALU.mult, ALU.add, reads=[st_b], writes=[st_b])
            P.ts("dve", var, var, 64e-5, None, ALU.add, reads=[st_b], writes=[st_b])
            P.act(rstd, var, AF.Sqrt, reads=[st_b], writes=[st_b])
            P.op("dve", lambda e_, rstd=rstd: e_.reciprocal(rstd, rstd), reads=[st_b], writes=[st_b])
            P.tt("dve", v16(y_), v16(y_), mean.unsqueeze(2).to_broadcast([128, 16, 64]), ALU.subtract, reads=[st_b, i3_b[sl]], writes=[i3_b[sl]])
            P.tt("dve", v16(y_), v16(y_), rstd.unsqueeze(2).to_broadcast([128, 16, 64]), ALU.mult, reads=[st_b, i3_b[sl]], writes=[i3_b[sl]])
            P.tt("pool", y_, y_, gn[:, 0, :], ALU.mult, reads=[k_b, i3_b[sl]], writes=[i3_b[sl]])
            P.tt("pool", y_, y_, gn[:, 1, :], ALU.add, reads=[k_b, i3_b[sl]], writes=[i3_b[sl]])
            P.tt("dve", y_, y_, bvt[sl], ALU.add, reads=[i3_b[sl]], writes=[i3_b[sl]])
            P.tt("dve", ot[sl], y_, gtt[sl], ALU.mult, reads=[i3_b[sl]], writes=[ot_b[sl]])
            P.dma("sp", ot_sem[sl], self.ob[tile, 0:D], ot[sl], reads=[ot_b[sl]], writes=[self.ob_b[t]])
        self.tm_proj_phase(DC, self.rw_wo, res, res_b, dst, dst_b, "rwkv")

    def build(self):
        P = self.P
        src, src_b = self.x_in, self.xin_b
        pp = [(self.xa, self.xa_b), (self.xb, self.xb_b)]
        ip = 0
        for l in self.layers:
            if self.do_mix:
                self.norm_phase(src, src_b, self.gmix[:, l, :])
                dst, dst_b = pp[ip]
                ip ^= 1
                if l == 0:
                    self.moba(src, src_b, dst, dst_b)
                if l == 1:
                    self.rwkv(src, src_b, dst, dst_b)
                if l == 2:
                    self.ssd(src, src_b, dst, dst_b)
                if l == 3:
                    self.conformer(src, src_b, dst, dst_b)
                src, src_b = dst, dst_b
            if self.do_ffn:
                self.norm_phase(src, src_b, self.gffn[:, l, :])
                dst, dst_b = pp[ip]
                ip ^= 1
                self.ffn_phase(l, src, src_b, dst, dst_b)
                src, src_b = dst, dst_b
        self.A.reset()
        gf = self.A.alloc([128, D], F32)
        gf_b = P.buf("gf")
        P.dma("sp", self.c_sem, gf, self.norm_final, writes=[gf_b])
        for t in range(NT):
            xt, xt_b, sl, k = self.load_x(src, src_b, t)
            rstd, ss_b = self.rms_tile(xt, xt_b, k)
            P.act(xt[:], xt[:], AF.Copy, reads=[xt_b, ss_b], writes=[xt_b], scale=rstd)
            P.tt("dve", xt[:], xt[:], gf, ALU.mult, reads=[xt_b, gf_b], writes=[xt_b])
            self.store_x(self.out, self.out_b, t, xt, xt_b, sl)
        return P.finish()


def _fm(v, nch):
    v = np.asarray(v, np.float32)
    lead = v.shape[:-1]
    a = v.reshape(lead + (nch, 128))
    a = np.moveaxis(a, -1, 0)
    return np.ascontiguousarray(a)


def _bc(v, n=128):
    v = np.asarray(v, np.float32).reshape(1, -1)
    return np.ascontiguousarray(np.broadcast_to(v, (n, v.shape[1])))


def make_shared(inp):
    m = {}
    f = lambda k: np.ascontiguousarray(np.asarray(inp[k], np.float32))
    m["ident"] = np.eye(128, dtype=np.float32).astype(ml_dtypes.bfloat16)
    m["norm_mix"] = _fm(inp["norm_mix"], DC)
    m["norm_ffn"] = _fm(inp["norm_ffn"], DC)
    m["norm_final"] = _bc(inp["norm_final"])
    wu = np.asarray(inp["ffn_w_up"], np.float32).reshape(4, DC, 128, 2, FC, 128)
    m["ffn_w_up"] = np.ascontiguousarray(wu.transpose(0, 4, 2, 1, 3, 5)).reshape(4, FC, 128, DC * 2 * 128)
    wdn = np.asarray(inp["ffn_w_down"], np.float32).reshape(4, FC, 128, D)
    m["ffn_w_down"] = np.ascontiguousarray(wdn.transpose(0, 2, 1, 3))
    cw = np.asarray(inp["ffn_conv_w"], np.float32)
    cw = cw.transpose(0, 2, 1).reshape(4, 2 * FC, 128, 3)
    m["ffn_cw"] = np.ascontiguousarray(cw.transpose(2, 0, 1, 3))
    m["ffn_cb"] = _fm(inp["ffn_conv_b"], 2 * FC)
    m["moba_w_qkv"] = f("moba_w_qkv")[0]
    m["moba_w_o"] = f("moba_w_o")[0]
    kk = np.arange(S) // 256
    m["blkind"] = (kk[None, :] == np.arange(16)[:, None]).astype(np.float32).astype(ml_dtypes.bfloat16)
    qb = np.arange(16)[:, None]
    nn = np.arange(16)[None, :]
    mcst = np.stack([np.where(nn < qb, 0.0, -1e30), (nn < qb).astype(np.float32), (nn == qb).astype(np.float32)]).astype(np.float32)
    m["mconst"] = np.ascontiguousarray(np.broadcast_to(mcst[None], (128, 3, 16, 16)))
    m["tri"] = (np.arange(128)[None, :] >= np.arange(128)[:, None]).astype(np.float32).astype(ml_dtypes.bfloat16)
    m["rwkv_w_rkv"] = f("rwkv_w_rkv")[0]
    m["rwkv_w_o"] = f("rwkv_w_o")[0]
    for k_ in ("w1", "a1", "g1", "w2", "a2", "g2"):
        m["rwkv_" + k_] = f("rwkv_" + k_)[0]
    mu_ = _fm(inp["rwkv_mu"][0], 8).reshape(128, 48)
    m["rw_p"] = np.ascontiguousarray(np.concatenate(
        [mu_] + [_fm(np.asarray(inp["rwkv_" + k_][0]).reshape(-1), 8) for k_ in ("w0", "a0", "k_k", "k_a", "r_k")], axis=1))
    m["rw_gn"] = np.ascontiguousarray(np.stack([_bc(inp["rwkv_gn_w"][0]), _bc(inp["rwkv_gn_b"][0])], axis=1))
    i128 = np.arange(128)
    bones = (i128[:, None] // 64 == i128[None, :] // 64).astype(np.float32)
    hsel = (i128[:, None] // 64 == np.arange(2)[None, :]).astype(np.float32)
    s64 = i128[:, None] % 64
    t64 = i128[None, :] % 64
    maskG = np.where(i128[None, :] < 64, s64 < t64, s64 <= t64).astype(np.float32)
    maskA = np.zeros((128, 64), np.float32)
    maskA[:64] = (np.arange(64)[None, :] < np.arange(64)[:, None]).astype(np.float32)
    m["rw_c"] = np.ascontiguousarray(np.concatenate([bones, hsel, maskG, maskA], axis=1)).astype(ml_dtypes.bfloat16)
    m["ssd_w_in"] = f("ssd_w_in")[0]
    m["ssd_w_out"] = f("ssd_w_out")[0]
    wx = np.asarray(inp["ssd_w_in"], np.float32)[0][:, 2 * D:2 * D + 3072].reshape(DC, 128, 24, 128)
    m["ssd_wx"] = np.ascontiguousarray(wx.transpose(2, 1, 0, 3)).reshape(24, 128, DC * 128)
    scw = np.asarray(inp["ssd_conv_w"], np.float32)[0]
    scw = scw.T.reshape(24, 128, 4).transpose(1, 0, 2).reshape(128, 96)
    m["ssd_p"] = np.ascontiguousarray(np.concatenate([scw, _fm(inp["ssd_conv_b"][0], 24)], axis=1))
    m["ssd_t"] = np.ascontiguousarray(np.concatenate([_bc(inp["ssd_dt_bias"][0]), _bc(inp["ssd_a_log"][0]), _bc(inp["ssd_d"][0])], axis=1))
    m["ssd_nw"] = _bc(inp["ssd_norm_w"][0])
    nbm = np.where(np.arange(128)[:, None] > np.arange(128)[None, :], -30000.0, 0.0).astype(np.float32)
    m["nbmask"] = np.ascontiguousarray(np.broadcast_to(nbm[:, None, :], (128, 4, 128))).astype(ml_dtypes.bfloat16)
    m["conf_w_pw1"] = f("conf_w_pw1")[0]
    m["conf_w_pw2"] = f("conf_w_pw2")[0]
    dww = np.asarray(inp["conf_dw_w"], np.float32)[0]
    dww = dww.T.reshape(8, 128, 31).transpose(1, 0, 2).reshape(128, 8 * 31)
    m["conf_p"] = np.ascontiguousarray(np.concatenate([
        _fm(inp["conf_b_pw1"][0], 16), dww, _fm(inp["conf_dw_b"][0], 8),
        _fm(inp["conf_ln_w"][0], 8), _fm(inp["conf_ln_b"][0], 8)], axis=1))
    m["conf_b2"] = _bc(inp["conf_b_pw2"][0])
    return m


def make_inputs(inp, b, shared=None):
    m = dict(shared if shared is not None else make_shared(inp))
    m["x"] = np.ascontiguousarray(inp["x"][b], dtype=np.float32)
    return m


_NC = {}


def kernel(**inputs):
    if "nc" not in _NC:
        _NC["nc"] = Model().build()
    nc = _NC["nc"]
    shared = make_shared(inputs)
    in_maps = [make_inputs(inputs, b, shared) for b in range(8)]
    res = run_bass_kernel_spmd(nc, in_maps, core_ids=list(range(8)))
    return np.stack([np.asarray(r["out"], np.float32) for r in res.results], axis=0)
```
